# Optimizing a Trainium2 kernel written in Bass

```python
import math
import jax, jax.numpy as jnp
from jax import lax
import numpy as np

D_MODEL = 2048
BATCH = 4
SEQ = 2048
DEPTH = 4
DEC_BATCH = 8
DEC_SEQ = 1
PAST_LEN = 16384
PAGE_SIZE = 128

N_MIXERS = 3
N_MAMBA = (DEPTH + 2) // 3
N_FOX = (DEPTH + 1) // 3
N_NSA = DEPTH // 3

DN_ALPHA = (2 * DEPTH) ** 0.25
DN_BETA = (8 * DEPTH) ** -0.25
LN_EPS = 1e-5
NEG = -1e30

M_D_INNER = 2 * D_MODEL
M_HEADDIM = 64
M_HEADS = M_D_INNER // M_HEADDIM
M_GROUPS = 8
M_STATE = 128
M_CONV = 4
M_CHUNK = 128
M_CONV_CH = M_D_INNER + 2 * M_GROUPS * M_STATE
M_IN = M_D_INNER + M_CONV_CH + M_HEADS

N_HEADS = 16
HEAD_DIM = D_MODEL // N_HEADS
N_KV = 4
REP = N_HEADS // N_KV
ATT_SCALE = HEAD_DIM ** -0.5
Q_BLOCK = 128
F_IN = N_HEADS * HEAD_DIM + 2 * N_KV * HEAD_DIM + N_HEADS

CMP_STRIDE = 16
CMP_LEN = 2 * CMP_STRIDE
SEL_BLOCK = 64
SEL_TOPK = 16
WINDOW = 512
SEL_QCHUNK = 16
FORCE_BONUS = 1e6
N_BRANCH_KV = 6
N_IN = N_HEADS * HEAD_DIM + N_BRANCH_KV * N_KV * HEAD_DIM + 3 * N_HEADS

MEM_LEN = 256
MEM_HEADS = 4
MEM_HD = D_MODEL // MEM_HEADS
MEM_SCALE = MEM_HD ** -0.5

P_HEADS = 8
P_NKEYS = 128
P_EXPERTS = P_NKEYS * P_NKEYS
P_QDIM = 256
P_TOPK = 16
P_TOKBLOCK = 128

kernel_name = 'hybrid_ssd_fox_nsa_peer_decode_step'


def layer_norm(x, g, b):
    xf = x.astype(jnp.float32)
    mu = jnp.mean(xf, axis=-1, keepdims=True)
    var = jnp.mean(jnp.square(xf - mu), axis=-1, keepdims=True)
    return ((xf - mu) * lax.rsqrt(var + LN_EPS) * g.astype(jnp.float32) + b.astype(jnp.float32)).astype(x.dtype)


def gather_pages(pool, page_table):
    rows = pool[page_table]
    return rows.reshape(page_table.shape[0], page_table.shape[1] * pool.shape[1], *pool.shape[2:])


def causal_dwconv(u, w, bias):
    c = u.shape[-1]
    out = lax.conv_general_dilated(u, w[:, None, :].astype(u.dtype), window_strides=(1,), padding='VALID',
                                   dimension_numbers=('NWC', 'WIO', 'NWC'), feature_group_count=c)
    return out + bias.astype(u.dtype)


def ssd_scan(x, dt, a_head, bm, cm, s0):
    b, L, H, P = x.shape
    G, N = bm.shape[2], bm.shape[3]
    R = H // G
    Q = M_CHUNK if L % M_CHUNK == 0 else L
    nc = L // Q
    la = (dt * a_head).reshape(b, nc, Q, G, R)
    xdt = (x * dt[..., None]).reshape(b, nc, Q, G, R, P)
    bc = bm.reshape(b, nc, Q, G, N)
    cc = cm.reshape(b, nc, Q, G, N)
    acum = jnp.cumsum(la, axis=2)
    causal = jnp.tril(jnp.ones((Q, Q), dtype=bool))
    seg = acum[:, :, :, None] - acum[:, :, None, :]
    decay = jnp.exp(jnp.where(causal[:, :, None, None], seg, -jnp.inf))
    cb = jnp.einsum('bctgn,bcsgn->bctsg', cc, bc)
    y_diag = jnp.einsum('bctsg,bctsgr,bcsgrp->bctgrp', cb, decay, xdt)
    to_end = jnp.exp(acum[:, :, -1:] - acum)
    chunk_states = jnp.einsum('bcsgn,bcsgr,bcsgrp->bcgrpn', bc, to_end, xdt)
    chunk_decay = jnp.exp(acum[:, :, -1])

    def step(s, inp):
        st, dec = inp
        return s * dec[..., None, None] + st, s

    s_fin, s_in = lax.scan(step, s0.reshape(b, G, R, P, N),
                           (jnp.moveaxis(chunk_states, 1, 0), jnp.moveaxis(chunk_decay, 1, 0)))
    s_in = jnp.moveaxis(s_in, 0, 1)
    y_off = jnp.einsum('bctgn,bcgrpn,bctgr->bctgrp', cc, s_in, jnp.exp(acum))
    y = (y_diag + y_off).reshape(b, L, H, P)
    return y, s_fin.reshape(b, H, P, N)


def grouped_rms_norm(y, g):
    b, L, dn = y.shape
    yf = y.astype(jnp.float32).reshape(b, L, M_GROUPS, dn // M_GROUPS)
    yf = yf * lax.rsqrt(jnp.mean(jnp.square(yf), axis=-1, keepdims=True) + LN_EPS)
    return yf.reshape(b, L, dn) * g.astype(jnp.float32)


def mamba_mixer(h, conv_buf, ssm0, w_in, conv_w, conv_b, dt_bias, a_log, d_skip, norm_g, w_out):
    b, L, _ = h.shape
    zxbcdt = h @ w_in
    z = zxbcdt[..., :M_D_INNER]
    xbc = zxbcdt[..., M_D_INNER:M_D_INNER + M_CONV_CH]
    dt = zxbcdt[..., M_D_INNER + M_CONV_CH:]
    u = jnp.concatenate([conv_buf.astype(xbc.dtype), xbc], axis=1)
    new_buf = u[:, u.shape[1] - (M_CONV - 1):]
    xbc = jax.nn.silu(causal_dwconv(u, conv_w, conv_b))
    nbc = M_GROUPS * M_STATE
    xs = xbc[..., :M_D_INNER].reshape(b, L, M_HEADS, M_HEADDIM).astype(jnp.float32)
    bm = xbc[..., M_D_INNER:M_D_INNER + nbc].reshape(b, L, M_GROUPS, M_STATE).astype(jnp.float32)
    cm = xbc[..., M_D_INNER + nbc:].reshape(b, L, M_GROUPS, M_STATE).astype(jnp.float32)
    dt = jax.nn.softplus(dt.astype(jnp.float32) + dt_bias.astype(jnp.float32))
    a_head = -jnp.exp(a_log.astype(jnp.float32))
    y, s_fin = ssd_scan(xs, dt, a_head, bm, cm, ssm0.astype(jnp.float32))
    y = y + d_skip.astype(jnp.float32)[:, None] * xs
    y = y.reshape(b, L, M_D_INNER) * jax.nn.silu(z.astype(jnp.float32))
    y = grouped_rms_norm(y, norm_g).astype(h.dtype)
    return y @ w_out, new_buf, s_fin.astype(h.dtype)


def fox_attend(q, cq, k, v, ck, q_offset):
    b, T = q.shape[0], q.shape[1]
    S = k.shape[1]
    qb = Q_BLOCK if T % Q_BLOCK == 0 else T
    nb = T // qb
    kpos = jnp.arange(S)
    ckh = ck.reshape(b, S, N_KV, REP).transpose(0, 2, 3, 1)
    qs = jnp.moveaxis(q.reshape(b, nb, qb, N_KV, REP, HEAD_DIM), 1, 0)
    cqs = jnp.moveaxis(cq.reshape(b, nb, qb, N_KV, REP), 1, 0)

    def block(args):
        i, qi, ci = args
        s = jnp.einsum('btgrd,bsgd->bgrts', qi, k).astype(jnp.float32) * ATT_SCALE
        s = s + ci.transpose(0, 2, 3, 1)[..., None] - ckh[:, :, :, None, :]
        qpos = q_offset + i * qb + jnp.arange(qb)
        s = jnp.where(kpos[None, :] <= qpos[:, None], s, NEG)
        p = jax.nn.softmax(s, axis=-1).astype(v.dtype)
        return jnp.einsum('bgrts,bsgd->btgrd', p, v)

    o = lax.map(block, (jnp.arange(nb), qs, cqs))
    return jnp.moveaxis(o, 0, 1).reshape(b, T, N_HEADS * HEAD_DIM)


def fox_mixer(h, past_kv, past_logf, w_in, b_f, w_out):
    b, T, _ = h.shape
    P = past_kv.shape[1]
    nq, nkv = N_HEADS * HEAD_DIM, N_KV * HEAD_DIM
    proj = h @ w_in
    q = proj[..., :nq].reshape(b, T, N_KV, REP, HEAD_DIM)
    kv = proj[..., nq:nq + 2 * nkv].reshape(b, T, 2, N_KV, HEAD_DIM)
    logf = jax.nn.log_sigmoid(proj[..., nq + 2 * nkv:].astype(jnp.float32) + b_f.astype(jnp.float32))
    kv_all = jnp.concatenate([past_kv.astype(kv.dtype), kv], axis=1)
    c_all = jnp.cumsum(jnp.concatenate([past_logf.astype(jnp.float32), logf], axis=1), axis=1)
    o = fox_attend(q, c_all[:, P:], kv_all[:, :, 0], kv_all[:, :, 1], c_all, P)
    return o.astype(h.dtype) @ w_out, kv, logf.astype(h.dtype)


def nsa_compress(k, wpos, proj):
    b, lp, g, d = k.shape
    kr = k.reshape(b, lp // CMP_STRIDE, CMP_STRIDE, g, d)
    head = jnp.einsum('bnigd,ig->bngd', kr, wpos[:CMP_STRIDE].astype(k.dtype))
    tail = jnp.einsum('bnigd,ig->bngd', kr, wpos[CMP_STRIDE:].astype(k.dtype))
    pooled = head[:, :-1] + tail[:, 1:]
    return jnp.einsum('bngd,gde->bnge', pooled, proj.astype(k.dtype))


def nsa_select(q, qpos, ksb, vsb, top_idx, top_ok):
    b, T = q.shape[0], q.shape[1]
    n = top_idx.shape[-1]
    qc = SEL_QCHUNK if T % SEL_QCHUNK == 0 else T
    nch = T // qc
    bi = jnp.arange(b)[:, None, None]
    gi = jnp.arange(N_KV)[None, :, None]

    def chunk(args):
        qi, pi, ii, oki = args
        flat = ii.transpose(0, 2, 1, 3).reshape(b, N_KV, qc * n)
        kg = ksb[bi, gi, flat].reshape(b, N_KV, qc, n * SEL_BLOCK, HEAD_DIM)
        vg = vsb[bi, gi, flat].reshape(b, N_KV, qc, n * SEL_BLOCK, HEAD_DIM)
        kpos = (ii[..., None] * SEL_BLOCK + jnp.arange(SEL_BLOCK)).reshape(b, qc, N_KV, n * SEL_BLOCK)
        ok = (kpos <= pi[None, :, None, None]) & jnp.repeat(oki, SEL_BLOCK, axis=-1)
        s = jnp.einsum('bcgrd,bgcmd->bcgrm', qi, kg).astype(jnp.float32) * ATT_SCALE
        s = jnp.where(ok[:, :, :, None, :], s, NEG)
        p = jax.nn.softmax(s, axis=-1).astype(vg.dtype)
        return jnp.einsum('bcgrm,bgcmd->bcgrd', p, vg)

    o = lax.map(chunk, (jnp.moveaxis(q.reshape(b, nch, qc, N_KV, REP, HEAD_DIM), 1, 0),
                        qpos.reshape(nch, qc),
                        jnp.moveaxis(top_idx.reshape(b, nch, qc, N_KV, n), 1, 0),
                        jnp.moveaxis(top_ok.reshape(b, nch, qc, N_KV, n), 1, 0)))
    return jnp.moveaxis(o, 0, 1).reshape(b, T, N_KV, REP, HEAD_DIM)


def window_attend(q, qpos, wk, wv, k_offset):
    b, T = q.shape[0], q.shape[1]
    qb = Q_BLOCK if T % Q_BLOCK == 0 else T
    nb = T // qb
    idx = jnp.arange(nb)[:, None] * qb + jnp.arange(WINDOW + qb)[None, :]
    kband = wk[:, idx]
    vband = wv[:, idx]
    kpos = k_offset + idx
    qp = qpos.reshape(nb, qb)
    kp = kpos[:, None, :]
    ok = (kp <= qp[:, :, None]) & (kp > qp[:, :, None] - WINDOW) & (kp >= 0)
    qr = q.reshape(b, nb, qb, N_KV, REP, HEAD_DIM)
    s = jnp.einsum('bitgrd,bisgd->bigrts', qr, kband).astype(jnp.float32) * ATT_SCALE
    s = jnp.where(ok[None, :, None, None], s, NEG)
    p = jax.nn.softmax(s, axis=-1).astype(vband.dtype)
    o = jnp.einsum('bigrts,bisgd->bitgrd', p, vband)
    return o.reshape(b, T, N_KV, REP, HEAD_DIM)


def nsa_mixer(h, past_kv, win_buf, w_in, b_gate, cmp_wpos, cmp_proj, w_out):
    b, T, _ = h.shape
    P = past_kv.shape[1]
    nq, nkv = N_HEADS * HEAD_DIM, N_KV * HEAD_DIM
    proj = h @ w_in
    q = proj[..., :nq].reshape(b, T, N_KV, REP, HEAD_DIM)
    kv = proj[..., nq:nq + N_BRANCH_KV * nkv].reshape(b, T, N_BRANCH_KV, N_KV, HEAD_DIM)
    gates = jax.nn.sigmoid(proj[..., nq + N_BRANCH_KV * nkv:].astype(jnp.float32)
                           + b_gate.astype(jnp.float32)).reshape(b, T, N_KV, REP, 3)
    paged_new = kv[:, :, :4]
    L = P + T
    lp = -(-L // SEL_BLOCK) * SEL_BLOCK
    full = jnp.concatenate([past_kv.astype(kv.dtype), paged_new], axis=1)
    full = jnp.pad(full, ((0, 0), (0, lp - L), (0, 0), (0, 0), (0, 0)))
    qpos = P + jnp.arange(T)
    kc = nsa_compress(full[:, :, 0], cmp_wpos[0], cmp_proj[0])
    vc = nsa_compress(full[:, :, 1], cmp_wpos[1], cmp_proj[1])
    nc = kc.shape[1]
    cstart = jnp.arange(nc) * CMP_STRIDE
    cmask = (cstart + CMP_LEN - 1)[None, :] <= qpos[:, None]
    sc = jnp.einsum('btgrd,bngd->btgrn', q, kc).astype(jnp.float32) * ATT_SCALE
    sc = jnp.where(cmask[None, :, None, None, :], sc, NEG)
    pc = jax.nn.softmax(sc, axis=-1) * cmask[None, :, None, None, :]
    o_cmp = jnp.einsum('btgrn,bngd->btgrd', pc.astype(vc.dtype), vc)
    ns = lp // SEL_BLOCK
    sstart = jnp.arange(ns) * SEL_BLOCK
    overlap = ((cstart[:, None] < sstart[None, :] + SEL_BLOCK)
               & (cstart[:, None] + CMP_LEN > sstart[None, :])).astype(jnp.float32)
    imp = jnp.einsum('btgrn,nm->btgm', pc, overlap)
    qblk = qpos // SEL_BLOCK
    sidx = jnp.arange(ns)
    forced = (sidx[None, :] == 0) | (sidx[None, :] == qblk[:, None]) | (sidx[None, :] == qblk[:, None] - 1)
    valid = sidx[None, :] <= qblk[:, None]
    imp = jnp.where(forced[None, :, None, :], imp + FORCE_BONUS, imp)
    imp = jnp.where(valid[None, :, None, :], imp, NEG)
    top_val, top_idx = lax.top_k(imp, min(SEL_TOPK, ns))
    top_ok = top_val > 0.5 * NEG
    ksb = full[:, :, 2].reshape(b, ns, SEL_BLOCK, N_KV, HEAD_DIM).transpose(0, 3, 1, 2, 4)
    vsb = full[:, :, 3].reshape(b, ns, SEL_BLOCK, N_KV, HEAD_DIM).transpose(0, 3, 1, 2, 4)
    o_slc = nsa_select(q, qpos, ksb, vsb, top_idx, top_ok)
    wk = jnp.concatenate([win_buf[:, :, 0].astype(kv.dtype), kv[:, :, 4]], axis=1)
    wv = jnp.concatenate([win_buf[:, :, 1].astype(kv.dtype), kv[:, :, 5]], axis=1)
    o_win = window_attend(q, qpos, wk, wv, P - WINDOW)
    new_win = jnp.stack([wk, wv], axis=2)[:, T:]
    o = gates[..., 0:1] * o_cmp + gates[..., 1:2] * o_slc + gates[..., 2:3] * o_win
    o = o.reshape(b, T, N_HEADS * HEAD_DIM).astype(h.dtype)
    return o @ w_out, paged_new, new_win


def memory_attend(h, mem_kv, wq, wo):
    b, L, _ = h.shape
    q = (h @ wq).reshape(b, L, MEM_HEADS, MEM_HD)
    s = jnp.einsum('blhd,bmhd->bhlm', q, mem_kv[:, :, 0].astype(q.dtype)).astype(jnp.float32) * MEM_SCALE
    p = jax.nn.softmax(s, axis=-1).astype(h.dtype)
    o = jnp.einsum('bhlm,bmhd->blhd', p, mem_kv[:, :, 1].astype(h.dtype)).reshape(b, L, D_MODEL)
    return o @ wo


def peer_ffn(h, wq, sub_keys, u_tab, v_tab):
    shp = h.shape
    t = h.reshape(-1, D_MODEL)
    T = t.shape[0]
    q = (t @ wq).reshape(T, P_HEADS, 2, P_QDIM // 2)
    s = jnp.einsum('thkd,hknd->thkn', q, sub_keys).astype(jnp.float32)
    v1, i1 = lax.top_k(s[:, :, 0], P_TOPK)
    v2, i2 = lax.top_k(s[:, :, 1], P_TOPK)
    cand = (v1[..., :, None] + v2[..., None, :]).reshape(T, P_HEADS, P_TOPK * P_TOPK)
    cidx = (i1[..., :, None] * P_NKEYS + i2[..., None, :]).reshape(T, P_HEADS, P_TOPK * P_TOPK)
    top_s, pos = lax.top_k(cand, P_TOPK)
    eidx = jnp.take_along_axis(cidx, pos, axis=-1)
    gate = jax.nn.softmax(top_s, axis=-1)
    blk = P_TOKBLOCK if T % P_TOKBLOCK == 0 else T
    nb = T // blk

    def block(args):
        tb, eb, gb = args
        act = jax.nn.gelu(jnp.einsum('td,thkd->thk', tb, u_tab[eb]).astype(jnp.float32), approximate=False)
        return jnp.einsum('thk,thkd->td', (gb * act).astype(v_tab.dtype), v_tab[eb])

    out = lax.map(block, (t.reshape(nb, blk, D_MODEL), eidx.reshape(nb, blk, P_HEADS, P_TOPK),
                          gate.reshape(nb, blk, P_HEADS, P_TOPK)))
    return out.reshape(shp).astype(h.dtype)


def setup_inputs(seed: int = 0) -> dict:
    key = jax.random.key(seed)
    keys = iter(jax.random.split(key, 64))

    def nrm(shape, scale):
        return jax.random.normal(next(keys), shape, jnp.float32) * scale

    def unif(shape, lo, hi):
        return jax.random.uniform(next(keys), shape, jnp.float32, lo, hi)

    n_pages = PAST_LEN // PAGE_SIZE
    n_used = DEC_BATCH * n_pages
    n_pool = n_used + max(1, n_used // 4)
    x_prompt = nrm((BATCH, SEQ, D_MODEL), 1.0)
    x_sample = nrm((DEC_BATCH, DEC_SEQ, D_MODEL), 1.0)
    cache_mem_kv = nrm((DEPTH, DEC_BATCH, MEM_LEN, 2, MEM_HEADS, MEM_HD), 1.0)
    state_ssm = nrm((N_MAMBA, DEC_BATCH, M_HEADS, M_HEADDIM, M_STATE), 0.5)
    state_conv = nrm((N_MAMBA, DEC_BATCH, M_CONV - 1, M_CONV_CH), 1.0)
    cache_fox_kv = nrm((N_FOX, n_pool, PAGE_SIZE, 2, N_KV, HEAD_DIM), 1.0)
    cache_fox_logf = jax.nn.log_sigmoid(2.0 + nrm((N_FOX, n_pool, PAGE_SIZE, N_HEADS), 0.5))
    cache_nsa_kv = nrm((N_NSA, n_pool, PAGE_SIZE, 4, N_KV, HEAD_DIM), 1.0)
    state_nsa_win = nrm((N_NSA, DEC_BATCH, WINDOW, 2, N_KV, HEAD_DIM), 1.0)
    page_table = jax.random.permutation(next(keys), n_pool)[:n_used].reshape(DEC_BATCH, n_pages).astype(jnp.int32)
    mem_prompt = nrm((BATCH, MEM_LEN, D_MODEL), 1.0)
    dt0 = jnp.exp(unif((N_MAMBA, M_HEADS), math.log(1e-3), math.log(1e-1)))
    sd = D_MODEL ** -0.5
    return {
        'x_prompt': x_prompt, 'x_sample': x_sample,
        'cache_mem_kv': cache_mem_kv, 'state_ssm': state_ssm, 'state_conv': state_conv,
        'cache_fox_kv': cache_fox_kv, 'cache_fox_logf': cache_fox_logf,
        'cache_nsa_kv': cache_nsa_kv, 'state_nsa_win': state_nsa_win,
        'page_table': page_table, 'mem_prompt': mem_prompt,
        'ln_g': 1.0 + nrm((DEPTH, 3, D_MODEL), 0.02),
        'ln_b': nrm((DEPTH, 3, D_MODEL), 0.02),
        'mem_wq': nrm((DEPTH, D_MODEL, D_MODEL), sd),
        'mem_wkv': nrm((DEPTH, D_MODEL, 2 * D_MODEL), sd),
        'mem_wo': nrm((DEPTH, D_MODEL, D_MODEL), sd * DN_BETA),
        'peer_wq': nrm((DEPTH, D_MODEL, P_HEADS * P_QDIM), sd),
        'peer_subkeys': nrm((DEPTH, P_HEADS, 2, P_NKEYS, P_QDIM // 2), (P_QDIM // 2) ** -0.5),
        'peer_u': nrm((DEPTH, P_EXPERTS, D_MODEL), sd),
        'peer_v': nrm((DEPTH, P_EXPERTS, D_MODEL), DN_BETA * P_HEADS ** -0.5),
        'mamba_w_in': nrm((N_MAMBA, D_MODEL, M_IN), sd),
        'mamba_conv_w': nrm((N_MAMBA, M_CONV, M_CONV_CH), M_CONV ** -0.5),
        'mamba_conv_b': nrm((N_MAMBA, M_CONV_CH), 0.02),
        'mamba_dt_bias': dt0 + jnp.log(-jnp.expm1(-dt0)),
        'mamba_a_log': jnp.log(unif((N_MAMBA, M_HEADS), 1.0, 16.0)),
        'mamba_d': 1.0 + nrm((N_MAMBA, M_HEADS), 0.02),
        'mamba_norm_g': 1.0 + nrm((N_MAMBA, M_D_INNER), 0.02),
        'mamba_w_out': nrm((N_MAMBA, M_D_INNER, D_MODEL), M_D_INNER ** -0.5 * DN_BETA),
        'fox_w_in': nrm((N_FOX, D_MODEL, F_IN), sd),
        'fox_b_f': 2.0 + nrm((N_FOX, N_HEADS), 0.1),
        'fox_w_out': nrm((N_FOX, D_MODEL, D_MODEL), sd * DN_BETA),
        'nsa_w_in': nrm((N_NSA, D_MODEL, N_IN), sd),
        'nsa_b_gate': nrm((N_NSA, 3 * N_HEADS), 0.02),
        'nsa_cmp_wpos': (1.0 + nrm((N_NSA, 2, CMP_LEN, N_KV), 0.1)) / CMP_LEN,
        'nsa_cmp_proj': nrm((N_NSA, 2, N_KV, HEAD_DIM, HEAD_DIM), HEAD_DIM ** -0.5),
        'nsa_w_out': nrm((N_NSA, D_MODEL, D_MODEL), sd * DN_BETA),
    }


def reference(x_prompt, x_sample, cache_mem_kv, state_ssm, state_conv, cache_fox_kv, cache_fox_logf,
              cache_nsa_kv, state_nsa_win, page_table, mem_prompt, ln_g, ln_b, mem_wq, mem_wkv, mem_wo,
              peer_wq, peer_subkeys, peer_u, peer_v, mamba_w_in, mamba_conv_w, mamba_conv_b, mamba_dt_bias,
              mamba_a_log, mamba_d, mamba_norm_g, mamba_w_out, fox_w_in, fox_b_f, fox_w_out,
              nsa_w_in, nsa_b_gate, nsa_cmp_wpos, nsa_cmp_proj, nsa_w_out):
    hp, hs = x_prompt, x_sample
    bp, bs = hp.shape[0], hs.shape[0]
    dtp = hp.dtype
    mem_kv_p, ssm_p, conv_p, fox_kv_p, fox_lf_p, nsa_kv_p, nsa_win_p = [], [], [], [], [], [], []
    ssm_s, conv_s, fox_kv_s, fox_lf_s, nsa_kv_s, nsa_win_s = [], [], [], [], [], []
    for i in range(DEPTH):
        kind, j = i % N_MIXERS, i // N_MIXERS
        if kind == 0:
            mw = (mamba_w_in[j], mamba_conv_w[j], mamba_conv_b[j], mamba_dt_bias[j], mamba_a_log[j],
                  mamba_d[j], mamba_norm_g[j], mamba_w_out[j])
            op, bufp, stp = mamba_mixer(hp, jnp.zeros((bp, M_CONV - 1, M_CONV_CH), dtp),
                                        jnp.zeros((bp, M_HEADS, M_HEADDIM, M_STATE), jnp.float32), *mw)
            os_, bufs, sts = mamba_mixer(hs, state_conv[j], state_ssm[j], *mw)
            conv_p.append(bufp); ssm_p.append(stp); conv_s.append(bufs); ssm_s.append(sts)
        elif kind == 1:
            past_kv = gather_pages(cache_fox_kv[j], page_table)
            past_lf = gather_pages(cache_fox_logf[j], page_table)
            op, kvp, lfp = fox_mixer(hp, jnp.zeros((bp, 0, 2, N_KV, HEAD_DIM), dtp),
                                     jnp.zeros((bp, 0, N_HEADS), dtp), fox_w_in[j], fox_b_f[j], fox_w_out[j])
            os_, kvs, lfs = fox_mixer(hs, past_kv, past_lf, fox_w_in[j], fox_b_f[j], fox_w_out[j])
            fox_kv_p.append(kvp); fox_lf_p.append(lfp); fox_kv_s.append(kvs); fox_lf_s.append(lfs)
        else:
            past = gather_pages(cache_nsa_kv[j], page_table)
            nw = (nsa_w_in[j], nsa_b_gate[j], nsa_cmp_wpos[j], nsa_cmp_proj[j], nsa_w_out[j])
            op, kvp, winp = nsa_mixer(hp, jnp.zeros((bp, 0, 4, N_KV, HEAD_DIM), dtp),
                                      jnp.zeros((bp, WINDOW, 2, N_KV, HEAD_DIM), dtp), *nw)
            os_, kvs, wins = nsa_mixer(hs, past, state_nsa_win[j], *nw)
            nsa_kv_p.append(kvp); nsa_win_p.append(winp); nsa_kv_s.append(kvs); nsa_win_s.append(wins)
        hp = layer_norm(DN_ALPHA * hp + op, ln_g[i, 0], ln_b[i, 0])
        hs = layer_norm(DN_ALPHA * hs + os_, ln_g[i, 0], ln_b[i, 0])
        mkv = (mem_prompt @ mem_wkv[i]).reshape(bp, MEM_LEN, 2, MEM_HEADS, MEM_HD)
        mem_kv_p.append(mkv)
        hp = layer_norm(DN_ALPHA * hp + memory_attend(hp, mkv, mem_wq[i], mem_wo[i]), ln_g[i, 1], ln_b[i, 1])
        hs = layer_norm(DN_ALPHA * hs + memory_attend(hs, cache_mem_kv[i], mem_wq[i], mem_wo[i]), ln_g[i, 1], ln_b[i, 1])
        hp = layer_norm(DN_ALPHA * hp + peer_ffn(hp, peer_wq[i], peer_subkeys[i], peer_u[i], peer_v[i]), ln_g[i, 2], ln_b[i, 2])
        hs = layer_norm(DN_ALPHA * hs + peer_ffn(hs, peer_wq[i], peer_subkeys[i], peer_u[i], peer_v[i]), ln_g[i, 2], ln_b[i, 2])
    return (hp, hs,
            jnp.stack(mem_kv_p), jnp.stack(ssm_p), jnp.stack(conv_p),
            jnp.stack(fox_kv_p), jnp.stack(fox_lf_p), jnp.stack(nsa_kv_p), jnp.stack(nsa_win_p),
            jnp.stack(ssm_s), jnp.stack(conv_s), jnp.stack(fox_kv_s), jnp.stack(fox_lf_s),
            jnp.stack(nsa_kv_s), jnp.stack(nsa_win_s))
```

```python
import numpy as np
from contextlib import ExitStack, contextmanager
import concourse.bass as bass
import concourse.mybir as mybir
from concourse.bass_utils import run_bass_kernel_spmd

F32 = mybir.dt.float32
BF16 = mybir.dt.bfloat16
I32 = mybir.dt.int32
ALU = mybir.AluOpType
AF = mybir.ActivationFunctionType
AX = mybir.AxisListType

D = 2048
DC = 16
SEQ = 2048
NT = 16
TT = 17
TTOK = TT * 128
DEPTH = 4
MEM = 256
DN_ALPHA = (2 * DEPTH) ** 0.25
LN_EPS = 1e-5


class Tok:
    __slots__ = ("w", "r", "excl")

    def __init__(self, excl=False):
        self.w = None
        self.r = {}
        self.excl = excl


class Chan:
    def __init__(self, sem, step):
        self.sem = sem
        self.step = step
        self.val = 0


class MK:
    def __init__(self, nc, stack):
        self.nc = nc
        self.stack = stack
        self.engs = {"pe": nc.tensor, "dve": nc.vector, "act": nc.scalar, "pool": nc.gpsimd, "sp": nc.sync}
        self.chan = {n: Chan(self._sem("c_" + n), 1) for n in ("pe", "dve", "act", "pool")}
        self.dchans = []
        self.free_chans = []
        self.scopes = []
        self.seen = {n: {} for n in self.engs}
        self.nins = 0
        self.uid = 0

    def _sem(self, name):
        return self.stack.enter_context(self.nc.semaphore(name))

    def dma_chan(self, name):
        if self.free_chans:
            c = self.free_chans.pop()
        else:
            c = Chan(self._sem(f"dch{len(self.dchans)}"), 16)
            self.dchans.append(c)
        if self.scopes:
            self.scopes[-1].append(c)
        return c

    def barrier(self):
        chans = self.dchans + list(self.chan.values())
        for en in self.engs:
            self._wait(en, [(c, c.val) for c in chans if c.val], allow_pe_self=True)

    @contextmanager
    def scope(self):
        self.scopes.append([])
        with ExitStack() as st:
            yield st
            self.barrier()
        self.free_chans.extend(self.scopes.pop())

    def sb(self, name, shape, dt, stack=None):
        self.uid += 1
        return (stack or self.stack).enter_context(self.nc.sbuf_tensor(f"{name}_u{self.uid}", shape, dt))

    def ps(self, name, shape, dt, stack=None):
        self.uid += 1
        return (stack or self.stack).enter_context(self.nc.psum_tensor(f"{name}_u{self.uid}", shape, dt))

    def _wait(self, ename, deps, allow_pe_self=False):
        need = {}
        pech = self.chan["pe"]
        for ch, v in deps:
            if ename == "pe" and ch is pech and not allow_pe_self:
                continue
            if v > need.get(id(ch), (ch, 0))[1]:
                need[id(ch)] = (ch, v)
        seen = self.seen[ename]
        for cid, (ch, v) in need.items():
            if seen.get(cid, 0) >= v:
                continue
            self.engs[ename].wait_ge(ch.sem, v)
            seen[cid] = v

    @staticmethod
    def _deps(reads, writes):
        deps = []
        for t in reads:
            if t.w:
                deps.append(t.w)
        for t in writes:
            if t.w:
                deps.append(t.w)
            deps.extend(t.r.values())
        return deps

    def op(self, ename, fn, reads=(), writes=()):
        ex = [t for t in reads if t.excl]
        if ex:
            reads = [t for t in reads if not t.excl]
            writes = list(writes) + ex
        self._wait(ename, self._deps(reads, writes))
        ins = fn(self.engs[ename])
        ch = self.chan[ename]
        ch.val += 1
        ins.then_inc(ch.sem, 1)
        for t in reads:
            t.r[id(ch)] = (ch, ch.val)
        for t in writes:
            t.w = (ch, ch.val)
            t.r = {}
        self.nins += 1
        return ins

    def dma(self, qname, out, in_, ch, reads=(), writes=(), **kw):
        self._wait(qname, self._deps(reads, writes))
        ins = self.engs[qname].dma_start(out=out, in_=in_, **kw)
        ch.val += 16
        ins.then_inc(ch.sem, 16)
        for t in reads:
            t.r[id(ch)] = (ch, ch.val)
        for t in writes:
            t.w = (ch, ch.val)
            t.r = {}
        self.nins += 1
        return ins

    def idma(self, out, in_, idx_ap, ch, reads=(), writes=()):
        self._wait("pool", self._deps(reads, writes))
        ins = self.nc.gpsimd.indirect_dma_start(out=out, out_offset=None, in_=in_,
                                                in_offset=bass.IndirectOffsetOnAxis(ap=idx_ap, axis=0))
        ch.val += 16
        ins.then_inc(ch.sem, 16)
        for t in reads:
            t.r[id(ch)] = (ch, ch.val)
        for t in writes:
            t.w = (ch, ch.val)
            t.r = {}
        self.nins += 1
        return ins

    def finish(self):
        for ch in self.dchans + list(self.chan.values()):
            if ch.val:
                self.engs["sp"].wait_ge(ch.sem, ch.val)


class Ring:
    def __init__(self, k, name, n, shape, dt, space="sb", stack=None, chan=False):
        alloc = k.sb if space == "sb" else k.ps
        self.bufs = [alloc(f"{name}{i}", shape, dt, stack) for i in range(n)]
        self.toks = [Tok() for _ in range(n)]
        self.chans = [k.dma_chan(f"ch_{name}{i}") for i in range(n)] if chan else None
        self.i = 0
        self.n = n

    def next(self):
        j = self.i % self.n
        self.i += 1
        if self.chans:
            return self.bufs[j], self.toks[j], self.chans[j]
        return self.bufs[j], self.toks[j]


import os
M_DI = 4096
M_IN = 10304
M_CH = 6144


def build_program(stages=("init", "memkv", "mamba0", "mem", "peer"), dbg=True):
    nc = bass.Bass("TRN2", target_bir_lowering=False)
    stages = set(stages)
    in_names = []

    def din(name, shape, dt=F32):
        in_names.append(name)
        return nc.dram_tensor(name, list(shape), dt, kind="ExternalInput").ap()

    def dout(name, shape, dt=F32):
        return nc.dram_tensor(name, list(shape), dt, kind="ExternalOutput").ap()

    def dscr(name, shape, dt=F32):
        return nc.dram_tensor(name, list(shape), dt, kind="Internal").ap()

    xp = din("xp", [SEQ, D])
    xs = din("xs", [1, D])
    memp = din("memp", [MEM, D])
    ident_d = din("ident", [128, 128])
    trit_d = din("trit", [128, 128])
    ln_gd = din("ln_g", [DEPTH, 3, 128, D])
    ln_bd = din("ln_b", [DEPTH, 3, 128, D])
    o_memkv = dout("o_memkv", [DEPTH, MEM, 2 * D])
    o_ssm_p = dout("o_ssm_p", [2, M_DI, 128])
    o_conv_p = dout("o_conv_p", [2, 3, M_CH])
    o_ssm_s = dout("o_ssm_s", [2, M_DI, 128])
    o_conv_s = dout("o_conv_s", [2, 3, M_CH])
    if "mem" in stages:
        mem_wkv = din("mem_wkv", [DEPTH, D, 2 * D])
        mem_wq = din("mem_wq", [DEPTH, D, D])
        mem_wo = din("mem_wo", [DEPTH, D, D])
        cmk = din("cmk", [DEPTH, MEM, 2 * D])
    if "fox" in stages:
        f_win = din("f_win", [D, 3088])
        f_wout = din("f_wout", [D, D])
        f_bf = din("f_bf", [128, 16])
        sel16_d = din("sel16", [16, 16 * 128])
        mneg_d = din("mneg", [128, 128])
        f_kpool = din("f_kpool", [1280 * 128, 512])
        f_vpool = din("f_vpool", [1280 * 128, 512])
        f_lpool = din("f_lpool", [1280 * 128, 16])
        pt_d = din("pt", [1, 128], I32)
        iota_d = din("iota", [128, 1])
        tris_d = din("tris", [128, 128])
        mask16_d = din("mask16", [16, 4])
        o_fox_kv = dout("o_fox_kv", [TTOK, 1024])
        o_fox_lf = dout("o_fox_lf", [TTOK, 16])
    if "nsa" in stages:
        n_win = din("n_win", [D, 5168])
        n_winbuf = din("n_winbuf", [512, 1024])
        n_wout = din("n_wout", [D, D])
        n_wp = din("n_wp", [2, 4, SEQ, 128])
        n_proj = din("n_proj", [2, 4, 128, 128])
        n_bg = din("n_bg", [128, 48])
        n_cm01 = din("n_cm01", [SEQ, 128])
        n_ov = din("n_ov", [128, 32])
        n_fv = din("n_fv", [SEQ, 3, 32])
        n_eaug = din("n_eaug", [33, SEQ])
        n_mfar = din("n_mfar", [128, 128])
        mneg_d2 = din("mneg2", [128, 128])
        n_pools = [din(f"n_pool{q}", [1280 * 128, 512]) for q in range(4)]
        n_wps = din("n_wps", [2, 4, 128, 17, 128])
        n_ovs = din("n_ovs", [9, 128, 257])
        n_fs = din("n_fs", [16, 257])
        n_gs = din("n_gs", [16, 16])
        o_nsa_kv = dout("o_nsa_kv", [TTOK, 2048])
        o_nsa_win_p = dout("o_nsa_win_p", [512, 1024])
        o_nsa_win_s = dout("o_nsa_win_s", [512, 1024])
    if "peer" in stages:
        peer_wq = din("peer_wq", [DEPTH, D, D])
        p_skT = din("p_skT", [DEPTH, 128, 16, 128])
        NPL = int(os.environ.get("K_LAYERS", "4"))
        p_uT = din("p_uT", [NPL, D, 16384])
        peer_v = din("peer_v", [NPL, 16384, D])
        pe_E = dscr("pe_E", [TTOK, 2064])
        pe_acc = dscr("pe_acc", [TTOK, D])
    if "mamba0" in stages:
        m_win = din("m_win", [2, D, M_IN])
        m_wout = din("m_wout", [2, M_DI, D])
        m_convw = din("m_convw", [2, 128, 48, 4])
        m_convb = din("m_convb", [2, 128, 48])
        m_dtb = din("m_dtb", [2, 128, 64])
        m_alog = din("m_alog", [2, 128, 64])
        m_dsk = din("m_dsk", [2, 128, 64])
        m_ng = din("m_ng", [2, 128, 32])
        m_cs = din("m_cs", [2, 3, M_CH])
        m_ss = din("m_ss", [2, M_DI, 128])
        m_cwr = din("m_cwr", [2, 4, M_CH])
        m_cbr = din("m_cbr", [2, M_CH])
    res = dscr("res", [TTOK, D])
    vbuf = dscr("vbuf", [TTOK, D])
    ynT_d = dscr("ynT_d", [M_DI, TTOK], BF16)
    o_y = dout("o_y", [TTOK, D])
    if dbg:
        o_dbg = dout("o_dbg", [TTOK, D])
        o_dbg2 = dout("o_dbg2", [D, TTOK], BF16)

    with ExitStack() as stack:
        k = MK(nc, stack)
        ident = k.sb("ident_f", [128, 128], F32)
        identb = k.sb("ident_b", [128, 128], BF16)
        trit = k.sb("trit", [128, 128], F32)
        ones = k.sb("ones_f", [128, 128], F32)
        zeros = k.sb("zeros_f", [128, 512], F32)
        zerob = k.sb("zeros_b", [128, 512], BF16)
        t_const = Tok()
        ch_const = k.dma_chan("ch_const")
        k.dma("sp", ident[:], ident_d, ch_const, writes=[t_const])
        k.dma("sp", trit[:], trit_d, ch_const, writes=[t_const])
        k.op("act", lambda e: e.copy(out=identb[:], in_=ident[:]), reads=[t_const], writes=[t_const])
        k.op("dve", lambda e: e.memset(ones[:], 1.0), writes=[t_const])
        k.op("dve", lambda e: e.memset(zeros[:], 0.0), writes=[t_const])
        k.op("dve", lambda e: e.memset(zerob[:], 0.0), writes=[t_const])

        hT = k.sb("hT", [128, DC, TTOK], BF16)
        t_hT = [Tok() for _ in range(TT)]
        t_res = [Tok() for _ in range(TT)]
        t_vbuf = [Tok() for _ in range(TT)]
        ch_res = [k.dma_chan("ch_res") for _ in range(TT)]
        ch_vbuf = [k.dma_chan("ch_vbuf") for _ in range(TT)]
        t_ynTd = [Tok() for _ in range(TT)]
        ch_ynTd = [k.dma_chan("ch_ynTd") for _ in range(TT)]
        ch_dbg = k.dma_chan("ch_dbg")

        PB = [k.ps(f"pb{i}", [128, 512], F32) for i in range(8)]
        t_PB = [Tok(excl=True) for _ in range(8)]

        class PRing:
            def __init__(self, idx):
                self.idx = list(idx)
                self.i = 0

            def next(self):
                j = self.idx[self.i % len(self.idx)]
                self.i += 1
                return PB[j], t_PB[j]

        def tsl(tt):
            return slice(tt * 128, (tt + 1) * 128)

        def gs8(g):
            return slice(g * 8, (g + 1) * 8)

        def transpose_rows(src, t_src, dst3, t_dst, xb_ring, pr):
            xb, t_xb = xb_ring.next()
            k.op("act", lambda e: e.copy(out=xb[:], in_=src), reads=[t_src], writes=[t_xb])
            for q in range(4):
                pt, t_pt = pr.next()
                ptb = pt[:].bitcast(BF16)
                for c4 in range(4):
                    c = q * 4 + c4
                    k.op("pe", lambda e: e.transpose(out=ptb[:, c4 * 128:(c4 + 1) * 128],
                                                     in_=xb[:, c * 128:(c + 1) * 128], identity=identb[:]),
                         reads=[t_xb, t_const], writes=[t_pt])
                k.op("dve", lambda e: e.tensor_copy(out=dst3[:, q * 4:(q + 1) * 4, :],
                                                    in_=ptb[:, 0:512].rearrange("p (c t) -> p c t", c=4)),
                     reads=[t_pt], writes=[t_dst])

        with k.scope() as st:
            xin = Ring(k, "xin", 2, [128, D], F32, stack=st, chan=True)
            xb_ring = Ring(k, "xb", 2, [128, D], BF16, stack=st)
            pr = PRing([0, 1, 2, 3])
            for tt in range(NT):
                k.dma("sp", res[tsl(tt), :], xp[tsl(tt), :], ch_res[tt], writes=[t_res[tt]])
            for q in range(4):
                k.dma("sp", res[SEQ + 1:TTOK, q * 512:(q + 1) * 512], zeros[0:127, :], ch_res[NT], reads=[t_const], writes=[t_res[NT]])
            k.dma("sp", res[SEQ:SEQ + 1, :], xs, ch_res[NT], writes=[t_res[NT]])
            for q in range(8):
                k.dma("sp", ynT_d[q * 512:(q + 1) * 512, SEQ:TTOK].rearrange("(c p) t -> p c t", p=128),
                      zerob[:].rearrange("p (c t) -> p c t", c=4), ch_ynTd[NT], reads=[t_const], writes=[t_ynTd[NT]])
            for tt in range(TT):
                xt, t_xt, ch = xin.next()
                src = xp[tsl(tt), :] if tt < NT else res[tsl(tt), :]
                k.dma("sp", xt[:], src, ch, reads=([] if tt < NT else [t_res[tt]]), writes=[t_xt])
                transpose_rows(xt[:], t_xt, hT[:, :, tsl(tt)], t_hT[tt], xb_ring, pr)

        def proj_ln(get_aT, KC, W, li, lk, wbufs=2):
            proj_v(get_aT, KC, W, wbufs)
            ln_pass(li, lk)

        def proj_v(get_aT, KC, W, wbufs=2):
            with k.scope() as st:
                wr = Ring(k, "pw", wbufs, [128, KC, 512], BF16, stack=st, chan=True)
                hb_r = Ring(k, "phb", 2, [128, 512], F32, stack=st, chan=True)
                vb_r = Ring(k, "pvb", 2, [128, 512], F32, stack=st)
                pm = PRing([0, 1, 2, 3])
                src_state = {}
                for fb in range(4):
                    wb, t_wb, chw = wr.next()
                    k.dma("pool", wb[:], W.rearrange("(c p) f -> p c f", p=128)[:, :, fb * 512:(fb + 1) * 512],
                          chw, writes=[t_wb])
                    for tt in range(TT):
                        aT, t_aT = get_aT(tt, st, src_state)
                        ps, t_ps = pm.next()
                        for c in range(KC):
                            k.op("pe", lambda e: e.matmul(out=ps[:], lhsT=aT[:, c, :], rhs=wb[:, c, :],
                                                          start=(c == 0), stop=(c == KC - 1)),
                                 reads=[t_aT, t_wb], writes=[t_ps])
                        hb, t_hb, chh = hb_r.next()
                        k.dma("sp", hb[:], res[tsl(tt), fb * 512:(fb + 1) * 512], chh, reads=[t_res[tt]], writes=[t_hb])
                        vb, t_vb = vb_r.next()
                        k.op("dve", lambda e: e.scalar_tensor_tensor(out=vb[:], in0=hb[:], scalar=float(DN_ALPHA), in1=ps[:],
                                                                     op0=ALU.mult, op1=ALU.add),
                             reads=[t_hb, t_ps], writes=[t_vb])
                        k.dma("sp", vbuf[tsl(tt), fb * 512:(fb + 1) * 512], vb[:], ch_vbuf[tt], reads=[t_vb], writes=[t_vbuf[tt]])
        def ln_pass(li, lk):
            with k.scope() as st:
                gb = k.sb("ln_gsb", [128, D], F32, st)
                bb = k.sb("ln_bsb", [128, D], F32, st)
                t_gb = Tok()
                ch_gb = k.dma_chan("ch_gb")
                k.dma("sp", gb[:], ln_gd[li, lk], ch_gb, writes=[t_gb])
                k.dma("sp", bb[:], ln_bd[li, lk], ch_gb, writes=[t_gb])
                v_r = Ring(k, "lnv", 2, [128, D], F32, stack=st, chan=True)
                sq_r = Ring(k, "lnsq", 1, [128, D], F32, stack=st)
                hn_r = Ring(k, "lnh", 2, [128, D], F32, stack=st)
                st_r = Ring(k, "lnst", 2, [128, 8], F32, stack=st)
                xb_ring = Ring(k, "lnxb", 2, [128, D], BF16, stack=st)
                pr = PRing([4, 5, 6, 7])
                for tt in range(TT):
                    v, t_v, chv = v_r.next()
                    k.dma("sp", v[:], vbuf[tsl(tt), :], chv, reads=[t_vbuf[tt]], writes=[t_v])
                    sq, t_sq = sq_r.next()
                    s8, t_s8 = st_r.next()
                    k.op("dve", lambda e: e.reduce_sum(out=s8[:, 0:1], in_=v[:], axis=AX.X), reads=[t_v], writes=[t_s8])
                    k.op("act", lambda e: e.activation(out=sq[:], in_=v[:], func=AF.Square, accum_out=s8[:, 1:2]),
                         reads=[t_v], writes=[t_sq, t_s8])
                    k.op("dve", lambda e: e.tensor_scalar(out=s8[:, 2:3], in0=s8[:, 0:1], scalar1=1.0 / D, scalar2=None,
                                                          op0=ALU.mult), reads=[t_s8], writes=[t_s8])
                    k.op("dve", lambda e: e.tensor_tensor(out=s8[:, 3:4], in0=s8[:, 2:3], in1=s8[:, 2:3], op=ALU.mult),
                         reads=[t_s8], writes=[t_s8])
                    k.op("dve", lambda e: e.scalar_tensor_tensor(out=s8[:, 3:4], in0=s8[:, 1:2], scalar=1.0 / D,
                                                                 in1=s8[:, 3:4], op0=ALU.mult, op1=ALU.subtract),
                         reads=[t_s8], writes=[t_s8])
                    k.op("act", lambda e: e.activation(out=s8[:, 4:5], in_=s8[:, 3:4], func=AF.Sqrt, bias=float(LN_EPS)),
                         reads=[t_s8], writes=[t_s8])
                    k.op("dve", lambda e: e.reciprocal(out=s8[:, 5:6], in_=s8[:, 4:5]), reads=[t_s8], writes=[t_s8])
                    k.op("dve", lambda e: e.tensor_scalar(out=s8[:, 6:7], in0=s8[:, 2:3], scalar1=s8[:, 5:6], scalar2=-1.0,
                                                          op0=ALU.mult, op1=ALU.mult), reads=[t_s8], writes=[t_s8])
                    hn, t_hn = hn_r.next()
                    k.op("act", lambda e: e.activation(out=hn[:], in_=v[:], func=AF.Identity, scale=s8[:, 5:6],
                                                       bias=s8[:, 6:7]), reads=[t_v, t_s8], writes=[t_hn])
                    k.op("dve", lambda e: e.tensor_tensor(out=hn[:], in0=hn[:], in1=gb[:], op=ALU.mult),
                         reads=[t_hn, t_gb], writes=[t_hn])
                    k.op("dve", lambda e: e.tensor_tensor(out=hn[:], in0=hn[:], in1=bb[:], op=ALU.add),
                         reads=[t_hn, t_gb], writes=[t_hn])
                    k.dma("sp", res[tsl(tt), :], hn[:], ch_res[tt], reads=[t_hn], writes=[t_res[tt]])
                    transpose_rows(hn[:], t_hn, hT[:, :, tsl(tt)], t_hT[tt], xb_ring, pr)


        def featproj(W, ncols, dst, t_dst):
            with k.scope() as st:
                wr = Ring(k, "fpw", 2, [128, DC, 512], BF16, stack=st, chan=True)
                pm = PRing([0, 1, 2, 3])
                n = 0
                for fb in range((ncols + 511) // 512):
                    cols = min(512, ncols - fb * 512)
                    wb, t_wb, chw = wr.next()
                    k.dma("pool", wb[:, :, 0:cols], W.rearrange("(c p) f -> p c f", p=128)[:, :, fb * 512:fb * 512 + cols],
                          chw, writes=[t_wb])
                    for q in range(cols // 128):
                        fc = fb * 4 + q
                        for tb in range(5):
                            t0 = tb * 512
                            tw = 512 if tb < 4 else 128
                            tiles = list(range(tb * 4, tb * 4 + 4)) if tb < 4 else [NT]
                            ps, t_ps = pm.next()
                            for kc in range(DC):
                                k.op("pe", lambda e: e.matmul(out=ps[:, 0:tw], lhsT=wb[:, kc, q * 128:(q + 1) * 128],
                                                              rhs=hT[:, kc, t0:t0 + tw], start=(kc == 0), stop=(kc == DC - 1)),
                                     reads=[t_wb] + [t_hT[t] for t in tiles], writes=[t_ps])
                            eng = "act" if n % 2 == 0 else "dve"
                            n += 1
                            if eng == "act":
                                k.op("act", lambda e: e.copy(out=dst[:, fc, t0:t0 + tw], in_=ps[:, 0:tw]),
                                     reads=[t_ps], writes=[t_dst[t] for t in tiles])
                            else:
                                k.op("dve", lambda e: e.tensor_copy(out=dst[:, fc, t0:t0 + tw], in_=ps[:, 0:tw]),
                                     reads=[t_ps], writes=[t_dst[t] for t in tiles])

        def transpose_bf(src_fn, nblk, dst_fn, t_src, t_dst, pr):
            for i0 in range(0, nblk, 4):
                nn = min(4, nblk - i0)
                pt, t_pt = pr.next()
                ptb = pt[:].bitcast(BF16)
                for q in range(nn):
                    k.op("pe", lambda e: e.transpose(out=ptb[:, q * 128:(q + 1) * 128], in_=src_fn(i0 + q), identity=identb[:]),
                         reads=[t_src, t_const], writes=[t_pt])
                k.op("act", lambda e: e.copy(out=dst_fn(i0, nn), in_=ptb[:, 0:nn * 128].rearrange("p (c t) -> p c t", c=nn)),
                     reads=[t_pt], writes=[t_dst])

        def memattn_layer(i, only_kv=False):
            MSC = float(512 ** -0.5)
            with k.scope() as sl:
                KT = k.sb("ma_KT", [128, DC, MEM], BF16, sl)
                Vv = k.sb("ma_V", [128, 2, D], BF16, sl)
                KTs = k.sb("ma_KTs", [128, DC, MEM], BF16, sl)
                Vs = k.sb("ma_Vs", [128, 2, D], BF16, sl)
                t_kv, t_kvs = Tok(), Tok()
                with k.scope() as st:
                    memT = k.sb("memT", [128, DC, MEM], BF16, st)
                    t_memT = Tok()
                    Kb = k.sb("ma_Kb", [128, 2, D], BF16, st)
                    t_Kb = Tok()
                    xin = Ring(k, "xinm", 2, [128, D], F32, stack=st, chan=True)
                    xb_ring = Ring(k, "xbm", 2, [128, D], BF16, stack=st)
                    pr = PRing([0, 1, 2, 3])
                    for mt in range(2):
                        xt, t_xt, ch = xin.next()
                        k.dma("sp", xt[:], memp[tsl(mt), :], ch, writes=[t_xt])
                        transpose_rows(xt[:], t_xt, memT[:, :, tsl(mt)], t_memT, xb_ring, pr)
                    wr = Ring(k, "wkv", 2, [128, DC, 512], BF16, stack=st, chan=True)
                    pm = PRing([4, 5])
                    ost = Ring(k, "ost", 2, [128, 512], F32, stack=st, chan=True)
                    for fb in range(8):
                        wb, t_wb, chw = wr.next()
                        k.dma("pool", wb[:], mem_wkv[i].rearrange("(c p) f -> p c f", p=128)[:, :, fb * 512:(fb + 1) * 512],
                              chw, writes=[t_wb])
                        for mt in range(2):
                            ps, t_ps = pm.next()
                            for c in range(DC):
                                k.op("pe", lambda e: e.matmul(out=ps[:], lhsT=memT[:, c, tsl(mt)],
                                                              rhs=wb[:, c, :], start=(c == 0), stop=(c == DC - 1)),
                                     reads=[t_memT, t_wb], writes=[t_ps])
                            ob, t_ob, cho = ost.next()
                            k.op("act", lambda e: e.copy(out=ob[:], in_=ps[:]), reads=[t_ps], writes=[t_ob])
                            k.dma("sp", o_memkv[i, tsl(mt), fb * 512:(fb + 1) * 512], ob[:], cho, reads=[t_ob])
                            if fb < 4:
                                k.op("dve", lambda e: e.tensor_copy(out=Kb[:, mt, fb * 512:(fb + 1) * 512], in_=ps[:]),
                                     reads=[t_ps], writes=[t_Kb])
                            else:
                                k.op("dve", lambda e: e.tensor_copy(out=Vv[:, mt, (fb - 4) * 512:(fb - 3) * 512], in_=ps[:]),
                                     reads=[t_ps], writes=[t_kv])
                    for mt in range(2):
                        transpose_bf(lambda c: Kb[:, mt, c * 128:(c + 1) * 128], DC,
                                     lambda c0, n: KT[:, c0:c0 + n, tsl(mt)], t_Kb, t_kv, pr)
                    for mt in range(2):
                        for half in range(2):
                            xt, t_xt, ch = xin.next()
                            k.dma("sp", xt[:], cmk[i, tsl(mt), half * D:(half + 1) * D], ch, writes=[t_xt])
                            if half == 0:
                                xb, t_xb = xb_ring.next()
                                k.op("dve", lambda e: e.tensor_copy(out=xb[:], in_=xt[:]), reads=[t_xt], writes=[t_xb])
                                transpose_bf(lambda c: xb[:, c * 128:(c + 1) * 128], DC,
                                             lambda c0, n: KTs[:, c0:c0 + n, tsl(mt)], t_xb, t_kvs, pr)
                            else:
                                k.op("dve", lambda e: e.tensor_copy(out=Vs[:, mt, :], in_=xt[:]), reads=[t_xt], writes=[t_kvs])
                if only_kv:
                    return
                qT = k.sb("qT_all", [128, DC, TTOK], BF16, sl)
                t_q = [Tok() for _ in range(TT)]
                featproj(mem_wq[i], D, qT, t_q)
                with k.scope() as st:
                    p_r = Ring(k, "ma_p", 2, [128, 4, 256], F32, stack=st)
                    pn_r = Ring(k, "ma_pn", 2, [128, 4, 256], BF16, stack=st)
                    pT_r = Ring(k, "ma_pT", 2, [128, 8, 128], BF16, stack=st)
                    sm_r = Ring(k, "ma_sm", 2, [128, 16], F32, stack=st)
                    for tt in range(TT):
                        kt, vv, t_k = (KT, Vv, t_kv) if tt < NT else (KTs, Vs, t_kvs)
                        for h in range(4):
                            bk, t_bk = PB[h // 2], t_PB[h // 2]
                            off = (h % 2) * 256
                            for dc in range(4):
                                k.op("pe", lambda e: e.matmul(out=bk[:, off:off + 256], lhsT=qT[:, h * 4 + dc, tsl(tt)],
                                                              rhs=kt[:, h * 4 + dc, :], start=(dc == 0), stop=(dc == 3)),
                                     reads=[t_q[tt], t_k], writes=[t_bk])
                        sm, t_sm = sm_r.next()
                        for hb in range(2):
                            k.op("dve", lambda e: e.reduce_max(out=sm[:, hb * 2:hb * 2 + 2],
                                                               in_=PB[hb][:].rearrange("p (a m) -> p a m", a=2), axis=AX.X),
                                 reads=[t_PB[hb]], writes=[t_sm])
                        k.op("dve", lambda e: e.tensor_scalar(out=sm[:, 4:8], in0=sm[:, 0:4], scalar1=-MSC, scalar2=None, op0=ALU.mult),
                             reads=[t_sm], writes=[t_sm])
                        p, t_p = p_r.next()
                        for h in range(4):
                            off = (h % 2) * 256
                            k.op("act", lambda e: e.activation(out=p[:, h, :], in_=PB[h // 2][:, off:off + 256], func=AF.Exp,
                                                               scale=MSC, bias=sm[:, 4 + h:5 + h], accum_out=sm[:, 8 + h:9 + h]),
                                 reads=[t_PB[h // 2], t_sm], writes=[t_p, t_sm])
                        k.op("dve", lambda e: e.reciprocal(out=sm[:, 12:16], in_=sm[:, 8:12]), reads=[t_sm], writes=[t_sm])
                        pn, t_pn = pn_r.next()
                        k.op("dve", lambda e: e.tensor_tensor(out=pn[:], in0=p[:],
                                                              in1=sm[:, 12:16].unsqueeze(2).to_broadcast([128, 4, 256]), op=ALU.mult),
                             reads=[t_p, t_sm], writes=[t_pn])
                        pT, t_pT = pT_r.next()
                        transpose_bf(lambda j: pn[:, j // 2, (j % 2) * 128:(j % 2 + 1) * 128], 8,
                                     lambda j0, n: pT[:, j0:j0 + n, :], t_pn, t_pT, PRing([2, 3]))
                        for h in range(4):
                            bk, t_bk = PB[4 + h % 2], t_PB[4 + h % 2]
                            for dc in range(4):
                                for mc in range(2):
                                    k.op("pe", lambda e: e.matmul(out=bk[:, dc * 128:(dc + 1) * 128],
                                                                  lhsT=vv[:, mc, h * 512 + dc * 128:h * 512 + (dc + 1) * 128],
                                                                  rhs=pT[:, h * 2 + mc, :], start=(mc == 0), stop=(mc == 1)),
                                         reads=[t_k, t_pT], writes=[t_bk])
                            k.op("act", lambda e: e.copy(out=qT[:, h * 4:(h + 1) * 4, tsl(tt)],
                                                         in_=bk[:].rearrange("p (c t) -> p c t", c=4)),
                                 reads=[t_bk], writes=[t_q[tt]])
                proj_v(lambda tt, st, state: (qT[:, :, tsl(tt)], t_q[tt]), DC, mem_wo[i], wbufs=1)
            ln_pass(i, 1)


        def peer_layer(i):
            NEB = int(os.environ.get("P_NEB", "16"))
            t_E = [Tok() for _ in range(TT)]
            ch_E = [k.dma_chan("ch_E") for _ in range(TT)]
            t_acc = [Tok() for _ in range(TT)]
            ch_acc = [k.dma_chan("ch_acc") for _ in range(TT)]
            with k.scope() as sl:
                qT = k.sb("pq_all", [128, DC, TTOK], BF16, sl)
                t_q = [Tok() for _ in range(TT)]
                skT = k.sb("p_skT_s", [128, 16, 128], BF16, sl)
                t_sk = Tok()
                k.dma("pool", skT[:], p_skT[i], k.dma_chan("ch_sk"), writes=[t_sk])
                featproj(peer_wq[i], D, qT, t_q)
                with k.scope() as st:
                    Et_r = Ring(k, "p_Et", 2, [128, 2064], F32, stack=st)
                    sm_r = Ring(k, "p_sm", 2, [128, 32], F32, stack=st)
                    v16_r = Ring(k, "p_v16", 2, [128, 16, 16], F32, stack=st)
                    tmp_r = Ring(k, "p_tmp", 2, [128, 256], F32, stack=st)
                    cand_r = Ring(k, "p_cand", 2, [128, 8, 256], F32, stack=st)
                    c8_r = Ring(k, "p_c8", 2, [128, 8, 16], F32, stack=st)
                    for tt in range(TT):
                        for hk in range(16):
                            bk, t_bk = PB[hk // 4], t_PB[hk // 4]
                            k.op("pe", lambda e: e.matmul(out=bk[:, (hk % 4) * 128:(hk % 4 + 1) * 128], lhsT=qT[:, hk, tsl(tt)],
                                                          rhs=skT[:, hk, :], start=True, stop=True),
                                 reads=[t_q[tt], t_sk], writes=[t_bk])
                        sm, t_sm = sm_r.next()
                        for bq in range(4):
                            k.op("dve", lambda e: e.reduce_max(out=sm[:, bq * 4:bq * 4 + 4],
                                                               in_=PB[bq][:].rearrange("p (a n) -> p a n", a=4), axis=AX.X),
                                 reads=[t_PB[bq]], writes=[t_sm])
                        k.op("dve", lambda e: e.tensor_scalar(out=sm[:, 16:32], in0=sm[:, 0:16], scalar1=-1.0, scalar2=None, op0=ALU.mult),
                             reads=[t_sm], writes=[t_sm])
                        Et, t_Et = Et_r.next()
                        for hk in range(16):
                            k.op("act", lambda e: e.activation(out=Et[:, hk * 128:(hk + 1) * 128],
                                                               in_=PB[hk // 4][:, (hk % 4) * 128:(hk % 4 + 1) * 128], func=AF.Exp,
                                                               bias=sm[:, 16 + hk:17 + hk]),
                                 reads=[t_PB[hk // 4], t_sm], writes=[t_Et])
                        v16, t_v16 = v16_r.next()
                        for hk in range(16):
                            tmp, t_tmp = tmp_r.next()
                            Eh = Et[:, hk * 128:(hk + 1) * 128]
                            k.op("dve", lambda e: e.max(out=v16[:, hk, 0:8], in_=Eh), reads=[t_Et], writes=[t_v16])
                            k.op("dve", lambda e: e.match_replace(out=tmp[:, 0:128], in_to_replace=v16[:, hk, 0:8], in_values=Eh,
                                                                  imm_value=-1.0), reads=[t_Et, t_v16], writes=[t_tmp])
                            k.op("dve", lambda e: e.max(out=v16[:, hk, 8:16], in_=tmp[:, 0:128]), reads=[t_tmp], writes=[t_v16])
                        cand, t_cand = cand_r.next()
                        c8, t_c8 = c8_r.next()
                        for h in range(8):
                            k.op("dve", lambda e: e.tensor_tensor(
                                out=cand[:, h, :].rearrange("p (a b) -> p a b", a=16),
                                in0=v16[:, 2 * h, :].unsqueeze(2).to_broadcast([128, 16, 16]),
                                in1=v16[:, 2 * h + 1, :].unsqueeze(1).to_broadcast([128, 16, 16]), op=ALU.mult),
                                 reads=[t_v16], writes=[t_cand])
                            tmp, t_tmp = tmp_r.next()
                            k.op("dve", lambda e: e.max(out=c8[:, h, 0:8], in_=cand[:, h, :]), reads=[t_cand], writes=[t_c8])
                            k.op("dve", lambda e: e.match_replace(out=tmp[:], in_to_replace=c8[:, h, 0:8], in_values=cand[:, h, :],
                                                                  imm_value=-1.0), reads=[t_cand, t_c8], writes=[t_tmp])
                            k.op("dve", lambda e: e.max(out=c8[:, h, 8:16], in_=tmp[:]), reads=[t_tmp], writes=[t_c8])
                            k.op("dve", lambda e: e.tensor_copy(out=Et[:, 2048 + h:2049 + h], in_=c8[:, h, 15:16]),
                                 reads=[t_c8], writes=[t_Et])
                            tmp2, t_tmp2 = tmp_r.next()
                            k.op("dve", lambda e: e.scalar_tensor_tensor(out=tmp2[:], in0=cand[:, h, :], scalar=c8[:, h, 15:16],
                                                                         in1=cand[:, h, :], op0=ALU.is_ge, op1=ALU.mult),
                                 reads=[t_cand, t_c8], writes=[t_tmp2])
                            k.op("dve", lambda e: e.reduce_sum(out=Et[:, 2056 + h:2057 + h], in_=tmp2[:], axis=AX.X),
                                 reads=[t_tmp2], writes=[t_Et])
                        k.op("dve", lambda e: e.reciprocal(out=Et[:, 2056:2064], in_=Et[:, 2056:2064]), reads=[t_Et], writes=[t_Et])
                        k.dma("sp", pe_E[tsl(tt), :], Et[:], ch_E[tt], reads=[t_Et], writes=[t_E[tt]])
            with k.scope() as sl:
                UT = k.sb("p_UT", [128, DC, 1024], BF16, sl)
                Vb = k.sb("p_Vb", [128, 8, D], BF16, sl)
                t_UT, t_Vb = Tok(), Tok()
                ch_UT, ch_Vb = k.dma_chan("ch_UT"), k.dma_chan("ch_Vb")
                Et_r = Ring(k, "p_Et2", 2, [128, 2064], F32, stack=sl, chan=True)
                acc_r = Ring(k, "p_acc", 2, [128, D], F32, stack=sl, chan=True)
                gA_r = Ring(k, "p_gA", 1, [128, 1024], F32, stack=sl)
                G_r = Ring(k, "p_G", 1, [128, 1024], F32, stack=sl)
                P_r = Ring(k, "p_P", 1, [128, 1024], F32, stack=sl)
                M_r = Ring(k, "p_M", 1, [128, 1024], F32, stack=sl)
                w_r = Ring(k, "p_w", 2, [128, 1024], BF16, stack=sl)
                wT_r = Ring(k, "p_wT", 2, [128, 8, 128], BF16, stack=sl)
                uT_src = p_uT[i].rearrange("(c p) e -> p c e", p=128)
                for eb in range(NEB):
                    i0 = eb * 8
                    for hh in range(2):
                        k.dma("pool", UT[:, :, hh * 512:(hh + 1) * 512], uT_src[:, :, eb * 1024 + hh * 512:eb * 1024 + (hh + 1) * 512],
                              ch_UT, writes=[t_UT])
                    for hh in range(2):
                        k.dma("pool", Vb[:, hh * 4:(hh + 1) * 4, :],
                              peer_v[i, eb * 1024 + hh * 512:eb * 1024 + (hh + 1) * 512, :].rearrange("(c p) d -> p c d", p=128),
                              ch_Vb, writes=[t_Vb])
                    for tt in range(TT):
                        Et, t_Et, chE = Et_r.next()
                        k.dma("sp", Et[:], pe_E[tsl(tt), :], chE, reads=[t_E[tt]], writes=[t_Et])
                        acc, t_ac, cha = acc_r.next()
                        if eb > 0:
                            k.dma("sp", acc[:], pe_acc[tsl(tt), :], cha, reads=[t_acc[tt]], writes=[t_ac])
                        gA, t_gA = gA_r.next()
                        for eh in range(2):
                            for kc in range(DC):
                                k.op("pe", lambda e: e.matmul(out=PB[eh][:], lhsT=hT[:, kc, tsl(tt)], rhs=UT[:, kc, eh * 512:(eh + 1) * 512],
                                                              start=(kc == 0), stop=(kc == DC - 1)),
                                     reads=[t_hT[tt], t_UT], writes=[t_PB[eh]])
                            k.op("act", lambda e: e.activation(out=gA[:, eh * 512:(eh + 1) * 512], in_=PB[eh][:], func=AF.Gelu),
                                 reads=[t_PB[eh]], writes=[t_gA])
                        G, t_G = G_r.next()
                        for h in range(8):
                            P, t_P = P_r.next()
                            k.op("pool", lambda e: e.tensor_tensor(
                                out=P[:].rearrange("p (a b) -> p a b", a=8),
                                in0=Et[:, 2 * h * 128 + i0:2 * h * 128 + i0 + 8].unsqueeze(2).to_broadcast([128, 8, 128]),
                                in1=Et[:, (2 * h + 1) * 128:(2 * h + 2) * 128].unsqueeze(1).to_broadcast([128, 8, 128]), op=ALU.mult),
                                 reads=[t_Et], writes=[t_P])
                            M, t_M = M_r.next()
                            k.op("dve", lambda e: e.scalar_tensor_tensor(out=M[:], in0=P[:], scalar=Et[:, 2048 + h:2049 + h], in1=P[:],
                                                                         op0=ALU.is_ge, op1=ALU.mult), reads=[t_P, t_Et], writes=[t_M])
                            if h == 0:
                                k.op("dve", lambda e: e.tensor_scalar(out=G[:], in0=M[:], scalar1=Et[:, 2056:2057], scalar2=None, op0=ALU.mult),
                                     reads=[t_M, t_Et], writes=[t_G])
                            else:
                                k.op("dve", lambda e: e.scalar_tensor_tensor(out=G[:], in0=M[:], scalar=Et[:, 2056 + h:2057 + h], in1=G[:],
                                                                             op0=ALU.mult, op1=ALU.add), reads=[t_M, t_Et, t_G], writes=[t_G])
                        w, t_w = w_r.next()
                        k.op("dve", lambda e: e.tensor_tensor(out=w[:], in0=G[:], in1=gA[:], op=ALU.mult), reads=[t_G, t_gA], writes=[t_w])
                        wT, t_wT = wT_r.next()
                        transpose_bf(lambda c: w[:, c * 128:(c + 1) * 128], 8, lambda c0, n: wT[:, c0:c0 + n, :], t_w, t_wT, PRing([2, 3]))
                        for db in range(4):
                            for ec in range(8):
                                k.op("pe", lambda e: e.matmul(out=PB[4 + db][:], lhsT=wT[:, ec, :], rhs=Vb[:, ec, db * 512:(db + 1) * 512],
                                                              start=(ec == 0), stop=(ec == 7)),
                                     reads=[t_wT, t_Vb], writes=[t_PB[4 + db]])
                            if eb > 0:
                                k.op("dve", lambda e: e.tensor_tensor(out=acc[:, db * 512:(db + 1) * 512], in0=acc[:, db * 512:(db + 1) * 512],
                                                                      in1=PB[4 + db][:], op=ALU.add), reads=[t_PB[4 + db], t_ac], writes=[t_ac])
                            else:
                                k.op("act", lambda e: e.copy(out=acc[:, db * 512:(db + 1) * 512], in_=PB[4 + db][:]),
                                     reads=[t_PB[4 + db]], writes=[t_ac])
                        k.dma("sp", pe_acc[tsl(tt), :], acc[:], ch_acc[tt], reads=[t_ac], writes=[t_acc[tt]])
            with k.scope() as st:
                r_r = Ring(k, "p_r", 2, [128, D], F32, stack=st, chan=True)
                a_r = Ring(k, "p_a", 2, [128, D], F32, stack=st, chan=True)
                for tt in range(TT):
                    r, t_r, chr_ = r_r.next()
                    a, t_a, cha = a_r.next()
                    k.dma("sp", r[:], res[tsl(tt), :], chr_, reads=[t_res[tt]], writes=[t_r])
                    k.dma("sp", a[:], pe_acc[tsl(tt), :], cha, reads=[t_acc[tt]], writes=[t_a])
                    k.op("dve", lambda e: e.scalar_tensor_tensor(out=a[:], in0=r[:], scalar=float(DN_ALPHA), in1=a[:],
                                                                 op0=ALU.mult, op1=ALU.add), reads=[t_r, t_a], writes=[t_a])
                    k.dma("sp", vbuf[tsl(tt), :], a[:], ch_vbuf[tt], reads=[t_a], writes=[t_vbuf[tt]])
            ln_pass(i, 2)


        def fox_layer(li):
            SC = float(128 ** -0.5)
            w_in = f_win.rearrange("(c p) f -> p c f", p=128)
            with k.scope() as sl:
                kT = k.sb("fx_kT", [128, 4, TTOK], BF16, sl)
                vtok = k.sb("fx_v", [128, TT, 512], BF16, sl)
                negcT = k.sb("fx_ncT", [16, SEQ], F32, sl)
                sel = k.sb("fx_sel", [16, 16 * 128], F32, sl)
                mneg = k.sb("fx_mneg", [128, 128], F32, sl)
                t_kv, t_nc, t_cst = Tok(), Tok(), Tok()
                knew = k.sb("fx_knew", [1, 1024], F32, sl)
                lfnew = k.sb("fx_lfnew", [1, 16], F32, sl)
                t_new = Tok()
                chc = k.dma_chan("ch_fxc")
                k.dma("sp", sel[:], sel16_d, chc, writes=[t_cst])
                k.dma("sp", mneg[:], mneg_d, chc, writes=[t_cst])
                with k.scope() as st:
                    wkv = k.sb("fx_wkv", [128, DC, 1024], BF16, st)
                    wf = k.sb("fx_wf", [128, DC, 16], BF16, st)
                    bf_s = k.sb("fx_bf", [128, 16], F32, st)
                    lf_all = k.sb("fx_lf", [128, NT, 16], F32, st)
                    t_w, t_lf = Tok(), Tok()
                    chw = k.dma_chan("ch_fxw")
                    for hh in range(2):
                        k.dma("pool", wkv[:, :, hh * 512:(hh + 1) * 512], w_in[:, :, 2048 + hh * 512:2048 + (hh + 1) * 512], chw, writes=[t_w])
                    k.dma("pool", wf[:], w_in[:, :, 3072:3088], chw, writes=[t_w])
                    k.dma("sp", bf_s[:], f_bf, chw, writes=[t_w])
                    kvo_r = Ring(k, "fx_kvo", 2, [128, 1024], F32, stack=st, chan=True)
                    kb_r = Ring(k, "fx_kb", 2, [128, 512], BF16, stack=st)
                    lt_r = Ring(k, "fx_lt", 2, [128, 4, 16], F32, stack=st, chan=True)
                    nc_r = Ring(k, "fx_nc", 2, [128, 16], F32, stack=st)
                    for tt in range(TT):
                        kvo, t_kvo, chk = kvo_r.next()
                        for hh in range(2):
                            for kc in range(DC):
                                k.op("pe", lambda e: e.matmul(out=PB[hh][:], lhsT=hT[:, kc, tsl(tt)], rhs=wkv[:, kc, hh * 512:(hh + 1) * 512],
                                                              start=(kc == 0), stop=(kc == DC - 1)), reads=[t_hT[tt], t_w], writes=[t_PB[hh]])
                            k.op("act", lambda e: e.copy(out=kvo[:, hh * 512:(hh + 1) * 512], in_=PB[hh][:]), reads=[t_PB[hh]], writes=[t_kvo])
                        k.dma("sp", o_fox_kv[tsl(tt), :], kvo[:], chk, reads=[t_kvo])
                        kb, t_kb = kb_r.next()
                        k.op("dve", lambda e: e.tensor_copy(out=kb[:], in_=kvo[:, 0:512]), reads=[t_kvo], writes=[t_kb])
                        k.op("dve", lambda e: e.tensor_copy(out=vtok[:, tt, :], in_=kvo[:, 512:1024]), reads=[t_kvo], writes=[t_kv])
                        transpose_bf(lambda c: kb[:, c * 128:(c + 1) * 128], 4, lambda c0, n: kT[:, c0:c0 + n, tsl(tt)], t_kb, t_kv, PRing([2, 3]))
                        for kc in range(DC):
                            k.op("pe", lambda e: e.matmul(out=PB[4][:, 0:16], lhsT=hT[:, kc, tsl(tt)], rhs=wf[:, kc, :],
                                                          start=(kc == 0), stop=(kc == DC - 1)), reads=[t_hT[tt], t_w], writes=[t_PB[4]])
                        lt, t_lt, chl = lt_r.next()
                        x0, ax, ee, lf = lt[:, 0, :], lt[:, 1, :], lt[:, 2, :], lt[:, 3, :]
                        k.op("dve", lambda e: e.tensor_tensor(out=x0, in0=PB[4][:, 0:16], in1=bf_s[:], op=ALU.add), reads=[t_PB[4], t_w], writes=[t_lt])
                        k.op("dve", lambda e: e.scalar_tensor_tensor(out=ax, in0=x0, scalar=-1.0, in1=x0, op0=ALU.mult, op1=ALU.max),
                             reads=[t_lt], writes=[t_lt])
                        k.op("act", lambda e: e.activation(out=ee, in_=ax, func=AF.Exp, scale=-1.0), reads=[t_lt], writes=[t_lt])
                        k.op("act", lambda e: e.activation(out=ee, in_=ee, func=AF.Ln, bias=1.0), reads=[t_lt], writes=[t_lt])
                        k.op("dve", lambda e: e.scalar_tensor_tensor(out=lf, in0=x0, scalar=0.0, in1=ee, op0=ALU.min, op1=ALU.subtract),
                             reads=[t_lt], writes=[t_lt])
                        k.dma("sp", o_fox_lf[tsl(tt), :], lf, chl, reads=[t_lt])
                        if tt == NT:
                            k.op("act", lambda e: e.copy(out=knew[:], in_=kvo[0:1, :]), reads=[t_kvo], writes=[t_new])
                            k.op("act", lambda e: e.copy(out=lfnew[:], in_=lt[0:1, 3, :]), reads=[t_lt], writes=[t_new])
                        if tt < NT:
                            k.op("dve", lambda e: e.tensor_copy(out=lf_all[:, tt, :], in_=lf), reads=[t_lt], writes=[t_lf])
                            for jj in range(tt + 1):
                                k.op("pe", lambda e: e.matmul(out=PB[5][:, 0:16], lhsT=(trit[:] if jj == tt else ones[:]), rhs=lf_all[:, jj, :],
                                                              start=(jj == 0), stop=(jj == tt)), reads=[t_lf, t_const], writes=[t_PB[5]])
                            ncx, t_ncx = nc_r.next()
                            k.op("dve", lambda e: e.tensor_scalar(out=ncx[:], in0=PB[5][:, 0:16], scalar1=-1.0 / SC, scalar2=None, op0=ALU.mult),
                                 reads=[t_PB[5]], writes=[t_ncx])
                            k.op("pe", lambda e: e.transpose(out=PB[6][0:16, 0:128], in_=ncx[:], identity=ident[:]),
                                 reads=[t_ncx, t_const], writes=[t_PB[6]])
                            k.op("act", lambda e: e.copy(out=negcT[:, tsl(tt)], in_=PB[6][0:16, 0:128]), reads=[t_PB[6]], writes=[t_nc])
                if os.environ.get("FOX_SAMPLE", "1") == "1":
                  with k.scope() as st:
                    NP_ = 128
                    pti = k.sb("fs_pti", [128, NP_], I32, st)
                    ptf = k.sb("fs_ptf", [128, NP_], F32, st)
                    idx = k.sb("fs_idx", [128, NP_], I32, st)
                    io = k.sb("fs_io", [128, 1], F32, st)
                    tris = k.sb("fs_tris", [128, 128], F32, st)
                    m16 = k.sb("fs_m16", [16, 4], F32, st)
                    t_ix = Tok()
                    chx = k.dma_chan("ch_fsx")
                    k.dma("sp", pti[:], pt_d.to_broadcast([128, NP_]), chx, writes=[t_ix])
                    k.dma("sp", io[:], iota_d, chx, writes=[t_ix])
                    k.dma("sp", tris[:], tris_d, chx, writes=[t_ix])
                    k.dma("sp", m16[:], mask16_d, chx, writes=[t_ix])
                    k.op("dve", lambda e: e.tensor_copy(out=ptf[:], in_=pti[:]), reads=[t_ix], writes=[t_ix])
                    k.op("dve", lambda e: e.tensor_scalar(out=ptf[:], in0=ptf[:], scalar1=128.0, scalar2=io[:, 0:1], op0=ALU.mult, op1=ALU.add),
                         reads=[t_ix], writes=[t_ix])
                    k.op("dve", lambda e: e.tensor_copy(out=idx[:], in_=ptf[:]), reads=[t_ix], writes=[t_ix])
                    qbc = k.sb("fs_qbc", [128, D], F32, st)
                    t_qr, t_qb = Tok(), Tok()
                    with k.scope() as sq:
                        qrow = k.sb("fs_qrow", [1, D], F32, sq)
                        wq_r = Ring(k, "fs_wq", 1, [128, DC, 512], BF16, stack=sq, chan=True)
                        for blk in range(4):
                            wq, t_wq, chq = wq_r.next()
                            k.dma("pool", wq[:], w_in[:, :, blk * 512:(blk + 1) * 512], chq, writes=[t_wq])
                            for kc in range(DC):
                                k.op("pe", lambda e: e.matmul(out=PB[0][0:1, :], lhsT=hT[:, kc, SEQ:SEQ + 1], rhs=wq[:, kc, :],
                                                              start=(kc == 0), stop=(kc == DC - 1)), reads=[t_hT[NT], t_wq], writes=[t_PB[0]])
                            k.op("act", lambda e: e.copy(out=qrow[0:1, blk * 512:(blk + 1) * 512], in_=PB[0][0:1, :]), reads=[t_PB[0]], writes=[t_qr])
                            k.op("pe", lambda e: e.matmul(out=PB[1][:], lhsT=ones[0:1, :], rhs=qrow[0:1, blk * 512:(blk + 1) * 512], start=True, stop=True),
                                 reads=[t_qr, t_const], writes=[t_PB[1]])
                            k.op("act", lambda e: e.copy(out=qbc[:, blk * 512:(blk + 1) * 512], in_=PB[1][:]), reads=[t_PB[1]], writes=[t_qb])
                    L = k.sb("fs_L", [128, NP_, 16], F32, st)
                    xa = k.sb("fs_xa", [128, NP_, 16], F32, st)
                    xb = k.sb("fs_xb", [128, NP_, 16], F32, st)
                    t_L, t_xa, t_xb = Tok(), Tok(), Tok()
                    chl2 = k.dma_chan("ch_fsl")
                    for pg in range(NP_):
                        k.idma(L[:, pg, :], f_lpool, idx[:, pg:pg + 1], chl2, reads=[t_ix], writes=[t_L])
                    src, t_src, dst, t_dst = L, t_L, xa, t_xa
                    dd = 1
                    while dd < NP_:
                        k.op("dve", lambda e: e.tensor_tensor(out=dst[:, 0:NP_ - dd, :], in0=src[:, 0:NP_ - dd, :], in1=src[:, dd:NP_, :], op=ALU.add),
                             reads=[t_src], writes=[t_dst])
                        k.op("dve", lambda e: e.tensor_copy(out=dst[:, NP_ - dd:NP_, :], in_=src[:, NP_ - dd:NP_, :]), reads=[t_src], writes=[t_dst])
                        if src is L:
                            src, t_src, dst, t_dst = xa, t_xa, xb, t_xb
                        else:
                            src, t_src, dst, t_dst = dst, t_dst, src, t_src
                        dd *= 2
                    k.op("dve", lambda e: e.tensor_tensor(out=dst[:], in0=src[:], in1=L[:], op=ALU.subtract), reads=[t_src, t_L], writes=[t_dst])
                    lfrep = k.sb("fs_lfrep", [1, NP_, 16], F32, st)
                    t_lr = Tok()
                    k.op("dve", lambda e: e.tensor_copy(out=lfrep[:], in_=lfnew[0:1, :].unsqueeze(1).to_broadcast([1, NP_, 16])),
                         reads=[t_new], writes=[t_lr])
                    Sx = k.sb("fs_S", [128, NP_ + 1, 16], F32, st)
                    t_S = Tok()
                    Lf = L[:].rearrange("p g h -> p (g h)")
                    Xf = dst[:].rearrange("p g h -> p (g h)")
                    Rf = lfrep[:].rearrange("p g h -> p (g h)")
                    Sf = Sx[:].rearrange("p g h -> p (g h)")
                    for blk in range(4):
                        cs = slice(blk * 512, (blk + 1) * 512)
                        k.op("pe", lambda e: e.matmul(out=PB[2][:], lhsT=tris[:], rhs=Lf[:, cs], start=True, stop=False), reads=[t_L, t_ix], writes=[t_PB[2]])
                        k.op("pe", lambda e: e.matmul(out=PB[2][:], lhsT=ones[:], rhs=Xf[:, cs], start=False, stop=False), reads=[t_dst, t_const], writes=[t_PB[2]])
                        k.op("pe", lambda e: e.matmul(out=PB[2][:], lhsT=ones[0:1, :], rhs=Rf[0:1, cs], start=False, stop=True), reads=[t_lr, t_const], writes=[t_PB[2]])
                        k.op("act", lambda e: e.copy(out=Sf[:, cs], in_=PB[2][:]), reads=[t_PB[2]], writes=[t_S])
                    kp_r = Ring(k, "fs_kp", 3, [128, 512], F32, stack=st, chan=True)
                    pr_r = Ring(k, "fs_pr", 2, [128, 4, 128], F32, stack=st)
                    s4_r = Ring(k, "fs_s4", 2, [128, 4], F32, stack=st)
                    for pg in range(NP_):
                        kp, t_kp, chk = kp_r.next()
                        k.idma(kp[:], f_kpool, idx[:, pg:pg + 1], chk, reads=[t_ix], writes=[t_kp])
                        for g in range(4):
                            pr, t_pr = pr_r.next()
                            k.op("dve", lambda e: e.tensor_tensor(out=pr[:], in0=kp[:, g * 128:(g + 1) * 128].unsqueeze(1).to_broadcast([128, 4, 128]),
                                                                  in1=qbc[:, g * 512:(g + 1) * 512].rearrange("p (r d) -> p r d", r=4), op=ALU.mult),
                                 reads=[t_kp, t_qb], writes=[t_pr])
                            s4, t_s4 = s4_r.next()
                            k.op("dve", lambda e: e.reduce_sum(out=s4[:], in_=pr[:], axis=AX.X), reads=[t_pr], writes=[t_s4])
                            k.op("dve", lambda e: e.scalar_tensor_tensor(out=Sx[:, pg, g * 4:(g + 1) * 4], in0=s4[:], scalar=SC,
                                                                         in1=Sx[:, pg, g * 4:(g + 1) * 4], op0=ALU.mult, op1=ALU.add),
                                 reads=[t_s4, t_S], writes=[t_S])
                    k.op("dve", lambda e: e.memset(Sx[:, NP_, :], -1e30), reads=[t_S], writes=[t_S])
                    for g in range(4):
                        pr, t_pr = pr_r.next()
                        k.op("dve", lambda e: e.tensor_tensor(out=pr[0:1], in0=knew[0:1, g * 128:(g + 1) * 128].unsqueeze(1).to_broadcast([1, 4, 128]),
                                                              in1=qbc[0:1, g * 512:(g + 1) * 512].rearrange("p (r d) -> p r d", r=4), op=ALU.mult),
                             reads=[t_new, t_qb], writes=[t_pr])
                        s4, t_s4 = s4_r.next()
                        k.op("dve", lambda e: e.reduce_sum(out=s4[0:1, :], in_=pr[0:1], axis=AX.X), reads=[t_pr], writes=[t_s4])
                        k.op("dve", lambda e: e.tensor_scalar(out=Sx[0:1, NP_, g * 4:(g + 1) * 4], in0=s4[0:1, :], scalar1=SC, scalar2=None, op0=ALU.mult),
                             reads=[t_s4, t_S], writes=[t_S])
                    sm = k.sb("fs_sm", [128, 64], F32, st)
                    sm16 = k.sb("fs_sm16", [16, 160], F32, st)
                    t_sm = Tok()
                    k.op("dve", lambda e: e.reduce_max(out=sm[:, 0:16], in_=Sx[:].rearrange("p g h -> p h g"), axis=AX.X), reads=[t_S], writes=[t_sm])
                    k.op("pe", lambda e: e.transpose(out=PB[3][0:16, 0:128], in_=sm[:, 0:16], identity=ident[:]), reads=[t_sm, t_const], writes=[t_PB[3]])
                    k.op("dve", lambda e: e.reduce_max(out=sm16[:, 0:1], in_=PB[3][0:16, 0:128], axis=AX.X), reads=[t_PB[3]], writes=[t_sm])
                    k.op("dve", lambda e: e.tensor_scalar(out=sm16[:, 16:32], in0=ident[0:16, 0:16], scalar1=sm16[:, 0:1], scalar2=None, op0=ALU.mult),
                         reads=[t_sm, t_const], writes=[t_sm])
                    k.op("pe", lambda e: e.matmul(out=PB[3][:, 128:144], lhsT=ones[0:16, :], rhs=sm16[:, 16:32], start=True, stop=True),
                         reads=[t_sm, t_const], writes=[t_PB[3]])
                    k.op("act", lambda e: e.copy(out=sm[:, 16:32], in_=PB[3][:, 128:144]), reads=[t_PB[3]], writes=[t_sm])
                    k.op("dve", lambda e: e.tensor_tensor(out=Sx[:], in0=Sx[:], in1=sm[:, 16:32].unsqueeze(1).to_broadcast([128, NP_ + 1, 16]), op=ALU.subtract),
                         reads=[t_sm, t_S], writes=[t_S])
                    k.op("act", lambda e: e.activation(out=Sx[:], in_=Sx[:], func=AF.Exp), reads=[t_S], writes=[t_S])
                    k.op("dve", lambda e: e.reduce_sum(out=sm[:, 32:48], in_=Sx[:].rearrange("p g h -> p h g"), axis=AX.X), reads=[t_S], writes=[t_sm])
                    k.op("pe", lambda e: e.matmul(out=PB[3][:, 256:272], lhsT=ones[:], rhs=sm[:, 32:48], start=True, stop=True),
                         reads=[t_sm, t_const], writes=[t_PB[3]])
                    k.op("dve", lambda e: e.reciprocal(out=sm[:, 48:64], in_=PB[3][:, 256:272]), reads=[t_PB[3]], writes=[t_sm])
                    k.op("dve", lambda e: e.tensor_tensor(out=Sx[:], in0=Sx[:], in1=sm[:, 48:64].unsqueeze(1).to_broadcast([128, NP_ + 1, 16]), op=ALU.mult),
                         reads=[t_sm, t_S], writes=[t_S])
                    for pg in range(NP_):
                        vp, t_vp, chv = kp_r.next()
                        k.idma(vp[:], f_vpool, idx[:, pg:pg + 1], chv, reads=[t_ix], writes=[t_vp])
                        k.op("pe", lambda e: e.matmul(out=PB[4][0:16, :], lhsT=Sx[:, pg, :], rhs=vp[:], start=(pg == 0), stop=False),
                             reads=[t_S, t_vp], writes=[t_PB[4]])
                    vn, t_vn, chv = kp_r.next()
                    k.op("dve", lambda e: e.memset(vn[:], 0.0), writes=[t_vn])
                    k.op("act", lambda e: e.copy(out=vn[0:1, :], in_=knew[0:1, 512:1024]), reads=[t_new, t_vn], writes=[t_vn])
                    k.op("pe", lambda e: e.matmul(out=PB[4][0:16, :], lhsT=Sx[:, NP_, :], rhs=vn[:], start=False, stop=True),
                         reads=[t_S, t_vn], writes=[t_PB[4]])
                    k.op("dve", lambda e: e.tensor_scalar(out=sm16[:, 32:160], in0=PB[4][0:16, 0:128], scalar1=m16[:, 0:1], scalar2=None, op0=ALU.mult),
                         reads=[t_PB[4], t_ix], writes=[t_sm])
                    for g in range(1, 4):
                        k.op("dve", lambda e: e.scalar_tensor_tensor(out=sm16[:, 32:160], in0=PB[4][0:16, g * 128:(g + 1) * 128], scalar=m16[:, g:g + 1],
                                                                     in1=sm16[:, 32:160], op0=ALU.mult, op1=ALU.add), reads=[t_PB[4], t_ix, t_sm], writes=[t_sm])
                    osb = k.sb("fs_osb", [16, 128], BF16, st)
                    t_osb = Tok()
                    k.op("act", lambda e: e.copy(out=osb[:], in_=sm16[:, 32:160]), reads=[t_sm], writes=[t_osb])
                    k.dma("sp", ynT_d[0:D, SEQ:SEQ + 1].rearrange("(h d) o -> h (d o)", h=16), osb[:], ch_ynTd[NT], reads=[t_osb], writes=[t_ynTd[NT]],
                          allow_slow_non_contiguous=True)
                for g in range(4):
                    with k.scope() as st:
                        qTg = k.sb("fx_qT", [128, 4, TTOK], BF16, st)
                        t_qg = [Tok() for _ in range(TT)]
                        featproj(f_win[:, g * 512:(g + 1) * 512], 512, qTg, t_qg)
                        p_r = Ring(k, "fx_p", 2, [128, SEQ], BF16, stack=st)
                        pT_r = Ring(k, "fx_pT", 2, [128, NT, 128], BF16, stack=st)
                        sm_r = Ring(k, "fx_sm", 2, [128, 16], F32, stack=st)
                        oT_r = Ring(k, "fx_oT", 2, [128, 4, 128], BF16, stack=st)
                        for qt in range(NT):
                            nkeys = (qt + 1) * 128
                            nb = (nkeys + 511) // 512
                            for r in range(4):
                                h = g * 4 + r
                                for bi in range(nb):
                                    c0 = bi * 512
                                    w_ = min(512, nkeys - c0)
                                    last = (bi == nb - 1)
                                    k.op("pe", lambda e: e.matmul(out=PB[bi][:, 0:w_], lhsT=qTg[:, r, tsl(qt)], rhs=kT[:, g, c0:c0 + w_],
                                                                  start=True, stop=False), reads=[t_qg[qt], t_kv], writes=[t_PB[bi]])
                                    k.op("pe", lambda e: e.matmul(out=PB[bi][:, 0:w_], lhsT=sel[:, h * 128:(h + 1) * 128], rhs=negcT[:, c0:c0 + w_],
                                                                  start=False, stop=(not last)), reads=[t_cst, t_nc], writes=[t_PB[bi]])
                                    if last:
                                        k.op("pe", lambda e: e.matmul(out=PB[bi][:, w_ - 128:w_], lhsT=ident[:], rhs=mneg[:], start=False, stop=True),
                                             reads=[t_cst, t_const], writes=[t_PB[bi]])
                                sm, t_sm = sm_r.next()
                                for bi in range(nb):
                                    w_ = min(512, nkeys - bi * 512)
                                    k.op("dve", lambda e: e.reduce_max(out=sm[:, bi:bi + 1], in_=PB[bi][:, 0:w_], axis=AX.X), reads=[t_PB[bi]], writes=[t_sm])
                                k.op("dve", lambda e: e.reduce_max(out=sm[:, 4:5], in_=sm[:, 0:nb], axis=AX.X), reads=[t_sm], writes=[t_sm])
                                k.op("dve", lambda e: e.tensor_scalar(out=sm[:, 5:6], in0=sm[:, 4:5], scalar1=-SC, scalar2=None, op0=ALU.mult),
                                     reads=[t_sm], writes=[t_sm])
                                p, t_p = p_r.next()
                                for bi in range(nb):
                                    c0 = bi * 512
                                    w_ = min(512, nkeys - c0)
                                    k.op("act", lambda e: e.activation(out=p[:, c0:c0 + w_], in_=PB[bi][:, 0:w_], func=AF.Exp, scale=SC,
                                                                       bias=sm[:, 5:6], accum_out=sm[:, 8 + bi:9 + bi]),
                                         reads=[t_PB[bi], t_sm], writes=[t_p, t_sm])
                                k.op("dve", lambda e: e.reduce_sum(out=sm[:, 6:7], in_=sm[:, 8:8 + nb], axis=AX.X), reads=[t_sm], writes=[t_sm])
                                k.op("dve", lambda e: e.reciprocal(out=sm[:, 7:8], in_=sm[:, 6:7]), reads=[t_sm], writes=[t_sm])
                                k.op("dve", lambda e: e.tensor_scalar(out=p[:, 0:nkeys], in0=p[:, 0:nkeys], scalar1=sm[:, 7:8], scalar2=None, op0=ALU.mult),
                                     reads=[t_sm, t_p], writes=[t_p])
                                pT, t_pT = pT_r.next()
                                transpose_bf(lambda jb: p[:, jb * 128:(jb + 1) * 128], qt + 1, lambda j0, n: pT[:, j0:j0 + n, :], t_p, t_pT, PRing([4, 5]))
                                for jb in range(qt + 1):
                                    k.op("pe", lambda e: e.matmul(out=PB[6][:, r * 128:(r + 1) * 128], lhsT=vtok[:, jb, g * 128:(g + 1) * 128],
                                                                  rhs=pT[:, jb, :], start=(jb == 0), stop=(jb == qt)),
                                         reads=[t_kv, t_pT], writes=[t_PB[6]])
                            oT, t_oT = oT_r.next()
                            k.op("act", lambda e: e.copy(out=oT[:], in_=PB[6][:].rearrange("p (c t) -> p c t", c=4)), reads=[t_PB[6]], writes=[t_oT])
                            k.dma("sp", ynT_d[g * 512:(g + 1) * 512, tsl(qt)].rearrange("(q p) t -> p q t", p=128), oT[:], ch_ynTd[qt],
                                  reads=[t_oT], writes=[t_ynTd[qt]])
            def get_aT(tt, st, state):
                if "ring" not in state:
                    state["ring"] = Ring(k, "f_aT", 2, [128, DC, 128], BF16, stack=st, chan=True)
                a, t_a, cha = state["ring"].next()
                k.dma("sp", a[:], ynT_d[0:D, tsl(tt)].rearrange("(c p) t -> p c t", p=128), cha, reads=[t_ynTd[tt]], writes=[t_a])
                return a, t_a
            proj_ln(get_aT, DC, f_wout, li, 0)


        def nsa_sample(li):
            SC = float(128 ** -0.5)
            NPG = 128
            w_in = n_win.rearrange("(c p) f -> p c f", p=128)
            with k.scope() as st:
                pti = k.sb("nz_pti", [128, NPG], I32, st)
                ptf = k.sb("nz_ptf", [128, NPG], F32, st)
                idx = k.sb("nz_idx", [128, NPG], I32, st)
                io = k.sb("nz_io", [128, 1], F32, st)
                m16 = k.sb("nz_m16", [16, 4], F32, st)
                gs = k.sb("nz_gs", [16, 16], F32, st)
                fs = k.sb("nz_fs", [16, 257], F32, st)
                bgr = k.sb("nz_bgr", [1, 48], F32, st)
                pj = k.sb("nz_pj", [128, 2, 4, 128], F32, st)
                t_ix = Tok()
                chx = k.dma_chan("ch_nzx")
                k.dma("sp", pti[:], pt_d.to_broadcast([128, NPG]), chx, writes=[t_ix])
                for dst_, src_ in ((io, iota_d), (m16, mask16_d), (gs, n_gs), (fs, n_fs), (bgr, n_bg[0:1, :])):
                    k.dma("sp", dst_[:], src_, chx, writes=[t_ix])
                k.dma("sp", pj[:], n_proj.rearrange("a g d e -> d a g e"), chx, writes=[t_ix])
                k.op("dve", lambda e: e.tensor_copy(out=ptf[:], in_=pti[:]), reads=[t_ix], writes=[t_ix])
                k.op("dve", lambda e: e.tensor_scalar(out=ptf[:], in0=ptf[:], scalar1=128.0, scalar2=io[:, 0:1], op0=ALU.mult, op1=ALU.add),
                     reads=[t_ix], writes=[t_ix])
                k.op("dve", lambda e: e.tensor_copy(out=idx[:], in_=ptf[:]), reads=[t_ix], writes=[t_ix])
                g3 = k.sb("nz_g3", [1, 3, 16], F32, st)
                gcol = k.sb("nz_gcol", [16, 4], F32, st)
                t_g = Tok()
                qpad = k.sb("nz_qpad", [128, 4, 16], F32, st)
                t_qp = Tok()
                newp = k.sb("nz_newp", [128, 6, 512], F32, st)
                t_np = Tok()
                seln = k.sb("nz_seln", [16, 258], F32, st)
                t_imp = Tok()
                with k.scope() as sp_:
                    prow = k.sb("nz_prow", [1, 5168], F32, sp_)
                    t_pr = Tok()
                    with k.scope() as sq:
                        wq_r = Ring(k, "nz_w", 2, [128, DC, 512], BF16, stack=sq, chan=True)
                        for blk in range(11):
                            c0 = blk * 512
                            w_ = min(512, 5168 - c0)
                            wq, t_wq, chq = wq_r.next()
                            k.dma("pool", wq[:, :, 0:w_], w_in[:, :, c0:c0 + w_], chq, writes=[t_wq])
                            for kc in range(DC):
                                k.op("pe", lambda e: e.matmul(out=PB[0][0:1, 0:w_], lhsT=hT[:, kc, SEQ:SEQ + 1], rhs=wq[:, kc, 0:w_],
                                                              start=(kc == 0), stop=(kc == DC - 1)), reads=[t_hT[NT], t_wq], writes=[t_PB[0]])
                            k.op("act", lambda e: e.copy(out=prow[0:1, c0:c0 + w_], in_=PB[0][0:1, 0:w_]), reads=[t_PB[0]], writes=[t_pr])
                    k.op("dve", lambda e: e.tensor_tensor(out=prow[0:1, 5120:5168], in0=prow[0:1, 5120:5168], in1=bgr[:], op=ALU.add),
                         reads=[t_pr, t_ix], writes=[t_pr])
                    k.op("act", lambda e: e.activation(out=prow[0:1, 5120:5168], in_=prow[0:1, 5120:5168], func=AF.Sigmoid), reads=[t_pr], writes=[t_pr])
                    k.op("dve", lambda e: e.tensor_copy(out=g3[:], in_=prow[0:1, 5120:5168].rearrange("o (h b) -> o b h", b=3)), reads=[t_pr], writes=[t_g])
                    for br in range(3):
                        k.op("pe", lambda e: e.matmul(out=PB[1][0:16, br:br + 1], lhsT=g3[0:1, br, :], rhs=ones[0:1, 0:1], start=True, stop=True),
                             reads=[t_g, t_const], writes=[t_PB[1]])
                    k.op("act", lambda e: e.copy(out=gcol[:, 0:3], in_=PB[1][0:16, 0:3]), reads=[t_PB[1]], writes=[t_g])
                    for h in range(16):
                        k.op("pe", lambda e: e.matmul(out=PB[2][:, h:h + 1], lhsT=prow[0:1, h * 128:(h + 1) * 128], rhs=ones[0:1, 0:1], start=True, stop=True),
                             reads=[t_pr, t_const], writes=[t_PB[2]])
                    k.op("dve", lambda e: e.memset(qpad[:], 0.0), writes=[t_qp])
                    for g in range(4):
                        k.op("act", lambda e: e.copy(out=qpad[:, g, g * 4:(g + 1) * 4], in_=PB[2][:, g * 4:(g + 1) * 4]), reads=[t_PB[2]], writes=[t_qp])
                    k.op("dve", lambda e: e.memset(newp[:], 0.0), writes=[t_np])
                    k.op("act", lambda e: e.copy(out=newp[0:1, :, :], in_=prow[0:1, 2048:5120].rearrange("o (b c) -> o b c", b=6)), reads=[t_pr, t_np], writes=[t_np])
                pg_r = Ring(k, "nz_pg", 4, [128, 512], F32, stack=st, chan=True)
                sm = k.sb("nz_sm", [16, 64], F32, st)
                t_sm = Tok()
                acc_o = k.sb("nz_acco", [16, 128], F32, st)
                t_ao = Tok()

                def get_page(branch, pg):
                    if pg == NPG:
                        return newp[:, branch, :], t_np
                    pt_, t_pt, chp = pg_r.next()
                    k.idma(pt_[:], n_pools[branch], idx[:, pg:pg + 1], chp, reads=[t_ix], writes=[t_pt])
                    return pt_[:], t_pt

                def softmax16(S2, n, t_S, gate_col=None):
                    k.op("dve", lambda e: e.reduce_max(out=sm[:, 0:1], in_=S2, axis=AX.X), reads=[t_S], writes=[t_sm])
                    k.op("dve", lambda e: e.tensor_scalar(out=sm[:, 1:2], in0=sm[:, 0:1], scalar1=-1.0, scalar2=None, op0=ALU.mult), reads=[t_sm], writes=[t_sm])
                    k.op("act", lambda e: e.activation(out=S2, in_=S2, func=AF.Exp, bias=sm[:, 1:2], accum_out=sm[:, 2:3]), reads=[t_S, t_sm], writes=[t_S, t_sm])
                    k.op("dve", lambda e: e.reciprocal(out=sm[:, 3:4], in_=sm[:, 2:3]), reads=[t_sm], writes=[t_sm])
                    if gate_col is not None:
                        k.op("dve", lambda e: e.tensor_tensor(out=sm[:, 3:4], in0=sm[:, 3:4], in1=gate_col, op=ALU.mult), reads=[t_sm, t_g], writes=[t_sm])
                    k.op("dve", lambda e: e.tensor_scalar(out=S2, in0=S2, scalar1=sm[:, 3:4], scalar2=None, op0=ALU.mult), reads=[t_S, t_sm], writes=[t_S])

                def select_add(pbank, first):
                    for g in range(4):
                        if first and g == 0:
                            k.op("dve", lambda e: e.tensor_scalar(out=acc_o[:], in0=pbank[0:16, 0:128], scalar1=m16[:, 0:1], scalar2=None, op0=ALU.mult),
                                 reads=[t_PB[PB.index(pbank)], t_ix], writes=[t_ao])
                        else:
                            k.op("dve", lambda e: e.scalar_tensor_tensor(out=acc_o[:], in0=pbank[0:16, g * 128:(g + 1) * 128], scalar=m16[:, g:g + 1],
                                                                         in1=acc_o[:], op0=ALU.mult, op1=ALU.add),
                                 reads=[t_PB[PB.index(pbank)], t_ix, t_ao], writes=[t_ao])

                with k.scope() as scm:
                    kcT = k.sb("nz_kcT", [128, 4, 1152], F32, scm)
                    vca = k.sb("nz_vc", [128, 9, 512], F32, scm)
                    t_kc = Tok()
                    with k.scope() as sc_:
                        wps = k.sb("nz_wps", [128, 4, 17, 128], F32, sc_)
                        pl_r = Ring(k, "nz_pl", 2, [128, 128], F32, stack=sc_)
                        for kv in range(2):
                            t_wps = Tok()
                            chw = k.dma_chan("ch_nzw")
                            for g in range(4):
                                k.dma("sp", wps[:, g], n_wps[kv, g], chw, writes=[t_wps])
                            for c in range(9):
                                pages = [(pg, pg - 16 * c) for pg in range(16 * c, min(16 * c + 16, NPG + 1))]
                                if 16 * c + 16 <= NPG:
                                    pages.append((16 * c + 16, 16))
                                for pi, (pg, slot) in enumerate(pages):
                                    xp, t_xp = get_page(kv, pg)
                                    for g in range(4):
                                        k.op("pe", lambda e: e.matmul(out=PB[4 + g][:, 0:128], lhsT=xp[:, g * 128:(g + 1) * 128], rhs=wps[:, g, slot, :],
                                                                      start=(pi == 0), stop=(pi == len(pages) - 1)),
                                             reads=[t_xp, t_wps], writes=[t_PB[4 + g]])
                                for g in range(4):
                                    pl, t_pl = pl_r.next()
                                    k.op("act", lambda e: e.copy(out=pl[:], in_=PB[4 + g][:, 0:128]), reads=[t_PB[4 + g]], writes=[t_pl])
                                    if kv == 0:
                                        k.op("pe", lambda e: e.matmul(out=PB[3][:, 0:128], lhsT=pj[:, 0, g, :], rhs=pl[:], start=True, stop=True),
                                             reads=[t_ix, t_pl], writes=[t_PB[3]])
                                        k.op("act", lambda e: e.copy(out=kcT[:, g, c * 128:(c + 1) * 128], in_=PB[3][:, 0:128]), reads=[t_PB[3]], writes=[t_kc])
                                    else:
                                        k.op("pe", lambda e: e.matmul(out=PB[3][:, 0:128], lhsT=pl[:], rhs=pj[:, 1, g, :], start=True, stop=True),
                                             reads=[t_ix, t_pl], writes=[t_PB[3]])
                                        k.op("act", lambda e: e.copy(out=vca[:, c, g * 128:(g + 1) * 128], in_=PB[3][:, 0:128]), reads=[t_PB[3]], writes=[t_kc])
                    Sc = k.sb("nz_Sc", [16, 1152], F32, scm)
                    pcT = k.sb("nz_pcT", [128, 9, 16], F32, scm)
                    imp = k.sb("nz_imp", [16, 4, 264], F32, scm)
                    t_Sc, t_pcT = Tok(), Tok()
                    for nb in range(3):
                        for g in range(4):
                            k.op("pe", lambda e: e.matmul(out=PB[0][0:16, 0:384], lhsT=qpad[:, g, :], rhs=kcT[:, g, nb * 384:(nb + 1) * 384],
                                                          start=(g == 0), stop=(g == 3)), reads=[t_qp, t_kc], writes=[t_PB[0]])
                        k.op("act", lambda e: e.activation(out=Sc[:, nb * 384:(nb + 1) * 384], in_=PB[0][0:16, 0:384], func=AF.Copy, scale=SC),
                             reads=[t_PB[0]], writes=[t_Sc])
                    k.op("dve", lambda e: e.memset(Sc[:, 1023:1152], -1e30), reads=[t_Sc], writes=[t_Sc])
                    softmax16(Sc[:], 1152, t_Sc)
                    for c in range(9):
                        k.op("pe", lambda e: e.transpose(out=PB[1][:, c * 16:(c + 1) * 16], in_=Sc[:, c * 128:(c + 1) * 128], identity=ident[0:16, 0:16]),
                             reads=[t_Sc, t_const], writes=[t_PB[1]])
                    k.op("act", lambda e: e.copy(out=pcT[:], in_=PB[1][:, 0:144].rearrange("p (c h) -> p c h", c=9)), reads=[t_PB[1]], writes=[t_pcT])
                    for c in range(9):
                        k.op("pe", lambda e: e.matmul(out=PB[6][0:16, :], lhsT=pcT[:, c, :], rhs=vca[:, c, :], start=(c == 0), stop=(c == 8)),
                             reads=[t_pcT, t_kc], writes=[t_PB[6]])
                    select_add(PB[6], True)
                    k.op("dve", lambda e: e.tensor_scalar(out=acc_o[:], in0=acc_o[:], scalar1=gcol[:, 0:1], scalar2=None, op0=ALU.mult), reads=[t_g, t_ao], writes=[t_ao])
                    with k.scope() as so:
                        ov_r = Ring(k, "nz_ov", 2, [128, 257], F32, stack=so, chan=True)
                        for c in range(9):
                            ovt, t_ov, cho = ov_r.next()
                            k.dma("sp", ovt[:], n_ovs[c], cho, writes=[t_ov])
                            k.op("pe", lambda e: e.matmul(out=PB[7][0:16, 0:257], lhsT=pcT[:, c, :], rhs=ovt[:], start=(c == 0), stop=(c == 8)),
                                 reads=[t_pcT, t_ov], writes=[t_PB[7]])
                    k.op("act", lambda e: e.copy(out=imp[:, 0, 0:257], in_=PB[7][0:16, 0:257]), reads=[t_PB[7]], writes=[t_imp])
                    k.op("pe", lambda e: e.matmul(out=PB[7][0:16, 0:257], lhsT=gs[:], rhs=imp[:, 0, 0:257], start=True, stop=True), reads=[t_imp, t_ix], writes=[t_PB[7]])
                    k.op("dve", lambda e: e.tensor_tensor(out=imp[:, 1, 0:257], in0=PB[7][0:16, 0:257], in1=fs[:], op=ALU.add), reads=[t_PB[7], t_ix], writes=[t_imp])
                    k.op("dve", lambda e: e.max(out=imp[:, 3, 0:8], in_=imp[:, 1, 0:257]), reads=[t_imp], writes=[t_imp])
                    k.op("dve", lambda e: e.match_replace(out=imp[:, 2, 0:257], in_to_replace=imp[:, 3, 0:8], in_values=imp[:, 1, 0:257], imm_value=-3e38),
                         reads=[t_imp], writes=[t_imp])
                    k.op("dve", lambda e: e.max(out=imp[:, 3, 8:16], in_=imp[:, 2, 0:257]), reads=[t_imp], writes=[t_imp])
                    k.op("dve", lambda e: e.memset(seln[:], -1e30), writes=[t_imp])
                    k.op("dve", lambda e: e.tensor_scalar(out=seln[:, 0:257], in0=imp[:, 1, 0:257], scalar1=imp[:, 3, 15:16], scalar2=1e30, op0=ALU.is_ge, op1=ALU.mult),
                         reads=[t_imp], writes=[t_imp])
                    k.op("dve", lambda e: e.tensor_scalar(out=seln[:, 0:257], in0=seln[:, 0:257], scalar1=-1e30, scalar2=None, op0=ALU.add), reads=[t_imp], writes=[t_imp])

                def attend(npages, kget, vget, mask_fn, gate_col, pbank):
                    with k.scope() as sa:
                        S = k.sb("nz_S", [16, npages, 128], F32, sa)
                        t_S = Tok()
                        kT_r = Ring(k, "nz_kTp", 2, [128, 512], F32, stack=sa)
                        pT_r = Ring(k, "nz_pTp", 2, [128, 16], F32, stack=sa)
                        for pg in range(npages):
                            kp, t_kp = kget(pg)
                            tb = PB[pg % 2]
                            for g in range(4):
                                k.op("pe", lambda e: e.transpose(out=tb[:, g * 128:(g + 1) * 128], in_=kp[:, g * 128:(g + 1) * 128], identity=ident[:]),
                                     reads=[t_kp, t_const], writes=[t_PB[pg % 2]])
                            kTp, t_kTp = kT_r.next()
                            k.op("act", lambda e: e.copy(out=kTp[:], in_=tb[:]), reads=[t_PB[pg % 2]], writes=[t_kTp])
                            sbk = PB[2 + pg % 2]
                            for g in range(4):
                                k.op("pe", lambda e: e.matmul(out=sbk[0:16, 0:128], lhsT=qpad[:, g, :], rhs=kTp[:, g * 128:(g + 1) * 128],
                                                              start=(g == 0), stop=(g == 3)), reads=[t_qp, t_kTp], writes=[t_PB[2 + pg % 2]])
                            k.op("dve", lambda e: e.tensor_scalar(out=S[:, pg, :], in0=sbk[0:16, 0:128], scalar1=SC, scalar2=None, op0=ALU.mult),
                                 reads=[t_PB[2 + pg % 2]], writes=[t_S])
                        mask_fn(S, t_S)
                        softmax16(S[:].rearrange("p g r -> p (g r)"), npages * 128, t_S, gate_col)
                        for pg in range(npages):
                            vp, t_vp = vget(pg)
                            k.op("pe", lambda e: e.transpose(out=PB[pg % 2][:, 0:16], in_=S[:, pg, :], identity=ident[0:16, 0:16]),
                                 reads=[t_S, t_const], writes=[t_PB[pg % 2]])
                            pT, t_pT = pT_r.next()
                            k.op("act", lambda e: e.copy(out=pT[:], in_=PB[pg % 2][:, 0:16]), reads=[t_PB[pg % 2]], writes=[t_pT])
                            k.op("pe", lambda e: e.matmul(out=pbank[0:16, :], lhsT=pT[:], rhs=vp, start=(pg == 0), stop=(pg == npages - 1)),
                                 reads=[t_pT, t_vp], writes=[t_PB[PB.index(pbank)]])
                        select_add(pbank, False)

                def mask_slc(S, t_S):
                    Sv = S[:].rearrange("p g r -> p (g r)")
                    k.op("dve", lambda e: e.tensor_tensor(out=Sv.rearrange("p (m s) -> p m s", s=64), in0=Sv.rearrange("p (m s) -> p m s", s=64),
                                                          in1=seln[:, 0:258].unsqueeze(2).to_broadcast([16, 258, 64]), op=ALU.add),
                         reads=[t_imp, t_S], writes=[t_S])
                    k.op("dve", lambda e: e.memset(S[:, NPG, 1:128], -1e30), reads=[t_S], writes=[t_S])

                attend(NPG + 1, lambda pg: get_page(2, pg), lambda pg: get_page(3, pg), mask_slc, gcol[:, 1:2], PB[4])

                wb_r = Ring(k, "nz_wb", 4, [128, 512], F32, stack=st, chan=True)

                def wget(half, new_branch):
                    def f(pg):
                        if pg == 4:
                            return newp[:, new_branch, :], t_np
                        t_, t_t, ch_ = wb_r.next()
                        k.dma("sp", t_[:], n_winbuf[pg * 128:(pg + 1) * 128, half * 512:(half + 1) * 512], ch_, writes=[t_t])
                        return t_[:], t_t
                    return f

                def mask_win(S, t_S):
                    k.op("dve", lambda e: e.memset(S[:, 0, 0:1], -1e30), reads=[t_S], writes=[t_S])
                    k.op("dve", lambda e: e.memset(S[:, 4, 1:128], -1e30), reads=[t_S], writes=[t_S])

                attend(5, wget(0, 4), wget(1, 5), mask_win, gcol[:, 2:3], PB[5])
                osb = k.sb("nz_osb", [16, 128], BF16, st)
                t_osb = Tok()
                k.op("act", lambda e: e.copy(out=osb[:], in_=acc_o[:]), reads=[t_ao], writes=[t_osb])
                k.dma("sp", ynT_d[0:D, SEQ:SEQ + 1].rearrange("(h d) o -> h (d o)", h=16), osb[:], ch_ynTd[NT], reads=[t_osb], writes=[t_ynTd[NT]],
                      allow_slow_non_contiguous=True)

        def nsa_layer(li, attn=True):
            SC = float(128 ** -0.5)
            w_in = n_win.rearrange("(c p) f -> p c f", p=128)
            if attn and os.environ.get("NSA_SAMPLE", "1") == "1":
                nsa_sample(li)
            with k.scope() as sl:
                kTs = k.sb("ns_kTs", [128, 4, SEQ], BF16, sl)
                vs = k.sb("ns_vs", [128, NT, 512], BF16, sl)
                kTw = k.sb("ns_kTw", [128, 4, SEQ], BF16, sl)
                vw = k.sb("ns_vw", [128, NT, 512], BF16, sl)
                kcT = k.sb("ns_kcT", [128, 4, 128], BF16, sl)
                vc = k.sb("ns_vc", [128, 4, 128], BF16, sl)
                gates = k.sb("ns_gates", [128, TT, 48], F32, sl)
                t_kv, t_cmp, t_gt = Tok(), Tok(), Tok()
                with k.scope() as st:
                    xc1 = k.sb("ns_xc", [128, NT, 512], BF16, st)
                    xc_tok = [xc1, xc1]
                    t_xc = Tok()
                    wp_r = Ring(k, "ns_wp", 2, [128, NT, 128], BF16, stack=st, chan=True)
                    pj_r = Ring(k, "ns_pj", 2, [128, 128], BF16, stack=st, chan=True)
                    pl_r = Ring(k, "ns_pl", 2, [128, 128], BF16, stack=st)

                    def pool_cmp(kv):
                        for g in range(4):
                            wp, t_wp, chp = wp_r.next()
                            k.dma("pool", wp[:], n_wp[kv, g].rearrange("(c p) n -> p c n", p=128), chp, writes=[t_wp])
                            pj, t_pj, chj = pj_r.next()
                            k.dma("pool", pj[:], n_proj[kv, g], chj, writes=[t_pj])
                            for c in range(NT):
                                k.op("pe", lambda e: e.matmul(out=PB[6][:, 0:128], lhsT=xc_tok[kv][:, c, g * 128:(g + 1) * 128], rhs=wp[:, c, :],
                                                              start=(c == 0), stop=(c == NT - 1)), reads=[t_xc, t_wp], writes=[t_PB[6]])
                            pl, t_pl = pl_r.next()
                            k.op("act", lambda e: e.copy(out=pl[:], in_=PB[6][:, 0:128]), reads=[t_PB[6]], writes=[t_pl])
                            if kv == 0:
                                k.op("pe", lambda e: e.matmul(out=PB[7][:, 0:128], lhsT=pj[:], rhs=pl[:], start=True, stop=True),
                                     reads=[t_pj, t_pl], writes=[t_PB[7]])
                                k.op("act", lambda e: e.copy(out=kcT[:, g, :], in_=PB[7][:, 0:128]), reads=[t_PB[7]], writes=[t_cmp])
                            else:
                                k.op("pe", lambda e: e.matmul(out=PB[7][:, 0:128], lhsT=pl[:], rhs=pj[:], start=True, stop=True),
                                     reads=[t_pj, t_pl], writes=[t_PB[7]])
                                k.op("act", lambda e: e.copy(out=vc[:, g, :], in_=PB[7][:, 0:128]), reads=[t_PB[7]], writes=[t_cmp])

                    wr = Ring(k, "ns_w", 1, [128, DC, 512], BF16, stack=st, chan=True)
                    ko_r = Ring(k, "ns_ko", 3, [128, 512], F32, stack=st, chan=True)
                    kb_r = Ring(k, "ns_kb", 2, [128, 512], BF16, stack=st)
                    pm = PRing([0, 1, 2, 3])
                    for fb in range(6):
                        wb, t_wb, chw = wr.next()
                        k.dma("pool", wb[:], w_in[:, :, 2048 + fb * 512:2048 + (fb + 1) * 512], chw, writes=[t_wb])
                        for tt in range(TT):
                            ps, t_ps = pm.next()
                            for kc in range(DC):
                                k.op("pe", lambda e: e.matmul(out=ps[:], lhsT=hT[:, kc, tsl(tt)], rhs=wb[:, kc, :],
                                                              start=(kc == 0), stop=(kc == DC - 1)), reads=[t_hT[tt], t_wb], writes=[t_ps])
                            ko, t_ko, cho = ko_r.next()
                            k.op("act", lambda e: e.copy(out=ko[:], in_=ps[:]), reads=[t_ps], writes=[t_ko])
                            if fb < 4:
                                k.dma("sp", o_nsa_kv[tsl(tt), fb * 512:(fb + 1) * 512], ko[:], cho, reads=[t_ko])
                            else:
                                cs = slice((fb - 4) * 512, (fb - 3) * 512)
                                if 12 <= tt < NT:
                                    k.dma("sp", o_nsa_win_p[(tt - 12) * 128:(tt - 11) * 128, cs], ko[:], cho, reads=[t_ko])
                                elif tt == NT:
                                    k.dma("sp", o_nsa_win_s[511:512, cs], ko[0:1, :], cho, reads=[t_ko])
                            if not attn or tt >= NT:
                                continue
                            if fb < 2:
                                k.op("dve", lambda e: e.tensor_copy(out=xc_tok[fb][:, tt, :], in_=ko[:]), reads=[t_ko], writes=[t_xc])
                            elif fb in (3, 5):
                                dstv = vs if fb == 3 else vw
                                k.op("dve", lambda e: e.tensor_copy(out=dstv[:, tt, :], in_=ko[:]), reads=[t_ko], writes=[t_kv])
                            else:
                                dstk = kTs if fb == 2 else kTw
                                kb, t_kb = kb_r.next()
                                k.op("dve", lambda e: e.tensor_copy(out=kb[:], in_=ko[:]), reads=[t_ko], writes=[t_kb])
                                transpose_bf(lambda c: kb[:, c * 128:(c + 1) * 128], 4, lambda c0, n: dstk[:, c0:c0 + n, tsl(tt)],
                                             t_kb, t_kv, PRing([4, 5]))
                        if attn and fb < 2:
                            pool_cmp(fb)
                    k.dma("sp", o_nsa_win_s[0:511, :], n_winbuf[1:512, :], k.dma_chan("ch_nwin"))
                    if attn:
                        wg = k.sb("ns_wg", [128, DC, 48], BF16, st)
                        bg = k.sb("ns_bg", [128, 48], F32, st)
                        t_wg = Tok()
                        chg = k.dma_chan("ch_nsg")
                        k.dma("pool", wg[:], w_in[:, :, 5120:5168], chg, writes=[t_wg])
                        k.dma("sp", bg[:], n_bg, chg, writes=[t_wg])
                        for tt in range(NT):
                            for kc in range(DC):
                                k.op("pe", lambda e: e.matmul(out=PB[0][:, 0:48], lhsT=hT[:, kc, tsl(tt)], rhs=wg[:, kc, :],
                                                              start=(kc == 0), stop=(kc == DC - 1)), reads=[t_hT[tt], t_wg], writes=[t_PB[0]])
                            k.op("dve", lambda e: e.tensor_tensor(out=gates[:, tt, :], in0=PB[0][:, 0:48], in1=bg[:], op=ALU.add),
                                 reads=[t_PB[0], t_wg], writes=[t_gt])
                            k.op("act", lambda e: e.activation(out=gates[:, tt, :], in_=gates[:, tt, :], func=AF.Sigmoid), reads=[t_gt], writes=[t_gt])
                if not attn:
                    return
                cst = k.sb("ns_cst", [128, 4, 128], F32, sl)
                ov = k.sb("ns_ov", [128, 32], F32, sl)
                eaug = k.sb("ns_eaug", [33, SEQ], BF16, sl)
                t_cst = Tok()
                chc = k.dma_chan("ch_nsc")
                k.dma("sp", cst[:, 0, :], mneg_d2, chc, writes=[t_cst])
                k.dma("sp", cst[:, 1, :], n_mfar, chc, writes=[t_cst])
                k.dma("sp", ov[:], n_ov, chc, writes=[t_cst])
                k.dma("pool", eaug[:], n_eaug, chc, writes=[t_cst])
                for g in range(4):
                    with k.scope() as st:
                        qTg = k.sb("ns_qT", [128, 4, TTOK], BF16, st)
                        t_qg = [Tok() for _ in range(TT)]
                        featproj(n_win[:, g * 512:(g + 1) * 512], 512, qTg, t_qg)
                        cm_r = Ring(k, "ns_cm", 2, [128, 2, 128], F32, stack=st, chan=True)
                        fv_r = Ring(k, "ns_fv", 2, [128, 3, 32], F32, stack=st, chan=True)
                        pc_r = Ring(k, "ns_pc", 2, [128, 4, 128], F32, stack=st)
                        pcg_r = Ring(k, "ns_pcg", 2, [128, 4, 128], BF16, stack=st)
                        pcT_r = Ring(k, "ns_pcT", 2, [128, 4, 128], F32, stack=st)
                        pgT_r = Ring(k, "ns_pgT", 2, [128, 4, 128], BF16, stack=st)
                        im_r = Ring(k, "ns_im", 2, [128, 4, 32], F32, stack=st)
                        selT_r = Ring(k, "ns_selT", 2, [33, 128], BF16, stack=st)
                        p_r = Ring(k, "ns_p", 2, [128, SEQ], BF16, stack=st)
                        pT_r = Ring(k, "ns_pT", 2, [128, NT, 128], BF16, stack=st)
                        sm_r = Ring(k, "ns_sm", 2, [128, 24], F32, stack=st)
                        oT_r = Ring(k, "ns_oT", 2, [128, 4, 128], BF16, stack=st)
                        for sb_, t_sb in zip(selT_r.bufs, selT_r.toks):
                            k.op("dve", lambda e: e.memset(sb_[:], 1.0), writes=[t_sb])
                        for qt in range(NT):
                            gcol = lambda r, br: (g * 4 + r) * 3 + br
                            cm, t_cm, chm = cm_r.next()
                            k.dma("sp", cm[:, 0, :], n_cm01[tsl(qt), :], chm, writes=[t_cm])
                            fv, t_fv, chf = fv_r.next()
                            k.dma("sp", fv[:], n_fv[tsl(qt)], chf, writes=[t_fv])
                            k.op("dve", lambda e: e.tensor_scalar(out=cm[:, 1, :], in0=cm[:, 0, :], scalar1=32768.0, scalar2=-32768.0,
                                                                  op0=ALU.mult, op1=ALU.add), reads=[t_cm], writes=[t_cm])
                            for r in range(4):
                                k.op("pe", lambda e: e.matmul(out=PB[0][:, r * 128:(r + 1) * 128], lhsT=qTg[:, r, tsl(qt)], rhs=kcT[:, g, :],
                                                              start=True, stop=False), reads=[t_qg[qt], t_cmp], writes=[t_PB[0]])
                                k.op("pe", lambda e: e.matmul(out=PB[0][:, r * 128:(r + 1) * 128], lhsT=ident[:], rhs=cm[:, 1, :],
                                                              start=False, stop=True), reads=[t_cm, t_const], writes=[t_PB[0]])
                            sm, t_sm = sm_r.next()
                            k.op("dve", lambda e: e.reduce_max(out=sm[:, 0:4], in_=PB[0][:].rearrange("p (r n) -> p r n", r=4), axis=AX.X),
                                 reads=[t_PB[0]], writes=[t_sm])
                            k.op("dve", lambda e: e.tensor_scalar(out=sm[:, 4:8], in0=sm[:, 0:4], scalar1=-SC, scalar2=None, op0=ALU.mult),
                                 reads=[t_sm], writes=[t_sm])
                            pc, t_pc = pc_r.next()
                            for r in range(4):
                                k.op("act", lambda e: e.activation(out=pc[:, r, :], in_=PB[0][:, r * 128:(r + 1) * 128], func=AF.Exp, scale=SC,
                                                                   bias=sm[:, 4 + r:5 + r], accum_out=sm[:, 8 + r:9 + r]),
                                     reads=[t_PB[0], t_sm], writes=[t_pc, t_sm])
                            k.op("dve", lambda e: e.reciprocal(out=sm[:, 12:16], in_=sm[:, 8:12]), reads=[t_sm], writes=[t_sm])
                            k.op("dve", lambda e: e.tensor_tensor(out=pc[:], in0=pc[:], in1=sm[:, 12:16].unsqueeze(2).to_broadcast([128, 4, 128]),
                                                                  op=ALU.mult), reads=[t_sm, t_pc], writes=[t_pc])
                            k.op("dve", lambda e: e.tensor_tensor(out=pc[:], in0=pc[:], in1=cm[:, 0, :].unsqueeze(1).to_broadcast([128, 4, 128]),
                                                                  op=ALU.mult), reads=[t_cm, t_pc], writes=[t_pc])
                            for r in range(4):
                                k.op("pe", lambda e: e.transpose(out=PB[1][:, r * 128:(r + 1) * 128], in_=pc[:, r, :], identity=ident[:]),
                                     reads=[t_pc, t_const], writes=[t_PB[1]])
                            pcT, t_pcT = pcT_r.next()
                            k.op("act", lambda e: e.copy(out=pcT[:], in_=PB[1][:].rearrange("p (r t) -> p r t", r=4)), reads=[t_PB[1]], writes=[t_pcT])
                            for r in range(4):
                                k.op("pe", lambda e: e.matmul(out=PB[2][:, 0:32], lhsT=pcT[:, r, :], rhs=ov[:], start=(r == 0), stop=(r == 3)),
                                     reads=[t_pcT, t_cst], writes=[t_PB[2]])
                            im, t_im = im_r.next()
                            imp, v16, tmpm, selm = im[:, 0, :], im[:, 1, :], im[:, 2, :], im[:, 3, :]
                            k.op("dve", lambda e: e.tensor_tensor(out=imp, in0=PB[2][:, 0:32], in1=fv[:, 0, :], op=ALU.add), reads=[t_PB[2], t_fv], writes=[t_im])
                            k.op("dve", lambda e: e.tensor_tensor(out=imp, in0=imp, in1=fv[:, 1, :], op=ALU.mult), reads=[t_fv, t_im], writes=[t_im])
                            k.op("dve", lambda e: e.tensor_tensor(out=imp, in0=imp, in1=fv[:, 2, :], op=ALU.add), reads=[t_fv, t_im], writes=[t_im])
                            k.op("dve", lambda e: e.max(out=v16[:, 0:8], in_=imp), reads=[t_im], writes=[t_im])
                            k.op("dve", lambda e: e.match_replace(out=tmpm, in_to_replace=v16[:, 0:8], in_values=imp, imm_value=-3e38),
                                 reads=[t_im], writes=[t_im])
                            k.op("dve", lambda e: e.max(out=v16[:, 8:16], in_=tmpm), reads=[t_im], writes=[t_im])
                            k.op("dve", lambda e: e.tensor_scalar(out=selm, in0=imp, scalar1=v16[:, 15:16], scalar2=None, op0=ALU.is_ge),
                                 reads=[t_im], writes=[t_im])
                            k.op("dve", lambda e: e.scalar_tensor_tensor(out=selm, in0=imp, scalar=-5e29, in1=selm, op0=ALU.is_gt, op1=ALU.mult),
                                 reads=[t_im], writes=[t_im])
                            k.op("pe", lambda e: e.transpose(out=PB[3][0:32, 0:128], in_=selm, identity=ident[:]), reads=[t_im, t_const], writes=[t_PB[3]])
                            selT, t_selT = selT_r.next()
                            k.op("act", lambda e: e.copy(out=selT[0:32, :], in_=PB[3][0:32, 0:128]), reads=[t_PB[3]], writes=[t_selT])
                            pcg, t_pcg = pcg_r.next()
                            for r in range(4):
                                k.op("dve", lambda e: e.tensor_scalar(out=pcg[:, r, :], in0=pc[:, r, :], scalar1=gates[:, qt, gcol(r, 0):gcol(r, 0) + 1],
                                                                      scalar2=None, op0=ALU.mult), reads=[t_pc, t_gt], writes=[t_pcg])
                            pgT, t_pgT = pgT_r.next()
                            transpose_bf(lambda r: pcg[:, r, :], 4, lambda r0, n: pgT[:, r0:r0 + n, :], t_pcg, t_pgT, PRing([1]))
                            for r in range(4):
                                h = g * 4 + r
                                oslc = PB[7][:, r * 128:(r + 1) * 128]
                                k.op("pe", lambda e: e.matmul(out=oslc, lhsT=vc[:, g, :], rhs=pgT[:, r, :], start=True, stop=False),
                                     reads=[t_cmp, t_pgT], writes=[t_PB[7]])
                                for br in (1, 2):
                                    kT_, v_ = (kTs, vs) if br == 1 else (kTw, vw)
                                    j0 = 0 if br == 1 else max(0, qt - 4)
                                    k0 = j0 * 128
                                    nkeys = (qt + 1) * 128 - k0
                                    nb = (nkeys + 511) // 512
                                    for bi in range(nb):
                                        c0 = bi * 512
                                        w_ = min(512, nkeys - c0)
                                        last = (bi == nb - 1)
                                        k.op("pe", lambda e: e.matmul(out=PB[2 + bi][:, 0:w_], lhsT=qTg[:, r, tsl(qt)], rhs=kT_[:, g, k0 + c0:k0 + c0 + w_],
                                                                      start=True, stop=False), reads=[t_qg[qt], t_kv], writes=[t_PB[2 + bi]])
                                        if br == 1:
                                            k.op("pe", lambda e: e.matmul(out=PB[2 + bi][:, 0:w_], lhsT=selT[:, :], rhs=eaug[:, c0:c0 + w_],
                                                                          start=False, stop=False), reads=[t_selT, t_cst], writes=[t_PB[2 + bi]])
                                        elif bi == 0 and qt >= 4:
                                            k.op("pe", lambda e: e.matmul(out=PB[2 + bi][:, 0:128], lhsT=ident[:], rhs=cst[:, 1, :],
                                                                          start=False, stop=False), reads=[t_cst, t_const], writes=[t_PB[2 + bi]])
                                        if last:
                                            k.op("pe", lambda e: e.matmul(out=PB[2 + bi][:, w_ - 128:w_], lhsT=ident[:], rhs=cst[:, 0, :], start=False, stop=True),
                                                 reads=[t_cst, t_const], writes=[t_PB[2 + bi]])
                                    sm2, t_sm2 = sm_r.next()
                                    for bi in range(nb):
                                        w_ = min(512, nkeys - bi * 512)
                                        k.op("dve", lambda e: e.reduce_max(out=sm2[:, bi:bi + 1], in_=PB[2 + bi][:, 0:w_], axis=AX.X),
                                             reads=[t_PB[2 + bi]], writes=[t_sm2])
                                    k.op("dve", lambda e: e.reduce_max(out=sm2[:, 4:5], in_=sm2[:, 0:nb], axis=AX.X), reads=[t_sm2], writes=[t_sm2])
                                    k.op("dve", lambda e: e.tensor_scalar(out=sm2[:, 5:6], in0=sm2[:, 4:5], scalar1=-SC, scalar2=None, op0=ALU.mult),
                                         reads=[t_sm2], writes=[t_sm2])
                                    p, t_p = p_r.next()
                                    for bi in range(nb):
                                        c0 = bi * 512
                                        w_ = min(512, nkeys - c0)
                                        k.op("act", lambda e: e.activation(out=p[:, c0:c0 + w_], in_=PB[2 + bi][:, 0:w_], func=AF.Exp, scale=SC,
                                                                           bias=sm2[:, 5:6], accum_out=sm2[:, 8 + bi:9 + bi]),
                                             reads=[t_PB[2 + bi], t_sm2], writes=[t_p, t_sm2])
                                    k.op("dve", lambda e: e.reduce_sum(out=sm2[:, 6:7], in_=sm2[:, 8:8 + nb], axis=AX.X), reads=[t_sm2], writes=[t_sm2])
                                    k.op("dve", lambda e: e.reciprocal(out=sm2[:, 7:8], in_=sm2[:, 6:7]), reads=[t_sm2], writes=[t_sm2])
                                    k.op("dve", lambda e: e.tensor_scalar(out=p[:, 0:nkeys], in0=p[:, 0:nkeys], scalar1=sm2[:, 7:8],
                                                                          scalar2=gates[:, qt, gcol(r, br):gcol(r, br) + 1], op0=ALU.mult, op1=ALU.mult),
                                         reads=[t_sm2, t_p, t_gt], writes=[t_p])
                                    nt_ = qt + 1 - j0
                                    pT, t_pT = pT_r.next()
                                    transpose_bf(lambda jb: p[:, jb * 128:(jb + 1) * 128], nt_, lambda jj0, n: pT[:, jj0:jj0 + n, :], t_p, t_pT, PRing([5, 6]))
                                    for jb in range(nt_):
                                        k.op("pe", lambda e: e.matmul(out=oslc, lhsT=v_[:, j0 + jb, g * 128:(g + 1) * 128], rhs=pT[:, jb, :],
                                                                      start=False, stop=(br == 2 and jb == nt_ - 1)),
                                             reads=[t_kv, t_pT], writes=[t_PB[7]])
                            oT, t_oT = oT_r.next()
                            k.op("act", lambda e: e.copy(out=oT[:], in_=PB[7][:].rearrange("p (c t) -> p c t", c=4)), reads=[t_PB[7]], writes=[t_oT])
                            k.dma("sp", ynT_d[g * 512:(g + 1) * 512, tsl(qt)].rearrange("(q p) t -> p q t", p=128), oT[:], ch_ynTd[qt],
                                  reads=[t_oT], writes=[t_ynTd[qt]])
            def get_aT(tt, st, state):
                if "ring" not in state:
                    state["ring"] = Ring(k, "n_aT", 2, [128, DC, 128], BF16, stack=st, chan=True)
                a, t_a, cha = state["ring"].next()
                k.dma("sp", a[:], ynT_d[0:D, tsl(tt)].rearrange("(c p) t -> p c t", p=128), cha, reads=[t_ynTd[tt]], writes=[t_a])
                return a, t_a
            if dbg:
                for tt in range(NT):
                    k.dma("sp", o_dbg2[:, tsl(tt)], ynT_d[0:D, tsl(tt)], ch_dbg, reads=[t_ynTd[tt]])
            proj_ln(get_aT, DC, n_wout, li, 0)

        def mamba_layer(j, li):
            w_in = m_win[j].rearrange("(c p) f -> p c f", p=128)
            with k.scope() as st:
                cw = k.sb("m_cw", [128, 48, 4], F32, st)
                cb = k.sb("m_cb", [128, 48], F32, st)
                dtb = k.sb("m_dtb_s", [128, 64], F32, st)
                abc = k.sb("m_abc", [128, 64], F32, st)
                dsk = k.sb("m_dsk_s", [128, 64], F32, st)
                ng = k.sb("m_ng_s", [128, 32], F32, st)
                t_par = Tok()
                ch_misc = k.dma_chan("ch_par")
                for dst, srcap in ((cw, m_convw[j]), (cb, m_convb[j]), (dtb, m_dtb[j]), (abc, m_alog[j]),
                                   (dsk, m_dsk[j]), (ng, m_ng[j])):
                    k.dma("sp", dst[:], srcap, ch_misc, writes=[t_par])
                k.op("act", lambda e: e.activation(out=abc[:], in_=abc[:], func=AF.Exp), reads=[t_par], writes=[t_par])
                k.op("dve", lambda e: e.tensor_scalar(out=abc[:], in0=abc[:], scalar1=-1.0, scalar2=None, op0=ALU.mult),
                     reads=[t_par], writes=[t_par])
                dt_all = k.sb("m_dt", [128, TT, 64], F32, st)
                la_all = k.sb("m_la", [128, TT, 64], F32, st)
                ac_all = k.sb("m_ac", [128, TT, 64], F32, st)
                ea_all = k.sb("m_ea", [128, TT, 64], F32, st)
                t_dt = [Tok() for _ in range(TT)]
                with k.scope() as st2:
                    wdt = k.sb("m_wdt", [128, DC, 64], BF16, st2)
                    t_wdt = Tok()
                    ch_misc = k.dma_chan("ch_wdt")
                    k.dma("pool", wdt[:], w_in[:, :, 10240:10304], ch_misc, writes=[t_wdt])
                    tmp_r = Ring(k, "m_dtt", 2, [128, 3, 64], F32, stack=st2)
                    pm = PRing([0, 1])
                    pa = PRing([2, 3])
                    for c in range(TT):
                        ps, t_ps = pm.next()
                        for kc in range(DC):
                            k.op("pe", lambda e: e.matmul(out=ps[:, 0:64], lhsT=hT[:, kc, tsl(c)], rhs=wdt[:, kc, :],
                                                          start=(kc == 0), stop=(kc == DC - 1)),
                                 reads=[t_hT[c], t_wdt], writes=[t_ps])
                        tm, t_tm = tmp_r.next()
                        x0, ax, ee = tm[:, 0, :], tm[:, 1, :], tm[:, 2, :]
                        k.op("dve", lambda e: e.tensor_tensor(out=x0, in0=ps[:, 0:64], in1=dtb[:], op=ALU.add),
                             reads=[t_ps, t_par], writes=[t_tm])
                        k.op("dve", lambda e: e.scalar_tensor_tensor(out=ax, in0=x0, scalar=-1.0, in1=x0, op0=ALU.mult,
                                                                     op1=ALU.max), reads=[t_tm], writes=[t_tm])
                        k.op("act", lambda e: e.activation(out=ee, in_=ax, func=AF.Exp, scale=-1.0), reads=[t_tm], writes=[t_tm])
                        k.op("act", lambda e: e.activation(out=ee, in_=ee, func=AF.Ln, bias=1.0), reads=[t_tm], writes=[t_tm])
                        k.op("dve", lambda e: e.scalar_tensor_tensor(out=dt_all[:, c, :], in0=x0, scalar=0.0, in1=ee,
                                                                     op0=ALU.max, op1=ALU.add), reads=[t_tm], writes=[t_dt[c]])
                        k.op("dve", lambda e: e.tensor_tensor(out=la_all[:, c, :], in0=dt_all[:, c, :], in1=abc[:], op=ALU.mult),
                             reads=[t_dt[c], t_par], writes=[t_dt[c]])
                        pa_, t_pa = pa.next()
                        k.op("pe", lambda e: e.matmul(out=pa_[:, 0:64], lhsT=trit[:], rhs=la_all[:, c, :], start=True, stop=True),
                             reads=[t_const, t_dt[c]], writes=[t_pa])
                        k.op("dve", lambda e: e.tensor_copy(out=ac_all[:, c, :], in_=pa_[:, 0:64]), reads=[t_pa], writes=[t_dt[c]])
                        k.op("act", lambda e: e.activation(out=ea_all[:, c, :], in_=ac_all[:, c, :], func=AF.Exp),
                             reads=[t_dt[c]], writes=[t_dt[c]])
                for g in range(int(os.environ.get('M_GROUPS', '8'))):
                    with k.scope() as sg:
                        xtok = k.sb("m_xtok", [128, NT, 512], F32, sg)
                        t_xtok = [Tok() for _ in range(NT)]
                        btok = k.sb("m_btok", [128, NT, 128], BF16, sg)
                        BT = k.sb("m_BT", [128, SEQ], BF16, sg)
                        CT = k.sb("m_CT", [128, SEQ], BF16, sg)
                        t_BC = Tok()
                        us = k.sb("m_us", [1, 768], F32, sg)
                        t_us = Tok()
                        wz = k.sb("m_wz", [128, DC, 512], BF16, sg)
                        t_wz = Tok()
                        k.dma("pool", wz[:], w_in[:, :, g * 512:(g + 1) * 512], k.dma_chan("ch_wz"), writes=[t_wz])
                        with k.scope() as s3:
                            wx = k.sb("m_wx", [128, DC, 768], BF16, s3)
                            t_w = Tok()
                            ch_w = k.dma_chan("ch_w")
                            k.dma("pool", wx[:, :, 0:512], w_in[:, :, 4096 + g * 512:4096 + (g + 1) * 512], ch_w, writes=[t_w])
                            k.dma("pool", wx[:, :, 512:640], w_in[:, :, 8192 + g * 128:8192 + (g + 1) * 128], ch_w, writes=[t_w])
                            k.dma("pool", wx[:, :, 640:768], w_in[:, :, 9216 + g * 128:9216 + (g + 1) * 128], ch_w, writes=[t_w])
                            up_r = Ring(k, "m_up", 1, [128, SEQ + 3], F32, stack=s3)
                            xc_r = Ring(k, "m_xc", 1, [128, SEQ], F32, stack=s3)
                            cvo = Ring(k, "m_cvo", 2, [128, 3], F32, stack=s3, chan=True)
                            pm = PRing([0, 1, 2, 3])
                            pt = PRing([4, 5, 6, 7])
                            for part, (c0, c1) in enumerate(((0, 512), (512, 768))):
                                for kc in range(DC):
                                    k.op("pe", lambda e: e.matmul(out=PB[4 + part][0:1, 0:c1 - c0], lhsT=hT[:, kc, SEQ:SEQ + 1],
                                                                  rhs=wx[:, kc, c0:c1], start=(kc == 0), stop=(kc == DC - 1)),
                                         reads=[t_hT[NT], t_w], writes=[t_PB[4 + part]])
                                k.op("act", lambda e: e.copy(out=us[0:1, c0:c1], in_=PB[4 + part][0:1, 0:c1 - c0]),
                                     reads=[t_PB[4 + part]], writes=[t_us])
                            for fc in range(6):
                                ch_idx = (g * 4 + fc) if fc < 4 else (32 + g if fc == 4 else 40 + g)
                                up, t_up = up_r.next()
                                k.op("dve", lambda e: e.memset(up[:, 0:3], 0.0), writes=[t_up])
                                for tb in range(4):
                                    ps, t_ps = pm.next()
                                    for kc in range(DC):
                                        k.op("pe", lambda e: e.matmul(out=ps[:], lhsT=wx[:, kc, fc * 128:(fc + 1) * 128],
                                                                      rhs=hT[:, kc, tb * 512:(tb + 1) * 512],
                                                                      start=(kc == 0), stop=(kc == DC - 1)),
                                             reads=[t_w] + [t_hT[tb * 4 + q] for q in range(4)], writes=[t_ps])
                                    k.op("act", lambda e: e.copy(out=up[:, 3 + tb * 512:3 + (tb + 1) * 512], in_=ps[:]),
                                         reads=[t_ps], writes=[t_up])
                                co, t_co, chc = cvo.next()
                                k.op("act", lambda e: e.copy(out=co[:], in_=up[:, SEQ:SEQ + 3]), reads=[t_up], writes=[t_co])
                                k.dma("sp", o_conv_p[j, :, ch_idx * 128:(ch_idx + 1) * 128].rearrange("t p -> p t"), co[:], chc,
                                      reads=[t_co], allow_slow_non_contiguous=True)
                                xc, t_xc = xc_r.next()
                                k.op("dve", lambda e: e.tensor_scalar(out=xc[:], in0=up[:, 0:SEQ], scalar1=cw[:, ch_idx, 0:1],
                                                                      scalar2=cb[:, ch_idx:ch_idx + 1], op0=ALU.mult, op1=ALU.add),
                                     reads=[t_up, t_par], writes=[t_xc])
                                for kk in range(1, 4):
                                    k.op("dve", lambda e: e.scalar_tensor_tensor(out=xc[:], in0=up[:, kk:kk + SEQ],
                                                                                 scalar=cw[:, ch_idx, kk:kk + 1], in1=xc[:],
                                                                                 op0=ALU.mult, op1=ALU.add),
                                         reads=[t_up, t_par, t_xc], writes=[t_xc])
                                if fc < 5:
                                    k.op("act", lambda e: e.activation(out=xc[:], in_=xc[:], func=AF.Silu), reads=[t_xc], writes=[t_xc])
                                if fc == 4:
                                    k.op("dve", lambda e: e.tensor_copy(out=BT[:], in_=xc[:]), reads=[t_xc], writes=[t_BC])
                                if fc == 5:
                                    k.op("act", lambda e: e.activation(out=CT[:], in_=xc[:], func=AF.Silu), reads=[t_xc], writes=[t_BC])
                                if fc < 5:
                                    for c4 in range(4):
                                        pp, t_pp = pt.next()
                                        for q in range(4):
                                            c = c4 * 4 + q
                                            k.op("pe", lambda e: e.transpose(out=pp[:, q * 128:(q + 1) * 128], in_=xc[:, tsl(c)],
                                                                             identity=ident[:]),
                                                 reads=[t_xc, t_const], writes=[t_pp])
                                        if fc < 4:
                                            k.op("act", lambda e: e.copy(
                                                out=xtok[:, c4 * 4:(c4 + 1) * 4, fc * 128:(fc + 1) * 128],
                                                in_=pp[:].rearrange("p (c f) -> p c f", c=4)),
                                                 reads=[t_pp], writes=[t_xtok[c4 * 4 + q] for q in range(4)])
                                        else:
                                            k.op("act", lambda e: e.copy(out=btok[:, c4 * 4:(c4 + 1) * 4, :],
                                                                         in_=pp[:].rearrange("p (c f) -> p c f", c=4)),
                                                 reads=[t_pp], writes=[t_BC])
                        with k.scope() as s3:
                            S = k.sb("m_S", [128, 512], F32, s3)
                            Sb = k.sb("m_Sb", [128, 512], BF16, s3)
                            t_S = Tok()
                            k.op("dve", lambda e: e.memset(S[:], 0.0), writes=[t_S])
                            k.op("dve", lambda e: e.memset(Sb[:], 0.0), writes=[t_S])
                            R_r = Ring(k, "m_R", 2, [128, 8, 128], F32, stack=s3)
                            sg_r = Ring(k, "m_seg", 2, [128, 8, 128], F32, stack=s3)
                            cbm_r = Ring(k, "m_cbm", 2, [128, 128], F32, stack=s3)
                            MT_r = Ring(k, "m_MT", 2, [128, 8, 128], BF16, stack=s3)
                            xdt_r = Ring(k, "m_xdt", 2, [128, 512], BF16, stack=s3)
                            xw_r = Ring(k, "m_xw", 2, [128, 512], BF16, stack=s3)
                            zs_r = Ring(k, "m_zs", 2, [128, 512], F32, stack=s3)
                            y_r = Ring(k, "m_y", 2, [128, 512], F32, stack=s3)
                            y2_r = Ring(k, "m_y2", 2, [128, 512], F32, stack=s3)
                            yn_r = Ring(k, "m_yn", 2, [128, 512], F32, stack=s3)
                            ynT_r = Ring(k, "m_ynT", 2, [128, 4, 128], BF16, stack=s3)
                            sm_r = Ring(k, "m_sm", 2, [128, 16], F32, stack=s3)
                            for c in range(int(os.environ.get('M_CHUNKS', '16'))):
                                gs = slice(g * 8, (g + 1) * 8)
                                pz, t_pz = PB[0], t_PB[0]
                                for kc in range(DC):
                                    k.op("pe", lambda e: e.matmul(out=pz[:], lhsT=hT[:, kc, tsl(c)], rhs=wz[:, kc, :],
                                                                  start=(kc == 0), stop=(kc == DC - 1)),
                                         reads=[t_hT[c], t_wz], writes=[t_pz])
                                zs, t_zs = zs_r.next()
                                k.op("act", lambda e: e.activation(out=zs[:], in_=pz[:], func=AF.Silu), reads=[t_pz], writes=[t_zs])
                                Rr, t_R = R_r.next()
                                k.op("pool", lambda e: e.tensor_tensor(out=Rr[:], in0=trit[:].unsqueeze(1).to_broadcast([128, 8, 128]),
                                                                       in1=la_all[:, c, gs].unsqueeze(2).to_broadcast([128, 8, 128]),
                                                                       op=ALU.mult),
                                     reads=[t_const, t_dt[c]], writes=[t_R])
                                Rf = Rr[:].rearrange("p r t -> p (r t)")
                                for hh in range(2):
                                    pa, t_pa = PB[1 + hh], t_PB[1 + hh]
                                    k.op("pe", lambda e: e.matmul(out=pa[:], lhsT=ones[:], rhs=Rf[:, hh * 512:(hh + 1) * 512],
                                                                  start=True, stop=True), reads=[t_const, t_R], writes=[t_pa])
                                seg, t_seg = sg_r.next()
                                sm, t_sm = sm_r.next()
                                for r in range(8):
                                    pa, t_pa = PB[1 + r // 4], t_PB[1 + r // 4]
                                    rr = r % 4
                                    k.op("dve", lambda e: e.tensor_scalar(out=seg[:, r, :], in0=pa[:, rr * 128:(rr + 1) * 128],
                                                                          scalar1=ac_all[:, c, g * 8 + r:g * 8 + r + 1], scalar2=0.0,
                                                                          op0=ALU.subtract, op1=ALU.min),
                                         reads=[t_pa, t_dt[c]], writes=[t_seg])
                                for hh in range(2):
                                    pa, t_pa = PB[1 + hh], t_PB[1 + hh]
                                    k.op("act", lambda e: e.activation(
                                        out=sm[:, hh * 4:(hh + 1) * 4],
                                        in_=pa[:].rearrange("p (r t) -> p r t", r=4)[:, :, 127], func=AF.Exp),
                                         reads=[t_pa], writes=[t_sm])
                                k.op("act", lambda e: e.activation(out=seg[:], in_=seg[:], func=AF.Exp), reads=[t_seg], writes=[t_seg])
                                pc, t_pc = PB[3], t_PB[3]
                                k.op("pe", lambda e: e.matmul(out=pc[:, 0:128], lhsT=BT[:, tsl(c)], rhs=CT[:, tsl(c)], start=True, stop=True),
                                     reads=[t_BC], writes=[t_pc])
                                cbm, t_cbm = cbm_r.next()
                                k.op("dve", lambda e: e.tensor_tensor(out=cbm[:], in0=pc[:, 0:128], in1=trit[:], op=ALU.mult),
                                     reads=[t_pc, t_const], writes=[t_cbm])
                                MT, t_MT = MT_r.next()
                                k.op("dve", lambda e: e.tensor_tensor(out=MT[:], in0=seg[:],
                                                                      in1=cbm[:].unsqueeze(1).to_broadcast([128, 8, 128]), op=ALU.mult),
                                     reads=[t_seg, t_cbm], writes=[t_MT])
                                xdt, t_xdt = xdt_r.next()
                                k.op("pool", lambda e: e.tensor_tensor(
                                    out=xdt[:].rearrange("p (r q) -> p r q", r=8),
                                    in0=xtok[:, c, :].rearrange("p (r q) -> p r q", r=8),
                                    in1=dt_all[:, c, gs].unsqueeze(2).to_broadcast([128, 8, 64]), op=ALU.mult),
                                     reads=[t_xtok[c], t_dt[c]], writes=[t_xdt])
                                py, t_py = PB[4], t_PB[4]
                                for r in range(8):
                                    k.op("pe", lambda e: e.matmul(out=py[:, r * 64:(r + 1) * 64], lhsT=MT[:, r, :],
                                                                  rhs=xdt[:, r * 64:(r + 1) * 64], start=True, stop=True),
                                         reads=[t_MT, t_xdt], writes=[t_py])
                                po, t_po = PB[5], t_PB[5]
                                k.op("pe", lambda e: e.matmul(out=po[:], lhsT=CT[:, tsl(c)], rhs=Sb[:], start=True, stop=True),
                                     reads=[t_BC, t_S], writes=[t_po])
                                y, t_y = y_r.next()
                                k.op("dve", lambda e: e.tensor_tensor(
                                    out=y[:].rearrange("p (r q) -> p r q", r=8), in0=po[:].rearrange("p (r q) -> p r q", r=8),
                                    in1=ea_all[:, c, gs].unsqueeze(2).to_broadcast([128, 8, 64]), op=ALU.mult),
                                     reads=[t_po, t_dt[c]], writes=[t_y])
                                k.op("dve", lambda e: e.tensor_tensor(out=y[:], in0=y[:], in1=py[:], op=ALU.add),
                                     reads=[t_y, t_py], writes=[t_y])
                                y2, t_y2 = y2_r.next()
                                k.op("pool", lambda e: e.tensor_tensor(
                                    out=y2[:].rearrange("p (r q) -> p r q", r=8), in0=xtok[:, c, :].rearrange("p (r q) -> p r q", r=8),
                                    in1=dsk[:, gs].unsqueeze(2).to_broadcast([128, 8, 64]), op=ALU.mult),
                                     reads=[t_xtok[c], t_par], writes=[t_y2])
                                k.op("dve", lambda e: e.tensor_tensor(out=y[:], in0=y[:], in1=y2[:], op=ALU.add),
                                     reads=[t_y, t_y2], writes=[t_y])
                                k.op("dve", lambda e: e.tensor_tensor(out=y[:], in0=y[:], in1=zs[:], op=ALU.mult),
                                     reads=[t_y, t_zs], writes=[t_y])
                                k.op("act", lambda e: e.activation(out=y2[:], in_=y[:], func=AF.Square, accum_out=sm[:, 8:9]),
                                     reads=[t_y], writes=[t_y2, t_sm])
                                k.op("act", lambda e: e.activation(out=sm[:, 9:10], in_=sm[:, 8:9], func=AF.Sqrt, scale=1.0 / 512,
                                                                   bias=float(LN_EPS)), reads=[t_sm], writes=[t_sm])
                                k.op("dve", lambda e: e.reciprocal(out=sm[:, 10:11], in_=sm[:, 9:10]), reads=[t_sm], writes=[t_sm])
                                yn, t_yn = yn_r.next()
                                k.op("dve", lambda e: e.tensor_scalar(out=yn[:], in0=y[:], scalar1=sm[:, 10:11], scalar2=None,
                                                                      op0=ALU.mult), reads=[t_y, t_sm], writes=[t_yn])
                                pT, t_pT = PB[6], t_PB[6]
                                for q in range(4):
                                    k.op("pe", lambda e: e.transpose(out=pT[:, q * 128:(q + 1) * 128], in_=yn[:, q * 128:(q + 1) * 128],
                                                                     identity=ident[:]), reads=[t_yn, t_const], writes=[t_pT])
                                ynT, t_ynT = ynT_r.next()
                                for q in range(4):
                                    k.op("act", lambda e: e.activation(out=ynT[:, q, :], in_=pT[:, q * 128:(q + 1) * 128], func=AF.Copy,
                                                                       scale=ng[:, g * 4 + q:g * 4 + q + 1]),
                                         reads=[t_pT, t_par], writes=[t_ynT])
                                k.dma("sp", ynT_d[g * 512:(g + 1) * 512, tsl(c)].rearrange("(q p) t -> p q t", p=128), ynT[:], ch_ynTd[c],
                                      reads=[t_ynT], writes=[t_ynTd[c]])
                                xw, t_xw = xw_r.next()
                                k.op("pool", lambda e: e.tensor_tensor(
                                    out=xw[:].rearrange("p (r q) -> p r q", r=8), in0=xdt[:].rearrange("p (r q) -> p r q", r=8),
                                    in1=seg[:, :, 127:128].to_broadcast([128, 8, 64]), op=ALU.mult),
                                     reads=[t_xdt, t_seg], writes=[t_xw])
                                pS, t_pS = PB[7], t_PB[7]
                                k.op("pe", lambda e: e.matmul(out=pS[:], lhsT=btok[:, c, :], rhs=xw[:], start=True, stop=True),
                                     reads=[t_BC, t_xw], writes=[t_pS])
                                k.op("dve", lambda e: e.tensor_tensor(
                                    out=S[:].rearrange("p (r q) -> p r q", r=8), in0=S[:].rearrange("p (r q) -> p r q", r=8),
                                    in1=sm[:, 0:8].unsqueeze(2).to_broadcast([128, 8, 64]), op=ALU.mult),
                                     reads=[t_sm, t_S], writes=[t_S])
                                k.op("dve", lambda e: e.tensor_tensor(out=S[:], in0=S[:], in1=pS[:], op=ALU.add),
                                     reads=[t_pS, t_S], writes=[t_S])
                                k.op("act", lambda e: e.copy(out=Sb[:], in_=S[:]), reads=[t_S], writes=[t_S])

                            pT, t_pT = PB[6], t_PB[6]
                            for q in range(4):
                                k.op("pe", lambda e: e.transpose(out=pT[:, q * 128:(q + 1) * 128], in_=S[:, q * 128:(q + 1) * 128],
                                                                 identity=ident[:]), reads=[t_S, t_const], writes=[t_pT])
                            so = k.sb("m_so", [128, 4, 128], F32, s3)
                            t_so = Tok()
                            k.op("act", lambda e: e.copy(out=so[:], in_=pT[:].rearrange("p (q n) -> p q n", q=4)), reads=[t_pT], writes=[t_so])
                            k.dma("sp", o_ssm_p[j, g * 512:(g + 1) * 512, :].rearrange("(q p) n -> p q n", p=128), so[:], k.dma_chan("ch_so"),
                                  reads=[t_so])
                        with k.scope() as s3:
                            chs = k.dma_chan("ch_smp")
                            csr = k.sb("ms_cs", [1, 3, 768], F32, s3)
                            cwr = k.sb("ms_cw", [1, 4, 768], F32, s3)
                            cbr = k.sb("ms_cb", [1, 768], F32, s3)
                            t_sp = Tok()
                            segs = ((0, 512, g * 512), (512, 640, 4096 + g * 128), (640, 768, 5120 + g * 128))
                            for (a0, a1, c0) in segs:
                                k.dma("sp", csr[0:1, :, a0:a1], m_cs[j:j + 1, :, c0:c0 + (a1 - a0)], chs, writes=[t_sp])
                                k.dma("sp", cwr[0:1, :, a0:a1], m_cwr[j:j + 1, :, c0:c0 + (a1 - a0)], chs, writes=[t_sp])
                                k.dma("sp", cbr[0:1, a0:a1], m_cbr[j:j + 1, c0:c0 + (a1 - a0)], chs, writes=[t_sp])
                            for (a0, a1, c0) in segs:
                                k.dma("sp", o_conv_s[j:j + 1, 0:2, c0:c0 + (a1 - a0)], csr[0:1, 1:3, a0:a1], chs, reads=[t_sp])
                                k.dma("sp", o_conv_s[j:j + 1, 2, c0:c0 + (a1 - a0)], us[0:1, a0:a1], chs, reads=[t_us])
                            xr = k.sb("ms_xr", [1, 768], F32, s3)
                            t_xr = Tok()
                            k.op("dve", lambda e: e.tensor_tensor(out=xr[:], in0=us[:], in1=cwr[0:1, 3, :], op=ALU.mult),
                                 reads=[t_us, t_sp], writes=[t_xr])
                            for kk in range(3):
                                tmpr = k.sb(f"ms_tr{kk}", [1, 768], F32, s3)
                                t_tr = Tok()
                                k.op("dve", lambda e: e.tensor_tensor(out=tmpr[:], in0=csr[0:1, kk, :], in1=cwr[0:1, kk, :], op=ALU.mult),
                                     reads=[t_sp], writes=[t_tr])
                                k.op("dve", lambda e: e.tensor_tensor(out=xr[:], in0=xr[:], in1=tmpr[:], op=ALU.add),
                                     reads=[t_tr, t_xr], writes=[t_xr])
                            k.op("dve", lambda e: e.tensor_tensor(out=xr[:], in0=xr[:], in1=cbr[:], op=ALU.add), reads=[t_xr, t_sp], writes=[t_xr])
                            k.op("act", lambda e: e.activation(out=xr[:], in_=xr[:], func=AF.Silu), reads=[t_xr], writes=[t_xr])
                            rep = k.sb("ms_rep", [1, 4, 512], F32, s3)
                            t_rep = Tok()
                            for qi, srcrow in enumerate((dt_all[0:1, NT, gs8(g)], la_all[0:1, NT, gs8(g)], dsk[0:1, gs8(g)])):
                                k.op("dve", lambda e: e.tensor_copy(out=rep[0:1, qi, :].rearrange("o (r q) -> o r q", r=8),
                                                                    in_=srcrow.unsqueeze(2).to_broadcast([1, 8, 64])),
                                     reads=[t_dt[NT], t_par], writes=[t_rep])
                            pz, t_pz = PB[0], t_PB[0]
                            for kc in range(DC):
                                k.op("pe", lambda e: e.matmul(out=pz[0:1, :], lhsT=hT[:, kc, SEQ:SEQ + 1], rhs=wz[:, kc, :],
                                                              start=(kc == 0), stop=(kc == DC - 1)), reads=[t_hT[NT], t_wz], writes=[t_pz])
                            k.op("act", lambda e: e.activation(out=rep[0:1, 3, :], in_=pz[0:1, :], func=AF.Silu), reads=[t_pz], writes=[t_rep])
                            pc_, t_pc_ = PB[1], t_PB[1]
                            for cc in range(4):
                                k.op("pe", lambda e: e.matmul(out=pc_[:, cc:cc + 1], lhsT=xr[0:1, cc * 128:(cc + 1) * 128], rhs=ones[0:1, 0:1],
                                                              start=True, stop=True), reads=[t_xr, t_const], writes=[t_pc_])
                                for qi in range(4):
                                    k.op("pe", lambda e: e.matmul(out=pc_[:, 4 + qi * 4 + cc:5 + qi * 4 + cc],
                                                                  lhsT=rep[0:1, qi, cc * 128:(cc + 1) * 128], rhs=ones[0:1, 0:1],
                                                                  start=True, stop=True), reads=[t_rep, t_const], writes=[t_pc_])
                            pbc, t_pbc = PB[2], t_PB[2]
                            k.op("pe", lambda e: e.matmul(out=pbc[:, 0:256], lhsT=ones[0:1, :], rhs=xr[0:1, 512:768], start=True, stop=True),
                                 reads=[t_xr, t_const], writes=[t_pbc])
                            cols = k.sb("ms_cols", [128, 32], F32, s3)
                            t_cols = Tok()
                            k.op("act", lambda e: e.copy(out=cols[:, 0:20], in_=pc_[:, 0:20]), reads=[t_pc_], writes=[t_cols])
                            bcb = k.sb("ms_bcb", [128, 256], F32, s3)
                            t_bcb = Tok()
                            k.op("act", lambda e: e.copy(out=bcb[:], in_=pbc[:, 0:256]), reads=[t_pbc], writes=[t_bcb])
                            k.op("act", lambda e: e.activation(out=cols[:, 20:24], in_=cols[:, 8:12], func=AF.Exp), reads=[t_cols], writes=[t_cols])
                            k.op("dve", lambda e: e.tensor_tensor(out=cols[:, 24:28], in0=cols[:, 0:4], in1=cols[:, 4:8], op=ALU.mult),
                                 reads=[t_cols], writes=[t_cols])
                            Ss = k.sb("ms_S", [128, 4, 128], F32, s3)
                            t_Ss = Tok()
                            k.dma("sp", Ss[:], m_ss[j, g * 512:(g + 1) * 512, :].rearrange("(c p) n -> p c n", p=128), chs, writes=[t_Ss])
                            tmpS = k.sb("ms_tS", [128, 128], F32, s3)
                            t_tS = Tok()
                            for cc in range(4):
                                k.op("dve", lambda e: e.tensor_scalar(out=tmpS[:], in0=bcb[:, 0:128], scalar1=cols[:, 24 + cc:25 + cc], scalar2=None,
                                                                      op0=ALU.mult), reads=[t_bcb, t_cols], writes=[t_tS])
                                k.op("dve", lambda e: e.scalar_tensor_tensor(out=Ss[:, cc, :], in0=Ss[:, cc, :], scalar=cols[:, 20 + cc:21 + cc],
                                                                             in1=tmpS[:], op0=ALU.mult, op1=ALU.add),
                                     reads=[t_Ss, t_tS, t_cols], writes=[t_Ss])
                                k.op("dve", lambda e: e.tensor_tensor(out=tmpS[:], in0=Ss[:, cc, :], in1=bcb[:, 128:256], op=ALU.mult),
                                     reads=[t_Ss, t_bcb], writes=[t_tS])
                                k.op("dve", lambda e: e.reduce_sum(out=cols[:, 28 + cc:29 + cc], in_=tmpS[:], axis=AX.X), reads=[t_tS], writes=[t_cols])
                            k.dma("sp", o_ssm_s[j, g * 512:(g + 1) * 512, :].rearrange("(c p) n -> p c n", p=128), Ss[:], chs, reads=[t_Ss])
                            ys = k.sb("ms_y", [128, 8], F32, s3)
                            t_ys = Tok()
                            k.op("dve", lambda e: e.tensor_tensor(out=ys[:, 0:4], in0=cols[:, 12:16], in1=cols[:, 0:4], op=ALU.mult), reads=[t_cols], writes=[t_ys])
                            k.op("dve", lambda e: e.tensor_tensor(out=ys[:, 0:4], in0=ys[:, 0:4], in1=cols[:, 28:32], op=ALU.add), reads=[t_cols, t_ys], writes=[t_ys])
                            k.op("dve", lambda e: e.tensor_tensor(out=ys[:, 0:4], in0=ys[:, 0:4], in1=cols[:, 16:20], op=ALU.mult), reads=[t_cols, t_ys], writes=[t_ys])
                            k.op("dve", lambda e: e.tensor_tensor(out=ys[:, 4:8], in0=ys[:, 0:4], in1=ys[:, 0:4], op=ALU.mult), reads=[t_ys], writes=[t_ys])
                            k.op("dve", lambda e: e.reduce_sum(out=cols[:, 0:1], in_=ys[:, 4:8], axis=AX.X), reads=[t_ys], writes=[t_cols])
                            pss, t_pss = PB[3], t_PB[3]
                            k.op("pe", lambda e: e.matmul(out=pss[:, 0:1], lhsT=ones[:], rhs=cols[:, 0:1], start=True, stop=True),
                                 reads=[t_cols, t_const], writes=[t_pss])
                            k.op("act", lambda e: e.activation(out=cols[:, 1:2], in_=pss[:, 0:1], func=AF.Sqrt, scale=1.0 / 512, bias=float(LN_EPS)),
                                 reads=[t_pss], writes=[t_cols])
                            k.op("dve", lambda e: e.reciprocal(out=cols[:, 2:3], in_=cols[:, 1:2]), reads=[t_cols], writes=[t_cols])
                            k.op("dve", lambda e: e.tensor_scalar(out=ys[:, 0:4], in0=ys[:, 0:4], scalar1=cols[:, 2:3], scalar2=None, op0=ALU.mult),
                                 reads=[t_cols, t_ys], writes=[t_ys])
                            ysb = k.sb("ms_yb", [128, 4], BF16, s3)
                            t_ysb = Tok()
                            k.op("dve", lambda e: e.tensor_tensor(out=ysb[:], in0=ys[:, 0:4], in1=ng[:, g * 4:(g + 1) * 4], op=ALU.mult),
                                 reads=[t_ys, t_par], writes=[t_ysb])
                            k.dma("sp", ynT_d[g * 512:(g + 1) * 512, SEQ:SEQ + 1].rearrange("(c p) o -> p (c o)", p=128), ysb[:], ch_ynTd[NT],
                                  reads=[t_ysb], writes=[t_ynTd[NT]], allow_slow_non_contiguous=True)
            def get_aT(tt, st, state):
                if "ring" not in state:
                    state["ring"] = Ring(k, "m_aT", 2, [128, 32, 128], BF16, stack=st, chan=True)
                a, t_a, cha = state["ring"].next()
                k.dma("sp", a[:], ynT_d[:, tsl(tt)].rearrange("(c p) t -> p c t", p=128), cha, reads=[t_ynTd[tt]], writes=[t_a])
                return a, t_a
            if os.environ.get('M_PROJ', '1') == '1':
                proj_ln(get_aT, 32, m_wout[j], li, 0)

        NL = int(os.environ.get("K_LAYERS", "4"))
        for li in range(NL):
            kind, j = li % 3, li // 3
            with k.scope():
                if kind == 0:
                    mamba_layer(j, li)
                elif kind == 1:
                    fox_layer(li)
                else:
                    nsa_layer(li)
            with k.scope():
                memattn_layer(li)
            with k.scope():
                peer_layer(li)
        if NL == 2:
            with k.scope():
                nsa_layer(2, attn=False)
            with k.scope():
                memattn_layer(2, only_kv=True)
                memattn_layer(3, only_kv=True)

        for tt in range(TT):
            k.dma("sp", o_y[tsl(tt), :], res[tsl(tt), :], ch_dbg, reads=[t_res[tt]])
        if dbg:
            for tt in range(TT):
                k.dma("sp", o_dbg[tsl(tt), :], res[tsl(tt), :], ch_dbg, reads=[t_res[tt]])
        k.finish()
        print("instructions:", k.nins)
    return nc, in_names


STAGES = tuple(os.environ.get("K_STAGES", "init,mamba0,mem,peer,fox,nsa").split(","))
DBG = os.environ.get("K_DBG", "0") == "1"
_PROG = None
_LAST = {}


def _bc(v, n=128):
    return np.ascontiguousarray(np.broadcast_to(np.asarray(v, np.float32)[None, :], (n, v.shape[-1])))


def kernel(**inputs):
    global _PROG
    f32 = np.float32
    x_prompt = np.asarray(inputs["x_prompt"])
    B = x_prompt.shape[0]
    if _PROG is None:
        _PROG = build_program(STAGES, DBG)
    nc, in_names = _PROG
    shared = {"ident": np.eye(128, dtype=f32), "trit": np.triu(np.ones((128, 128), f32))}
    if "ln_g" in in_names:
        lg = np.asarray(inputs["ln_g"], f32)
        lb = np.asarray(inputs["ln_b"], f32)
        shared["ln_g"] = np.ascontiguousarray(np.broadcast_to(lg[:, :, None, :], (DEPTH, 3, 128, D)))
        shared["ln_b"] = np.ascontiguousarray(np.broadcast_to(lb[:, :, None, :], (DEPTH, 3, 128, D)))
    if "mem_wkv" in in_names:
        shared["mem_wkv"] = np.ascontiguousarray(inputs["mem_wkv"], dtype=f32)
    if "mem_wq" in in_names:
        shared["mem_wq"] = np.ascontiguousarray(inputs["mem_wq"], dtype=f32)
        shared["mem_wo"] = np.ascontiguousarray(inputs["mem_wo"], dtype=f32)
    if "peer_wq" in in_names:
        shared["peer_wq"] = np.ascontiguousarray(inputs["peer_wq"], dtype=f32)
        sk = np.asarray(inputs["peer_subkeys"], f32)
        shared["p_skT"] = np.ascontiguousarray(sk.reshape(DEPTH, 16, 128, 128).transpose(0, 3, 1, 2))
        npl = int(os.environ.get("K_LAYERS", "4"))
        shared["p_uT"] = np.ascontiguousarray(np.asarray(inputs["peer_u"][:npl], f32).transpose(0, 2, 1))
        shared["peer_v"] = np.ascontiguousarray(inputs["peer_v"][:npl], dtype=f32)
    if "iota" in in_names and "f_win" not in in_names:
        shared["iota"] = np.arange(128, dtype=f32)[:, None]
        m16 = np.zeros((16, 4), f32)
        m16[np.arange(16), np.arange(16) // 4] = 1.0
        shared["mask16"] = m16
    if "f_win" in in_names:
        shared["f_win"] = np.ascontiguousarray(inputs["fox_w_in"][0], dtype=f32)
        shared["f_wout"] = np.ascontiguousarray(inputs["fox_w_out"][0], dtype=f32)
        shared["f_bf"] = _bc(np.asarray(inputs["fox_b_f"][0], f32))
        s16 = np.zeros((16, 16, 128), f32)
        for hh in range(16):
            s16[hh, hh, :] = 1.0
        shared["sel16"] = s16.reshape(16, 2048)
        ckv = np.asarray(inputs["cache_fox_kv"][0], f32).reshape(1280 * 128, 2, 512)
        shared["f_kpool"] = np.ascontiguousarray(ckv[:, 0])
        shared["f_vpool"] = np.ascontiguousarray(ckv[:, 1])
        shared["f_lpool"] = np.ascontiguousarray(np.asarray(inputs["cache_fox_logf"][0], f32).reshape(1280 * 128, 16))
        shared["iota"] = np.arange(128, dtype=f32)[:, None]
        shared["tris"] = np.tril(np.ones((128, 128), f32), -1)
        m16 = np.zeros((16, 4), f32)
        m16[np.arange(16), np.arange(16) // 4] = 1.0
        shared["mask16"] = m16
        shared["mneg"] = np.where(np.arange(128)[None, :] <= np.arange(128)[:, None], 0.0, -1e30).astype(f32)
    if "n_win" in in_names:
        shared["n_win"] = np.ascontiguousarray(inputs["nsa_w_in"][0], dtype=f32)
        shared["n_wout"] = np.ascontiguousarray(inputs["nsa_w_out"][0], dtype=f32)
        cn = np.asarray(inputs["cache_nsa_kv"][0], f32).reshape(1280 * 128, 4, 512)
        for q in range(4):
            shared[f"n_pool{q}"] = np.ascontiguousarray(cn[:, q])
        wpos_ = np.asarray(inputs["nsa_cmp_wpos"][0], f32)
        wps = np.zeros((2, 4, 128, 17, 128), f32)
        for slot in range(16):
            for jj in range(8):
                col = 8 * slot + jj
                nrow = min(32, 128 - 16 * jj)
                wps[:, :, 16 * jj:16 * jj + nrow, slot, col] = wpos_[:, :nrow, :].transpose(0, 2, 1)
            if slot > 0:
                wps[:, :, 0:16, slot, 8 * slot - 1] = wpos_[:, 16:32, :].transpose(0, 2, 1)
        wps[:, :, 0:16, 16, 127] = wpos_[:, 16:32, :].transpose(0, 2, 1)
        shared["n_wps"] = wps
        nall = np.arange(9 * 128)
        mall = np.arange(257)
        ovs = ((16 * nall[:, None] < 64 * mall[None, :] + 64) & (16 * nall[:, None] + 32 > 64 * mall[None, :]) & (nall[:, None] < 1027)).astype(f32)
        shared["n_ovs"] = np.ascontiguousarray(ovs.reshape(9, 128, 257))
        frow = np.zeros((257,), f32)
        frow[[0, 255, 256]] = 1e6
        shared["n_fs"] = np.ascontiguousarray(np.broadcast_to(frow[None, :], (16, 257)))
        shared["n_gs"] = (np.arange(16)[:, None] // 4 == np.arange(16)[None, :] // 4).astype(f32)
        wpos = np.asarray(inputs["nsa_cmp_wpos"][0], f32)
        wp = np.zeros((2, 4, SEQ, 128), f32)
        for n in range(127):
            wp[:, :, 16 * n:16 * n + 32, n] = wpos.transpose(0, 2, 1)
        shared["n_wp"] = wp
        shared["n_proj"] = np.ascontiguousarray(inputs["nsa_cmp_proj"][0], dtype=f32)
        shared["n_bg"] = _bc(np.asarray(inputs["nsa_b_gate"][0], f32))
        tpos = np.arange(SEQ)
        nn = np.arange(128)
        shared["n_cm01"] = (((16 * nn[None, :] + 31) <= tpos[:, None]) & (nn[None, :] < 127)).astype(f32)
        mm = np.arange(32)
        ovl = ((16 * nn[:, None] < 64 * mm[None, :] + 64) & (16 * nn[:, None] + 32 > 64 * mm[None, :]) & (nn[:, None] < 127)).astype(f32)
        shared["n_ov"] = ovl
        qblk = tpos // 64
        forced = (mm[None, :] == 0) | (mm[None, :] == qblk[:, None]) | (mm[None, :] == qblk[:, None] - 1)
        valid = mm[None, :] <= qblk[:, None]
        fvv = np.zeros((SEQ, 3, 32), f32)
        fvv[:, 0] = forced * 1e6
        fvv[:, 1] = valid
        fvv[:, 2] = np.where(valid, 0.0, -1e30)
        shared["n_fv"] = fvv
        ea = np.zeros((33, SEQ), f32)
        ea[tpos // 64, tpos] = 32768.0
        ea[32, :] = -32768.0
        shared["n_eaug"] = ea
        loc = np.arange(128)
        shared["n_mfar"] = np.where(loc[None, :] > loc[:, None], 0.0, -1e30).astype(f32)
        shared["mneg2"] = np.where(loc[None, :] <= loc[:, None], 0.0, -1e30).astype(f32)
    if "m_win" in in_names:
        shared["m_win"] = np.ascontiguousarray(inputs["mamba_w_in"], dtype=f32)
        shared["m_wout"] = np.ascontiguousarray(inputs["mamba_w_out"], dtype=f32)
        cw = np.asarray(inputs["mamba_conv_w"], f32)
        shared["m_convw"] = np.ascontiguousarray(cw.reshape(2, 4, 48, 128).transpose(0, 3, 2, 1))
        shared["m_convb"] = np.ascontiguousarray(np.asarray(inputs["mamba_conv_b"], f32).reshape(2, 48, 128).transpose(0, 2, 1))
        shared["m_cwr"] = np.ascontiguousarray(cw)
        shared["m_cbr"] = np.ascontiguousarray(inputs["mamba_conv_b"], dtype=f32)
        shared["m_dtb"] = np.stack([_bc(inputs["mamba_dt_bias"][j]) for j in range(2)])
        shared["m_alog"] = np.stack([_bc(inputs["mamba_a_log"][j]) for j in range(2)])
        shared["m_dsk"] = np.stack([_bc(inputs["mamba_d"][j]) for j in range(2)])
        shared["m_ng"] = np.ascontiguousarray(np.asarray(inputs["mamba_norm_g"], f32).reshape(2, 32, 128).transpose(0, 2, 1))
    in_maps = []
    for c in range(8):
        b = c % B
        m = {"xp": np.ascontiguousarray(x_prompt[b]), "xs": np.ascontiguousarray(inputs["x_sample"][c]),
             "memp": np.ascontiguousarray(inputs["mem_prompt"][b])}
        if "cmk" in in_names:
            m["cmk"] = np.ascontiguousarray(np.asarray(inputs["cache_mem_kv"])[:, c].reshape(DEPTH, MEM, 2 * D))
        if "pt" in in_names:
            m["pt"] = np.ascontiguousarray(np.asarray(inputs["page_table"])[c:c + 1].astype(np.int32))
        if "n_winbuf" in in_names:
            m["n_winbuf"] = np.ascontiguousarray(np.asarray(inputs["state_nsa_win"], f32)[0, c].reshape(512, 1024))
        if "m_cs" in in_names:
            m["m_cs"] = np.ascontiguousarray(np.asarray(inputs["state_conv"], f32)[:, c])
            m["m_ss"] = np.ascontiguousarray(np.asarray(inputs["state_ssm"], f32)[:, c].reshape(2, M_DI, 128))
        m.update(shared)
        in_maps.append({n: m[n] for n in in_names})
    res = run_bass_kernel_spmd(nc, in_maps, core_ids=list(range(8)))
    R = res.results
    _LAST["R"] = R
    mem_kv_p = np.stack([R[b]["o_memkv"] for b in range(B)], axis=1).reshape(DEPTH, B, MEM, 2, 4, 512)
    ssm_p = np.stack([R[b]["o_ssm_p"] for b in range(B)], axis=1).reshape(2, B, 64, 64, 128)
    conv_p = np.stack([R[b]["o_conv_p"] for b in range(B)], axis=1).reshape(2, B, 3, 6144)

    ssm_s = np.stack([R[c]["o_ssm_s"] for c in range(8)], axis=1).reshape(2, 8, 64, 64, 128)
    conv_s = np.stack([R[c]["o_conv_s"] for c in range(8)], axis=1).reshape(2, 8, 3, 6144)

    fox_kv_p = fox_lf_p = fox_kv_s = fox_lf_s = None
    if "o_fox_kv" in R[0]:
        fox_kv_p = np.stack([R[b]["o_fox_kv"][:SEQ] for b in range(B)]).reshape(1, B, SEQ, 2, 4, 128)
        fox_lf_p = np.stack([R[b]["o_fox_lf"][:SEQ] for b in range(B)]).reshape(1, B, SEQ, 16)
        fox_kv_s = np.stack([R[c]["o_fox_kv"][SEQ:SEQ + 1] for c in range(8)]).reshape(1, 8, 1, 2, 4, 128)
        fox_lf_s = np.stack([R[c]["o_fox_lf"][SEQ:SEQ + 1] for c in range(8)]).reshape(1, 8, 1, 16)

    nsa_kv_p = nsa_win_p = nsa_kv_s = nsa_win_s = None
    if "o_nsa_kv" in R[0]:
        nsa_kv_p = np.stack([R[b]["o_nsa_kv"][:SEQ] for b in range(B)]).reshape(1, B, SEQ, 4, 4, 128)
        nsa_win_p = np.stack([R[b]["o_nsa_win_p"] for b in range(B)]).reshape(1, B, 512, 2, 4, 128)
        nsa_kv_s = np.stack([R[c]["o_nsa_kv"][SEQ:SEQ + 1] for c in range(8)]).reshape(1, 8, 1, 4, 4, 128)
        nsa_win_s = np.stack([R[c]["o_nsa_win_s"] for c in range(8)]).reshape(1, 8, 512, 2, 4, 128)

    def z(*s):
        return np.zeros(s, f32)

    y_p = np.stack([R[b]["o_y"][:SEQ] for b in range(B)])
    y_s = np.stack([R[c]["o_y"][SEQ:SEQ + 1] for c in range(8)])
    return (y_p, y_s, mem_kv_p, ssm_p, conv_p,
            fox_kv_p if fox_kv_p is not None else z(1, 4, 2048, 2, 4, 128), fox_lf_p if fox_lf_p is not None else z(1, 4, 2048, 16),
            nsa_kv_p if nsa_kv_p is not None else z(1, 4, 2048, 4, 4, 128), nsa_win_p if nsa_win_p is not None else z(1, 4, 512, 2, 4, 128),
            ssm_s, conv_s, fox_kv_s if fox_kv_s is not None else z(1, 8, 1, 2, 4, 128),
            fox_lf_s if fox_lf_s is not None else z(1, 8, 1, 16), nsa_kv_s if nsa_kv_s is not None else z(1, 8, 1, 4, 4, 128),
            nsa_win_s if nsa_win_s is not None else z(1, 8, 512, 2, 4, 128))
```

```python
import numpy as np
from contextlib import ExitStack, contextmanager
import concourse.bass as bass
import concourse.mybir as mybir
from concourse.bass_utils import run_bass_kernel_spmd

F32 = mybir.dt.float32
BF16 = mybir.dt.bfloat16
I32 = mybir.dt.int32
ALU = mybir.AluOpType
AF = mybir.ActivationFunctionType
AX = mybir.AxisListType

D = 2048
DC = 16
SEQ = 2048
NT = 16
TT = 17
TTOK = TT * 128
DEPTH = 4
MEM = 256
DN_ALPHA = (2 * DEPTH) ** 0.25
LN_EPS = 1e-5


class Tok:
    __slots__ = ("w", "r", "excl")

    def __init__(self, excl=False):
        self.w = None
        self.r = {}
        self.excl = excl


class Chan:
    def __init__(self, sem, step):
        self.sem = sem
        self.step = step
        self.val = 0


class MK:
    def __init__(self, nc, stack):
        self.nc = nc
        self.stack = stack
        self.engs = {"pe": nc.tensor, "dve": nc.vector, "act": nc.scalar, "pool": nc.gpsimd, "sp": nc.sync}
        self.chan = {n: Chan(self._sem("c_" + n), 1) for n in ("pe", "dve", "act", "pool")}
        self.dchans = []
        self.free_chans = []
        self.scopes = []
        self.seen = {n: {} for n in self.engs}
        self.nins = 0
        self.uid = 0

    def _sem(self, name):
        return self.stack.enter_context(self.nc.semaphore(name))

    def dma_chan(self, name):
        if self.free_chans:
            c = self.free_chans.pop()
        else:
            c = Chan(self._sem(f"dch{len(self.dchans)}"), 16)
            self.dchans.append(c)
        if self.scopes:
            self.scopes[-1].append(c)
        return c

    def barrier(self):
        chans = self.dchans + list(self.chan.values())
        for en in self.engs:
            self._wait(en, [(c, c.val) for c in chans if c.val], allow_pe_self=True)

    @contextmanager
    def scope(self):
        self.scopes.append([])
        with ExitStack() as st:
            yield st
            self.barrier()
        self.free_chans.extend(self.scopes.pop())

    def sb(self, name, shape, dt, stack=None):
        self.uid += 1
        return (stack or self.stack).enter_context(self.nc.sbuf_tensor(f"{name}_u{self.uid}", shape, dt))

    def ps(self, name, shape, dt, stack=None):
        self.uid += 1
        return (stack or self.stack).enter_context(self.nc.psum_tensor(f"{name}_u{self.uid}", shape, dt))

    def _wait(self, ename, deps, allow_pe_self=False):
        need = {}
        pech = self.chan["pe"]
        for ch, v in deps:
            if ename == "pe" and ch is pech and not allow_pe_self:
                continue
            if v > need.get(id(ch), (ch, 0))[1]:
                need[id(ch)] = (ch, v)
        seen = self.seen[ename]
        for cid, (ch, v) in need.items():
            if seen.get(cid, 0) >= v:
                continue
            self.engs[ename].wait_ge(ch.sem, v)
            seen[cid] = v

    @staticmethod
    def _deps(reads, writes):
        deps = []
        for t in reads:
            if t.w:
                deps.append(t.w)
        for t in writes:
            if t.w:
                deps.append(t.w)
            deps.extend(t.r.values())
        return deps

    def op(self, ename, fn, reads=(), writes=()):
        ex = [t for t in reads if t.excl]
        if ex:
            reads = [t for t in reads if not t.excl]
            writes = list(writes) + ex
        self._wait(ename, self._deps(reads, writes))
        ins = fn(self.engs[ename])
        ch = self.chan[ename]
        ch.val += 1
        ins.then_inc(ch.sem, 1)
        for t in reads:
            t.r[id(ch)] = (ch, ch.val)
        for t in writes:
            t.w = (ch, ch.val)
            t.r = {}
        self.nins += 1
        return ins

    def dma(self, qname, out, in_, ch, reads=(), writes=(), **kw):
        self._wait(qname, self._deps(reads, writes))
        ins = self.engs[qname].dma_start(out=out, in_=in_, **kw)
        ch.val += 16
        ins.then_inc(ch.sem, 16)
        for t in reads:
            t.r[id(ch)] = (ch, ch.val)
        for t in writes:
            t.w = (ch, ch.val)
            t.r = {}
        self.nins += 1
        return ins

    def idma(self, out, in_, idx_ap, ch, reads=(), writes=()):
        self._wait("pool", self._deps(reads, writes))
        ins = self.nc.gpsimd.indirect_dma_start(out=out, out_offset=None, in_=in_,
                                                in_offset=bass.IndirectOffsetOnAxis(ap=idx_ap, axis=0))
        ch.val += 16
        ins.then_inc(ch.sem, 16)
        for t in reads:
            t.r[id(ch)] = (ch, ch.val)
        for t in writes:
            t.w = (ch, ch.val)
            t.r = {}
        self.nins += 1
        return ins

    def finish(self):
        for ch in self.dchans + list(self.chan.values()):
            if ch.val:
                self.engs["sp"].wait_ge(ch.sem, ch.val)


class Ring:
    def __init__(self, k, name, n, shape, dt, space="sb", stack=None, chan=False):
        alloc = k.sb if space == "sb" else k.ps
        self.bufs = [alloc(f"{name}{i}", shape, dt, stack) for i in range(n)]
        self.toks = [Tok() for _ in range(n)]
        self.chans = [k.dma_chan(f"ch_{name}{i}") for i in range(n)] if chan else None
        self.i = 0
        self.n = n

    def next(self):
        j = self.i % self.n
        self.i += 1
        if self.chans:
            return self.bufs[j], self.toks[j], self.chans[j]
        return self.bufs[j], self.toks[j]


import os
M_DI = 4096
M_IN = 10304
M_CH = 6144


def build_program(stages=("init", "memkv", "mamba0", "mem", "peer"), dbg=True):
    nc = bass.Bass("TRN2", target_bir_lowering=False)
    stages = set(stages)
    in_names = []

    def din(name, shape, dt=F32):
        in_names.append(name)
        return nc.dram_tensor(name, list(shape), dt, kind="ExternalInput").ap()

    def dout(name, shape, dt=F32):
        return nc.dram_tensor(name, list(shape), dt, kind="ExternalOutput").ap()

    def dscr(name, shape, dt=F32):
        return nc.dram_tensor(name, list(shape), dt, kind="Internal").ap()

    xp = din("xp", [SEQ, D])
    xs = din("xs", [1, D])
    memp = din("memp", [MEM, D])
    ident_d = din("ident", [128, 128])
    trit_d = din("trit", [128, 128])
    ln_gd = din("ln_g", [DEPTH, 3, 128, D])
    ln_bd = din("ln_b", [DEPTH, 3, 128, D])
    o_memkv = dout("o_memkv", [DEPTH, MEM, 2 * D])
    o_ssm_p = dout("o_ssm_p", [2, M_DI, 128])
    o_conv_p = dout("o_conv_p", [2, 3, M_CH])
    o_ssm_s = dout("o_ssm_s", [2, M_DI, 128])
    o_conv_s = dout("o_conv_s", [2, 3, M_CH])
    if "mem" in stages:
        mem_wkv = din("mem_wkv", [DEPTH, D, 2 * D])
        mem_wq = din("mem_wq", [DEPTH, D, D])
        mem_wo = din("mem_wo", [DEPTH, D, D])
        cmk = din("cmk", [DEPTH, MEM, 2 * D])
    if "fox" in stages:
        f_win = din("f_win", [D, 3088])
        f_wout = din("f_wout", [D, D])
        f_bf = din("f_bf", [128, 16])
        sel16_d = din("sel16", [16, 16 * 128])
        mneg_d = din("mneg", [128, 128])
        f_kpool = din("f_kpool", [1280 * 128, 512])
        f_vpool = din("f_vpool", [1280 * 128, 512])
        f_lpool = din("f_lpool", [1280 * 128, 16])
        pt_d = din("pt", [1, 128], I32)
        iota_d = din("iota", [128, 1])
        tris_d = din("tris", [128, 128])
        mask16_d = din("mask16", [16, 4])
        o_fox_kv = dout("o_fox_kv", [TTOK, 1024])
        o_fox_lf = dout("o_fox_lf", [TTOK, 16])
    if "nsa" in stages:
        n_win = din("n_win", [D, 5168])
        n_winbuf = din("n_winbuf", [512, 1024])
        n_wout = din("n_wout", [D, D])
        n_wp = din("n_wp", [2, 4, SEQ, 128])
        n_proj = din("n_proj", [2, 4, 128, 128])
        n_bg = din("n_bg", [128, 48])
        n_cm01 = din("n_cm01", [SEQ, 128])
        n_ov = din("n_ov", [128, 32])
        n_fv = din("n_fv", [SEQ, 3, 32])
        n_eaug = din("n_eaug", [33, SEQ])
        n_mfar = din("n_mfar", [128, 128])
        mneg_d2 = din("mneg2", [128, 128])
        n_pools = [din(f"n_pool{q}", [1280 * 128, 512]) for q in range(4)]
        n_wps = din("n_wps", [2, 4, 128, 17, 128])
        n_ovs = din("n_ovs", [9, 128, 257])
        n_fs = din("n_fs", [16, 257])
        n_gs = din("n_gs", [16, 16])
        o_nsa_kv = dout("o_nsa_kv", [TTOK, 2048])
        o_nsa_win_p = dout("o_nsa_win_p", [512, 1024])
        o_nsa_win_s = dout("o_nsa_win_s", [512, 1024])
    if "peer" in stages:
        peer_wq = din("peer_wq", [DEPTH, D, D])
        p_skT = din("p_skT", [DEPTH, 128, 16, 128])
        NPL = int(os.environ.get("K_LAYERS", "4"))
        p_uT = din("p_uT", [NPL, D, 16384])
        peer_v = din("peer_v", [NPL, 16384, D])
        pe_E = dscr("pe_E", [TTOK, 2064])
        pe_acc = dscr("pe_acc", [TTOK, D])
    if "mamba0" in stages:
        m_win = din("m_win", [2, D, M_IN])
        m_wout = din("m_wout", [2, M_DI, D])
        m_convw = din("m_convw", [2, 128, 48, 4])
        m_convb = din("m_convb", [2, 128, 48])
        m_dtb = din("m_dtb", [2, 128, 64])
        m_alog = din("m_alog", [2, 128, 64])
        m_dsk = din("m_dsk", [2, 128, 64])
        m_ng = din("m_ng", [2, 128, 32])
        m_cs = din("m_cs", [2, 3, M_CH])
        m_ss = din("m_ss", [2, M_DI, 128])
        m_cwr = din("m_cwr", [2, 4, M_CH])
        m_cbr = din("m_cbr", [2, M_CH])
    res = dscr("res", [TTOK, D])
    vbuf = dscr("vbuf", [TTOK, D])
    ynT_d = dscr("ynT_d", [M_DI, TTOK], BF16)
    o_y = dout("o_y", [TTOK, D])
    if dbg:
        o_dbg = dout("o_dbg", [TTOK, D])
        o_dbg2 = dout("o_dbg2", [D, TTOK], BF16)

    with ExitStack() as stack:
        k = MK(nc, stack)
        ident = k.sb("ident_f", [128, 128], F32)
        identb = k.sb("ident_b", [128, 128], BF16)
        trit = k.sb("trit", [128, 128], F32)
        ones = k.sb("ones_f", [128, 128], F32)
        zeros = k.sb("zeros_f", [128, 512], F32)
        zerob = k.sb("zeros_b", [128, 512], BF16)
        t_const = Tok()
        ch_const = k.dma_chan("ch_const")
        k.dma("sp", ident[:], ident_d, ch_const, writes=[t_const])
        k.dma("sp", trit[:], trit_d, ch_const, writes=[t_const])
        k.op("act", lambda e: e.copy(out=identb[:], in_=ident[:]), reads=[t_const], writes=[t_const])
        k.op("dve", lambda e: e.memset(ones[:], 1.0), writes=[t_const])
        k.op("dve", lambda e: e.memset(zeros[:], 0.0), writes=[t_const])
        k.op("dve", lambda e: e.memset(zerob[:], 0.0), writes=[t_const])

        hT = k.sb("hT", [128, DC, TTOK], BF16)
        t_hT = [Tok() for _ in range(TT)]
        t_res = [Tok() for _ in range(TT)]
        t_vbuf = [Tok() for _ in range(TT)]
        ch_res = [k.dma_chan("ch_res") for _ in range(TT)]
        ch_vbuf = [k.dma_chan("ch_vbuf") for _ in range(TT)]
        t_ynTd = [Tok() for _ in range(TT)]
        ch_ynTd = [k.dma_chan("ch_ynTd") for _ in range(TT)]
        ch_dbg = k.dma_chan("ch_dbg")

        PB = [k.ps(f"pb{i}", [128, 512], F32) for i in range(8)]
        t_PB = [Tok(excl=True) for _ in range(8)]

        class PRing:
            def __init__(self, idx):
                self.idx = list(idx)
                self.i = 0

            def next(self):
                j = self.idx[self.i % len(self.idx)]
                self.i += 1
                return PB[j], t_PB[j]

        def tsl(tt):
            return slice(tt * 128, (tt + 1) * 128)

        def gs8(g):
            return slice(g * 8, (g + 1) * 8)

        def transpose_rows(src, t_src, dst3, t_dst, xb_ring, pr):
            xb, t_xb = xb_ring.next()
            k.op("act", lambda e: e.copy(out=xb[:], in_=src), reads=[t_src], writes=[t_xb])
            for q in range(4):
                pt, t_pt = pr.next()
                ptb = pt[:].bitcast(BF16)
                for c4 in range(4):
                    c = q * 4 + c4
                    k.op("pe", lambda e: e.transpose(out=ptb[:, c4 * 128:(c4 + 1) * 128],
                                                     in_=xb[:, c * 128:(c + 1) * 128], identity=identb[:]),
                         reads=[t_xb, t_const], writes=[t_pt])
                k.op("dve", lambda e: e.tensor_copy(out=dst3[:, q * 4:(q + 1) * 4, :],
                                                    in_=ptb[:, 0:512].rearrange("p (c t) -> p c t", c=4)),
                     reads=[t_pt], writes=[t_dst])

        with k.scope() as st:
            xin = Ring(k, "xin", 2, [128, D], F32, stack=st, chan=True)
            xb_ring = Ring(k, "xb", 2, [128, D], BF16, stack=st)
            pr = PRing([0, 1, 2, 3])
            for tt in range(NT):
                k.dma("sp", res[tsl(tt), :], xp[tsl(tt), :], ch_res[tt], writes=[t_res[tt]])
            for q in range(4):
                k.dma("sp", res[SEQ + 1:TTOK, q * 512:(q + 1) * 512], zeros[0:127, :], ch_res[NT], reads=[t_const], writes=[t_res[NT]])
            k.dma("sp", res[SEQ:SEQ + 1, :], xs, ch_res[NT], writes=[t_res[NT]])
            for q in range(8):
                k.dma("sp", ynT_d[q * 512:(q + 1) * 512, SEQ:TTOK].rearrange("(c p) t -> p c t", p=128),
                      zerob[:].rearrange("p (c t) -> p c t", c=4), ch_ynTd[NT], reads=[t_const], writes=[t_ynTd[NT]])
            for tt in range(TT):
                xt, t_xt, ch = xin.next()
                src = xp[tsl(tt), :] if tt < NT else res[tsl(tt), :]
                k.dma("sp", xt[:], src, ch, reads=([] if tt < NT else [t_res[tt]]), writes=[t_xt])
                transpose_rows(xt[:], t_xt, hT[:, :, tsl(tt)], t_hT[tt], xb_ring, pr)

        def proj_ln(get_aT, KC, W, li, lk, wbufs=2):
            proj_v(get_aT, KC, W, wbufs)
            ln_pass(li, lk)

        def proj_v(get_aT, KC, W, wbufs=2):
            with k.scope() as st:
                wr = Ring(k, "pw", wbufs, [128, KC, 512], BF16, stack=st, chan=True)
                hb_r = Ring(k, "phb", 2, [128, 512], F32, stack=st, chan=True)
                vb_r = Ring(k, "pvb", 2, [128, 512], F32, stack=st)
                pm = PRing([0, 1, 2, 3])
                src_state = {}
                for fb in range(4):
                    wb, t_wb, chw = wr.next()
                    k.dma("pool", wb[:], W.rearrange("(c p) f -> p c f", p=128)[:, :, fb * 512:(fb + 1) * 512],
                          chw, writes=[t_wb])
                    for tt in range(TT):
                        aT, t_aT = get_aT(tt, st, src_state)
                        ps, t_ps = pm.next()
                        for c in range(KC):
                            k.op("pe", lambda e: e.matmul(out=ps[:], lhsT=aT[:, c, :], rhs=wb[:, c, :],
                                                          start=(c == 0), stop=(c == KC - 1)),
                                 reads=[t_aT, t_wb], writes=[t_ps])
                        hb, t_hb, chh = hb_r.next()
                        k.dma("sp", hb[:], res[tsl(tt), fb * 512:(fb + 1) * 512], chh, reads=[t_res[tt]], writes=[t_hb])
                        vb, t_vb = vb_r.next()
                        k.op("dve", lambda e: e.scalar_tensor_tensor(out=vb[:], in0=hb[:], scalar=float(DN_ALPHA), in1=ps[:],
                                                                     op0=ALU.mult, op1=ALU.add),
                             reads=[t_hb, t_ps], writes=[t_vb])
                        k.dma("sp", vbuf[tsl(tt), fb * 512:(fb + 1) * 512], vb[:], ch_vbuf[tt], reads=[t_vb], writes=[t_vbuf[tt]])
        def ln_pass(li, lk):
            with k.scope() as st:
                gb = k.sb("ln_gsb", [128, D], F32, st)
                bb = k.sb("ln_bsb", [128, D], F32, st)
                t_gb = Tok()
                ch_gb = k.dma_chan("ch_gb")
                k.dma("sp", gb[:], ln_gd[li, lk], ch_gb, writes=[t_gb])
                k.dma("sp", bb[:], ln_bd[li, lk], ch_gb, writes=[t_gb])
                v_r = Ring(k, "lnv", 2, [128, D], F32, stack=st, chan=True)
                sq_r = Ring(k, "lnsq", 1, [128, D], F32, stack=st)
                hn_r = Ring(k, "lnh", 2, [128, D], F32, stack=st)
                st_r = Ring(k, "lnst", 2, [128, 8], F32, stack=st)
                xb_ring = Ring(k, "lnxb", 2, [128, D], BF16, stack=st)
                pr = PRing([4, 5, 6, 7])
                for tt in range(TT):
                    v, t_v, chv = v_r.next()
                    k.dma("sp", v[:], vbuf[tsl(tt), :], chv, reads=[t_vbuf[tt]], writes=[t_v])
                    sq, t_sq = sq_r.next()
                    s8, t_s8 = st_r.next()
                    k.op("dve", lambda e: e.reduce_sum(out=s8[:, 0:1], in_=v[:], axis=AX.X), reads=[t_v], writes=[t_s8])
                    k.op("act", lambda e: e.activation(out=sq[:], in_=v[:], func=AF.Square, accum_out=s8[:, 1:2]),
                         reads=[t_v], writes=[t_sq, t_s8])
                    k.op("dve", lambda e: e.tensor_scalar(out=s8[:, 2:3], in0=s8[:, 0:1], scalar1=1.0 / D, scalar2=None,
                                                          op0=ALU.mult), reads=[t_s8], writes=[t_s8])
                    k.op("dve", lambda e: e.tensor_tensor(out=s8[:, 3:4], in0=s8[:, 2:3], in1=s8[:, 2:3], op=ALU.mult),
                         reads=[t_s8], writes=[t_s8])
                    k.op("dve", lambda e: e.scalar_tensor_tensor(out=s8[:, 3:4], in0=s8[:, 1:2], scalar=1.0 / D,
                                                                 in1=s8[:, 3:4], op0=ALU.mult, op1=ALU.subtract),
                         reads=[t_s8], writes=[t_s8])
                    k.op("act", lambda e: e.activation(out=s8[:, 4:5], in_=s8[:, 3:4], func=AF.Sqrt, bias=float(LN_EPS)),
                         reads=[t_s8], writes=[t_s8])
                    k.op("dve", lambda e: e.reciprocal(out=s8[:, 5:6], in_=s8[:, 4:5]), reads=[t_s8], writes=[t_s8])
                    k.op("dve", lambda e: e.tensor_scalar(out=s8[:, 6:7], in0=s8[:, 2:3], scalar1=s8[:, 5:6], scalar2=-1.0,
                                                          op0=ALU.mult, op1=ALU.mult), reads=[t_s8], writes=[t_s8])
                    hn, t_hn = hn_r.next()
                    k.op("act", lambda e: e.activation(out=hn[:], in_=v[:], func=AF.Identity, scale=s8[:, 5:6],
                                                       bias=s8[:, 6:7]), reads=[t_v, t_s8], writes=[t_hn])
                    k.op("dve", lambda e: e.tensor_tensor(out=hn[:], in0=hn[:], in1=gb[:], op=ALU.mult),
                         reads=[t_hn, t_gb], writes=[t_hn])
                    k.op("dve", lambda e: e.tensor_tensor(out=hn[:], in0=hn[:], in1=bb[:], op=ALU.add),
                         reads=[t_hn, t_gb], writes=[t_hn])
                    k.dma("sp", res[tsl(tt), :], hn[:], ch_res[tt], reads=[t_hn], writes=[t_res[tt]])
                    transpose_rows(hn[:], t_hn, hT[:, :, tsl(tt)], t_hT[tt], xb_ring, pr)


        def featproj(W, ncols, dst, t_dst):
            with k.scope() as st:
                wr = Ring(k, "fpw", 2, [128, DC, 512], BF16, stack=st, chan=True)
                pm = PRing([0, 1, 2, 3])
                n = 0
                for fb in range((ncols + 511) // 512):
                    cols = min(512, ncols - fb * 512)
                    wb, t_wb, chw = wr.next()
                    k.dma("pool", wb[:, :, 0:cols], W.rearrange("(c p) f -> p c f", p=128)[:, :, fb * 512:fb * 512 + cols],
                          chw, writes=[t_wb])
                    for q in range(cols // 128):
                        fc = fb * 4 + q
                        for tb in range(5):
                            t0 = tb * 512
                            tw = 512 if tb < 4 else 128
                            tiles = list(range(tb * 4, tb * 4 + 4)) if tb < 4 else [NT]
                            ps, t_ps = pm.next()
                            for kc in range(DC):
                                k.op("pe", lambda e: e.matmul(out=ps[:, 0:tw], lhsT=wb[:, kc, q * 128:(q + 1) * 128],
                                                              rhs=hT[:, kc, t0:t0 + tw], start=(kc == 0), stop=(kc == DC - 1)),
                                     reads=[t_wb] + [t_hT[t] for t in tiles], writes=[t_ps])
                            eng = "act" if n % 2 == 0 else "dve"
                            n += 1
                            if eng == "act":
                                k.op("act", lambda e: e.copy(out=dst[:, fc, t0:t0 + tw], in_=ps[:, 0:tw]),
                                     reads=[t_ps], writes=[t_dst[t] for t in tiles])
                            else:
                                k.op("dve", lambda e: e.tensor_copy(out=dst[:, fc, t0:t0 + tw], in_=ps[:, 0:tw]),
                                     reads=[t_ps], writes=[t_dst[t] for t in tiles])

        def transpose_bf(src_fn, nblk, dst_fn, t_src, t_dst, pr):
            for i0 in range(0, nblk, 4):
                nn = min(4, nblk - i0)
                pt, t_pt = pr.next()
                ptb = pt[:].bitcast(BF16)
                for q in range(nn):
                    k.op("pe", lambda e: e.transpose(out=ptb[:, q * 128:(q + 1) * 128], in_=src_fn(i0 + q), identity=identb[:]),
                         reads=[t_src, t_const], writes=[t_pt])
                k.op("act", lambda e: e.copy(out=dst_fn(i0, nn), in_=ptb[:, 0:nn * 128].rearrange("p (c t) -> p c t", c=nn)),
                     reads=[t_pt], writes=[t_dst])

        def memattn_layer(i, only_kv=False):
            MSC = float(512 ** -0.5)
            with k.scope() as sl:
                KT = k.sb("ma_KT", [128, DC, MEM], BF16, sl)
                Vv = k.sb("ma_V", [128, 2, D], BF16, sl)
                KTs = k.sb("ma_KTs", [128, DC, MEM], BF16, sl)
                Vs = k.sb("ma_Vs", [128, 2, D], BF16, sl)
                t_kv, t_kvs = Tok(), Tok()
                with k.scope() as st:
                    memT = k.sb("memT", [128, DC, MEM], BF16, st)
                    t_memT = Tok()
                    Kb = k.sb("ma_Kb", [128, 2, D], BF16, st)
                    t_Kb = Tok()
                    xin = Ring(k, "xinm", 2, [128, D], F32, stack=st, chan=True)
                    xb_ring = Ring(k, "xbm", 2, [128, D], BF16, stack=st)
                    pr = PRing([0, 1, 2, 3])
                    for mt in range(2):
                        xt, t_xt, ch = xin.next()
                        k.dma("sp", xt[:], memp[tsl(mt), :], ch, writes=[t_xt])
                        transpose_rows(xt[:], t_xt, memT[:, :, tsl(mt)], t_memT, xb_ring, pr)
                    wr = Ring(k, "wkv", 2, [128, DC, 512], BF16, stack=st, chan=True)
                    pm = PRing([4, 5])
                    ost = Ring(k, "ost", 2, [128, 512], F32, stack=st, chan=True)
                    for fb in range(8):
                        wb, t_wb, chw = wr.next()
                        k.dma("pool", wb[:], mem_wkv[i].rearrange("(c p) f -> p c f", p=128)[:, :, fb * 512:(fb + 1) * 512],
                              chw, writes=[t_wb])
                        for mt in range(2):
                            ps, t_ps = pm.next()
                            for c in range(DC):
                                k.op("pe", lambda e: e.matmul(out=ps[:], lhsT=memT[:, c, tsl(mt)],
                                                              rhs=wb[:, c, :], start=(c == 0), stop=(c == DC - 1)),
                                     reads=[t_memT, t_wb], writes=[t_ps])
                            ob, t_ob, cho = ost.next()
                            k.op("act", lambda e: e.copy(out=ob[:], in_=ps[:]), reads=[t_ps], writes=[t_ob])
                            k.dma("sp", o_memkv[i, tsl(mt), fb * 512:(fb + 1) * 512], ob[:], cho, reads=[t_ob])
                            if fb < 4:
                                k.op("dve", lambda e: e.tensor_copy(out=Kb[:, mt, fb * 512:(fb + 1) * 512], in_=ps[:]),
                                     reads=[t_ps], writes=[t_Kb])
                            else:
                                k.op("dve", lambda e: e.tensor_copy(out=Vv[:, mt, (fb - 4) * 512:(fb - 3) * 512], in_=ps[:]),
                                     reads=[t_ps], writes=[t_kv])
                    for mt in range(2):
                        transpose_bf(lambda c: Kb[:, mt, c * 128:(c + 1) * 128], DC,
                                     lambda c0, n: KT[:, c0:c0 + n, tsl(mt)], t_Kb, t_kv, pr)
                    for mt in range(2):
                        for half in range(2):
                            xt, t_xt, ch = xin.next()
                            k.dma("sp", xt[:], cmk[i, tsl(mt), half * D:(half + 1) * D], ch, writes=[t_xt])
                            if half == 0:
                                xb, t_xb = xb_ring.next()
                                k.op("dve", lambda e: e.tensor_copy(out=xb[:], in_=xt[:]), reads=[t_xt], writes=[t_xb])
                                transpose_bf(lambda c: xb[:, c * 128:(c + 1) * 128], DC,
                                             lambda c0, n: KTs[:, c0:c0 + n, tsl(mt)], t_xb, t_kvs, pr)
                            else:
                                k.op("dve", lambda e: e.tensor_copy(out=Vs[:, mt, :], in_=xt[:]), reads=[t_xt], writes=[t_kvs])
                if only_kv:
                    return
                qT = k.sb("qT_all", [128, DC, TTOK], BF16, sl)
                t_q = [Tok() for _ in range(TT)]
                featproj(mem_wq[i], D, qT, t_q)
                with k.scope() as st:
                    p_r = Ring(k, "ma_p", 2, [128, 4, 256], F32, stack=st)
                    pn_r = Ring(k, "ma_pn", 2, [128, 4, 256], BF16, stack=st)
                    pT_r = Ring(k, "ma_pT", 2, [128, 8, 128], BF16, stack=st)
                    sm_r = Ring(k, "ma_sm", 2, [128, 16], F32, stack=st)
                    for tt in range(TT):
                        kt, vv, t_k = (KT, Vv, t_kv) if tt < NT else (KTs, Vs, t_kvs)
                        for h in range(4):
                            bk, t_bk = PB[h // 2], t_PB[h // 2]
                            off = (h % 2) * 256
                            for dc in range(4):
                                k.op("pe", lambda e: e.matmul(out=bk[:, off:off + 256], lhsT=qT[:, h * 4 + dc, tsl(tt)],
                                                              rhs=kt[:, h * 4 + dc, :], start=(dc == 0), stop=(dc == 3)),
                                     reads=[t_q[tt], t_k], writes=[t_bk])
                        sm, t_sm = sm_r.next()
                        for hb in range(2):
                            k.op("dve", lambda e: e.reduce_max(out=sm[:, hb * 2:hb * 2 + 2],
                                                               in_=PB[hb][:].rearrange("p (a m) -> p a m", a=2), axis=AX.X),
                                 reads=[t_PB[hb]], writes=[t_sm])
                        k.op("dve", lambda e: e.tensor_scalar(out=sm[:, 4:8], in0=sm[:, 0:4], scalar1=-MSC, scalar2=None, op0=ALU.mult),
                             reads=[t_sm], writes=[t_sm])
                        p, t_p = p_r.next()
                        for h in range(4):
                            off = (h % 2) * 256
                            k.op("act", lambda e: e.activation(out=p[:, h, :], in_=PB[h // 2][:, off:off + 256], func=AF.Exp,
                                                               scale=MSC, bias=sm[:, 4 + h:5 + h], accum_out=sm[:, 8 + h:9 + h]),
                                 reads=[t_PB[h // 2], t_sm], writes=[t_p, t_sm])
                        k.op("dve", lambda e: e.reciprocal(out=sm[:, 12:16], in_=sm[:, 8:12]), reads=[t_sm], writes=[t_sm])
                        pn, t_pn = pn_r.next()
                        k.op("dve", lambda e: e.tensor_tensor(out=pn[:], in0=p[:],
                                                              in1=sm[:, 12:16].unsqueeze(2).to_broadcast([128, 4, 256]), op=ALU.mult),
                             reads=[t_p, t_sm], writes=[t_pn])
                        pT, t_pT = pT_r.next()
                        transpose_bf(lambda j: pn[:, j // 2, (j % 2) * 128:(j % 2 + 1) * 128], 8,
                                     lambda j0, n: pT[:, j0:j0 + n, :], t_pn, t_pT, PRing([2, 3]))
                        for h in range(4):
                            bk, t_bk = PB[4 + h % 2], t_PB[4 + h % 2]
                            for dc in range(4):
                                for mc in range(2):
                                    k.op("pe", lambda e: e.matmul(out=bk[:, dc * 128:(dc + 1) * 128],
                                                                  lhsT=vv[:, mc, h * 512 + dc * 128:h * 512 + (dc + 1) * 128],
                                                                  rhs=pT[:, h * 2 + mc, :], start=(mc == 0), stop=(mc == 1)),
                                         reads=[t_k, t_pT], writes=[t_bk])
                            k.op("act", lambda e: e.copy(out=qT[:, h * 4:(h + 1) * 4, tsl(tt)],
                                                         in_=bk[:].rearrange("p (c t) -> p c t", c=4)),
                                 reads=[t_bk], writes=[t_q[tt]])
                proj_v(lambda tt, st, state: (qT[:, :, tsl(tt)], t_q[tt]), DC, mem_wo[i], wbufs=1)
            ln_pass(i, 1)


        def peer_layer(i):
            NEB = int(os.environ.get("P_NEB", "16"))
            t_E = [Tok() for _ in range(TT)]
            ch_E = [k.dma_chan("ch_E") for _ in range(TT)]
            t_acc = [Tok() for _ in range(TT)]
            ch_acc = [k.dma_chan("ch_acc") for _ in range(TT)]
            with k.scope() as sl:
                qT = k.sb("pq_all", [128, DC, TTOK], BF16, sl)
                t_q = [Tok() for _ in range(TT)]
                skT = k.sb("p_skT_s", [128, 16, 128], BF16, sl)
                t_sk = Tok()
                k.dma("pool", skT[:], p_skT[i], k.dma_chan("ch_sk"), writes=[t_sk])
                featproj(peer_wq[i], D, qT, t_q)
                with k.scope() as st:
                    Et_r = Ring(k, "p_Et", 2, [128, 2064], F32, stack=st)
                    sm_r = Ring(k, "p_sm", 2, [128, 32], F32, stack=st)
                    v16_r = Ring(k, "p_v16", 2, [128, 16, 16], F32, stack=st)
                    tmp_r = Ring(k, "p_tmp", 2, [128, 256], F32, stack=st)
                    cand_r = Ring(k, "p_cand", 2, [128, 8, 256], F32, stack=st)
                    c8_r = Ring(k, "p_c8", 2, [128, 8, 16], F32, stack=st)
                    for tt in range(TT):
                        for hk in range(16):
                            bk, t_bk = PB[hk // 4], t_PB[hk // 4]
                            k.op("pe", lambda e: e.matmul(out=bk[:, (hk % 4) * 128:(hk % 4 + 1) * 128], lhsT=qT[:, hk, tsl(tt)],
                                                          rhs=skT[:, hk, :], start=True, stop=True),
                                 reads=[t_q[tt], t_sk], writes=[t_bk])
                        sm, t_sm = sm_r.next()
                        for bq in range(4):
                            k.op("dve", lambda e: e.reduce_max(out=sm[:, bq * 4:bq * 4 + 4],
                                                               in_=PB[bq][:].rearrange("p (a n) -> p a n", a=4), axis=AX.X),
                                 reads=[t_PB[bq]], writes=[t_sm])
                        k.op("dve", lambda e: e.tensor_scalar(out=sm[:, 16:32], in0=sm[:, 0:16], scalar1=-1.0, scalar2=None, op0=ALU.mult),
                             reads=[t_sm], writes=[t_sm])
                        Et, t_Et = Et_r.next()
                        for hk in range(16):
                            k.op("act", lambda e: e.activation(out=Et[:, hk * 128:(hk + 1) * 128],
                                                               in_=PB[hk // 4][:, (hk % 4) * 128:(hk % 4 + 1) * 128], func=AF.Exp,
                                                               bias=sm[:, 16 + hk:17 + hk]),
                                 reads=[t_PB[hk // 4], t_sm], writes=[t_Et])
                        v16, t_v16 = v16_r.next()
                        for hk in range(16):
                            tmp, t_tmp = tmp_r.next()
                            Eh = Et[:, hk * 128:(hk + 1) * 128]
                            k.op("dve", lambda e: e.max(out=v16[:, hk, 0:8], in_=Eh), reads=[t_Et], writes=[t_v16])
                            k.op("dve", lambda e: e.match_replace(out=tmp[:, 0:128], in_to_replace=v16[:, hk, 0:8], in_values=Eh,
                                                                  imm_value=-1.0), reads=[t_Et, t_v16], writes=[t_tmp])
                            k.op("dve", lambda e: e.max(out=v16[:, hk, 8:16], in_=tmp[:, 0:128]), reads=[t_tmp], writes=[t_v16])
                        cand, t_cand = cand_r.next()
                        c8, t_c8 = c8_r.next()
                        for h in range(8):
                            k.op("dve", lambda e: e.tensor_tensor(
                                out=cand[:, h, :].rearrange("p (a b) -> p a b", a=16),
                                in0=v16[:, 2 * h, :].unsqueeze(2).to_broadcast([128, 16, 16]),
                                in1=v16[:, 2 * h + 1, :].unsqueeze(1).to_broadcast([128, 16, 16]), op=ALU.mult),
                                 reads=[t_v16], writes=[t_cand])
                            tmp, t_tmp = tmp_r.next()
                            k.op("dve", lambda e: e.max(out=c8[:, h, 0:8], in_=cand[:, h, :]), reads=[t_cand], writes=[t_c8])
                            k.op("dve", lambda e: e.match_replace(out=tmp[:], in_to_replace=c8[:, h, 0:8], in_values=cand[:, h, :],
                                                                  imm_value=-1.0), reads=[t_cand, t_c8], writes=[t_tmp])
                            k.op("dve", lambda e: e.max(out=c8[:, h, 8:16], in_=tmp[:]), reads=[t_tmp], writes=[t_c8])
                            k.op("dve", lambda e: e.tensor_copy(out=Et[:, 2048 + h:2049 + h], in_=c8[:, h, 15:16]),
                                 reads=[t_c8], writes=[t_Et])
                            tmp2, t_tmp2 = tmp_r.next()
                            k.op("dve", lambda e: e.scalar_tensor_tensor(out=tmp2[:], in0=cand[:, h, :], scalar=c8[:, h, 15:16],
                                                                         in1=cand[:, h, :], op0=ALU.is_ge, op1=ALU.mult),
                                 reads=[t_cand, t_c8], writes=[t_tmp2])
                            k.op("dve", lambda e: e.reduce_sum(out=Et[:, 2056 + h:2057 + h], in_=tmp2[:], axis=AX.X),
                                 reads=[t_tmp2], writes=[t_Et])
                        k.op("dve", lambda e: e.reciprocal(out=Et[:, 2056:2064], in_=Et[:, 2056:2064]), reads=[t_Et], writes=[t_Et])
                        k.dma("sp", pe_E[tsl(tt), :], Et[:], ch_E[tt], reads=[t_Et], writes=[t_E[tt]])
            with k.scope() as sl:
                UT = k.sb("p_UT", [128, DC, 1024], BF16, sl)
                Vb = k.sb("p_Vb", [128, 8, D], BF16, sl)
                t_UT, t_Vb = Tok(), Tok()
                ch_UT, ch_Vb = k.dma_chan("ch_UT"), k.dma_chan("ch_Vb")
                Et_r = Ring(k, "p_Et2", 2, [128, 2064], F32, stack=sl, chan=True)
                acc_r = Ring(k, "p_acc", 2, [128, D], F32, stack=sl, chan=True)
                gA_r = Ring(k, "p_gA", 1, [128, 1024], F32, stack=sl)
                G_r = Ring(k, "p_G", 1, [128, 1024], F32, stack=sl)
                P_r = Ring(k, "p_P", 3, [128, 1024], F32, stack=sl)
                w_r = Ring(k, "p_w", 2, [128, 1024], BF16, stack=sl)
                wT_r = Ring(k, "p_wT", 2, [128, 8, 128], BF16, stack=sl)
                uT_src = p_uT[i].rearrange("(c p) e -> p c e", p=128)
                for eb in range(NEB):
                    i0 = eb * 8
                    for hh in range(2):
                        k.dma("pool", UT[:, :, hh * 512:(hh + 1) * 512], uT_src[:, :, eb * 1024 + hh * 512:eb * 1024 + (hh + 1) * 512],
                              ch_UT, writes=[t_UT])
                    for hh in range(2):
                        k.dma("pool", Vb[:, hh * 4:(hh + 1) * 4, :],
                              peer_v[i, eb * 1024 + hh * 512:eb * 1024 + (hh + 1) * 512, :].rearrange("(c p) d -> p c d", p=128),
                              ch_Vb, writes=[t_Vb])
                    for tt in range(TT):
                        Et, t_Et, chE = Et_r.next()
                        k.dma("sp", Et[:], pe_E[tsl(tt), :], chE, reads=[t_E[tt]], writes=[t_Et])
                        acc, t_ac, cha = acc_r.next()
                        if eb > 0:
                            k.dma("sp", acc[:], pe_acc[tsl(tt), :], cha, reads=[t_acc[tt]], writes=[t_ac])
                        gA, t_gA = gA_r.next()
                        for eh in range(2):
                            for kc in range(DC):
                                k.op("pe", lambda e: e.matmul(out=PB[eh][:], lhsT=hT[:, kc, tsl(tt)], rhs=UT[:, kc, eh * 512:(eh + 1) * 512],
                                                              start=(kc == 0), stop=(kc == DC - 1)),
                                     reads=[t_hT[tt], t_UT], writes=[t_PB[eh]])
                            k.op("act", lambda e: e.activation(out=gA[:, eh * 512:(eh + 1) * 512], in_=PB[eh][:], func=AF.Gelu),
                                 reads=[t_PB[eh]], writes=[t_gA])
                        G, t_G = G_r.next()
                        for h in range(8):
                            P, t_P = P_r.next()
                            k.op("pool", lambda e: e.tensor_tensor(
                                out=P[:].rearrange("p (a b) -> p a b", a=8),
                                in0=Et[:, 2 * h * 128 + i0:2 * h * 128 + i0 + 8].unsqueeze(2).to_broadcast([128, 8, 128]),
                                in1=Et[:, (2 * h + 1) * 128:(2 * h + 2) * 128].unsqueeze(1).to_broadcast([128, 8, 128]), op=ALU.mult),
                                 reads=[t_Et], writes=[t_P])
                            M, t_M = P, t_P
                            k.op("dve", lambda e: e.scalar_tensor_tensor(out=P[:], in0=P[:], scalar=Et[:, 2048 + h:2049 + h], in1=P[:],
                                                                         op0=ALU.is_ge, op1=ALU.mult), reads=[t_Et], writes=[t_P])
                            if h == 0:
                                k.op("dve", lambda e: e.tensor_scalar(out=G[:], in0=M[:], scalar1=Et[:, 2056:2057], scalar2=None, op0=ALU.mult),
                                     reads=[t_M, t_Et], writes=[t_G])
                            else:
                                k.op("dve", lambda e: e.scalar_tensor_tensor(out=G[:], in0=M[:], scalar=Et[:, 2056 + h:2057 + h], in1=G[:],
                                                                             op0=ALU.mult, op1=ALU.add), reads=[t_M, t_Et, t_G], writes=[t_G])
                        w, t_w = w_r.next()
                        k.op("dve", lambda e: e.tensor_tensor(out=w[:], in0=G[:], in1=gA[:], op=ALU.mult), reads=[t_G, t_gA], writes=[t_w])
                        wT, t_wT = wT_r.next()
                        transpose_bf(lambda c: w[:, c * 128:(c + 1) * 128], 8, lambda c0, n: wT[:, c0:c0 + n, :], t_w, t_wT, PRing([2, 3]))
                        for db in range(4):
                            for ec in range(8):
                                k.op("pe", lambda e: e.matmul(out=PB[4 + db][:], lhsT=wT[:, ec, :], rhs=Vb[:, ec, db * 512:(db + 1) * 512],
                                                              start=(ec == 0), stop=(ec == 7)),
                                     reads=[t_wT, t_Vb], writes=[t_PB[4 + db]])
                            if eb > 0:
                                k.op("dve", lambda e: e.tensor_tensor(out=acc[:, db * 512:(db + 1) * 512], in0=acc[:, db * 512:(db + 1) * 512],
                                                                      in1=PB[4 + db][:], op=ALU.add), reads=[t_PB[4 + db], t_ac], writes=[t_ac])
                            else:
                                k.op("act", lambda e: e.copy(out=acc[:, db * 512:(db + 1) * 512], in_=PB[4 + db][:]),
                                     reads=[t_PB[4 + db]], writes=[t_ac])
                        k.dma("sp", pe_acc[tsl(tt), :], acc[:], ch_acc[tt], reads=[t_ac], writes=[t_acc[tt]])
            with k.scope() as st:
                r_r = Ring(k, "p_r", 2, [128, D], F32, stack=st, chan=True)
                a_r = Ring(k, "p_a", 2, [128, D], F32, stack=st, chan=True)
                for tt in range(TT):
                    r, t_r, chr_ = r_r.next()
                    a, t_a, cha = a_r.next()
                    k.dma("sp", r[:], res[tsl(tt), :], chr_, reads=[t_res[tt]], writes=[t_r])
                    k.dma("sp", a[:], pe_acc[tsl(tt), :], cha, reads=[t_acc[tt]], writes=[t_a])
                    k.op("dve", lambda e: e.scalar_tensor_tensor(out=a[:], in0=r[:], scalar=float(DN_ALPHA), in1=a[:],
                                                                 op0=ALU.mult, op1=ALU.add), reads=[t_r, t_a], writes=[t_a])
                    k.dma("sp", vbuf[tsl(tt), :], a[:], ch_vbuf[tt], reads=[t_a], writes=[t_vbuf[tt]])
            ln_pass(i, 2)


        def fox_layer(li):
            SC = float(128 ** -0.5)
            w_in = f_win.rearrange("(c p) f -> p c f", p=128)
            with k.scope() as sl:
                kT = k.sb("fx_kT", [128, 4, TTOK], BF16, sl)
                vtok = k.sb("fx_v", [128, TT, 512], BF16, sl)
                negcT = k.sb("fx_ncT", [16, SEQ], F32, sl)
                sel = k.sb("fx_sel", [16, 16 * 128], F32, sl)
                mneg = k.sb("fx_mneg", [128, 128], F32, sl)
                t_kv, t_nc, t_cst = Tok(), Tok(), Tok()
                knew = k.sb("fx_knew", [1, 1024], F32, sl)
                lfnew = k.sb("fx_lfnew", [1, 16], F32, sl)
                t_new = Tok()
                chc = k.dma_chan("ch_fxc")
                k.dma("sp", sel[:], sel16_d, chc, writes=[t_cst])
                k.dma("sp", mneg[:], mneg_d, chc, writes=[t_cst])
                with k.scope() as st:
                    wkv = k.sb("fx_wkv", [128, DC, 1024], BF16, st)
                    wf = k.sb("fx_wf", [128, DC, 16], BF16, st)
                    bf_s = k.sb("fx_bf", [128, 16], F32, st)
                    lf_all = k.sb("fx_lf", [128, NT, 16], F32, st)
                    t_w, t_lf = Tok(), Tok()
                    chw = k.dma_chan("ch_fxw")
                    for hh in range(2):
                        k.dma("pool", wkv[:, :, hh * 512:(hh + 1) * 512], w_in[:, :, 2048 + hh * 512:2048 + (hh + 1) * 512], chw, writes=[t_w])
                    k.dma("pool", wf[:], w_in[:, :, 3072:3088], chw, writes=[t_w])
                    k.dma("sp", bf_s[:], f_bf, chw, writes=[t_w])
                    kvo_r = Ring(k, "fx_kvo", 2, [128, 1024], F32, stack=st, chan=True)
                    kb_r = Ring(k, "fx_kb", 2, [128, 512], BF16, stack=st)
                    lt_r = Ring(k, "fx_lt", 2, [128, 4, 16], F32, stack=st, chan=True)
                    nc_r = Ring(k, "fx_nc", 2, [128, 16], F32, stack=st)
                    for tt in range(TT):
                        kvo, t_kvo, chk = kvo_r.next()
                        for hh in range(2):
                            for kc in range(DC):
                                k.op("pe", lambda e: e.matmul(out=PB[hh][:], lhsT=hT[:, kc, tsl(tt)], rhs=wkv[:, kc, hh * 512:(hh + 1) * 512],
                                                              start=(kc == 0), stop=(kc == DC - 1)), reads=[t_hT[tt], t_w], writes=[t_PB[hh]])
                            k.op("act", lambda e: e.copy(out=kvo[:, hh * 512:(hh + 1) * 512], in_=PB[hh][:]), reads=[t_PB[hh]], writes=[t_kvo])
                        k.dma("sp", o_fox_kv[tsl(tt), :], kvo[:], chk, reads=[t_kvo])
                        kb, t_kb = kb_r.next()
                        k.op("dve", lambda e: e.tensor_copy(out=kb[:], in_=kvo[:, 0:512]), reads=[t_kvo], writes=[t_kb])
                        k.op("dve", lambda e: e.tensor_copy(out=vtok[:, tt, :], in_=kvo[:, 512:1024]), reads=[t_kvo], writes=[t_kv])
                        transpose_bf(lambda c: kb[:, c * 128:(c + 1) * 128], 4, lambda c0, n: kT[:, c0:c0 + n, tsl(tt)], t_kb, t_kv, PRing([2, 3]))
                        for kc in range(DC):
                            k.op("pe", lambda e: e.matmul(out=PB[4][:, 0:16], lhsT=hT[:, kc, tsl(tt)], rhs=wf[:, kc, :],
                                                          start=(kc == 0), stop=(kc == DC - 1)), reads=[t_hT[tt], t_w], writes=[t_PB[4]])
                        lt, t_lt, chl = lt_r.next()
                        x0, ax, ee, lf = lt[:, 0, :], lt[:, 1, :], lt[:, 2, :], lt[:, 3, :]
                        k.op("dve", lambda e: e.tensor_tensor(out=x0, in0=PB[4][:, 0:16], in1=bf_s[:], op=ALU.add), reads=[t_PB[4], t_w], writes=[t_lt])
                        k.op("dve", lambda e: e.scalar_tensor_tensor(out=ax, in0=x0, scalar=-1.0, in1=x0, op0=ALU.mult, op1=ALU.max),
                             reads=[t_lt], writes=[t_lt])
                        k.op("act", lambda e: e.activation(out=ee, in_=ax, func=AF.Exp, scale=-1.0), reads=[t_lt], writes=[t_lt])
                        k.op("act", lambda e: e.activation(out=ee, in_=ee, func=AF.Ln, bias=1.0), reads=[t_lt], writes=[t_lt])
                        k.op("dve", lambda e: e.scalar_tensor_tensor(out=lf, in0=x0, scalar=0.0, in1=ee, op0=ALU.min, op1=ALU.subtract),
                             reads=[t_lt], writes=[t_lt])
                        k.dma("sp", o_fox_lf[tsl(tt), :], lf, chl, reads=[t_lt])
                        if tt == NT:
                            k.op("act", lambda e: e.copy(out=knew[:], in_=kvo[0:1, :]), reads=[t_kvo], writes=[t_new])
                            k.op("act", lambda e: e.copy(out=lfnew[:], in_=lt[0:1, 3, :]), reads=[t_lt], writes=[t_new])
                        if tt < NT:
                            k.op("dve", lambda e: e.tensor_copy(out=lf_all[:, tt, :], in_=lf), reads=[t_lt], writes=[t_lf])
                            for jj in range(tt + 1):
                                k.op("pe", lambda e: e.matmul(out=PB[5][:, 0:16], lhsT=(trit[:] if jj == tt else ones[:]), rhs=lf_all[:, jj, :],
                                                              start=(jj == 0), stop=(jj == tt)), reads=[t_lf, t_const], writes=[t_PB[5]])
                            ncx, t_ncx = nc_r.next()
                            k.op("dve", lambda e: e.tensor_scalar(out=ncx[:], in0=PB[5][:, 0:16], scalar1=-1.0 / SC, scalar2=None, op0=ALU.mult),
                                 reads=[t_PB[5]], writes=[t_ncx])
                            k.op("pe", lambda e: e.transpose(out=PB[6][0:16, 0:128], in_=ncx[:], identity=ident[:]),
                                 reads=[t_ncx, t_const], writes=[t_PB[6]])
                            k.op("act", lambda e: e.copy(out=negcT[:, tsl(tt)], in_=PB[6][0:16, 0:128]), reads=[t_PB[6]], writes=[t_nc])
                if os.environ.get("FOX_SAMPLE", "1") == "1":
                  with k.scope() as st:
                    NP_ = 128
                    pti = k.sb("fs_pti", [128, NP_], I32, st)
                    ptf = k.sb("fs_ptf", [128, NP_], F32, st)
                    idx = k.sb("fs_idx", [128, NP_], I32, st)
                    io = k.sb("fs_io", [128, 1], F32, st)
                    tris = k.sb("fs_tris", [128, 128], F32, st)
                    m16 = k.sb("fs_m16", [16, 4], F32, st)
                    t_ix = Tok()
                    chx = k.dma_chan("ch_fsx")
                    k.dma("sp", pti[:], pt_d.to_broadcast([128, NP_]), chx, writes=[t_ix])
                    k.dma("sp", io[:], iota_d, chx, writes=[t_ix])
                    k.dma("sp", tris[:], tris_d, chx, writes=[t_ix])
                    k.dma("sp", m16[:], mask16_d, chx, writes=[t_ix])
                    k.op("dve", lambda e: e.tensor_copy(out=ptf[:], in_=pti[:]), reads=[t_ix], writes=[t_ix])
                    k.op("dve", lambda e: e.tensor_scalar(out=ptf[:], in0=ptf[:], scalar1=128.0, scalar2=io[:, 0:1], op0=ALU.mult, op1=ALU.add),
                         reads=[t_ix], writes=[t_ix])
                    k.op("dve", lambda e: e.tensor_copy(out=idx[:], in_=ptf[:]), reads=[t_ix], writes=[t_ix])
                    qbc = k.sb("fs_qbc", [128, D], F32, st)
                    t_qr, t_qb = Tok(), Tok()
                    with k.scope() as sq:
                        qrow = k.sb("fs_qrow", [1, D], F32, sq)
                        wq_r = Ring(k, "fs_wq", 1, [128, DC, 512], BF16, stack=sq, chan=True)
                        for blk in range(4):
                            wq, t_wq, chq = wq_r.next()
                            k.dma("pool", wq[:], w_in[:, :, blk * 512:(blk + 1) * 512], chq, writes=[t_wq])
                            for kc in range(DC):
                                k.op("pe", lambda e: e.matmul(out=PB[0][0:1, :], lhsT=hT[:, kc, SEQ:SEQ + 1], rhs=wq[:, kc, :],
                                                              start=(kc == 0), stop=(kc == DC - 1)), reads=[t_hT[NT], t_wq], writes=[t_PB[0]])
                            k.op("act", lambda e: e.copy(out=qrow[0:1, blk * 512:(blk + 1) * 512], in_=PB[0][0:1, :]), reads=[t_PB[0]], writes=[t_qr])
                            k.op("pe", lambda e: e.matmul(out=PB[1][:], lhsT=ones[0:1, :], rhs=qrow[0:1, blk * 512:(blk + 1) * 512], start=True, stop=True),
                                 reads=[t_qr, t_const], writes=[t_PB[1]])
                            k.op("act", lambda e: e.copy(out=qbc[:, blk * 512:(blk + 1) * 512], in_=PB[1][:]), reads=[t_PB[1]], writes=[t_qb])
                    L = k.sb("fs_L", [128, NP_, 16], F32, st)
                    xa = k.sb("fs_xa", [128, NP_, 16], F32, st)
                    xb = k.sb("fs_xb", [128, NP_, 16], F32, st)
                    t_L, t_xa, t_xb = Tok(), Tok(), Tok()
                    chl2 = k.dma_chan("ch_fsl")
                    for pg in range(NP_):
                        k.idma(L[:, pg, :], f_lpool, idx[:, pg:pg + 1], chl2, reads=[t_ix], writes=[t_L])
                    src, t_src, dst, t_dst = L, t_L, xa, t_xa
                    dd = 1
                    while dd < NP_:
                        k.op("dve", lambda e: e.tensor_tensor(out=dst[:, 0:NP_ - dd, :], in0=src[:, 0:NP_ - dd, :], in1=src[:, dd:NP_, :], op=ALU.add),
                             reads=[t_src], writes=[t_dst])
                        k.op("dve", lambda e: e.tensor_copy(out=dst[:, NP_ - dd:NP_, :], in_=src[:, NP_ - dd:NP_, :]), reads=[t_src], writes=[t_dst])
                        if src is L:
                            src, t_src, dst, t_dst = xa, t_xa, xb, t_xb
                        else:
                            src, t_src, dst, t_dst = dst, t_dst, src, t_src
                        dd *= 2
                    k.op("dve", lambda e: e.tensor_tensor(out=dst[:], in0=src[:], in1=L[:], op=ALU.subtract), reads=[t_src, t_L], writes=[t_dst])
                    lfrep = k.sb("fs_lfrep", [1, NP_, 16], F32, st)
                    t_lr = Tok()
                    k.op("dve", lambda e: e.tensor_copy(out=lfrep[:], in_=lfnew[0:1, :].unsqueeze(1).to_broadcast([1, NP_, 16])),
                         reads=[t_new], writes=[t_lr])
                    Sx = k.sb("fs_S", [128, NP_ + 1, 16], F32, st)
                    t_S = Tok()
                    Lf = L[:].rearrange("p g h -> p (g h)")
                    Xf = dst[:].rearrange("p g h -> p (g h)")
                    Rf = lfrep[:].rearrange("p g h -> p (g h)")
                    Sf = Sx[:].rearrange("p g h -> p (g h)")
                    for blk in range(4):
                        cs = slice(blk * 512, (blk + 1) * 512)
                        k.op("pe", lambda e: e.matmul(out=PB[2][:], lhsT=tris[:], rhs=Lf[:, cs], start=True, stop=False), reads=[t_L, t_ix], writes=[t_PB[2]])
                        k.op("pe", lambda e: e.matmul(out=PB[2][:], lhsT=ones[:], rhs=Xf[:, cs], start=False, stop=False), reads=[t_dst, t_const], writes=[t_PB[2]])
                        k.op("pe", lambda e: e.matmul(out=PB[2][:], lhsT=ones[0:1, :], rhs=Rf[0:1, cs], start=False, stop=True), reads=[t_lr, t_const], writes=[t_PB[2]])
                        k.op("act", lambda e: e.copy(out=Sf[:, cs], in_=PB[2][:]), reads=[t_PB[2]], writes=[t_S])
                    kp_r = Ring(k, "fs_kp", 3, [128, 512], F32, stack=st, chan=True)
                    pr_r = Ring(k, "fs_pr", 2, [128, 4, 128], F32, stack=st)
                    s4_r = Ring(k, "fs_s4", 2, [128, 4], F32, stack=st)
                    for pg in range(NP_):
                        kp, t_kp, chk = kp_r.next()
                        k.idma(kp[:], f_kpool, idx[:, pg:pg + 1], chk, reads=[t_ix], writes=[t_kp])
                        for g in range(4):
                            pr, t_pr = pr_r.next()
                            k.op("dve", lambda e: e.tensor_tensor(out=pr[:], in0=kp[:, g * 128:(g + 1) * 128].unsqueeze(1).to_broadcast([128, 4, 128]),
                                                                  in1=qbc[:, g * 512:(g + 1) * 512].rearrange("p (r d) -> p r d", r=4), op=ALU.mult),
                                 reads=[t_kp, t_qb], writes=[t_pr])
                            s4, t_s4 = s4_r.next()
                            k.op("dve", lambda e: e.reduce_sum(out=s4[:], in_=pr[:], axis=AX.X), reads=[t_pr], writes=[t_s4])
                            k.op("dve", lambda e: e.scalar_tensor_tensor(out=Sx[:, pg, g * 4:(g + 1) * 4], in0=s4[:], scalar=SC,
                                                                         in1=Sx[:, pg, g * 4:(g + 1) * 4], op0=ALU.mult, op1=ALU.add),
                                 reads=[t_s4, t_S], writes=[t_S])
                    k.op("dve", lambda e: e.memset(Sx[:, NP_, :], -1e30), reads=[t_S], writes=[t_S])
                    for g in range(4):
                        pr, t_pr = pr_r.next()
                        k.op("dve", lambda e: e.tensor_tensor(out=pr[0:1], in0=knew[0:1, g * 128:(g + 1) * 128].unsqueeze(1).to_broadcast([1, 4, 128]),
                                                              in1=qbc[0:1, g * 512:(g + 1) * 512].rearrange("p (r d) -> p r d", r=4), op=ALU.mult),
                             reads=[t_new, t_qb], writes=[t_pr])
                        s4, t_s4 = s4_r.next()
                        k.op("dve", lambda e: e.reduce_sum(out=s4[0:1, :], in_=pr[0:1], axis=AX.X), reads=[t_pr], writes=[t_s4])
                        k.op("dve", lambda e: e.tensor_scalar(out=Sx[0:1, NP_, g * 4:(g + 1) * 4], in0=s4[0:1, :], scalar1=SC, scalar2=None, op0=ALU.mult),
                             reads=[t_s4, t_S], writes=[t_S])
                    sm = k.sb("fs_sm", [128, 64], F32, st)
                    sm16 = k.sb("fs_sm16", [16, 160], F32, st)
                    t_sm = Tok()
                    k.op("dve", lambda e: e.reduce_max(out=sm[:, 0:16], in_=Sx[:].rearrange("p g h -> p h g"), axis=AX.X), reads=[t_S], writes=[t_sm])
                    k.op("pe", lambda e: e.transpose(out=PB[3][0:16, 0:128], in_=sm[:, 0:16], identity=ident[:]), reads=[t_sm, t_const], writes=[t_PB[3]])
                    k.op("dve", lambda e: e.reduce_max(out=sm16[:, 0:1], in_=PB[3][0:16, 0:128], axis=AX.X), reads=[t_PB[3]], writes=[t_sm])
                    k.op("dve", lambda e: e.tensor_scalar(out=sm16[:, 16:32], in0=ident[0:16, 0:16], scalar1=sm16[:, 0:1], scalar2=None, op0=ALU.mult),
                         reads=[t_sm, t_const], writes=[t_sm])
                    k.op("pe", lambda e: e.matmul(out=PB[3][:, 128:144], lhsT=ones[0:16, :], rhs=sm16[:, 16:32], start=True, stop=True),
                         reads=[t_sm, t_const], writes=[t_PB[3]])
                    k.op("act", lambda e: e.copy(out=sm[:, 16:32], in_=PB[3][:, 128:144]), reads=[t_PB[3]], writes=[t_sm])
                    k.op("dve", lambda e: e.tensor_tensor(out=Sx[:], in0=Sx[:], in1=sm[:, 16:32].unsqueeze(1).to_broadcast([128, NP_ + 1, 16]), op=ALU.subtract),
                         reads=[t_sm, t_S], writes=[t_S])
                    k.op("act", lambda e: e.activation(out=Sx[:], in_=Sx[:], func=AF.Exp), reads=[t_S], writes=[t_S])
                    k.op("dve", lambda e: e.reduce_sum(out=sm[:, 32:48], in_=Sx[:].rearrange("p g h -> p h g"), axis=AX.X), reads=[t_S], writes=[t_sm])
                    k.op("pe", lambda e: e.matmul(out=PB[3][:, 256:272], lhsT=ones[:], rhs=sm[:, 32:48], start=True, stop=True),
                         reads=[t_sm, t_const], writes=[t_PB[3]])
                    k.op("dve", lambda e: e.reciprocal(out=sm[:, 48:64], in_=PB[3][:, 256:272]), reads=[t_PB[3]], writes=[t_sm])
                    k.op("dve", lambda e: e.tensor_tensor(out=Sx[:], in0=Sx[:], in1=sm[:, 48:64].unsqueeze(1).to_broadcast([128, NP_ + 1, 16]), op=ALU.mult),
                         reads=[t_sm, t_S], writes=[t_S])
                    for pg in range(NP_):
                        vp, t_vp, chv = kp_r.next()
                        k.idma(vp[:], f_vpool, idx[:, pg:pg + 1], chv, reads=[t_ix], writes=[t_vp])
                        k.op("pe", lambda e: e.matmul(out=PB[4][0:16, :], lhsT=Sx[:, pg, :], rhs=vp[:], start=(pg == 0), stop=False),
                             reads=[t_S, t_vp], writes=[t_PB[4]])
                    vn, t_vn, chv = kp_r.next()
                    k.op("dve", lambda e: e.memset(vn[:], 0.0), writes=[t_vn])
                    k.op("act", lambda e: e.copy(out=vn[0:1, :], in_=knew[0:1, 512:1024]), reads=[t_new, t_vn], writes=[t_vn])
                    k.op("pe", lambda e: e.matmul(out=PB[4][0:16, :], lhsT=Sx[:, NP_, :], rhs=vn[:], start=False, stop=True),
                         reads=[t_S, t_vn], writes=[t_PB[4]])
                    k.op("dve", lambda e: e.tensor_scalar(out=sm16[:, 32:160], in0=PB[4][0:16, 0:128], scalar1=m16[:, 0:1], scalar2=None, op0=ALU.mult),
                         reads=[t_PB[4], t_ix], writes=[t_sm])
                    for g in range(1, 4):
                        k.op("dve", lambda e: e.scalar_tensor_tensor(out=sm16[:, 32:160], in0=PB[4][0:16, g * 128:(g + 1) * 128], scalar=m16[:, g:g + 1],
                                                                     in1=sm16[:, 32:160], op0=ALU.mult, op1=ALU.add), reads=[t_PB[4], t_ix, t_sm], writes=[t_sm])
                    osb = k.sb("fs_osb", [16, 128], BF16, st)
                    t_osb = Tok()
                    k.op("act", lambda e: e.copy(out=osb[:], in_=sm16[:, 32:160]), reads=[t_sm], writes=[t_osb])
                    k.dma("sp", ynT_d[0:D, SEQ:SEQ + 1].rearrange("(h d) o -> h (d o)", h=16), osb[:], ch_ynTd[NT], reads=[t_osb], writes=[t_ynTd[NT]],
                          allow_slow_non_contiguous=True)
                for g in range(4):
                    with k.scope() as st:
                        qTg = k.sb("fx_qT", [128, 4, TTOK], BF16, st)
                        t_qg = [Tok() for _ in range(TT)]
                        featproj(f_win[:, g * 512:(g + 1) * 512], 512, qTg, t_qg)
                        p_r = Ring(k, "fx_p", 2, [128, SEQ], BF16, stack=st)
                        pT_r = Ring(k, "fx_pT", 2, [128, NT, 128], BF16, stack=st)
                        sm_r = Ring(k, "fx_sm", 2, [128, 16], F32, stack=st)
                        oT_r = Ring(k, "fx_oT", 2, [128, 4, 128], BF16, stack=st)
                        for qt in range(NT):
                            nkeys = (qt + 1) * 128
                            nb = (nkeys + 511) // 512
                            for r in range(4):
                                h = g * 4 + r
                                for bi in range(nb):
                                    c0 = bi * 512
                                    w_ = min(512, nkeys - c0)
                                    last = (bi == nb - 1)
                                    k.op("pe", lambda e: e.matmul(out=PB[bi][:, 0:w_], lhsT=qTg[:, r, tsl(qt)], rhs=kT[:, g, c0:c0 + w_],
                                                                  start=True, stop=False), reads=[t_qg[qt], t_kv], writes=[t_PB[bi]])
                                    k.op("pe", lambda e: e.matmul(out=PB[bi][:, 0:w_], lhsT=sel[:, h * 128:(h + 1) * 128], rhs=negcT[:, c0:c0 + w_],
                                                                  start=False, stop=(not last)), reads=[t_cst, t_nc], writes=[t_PB[bi]])
                                    if last:
                                        k.op("pe", lambda e: e.matmul(out=PB[bi][:, w_ - 128:w_], lhsT=ident[:], rhs=mneg[:], start=False, stop=True),
                                             reads=[t_cst, t_const], writes=[t_PB[bi]])
                                sm, t_sm = sm_r.next()
                                for bi in range(nb):
                                    w_ = min(512, nkeys - bi * 512)
                                    k.op("dve", lambda e: e.reduce_max(out=sm[:, bi:bi + 1], in_=PB[bi][:, 0:w_], axis=AX.X), reads=[t_PB[bi]], writes=[t_sm])
                                k.op("dve", lambda e: e.reduce_max(out=sm[:, 4:5], in_=sm[:, 0:nb], axis=AX.X), reads=[t_sm], writes=[t_sm])
                                k.op("dve", lambda e: e.tensor_scalar(out=sm[:, 5:6], in0=sm[:, 4:5], scalar1=-SC, scalar2=None, op0=ALU.mult),
                                     reads=[t_sm], writes=[t_sm])
                                p, t_p = p_r.next()
                                for bi in range(nb):
                                    c0 = bi * 512
                                    w_ = min(512, nkeys - c0)
                                    k.op("act", lambda e: e.activation(out=p[:, c0:c0 + w_], in_=PB[bi][:, 0:w_], func=AF.Exp, scale=SC,
                                                                       bias=sm[:, 5:6], accum_out=sm[:, 8 + bi:9 + bi]),
                                         reads=[t_PB[bi], t_sm], writes=[t_p, t_sm])
                                k.op("dve", lambda e: e.reduce_sum(out=sm[:, 6:7], in_=sm[:, 8:8 + nb], axis=AX.X), reads=[t_sm], writes=[t_sm])
                                k.op("dve", lambda e: e.reciprocal(out=sm[:, 7:8], in_=sm[:, 6:7]), reads=[t_sm], writes=[t_sm])
                                k.op("dve", lambda e: e.tensor_scalar(out=p[:, 0:nkeys], in0=p[:, 0:nkeys], scalar1=sm[:, 7:8], scalar2=None, op0=ALU.mult),
                                     reads=[t_sm, t_p], writes=[t_p])
                                pT, t_pT = pT_r.next()
                                transpose_bf(lambda jb: p[:, jb * 128:(jb + 1) * 128], qt + 1, lambda j0, n: pT[:, j0:j0 + n, :], t_p, t_pT, PRing([4, 5]))
                                for jb in range(qt + 1):
                                    k.op("pe", lambda e: e.matmul(out=PB[6][:, r * 128:(r + 1) * 128], lhsT=vtok[:, jb, g * 128:(g + 1) * 128],
                                                                  rhs=pT[:, jb, :], start=(jb == 0), stop=(jb == qt)),
                                         reads=[t_kv, t_pT], writes=[t_PB[6]])
                            oT, t_oT = oT_r.next()
                            k.op("act", lambda e: e.copy(out=oT[:], in_=PB[6][:].rearrange("p (c t) -> p c t", c=4)), reads=[t_PB[6]], writes=[t_oT])
                            k.dma("sp", ynT_d[g * 512:(g + 1) * 512, tsl(qt)].rearrange("(q p) t -> p q t", p=128), oT[:], ch_ynTd[qt],
                                  reads=[t_oT], writes=[t_ynTd[qt]])
            def get_aT(tt, st, state):
                if "ring" not in state:
                    state["ring"] = Ring(k, "f_aT", 2, [128, DC, 128], BF16, stack=st, chan=True)
                a, t_a, cha = state["ring"].next()
                k.dma("sp", a[:], ynT_d[0:D, tsl(tt)].rearrange("(c p) t -> p c t", p=128), cha, reads=[t_ynTd[tt]], writes=[t_a])
                return a, t_a
            proj_ln(get_aT, DC, f_wout, li, 0)


        def nsa_sample(li):
            SC = float(128 ** -0.5)
            NPG = 128
            w_in = n_win.rearrange("(c p) f -> p c f", p=128)
            with k.scope() as st:
                pti = k.sb("nz_pti", [128, NPG], I32, st)
                ptf = k.sb("nz_ptf", [128, NPG], F32, st)
                idx = k.sb("nz_idx", [128, NPG], I32, st)
                io = k.sb("nz_io", [128, 1], F32, st)
                m16 = k.sb("nz_m16", [16, 4], F32, st)
                gs = k.sb("nz_gs", [16, 16], F32, st)
                fs = k.sb("nz_fs", [16, 257], F32, st)
                bgr = k.sb("nz_bgr", [1, 48], F32, st)
                pj = k.sb("nz_pj", [128, 2, 4, 128], F32, st)
                t_ix = Tok()
                chx = k.dma_chan("ch_nzx")
                k.dma("sp", pti[:], pt_d.to_broadcast([128, NPG]), chx, writes=[t_ix])
                for dst_, src_ in ((io, iota_d), (m16, mask16_d), (gs, n_gs), (fs, n_fs), (bgr, n_bg[0:1, :])):
                    k.dma("sp", dst_[:], src_, chx, writes=[t_ix])
                k.dma("sp", pj[:], n_proj.rearrange("a g d e -> d a g e"), chx, writes=[t_ix])
                k.op("dve", lambda e: e.tensor_copy(out=ptf[:], in_=pti[:]), reads=[t_ix], writes=[t_ix])
                k.op("dve", lambda e: e.tensor_scalar(out=ptf[:], in0=ptf[:], scalar1=128.0, scalar2=io[:, 0:1], op0=ALU.mult, op1=ALU.add),
                     reads=[t_ix], writes=[t_ix])
                k.op("dve", lambda e: e.tensor_copy(out=idx[:], in_=ptf[:]), reads=[t_ix], writes=[t_ix])
                g3 = k.sb("nz_g3", [1, 3, 16], F32, st)
                gcol = k.sb("nz_gcol", [16, 4], F32, st)
                t_g = Tok()
                qpad = k.sb("nz_qpad", [128, 4, 16], F32, st)
                t_qp = Tok()
                newp = k.sb("nz_newp", [128, 6, 512], F32, st)
                t_np = Tok()
                seln = k.sb("nz_seln", [16, 258], F32, st)
                t_imp = Tok()
                with k.scope() as sp_:
                    prow = k.sb("nz_prow", [1, 5168], F32, sp_)
                    t_pr = Tok()
                    with k.scope() as sq:
                        wq_r = Ring(k, "nz_w", 2, [128, DC, 512], BF16, stack=sq, chan=True)
                        for blk in range(11):
                            c0 = blk * 512
                            w_ = min(512, 5168 - c0)
                            wq, t_wq, chq = wq_r.next()
                            k.dma("pool", wq[:, :, 0:w_], w_in[:, :, c0:c0 + w_], chq, writes=[t_wq])
                            for kc in range(DC):
                                k.op("pe", lambda e: e.matmul(out=PB[0][0:1, 0:w_], lhsT=hT[:, kc, SEQ:SEQ + 1], rhs=wq[:, kc, 0:w_],
                                                              start=(kc == 0), stop=(kc == DC - 1)), reads=[t_hT[NT], t_wq], writes=[t_PB[0]])
                            k.op("act", lambda e: e.copy(out=prow[0:1, c0:c0 + w_], in_=PB[0][0:1, 0:w_]), reads=[t_PB[0]], writes=[t_pr])
                    k.op("dve", lambda e: e.tensor_tensor(out=prow[0:1, 5120:5168], in0=prow[0:1, 5120:5168], in1=bgr[:], op=ALU.add),
                         reads=[t_pr, t_ix], writes=[t_pr])
                    k.op("act", lambda e: e.activation(out=prow[0:1, 5120:5168], in_=prow[0:1, 5120:5168], func=AF.Sigmoid), reads=[t_pr], writes=[t_pr])
                    k.op("dve", lambda e: e.tensor_copy(out=g3[:], in_=prow[0:1, 5120:5168].rearrange("o (h b) -> o b h", b=3)), reads=[t_pr], writes=[t_g])
                    for br in range(3):
                        k.op("pe", lambda e: e.matmul(out=PB[1][0:16, br:br + 1], lhsT=g3[0:1, br, :], rhs=ones[0:1, 0:1], start=True, stop=True),
                             reads=[t_g, t_const], writes=[t_PB[1]])
                    k.op("act", lambda e: e.copy(out=gcol[:, 0:3], in_=PB[1][0:16, 0:3]), reads=[t_PB[1]], writes=[t_g])
                    for h in range(16):
                        k.op("pe", lambda e: e.matmul(out=PB[2][:, h:h + 1], lhsT=prow[0:1, h * 128:(h + 1) * 128], rhs=ones[0:1, 0:1], start=True, stop=True),
                             reads=[t_pr, t_const], writes=[t_PB[2]])
                    k.op("dve", lambda e: e.memset(qpad[:], 0.0), writes=[t_qp])
                    for g in range(4):
                        k.op("act", lambda e: e.copy(out=qpad[:, g, g * 4:(g + 1) * 4], in_=PB[2][:, g * 4:(g + 1) * 4]), reads=[t_PB[2]], writes=[t_qp])
                    k.op("dve", lambda e: e.memset(newp[:], 0.0), writes=[t_np])
                    k.op("act", lambda e: e.copy(out=newp[0:1, :, :], in_=prow[0:1, 2048:5120].rearrange("o (b c) -> o b c", b=6)), reads=[t_pr, t_np], writes=[t_np])
                pg_r = Ring(k, "nz_pg", 4, [128, 512], F32, stack=st, chan=True)
                sm = k.sb("nz_sm", [16, 64], F32, st)
                t_sm = Tok()
                acc_o = k.sb("nz_acco", [16, 128], F32, st)
                t_ao = Tok()

                def get_page(branch, pg):
                    if pg == NPG:
                        return newp[:, branch, :], t_np
                    pt_, t_pt, chp = pg_r.next()
                    k.idma(pt_[:], n_pools[branch], idx[:, pg:pg + 1], chp, reads=[t_ix], writes=[t_pt])
                    return pt_[:], t_pt

                def softmax16(S2, n, t_S, gate_col=None):
                    k.op("dve", lambda e: e.reduce_max(out=sm[:, 0:1], in_=S2, axis=AX.X), reads=[t_S], writes=[t_sm])
                    k.op("dve", lambda e: e.tensor_scalar(out=sm[:, 1:2], in0=sm[:, 0:1], scalar1=-1.0, scalar2=None, op0=ALU.mult), reads=[t_sm], writes=[t_sm])
                    k.op("act", lambda e: e.activation(out=S2, in_=S2, func=AF.Exp, bias=sm[:, 1:2], accum_out=sm[:, 2:3]), reads=[t_S, t_sm], writes=[t_S, t_sm])
                    k.op("dve", lambda e: e.reciprocal(out=sm[:, 3:4], in_=sm[:, 2:3]), reads=[t_sm], writes=[t_sm])
                    if gate_col is not None:
                        k.op("dve", lambda e: e.tensor_tensor(out=sm[:, 3:4], in0=sm[:, 3:4], in1=gate_col, op=ALU.mult), reads=[t_sm, t_g], writes=[t_sm])
                    k.op("dve", lambda e: e.tensor_scalar(out=S2, in0=S2, scalar1=sm[:, 3:4], scalar2=None, op0=ALU.mult), reads=[t_S, t_sm], writes=[t_S])

                def select_add(pbank, first):
                    for g in range(4):
                        if first and g == 0:
                            k.op("dve", lambda e: e.tensor_scalar(out=acc_o[:], in0=pbank[0:16, 0:128], scalar1=m16[:, 0:1], scalar2=None, op0=ALU.mult),
                                 reads=[t_PB[PB.index(pbank)], t_ix], writes=[t_ao])
                        else:
                            k.op("dve", lambda e: e.scalar_tensor_tensor(out=acc_o[:], in0=pbank[0:16, g * 128:(g + 1) * 128], scalar=m16[:, g:g + 1],
                                                                         in1=acc_o[:], op0=ALU.mult, op1=ALU.add),
                                 reads=[t_PB[PB.index(pbank)], t_ix, t_ao], writes=[t_ao])

                with k.scope() as scm:
                    kcT = k.sb("nz_kcT", [128, 4, 1152], F32, scm)
                    vca = k.sb("nz_vc", [128, 9, 512], F32, scm)
                    t_kc = Tok()
                    with k.scope() as sc_:
                        wps = k.sb("nz_wps", [128, 4, 17, 128], F32, sc_)
                        pl_r = Ring(k, "nz_pl", 2, [128, 128], F32, stack=sc_)
                        for kv in range(2):
                            t_wps = Tok()
                            chw = k.dma_chan("ch_nzw")
                            for g in range(4):
                                k.dma("sp", wps[:, g], n_wps[kv, g], chw, writes=[t_wps])
                            for c in range(9):
                                pages = [(pg, pg - 16 * c) for pg in range(16 * c, min(16 * c + 16, NPG + 1))]
                                if 16 * c + 16 <= NPG:
                                    pages.append((16 * c + 16, 16))
                                for pi, (pg, slot) in enumerate(pages):
                                    xp, t_xp = get_page(kv, pg)
                                    for g in range(4):
                                        k.op("pe", lambda e: e.matmul(out=PB[4 + g][:, 0:128], lhsT=xp[:, g * 128:(g + 1) * 128], rhs=wps[:, g, slot, :],
                                                                      start=(pi == 0), stop=(pi == len(pages) - 1)),
                                             reads=[t_xp, t_wps], writes=[t_PB[4 + g]])
                                for g in range(4):
                                    pl, t_pl = pl_r.next()
                                    k.op("act", lambda e: e.copy(out=pl[:], in_=PB[4 + g][:, 0:128]), reads=[t_PB[4 + g]], writes=[t_pl])
                                    if kv == 0:
                                        k.op("pe", lambda e: e.matmul(out=PB[3][:, 0:128], lhsT=pj[:, 0, g, :], rhs=pl[:], start=True, stop=True),
                                             reads=[t_ix, t_pl], writes=[t_PB[3]])
                                        k.op("act", lambda e: e.copy(out=kcT[:, g, c * 128:(c + 1) * 128], in_=PB[3][:, 0:128]), reads=[t_PB[3]], writes=[t_kc])
                                    else:
                                        k.op("pe", lambda e: e.matmul(out=PB[3][:, 0:128], lhsT=pl[:], rhs=pj[:, 1, g, :], start=True, stop=True),
                                             reads=[t_ix, t_pl], writes=[t_PB[3]])
                                        k.op("act", lambda e: e.copy(out=vca[:, c, g * 128:(g + 1) * 128], in_=PB[3][:, 0:128]), reads=[t_PB[3]], writes=[t_kc])
                    Sc = k.sb("nz_Sc", [16, 1152], F32, scm)
                    pcT = k.sb("nz_pcT", [128, 9, 16], F32, scm)
                    imp = k.sb("nz_imp", [16, 4, 264], F32, scm)
                    t_Sc, t_pcT = Tok(), Tok()
                    for nb in range(3):
                        for g in range(4):
                            k.op("pe", lambda e: e.matmul(out=PB[0][0:16, 0:384], lhsT=qpad[:, g, :], rhs=kcT[:, g, nb * 384:(nb + 1) * 384],
                                                          start=(g == 0), stop=(g == 3)), reads=[t_qp, t_kc], writes=[t_PB[0]])
                        k.op("act", lambda e: e.activation(out=Sc[:, nb * 384:(nb + 1) * 384], in_=PB[0][0:16, 0:384], func=AF.Copy, scale=SC),
                             reads=[t_PB[0]], writes=[t_Sc])
                    k.op("dve", lambda e: e.memset(Sc[:, 1023:1152], -1e30), reads=[t_Sc], writes=[t_Sc])
                    softmax16(Sc[:], 1152, t_Sc)
                    for c in range(9):
                        k.op("pe", lambda e: e.transpose(out=PB[1][:, c * 16:(c + 1) * 16], in_=Sc[:, c * 128:(c + 1) * 128], identity=ident[0:16, 0:16]),
                             reads=[t_Sc, t_const], writes=[t_PB[1]])
                    k.op("act", lambda e: e.copy(out=pcT[:], in_=PB[1][:, 0:144].rearrange("p (c h) -> p c h", c=9)), reads=[t_PB[1]], writes=[t_pcT])
                    for c in range(9):
                        k.op("pe", lambda e: e.matmul(out=PB[6][0:16, :], lhsT=pcT[:, c, :], rhs=vca[:, c, :], start=(c == 0), stop=(c == 8)),
                             reads=[t_pcT, t_kc], writes=[t_PB[6]])
                    select_add(PB[6], True)
                    k.op("dve", lambda e: e.tensor_scalar(out=acc_o[:], in0=acc_o[:], scalar1=gcol[:, 0:1], scalar2=None, op0=ALU.mult), reads=[t_g, t_ao], writes=[t_ao])
                    with k.scope() as so:
                        ov_r = Ring(k, "nz_ov", 2, [128, 257], F32, stack=so, chan=True)
                        for c in range(9):
                            ovt, t_ov, cho = ov_r.next()
                            k.dma("sp", ovt[:], n_ovs[c], cho, writes=[t_ov])
                            k.op("pe", lambda e: e.matmul(out=PB[7][0:16, 0:257], lhsT=pcT[:, c, :], rhs=ovt[:], start=(c == 0), stop=(c == 8)),
                                 reads=[t_pcT, t_ov], writes=[t_PB[7]])
                    k.op("act", lambda e: e.copy(out=imp[:, 0, 0:257], in_=PB[7][0:16, 0:257]), reads=[t_PB[7]], writes=[t_imp])
                    k.op("pe", lambda e: e.matmul(out=PB[7][0:16, 0:257], lhsT=gs[:], rhs=imp[:, 0, 0:257], start=True, stop=True), reads=[t_imp, t_ix], writes=[t_PB[7]])
                    k.op("dve", lambda e: e.tensor_tensor(out=imp[:, 1, 0:257], in0=PB[7][0:16, 0:257], in1=fs[:], op=ALU.add), reads=[t_PB[7], t_ix], writes=[t_imp])
                    k.op("dve", lambda e: e.max(out=imp[:, 3, 0:8], in_=imp[:, 1, 0:257]), reads=[t_imp], writes=[t_imp])
                    k.op("dve", lambda e: e.match_replace(out=imp[:, 2, 0:257], in_to_replace=imp[:, 3, 0:8], in_values=imp[:, 1, 0:257], imm_value=-3e38),
                         reads=[t_imp], writes=[t_imp])
                    k.op("dve", lambda e: e.max(out=imp[:, 3, 8:16], in_=imp[:, 2, 0:257]), reads=[t_imp], writes=[t_imp])
                    k.op("dve", lambda e: e.memset(seln[:], -1e30), writes=[t_imp])
                    k.op("dve", lambda e: e.tensor_scalar(out=seln[:, 0:257], in0=imp[:, 1, 0:257], scalar1=imp[:, 3, 15:16], scalar2=1e30, op0=ALU.is_ge, op1=ALU.mult),
                         reads=[t_imp], writes=[t_imp])
                    k.op("dve", lambda e: e.tensor_scalar(out=seln[:, 0:257], in0=seln[:, 0:257], scalar1=-1e30, scalar2=None, op0=ALU.add), reads=[t_imp], writes=[t_imp])

                def attend(npages, kget, vget, mask_fn, gate_col, pbank):
                    with k.scope() as sa:
                        S = k.sb("nz_S", [16, npages, 128], F32, sa)
                        t_S = Tok()
                        kT_r = Ring(k, "nz_kTp", 2, [128, 512], F32, stack=sa)
                        pT_r = Ring(k, "nz_pTp", 2, [128, 16], F32, stack=sa)
                        for pg in range(npages):
                            kp, t_kp = kget(pg)
                            tb = PB[pg % 2]
                            for g in range(4):
                                k.op("pe", lambda e: e.transpose(out=tb[:, g * 128:(g + 1) * 128], in_=kp[:, g * 128:(g + 1) * 128], identity=ident[:]),
                                     reads=[t_kp, t_const], writes=[t_PB[pg % 2]])
                            kTp, t_kTp = kT_r.next()
                            k.op("act", lambda e: e.copy(out=kTp[:], in_=tb[:]), reads=[t_PB[pg % 2]], writes=[t_kTp])
                            sbk = PB[2 + pg % 2]
                            for g in range(4):
                                k.op("pe", lambda e: e.matmul(out=sbk[0:16, 0:128], lhsT=qpad[:, g, :], rhs=kTp[:, g * 128:(g + 1) * 128],
                                                              start=(g == 0), stop=(g == 3)), reads=[t_qp, t_kTp], writes=[t_PB[2 + pg % 2]])
                            k.op("dve", lambda e: e.tensor_scalar(out=S[:, pg, :], in0=sbk[0:16, 0:128], scalar1=SC, scalar2=None, op0=ALU.mult),
                                 reads=[t_PB[2 + pg % 2]], writes=[t_S])
                        mask_fn(S, t_S)
                        softmax16(S[:].rearrange("p g r -> p (g r)"), npages * 128, t_S, gate_col)
                        for pg in range(npages):
                            vp, t_vp = vget(pg)
                            k.op("pe", lambda e: e.transpose(out=PB[pg % 2][:, 0:16], in_=S[:, pg, :], identity=ident[0:16, 0:16]),
                                 reads=[t_S, t_const], writes=[t_PB[pg % 2]])
                            pT, t_pT = pT_r.next()
                            k.op("act", lambda e: e.copy(out=pT[:], in_=PB[pg % 2][:, 0:16]), reads=[t_PB[pg % 2]], writes=[t_pT])
                            k.op("pe", lambda e: e.matmul(out=pbank[0:16, :], lhsT=pT[:], rhs=vp, start=(pg == 0), stop=(pg == npages - 1)),
                                 reads=[t_pT, t_vp], writes=[t_PB[PB.index(pbank)]])
                        select_add(pbank, False)

                def mask_slc(S, t_S):
                    Sv = S[:].rearrange("p g r -> p (g r)")
                    k.op("dve", lambda e: e.tensor_tensor(out=Sv.rearrange("p (m s) -> p m s", s=64), in0=Sv.rearrange("p (m s) -> p m s", s=64),
                                                          in1=seln[:, 0:258].unsqueeze(2).to_broadcast([16, 258, 64]), op=ALU.add),
                         reads=[t_imp, t_S], writes=[t_S])
                    k.op("dve", lambda e: e.memset(S[:, NPG, 1:128], -1e30), reads=[t_S], writes=[t_S])

                attend(NPG + 1, lambda pg: get_page(2, pg), lambda pg: get_page(3, pg), mask_slc, gcol[:, 1:2], PB[4])

                wb_r = Ring(k, "nz_wb", 4, [128, 512], F32, stack=st, chan=True)

                def wget(half, new_branch):
                    def f(pg):
                        if pg == 4:
                            return newp[:, new_branch, :], t_np
                        t_, t_t, ch_ = wb_r.next()
                        k.dma("sp", t_[:], n_winbuf[pg * 128:(pg + 1) * 128, half * 512:(half + 1) * 512], ch_, writes=[t_t])
                        return t_[:], t_t
                    return f

                def mask_win(S, t_S):
                    k.op("dve", lambda e: e.memset(S[:, 0, 0:1], -1e30), reads=[t_S], writes=[t_S])
                    k.op("dve", lambda e: e.memset(S[:, 4, 1:128], -1e30), reads=[t_S], writes=[t_S])

                attend(5, wget(0, 4), wget(1, 5), mask_win, gcol[:, 2:3], PB[5])
                osb = k.sb("nz_osb", [16, 128], BF16, st)
                t_osb = Tok()
                k.op("act", lambda e: e.copy(out=osb[:], in_=acc_o[:]), reads=[t_ao], writes=[t_osb])
                k.dma("sp", ynT_d[0:D, SEQ:SEQ + 1].rearrange("(h d) o -> h (d o)", h=16), osb[:], ch_ynTd[NT], reads=[t_osb], writes=[t_ynTd[NT]],
                      allow_slow_non_contiguous=True)

        def nsa_layer(li, attn=True):
            SC = float(128 ** -0.5)
            w_in = n_win.rearrange("(c p) f -> p c f", p=128)
            if attn and os.environ.get("NSA_SAMPLE", "1") == "1":
                nsa_sample(li)
            with k.scope() as sl:
                kTs = k.sb("ns_kTs", [128, 4, SEQ], BF16, sl)
                vs = k.sb("ns_vs", [128, NT, 512], BF16, sl)
                kTw = k.sb("ns_kTw", [128, 4, SEQ], BF16, sl)
                vw = k.sb("ns_vw", [128, NT, 512], BF16, sl)
                kcT = k.sb("ns_kcT", [128, 4, 128], BF16, sl)
                vc = k.sb("ns_vc", [128, 4, 128], BF16, sl)
                gates = k.sb("ns_gates", [128, TT, 48], F32, sl)
                t_kv, t_cmp, t_gt = Tok(), Tok(), Tok()
                with k.scope() as st:
                    xc1 = k.sb("ns_xc", [128, NT, 512], BF16, st)
                    xc_tok = [xc1, xc1]
                    t_xc = Tok()
                    wp_r = Ring(k, "ns_wp", 2, [128, NT, 128], BF16, stack=st, chan=True)
                    pj_r = Ring(k, "ns_pj", 2, [128, 128], BF16, stack=st, chan=True)
                    pl_r = Ring(k, "ns_pl", 2, [128, 128], BF16, stack=st)

                    def pool_cmp(kv):
                        for g in range(4):
                            wp, t_wp, chp = wp_r.next()
                            k.dma("pool", wp[:], n_wp[kv, g].rearrange("(c p) n -> p c n", p=128), chp, writes=[t_wp])
                            pj, t_pj, chj = pj_r.next()
                            k.dma("pool", pj[:], n_proj[kv, g], chj, writes=[t_pj])
                            for c in range(NT):
                                k.op("pe", lambda e: e.matmul(out=PB[6][:, 0:128], lhsT=xc_tok[kv][:, c, g * 128:(g + 1) * 128], rhs=wp[:, c, :],
                                                              start=(c == 0), stop=(c == NT - 1)), reads=[t_xc, t_wp], writes=[t_PB[6]])
                            pl, t_pl = pl_r.next()
                            k.op("act", lambda e: e.copy(out=pl[:], in_=PB[6][:, 0:128]), reads=[t_PB[6]], writes=[t_pl])
                            if kv == 0:
                                k.op("pe", lambda e: e.matmul(out=PB[7][:, 0:128], lhsT=pj[:], rhs=pl[:], start=True, stop=True),
                                     reads=[t_pj, t_pl], writes=[t_PB[7]])
                                k.op("act", lambda e: e.copy(out=kcT[:, g, :], in_=PB[7][:, 0:128]), reads=[t_PB[7]], writes=[t_cmp])
                            else:
                                k.op("pe", lambda e: e.matmul(out=PB[7][:, 0:128], lhsT=pl[:], rhs=pj[:], start=True, stop=True),
                                     reads=[t_pj, t_pl], writes=[t_PB[7]])
                                k.op("act", lambda e: e.copy(out=vc[:, g, :], in_=PB[7][:, 0:128]), reads=[t_PB[7]], writes=[t_cmp])

                    wr = Ring(k, "ns_w", 1, [128, DC, 512], BF16, stack=st, chan=True)
                    ko_r = Ring(k, "ns_ko", 3, [128, 512], F32, stack=st, chan=True)
                    kb_r = Ring(k, "ns_kb", 2, [128, 512], BF16, stack=st)
                    pm = PRing([0, 1, 2, 3])
                    for fb in range(6):
                        wb, t_wb, chw = wr.next()
                        k.dma("pool", wb[:], w_in[:, :, 2048 + fb * 512:2048 + (fb + 1) * 512], chw, writes=[t_wb])
                        for tt in range(TT):
                            ps, t_ps = pm.next()
                            for kc in range(DC):
                                k.op("pe", lambda e: e.matmul(out=ps[:], lhsT=hT[:, kc, tsl(tt)], rhs=wb[:, kc, :],
                                                              start=(kc == 0), stop=(kc == DC - 1)), reads=[t_hT[tt], t_wb], writes=[t_ps])
                            ko, t_ko, cho = ko_r.next()
                            k.op("act", lambda e: e.copy(out=ko[:], in_=ps[:]), reads=[t_ps], writes=[t_ko])
                            if fb < 4:
                                k.dma("sp", o_nsa_kv[tsl(tt), fb * 512:(fb + 1) * 512], ko[:], cho, reads=[t_ko])
                            else:
                                cs = slice((fb - 4) * 512, (fb - 3) * 512)
                                if 12 <= tt < NT:
                                    k.dma("sp", o_nsa_win_p[(tt - 12) * 128:(tt - 11) * 128, cs], ko[:], cho, reads=[t_ko])
                                elif tt == NT:
                                    k.dma("sp", o_nsa_win_s[511:512, cs], ko[0:1, :], cho, reads=[t_ko])
                            if not attn or tt >= NT:
                                continue
                            if fb < 2:
                                k.op("dve", lambda e: e.tensor_copy(out=xc_tok[fb][:, tt, :], in_=ko[:]), reads=[t_ko], writes=[t_xc])
                            elif fb in (3, 5):
                                dstv = vs if fb == 3 else vw
                                k.op("dve", lambda e: e.tensor_copy(out=dstv[:, tt, :], in_=ko[:]), reads=[t_ko], writes=[t_kv])
                            else:
                                dstk = kTs if fb == 2 else kTw
                                kb, t_kb = kb_r.next()
                                k.op("dve", lambda e: e.tensor_copy(out=kb[:], in_=ko[:]), reads=[t_ko], writes=[t_kb])
                                transpose_bf(lambda c: kb[:, c * 128:(c + 1) * 128], 4, lambda c0, n: dstk[:, c0:c0 + n, tsl(tt)],
                                             t_kb, t_kv, PRing([4, 5]))
                        if attn and fb < 2:
                            pool_cmp(fb)
                    k.dma("sp", o_nsa_win_s[0:511, :], n_winbuf[1:512, :], k.dma_chan("ch_nwin"))
                    if attn:
                        wg = k.sb("ns_wg", [128, DC, 48], BF16, st)
                        bg = k.sb("ns_bg", [128, 48], F32, st)
                        t_wg = Tok()
                        chg = k.dma_chan("ch_nsg")
                        k.dma("pool", wg[:], w_in[:, :, 5120:5168], chg, writes=[t_wg])
                        k.dma("sp", bg[:], n_bg, chg, writes=[t_wg])
                        for tt in range(NT):
                            for kc in range(DC):
                                k.op("pe", lambda e: e.matmul(out=PB[0][:, 0:48], lhsT=hT[:, kc, tsl(tt)], rhs=wg[:, kc, :],
                                                              start=(kc == 0), stop=(kc == DC - 1)), reads=[t_hT[tt], t_wg], writes=[t_PB[0]])
                            k.op("dve", lambda e: e.tensor_tensor(out=gates[:, tt, :], in0=PB[0][:, 0:48], in1=bg[:], op=ALU.add),
                                 reads=[t_PB[0], t_wg], writes=[t_gt])
                            k.op("act", lambda e: e.activation(out=gates[:, tt, :], in_=gates[:, tt, :], func=AF.Sigmoid), reads=[t_gt], writes=[t_gt])
                if not attn:
                    return
                cst = k.sb("ns_cst", [128, 4, 128], F32, sl)
                ov = k.sb("ns_ov", [128, 32], F32, sl)
                eaug = k.sb("ns_eaug", [33, SEQ], BF16, sl)
                t_cst = Tok()
                chc = k.dma_chan("ch_nsc")
                k.dma("sp", cst[:, 0, :], mneg_d2, chc, writes=[t_cst])
                k.dma("sp", cst[:, 1, :], n_mfar, chc, writes=[t_cst])
                k.dma("sp", ov[:], n_ov, chc, writes=[t_cst])
                k.dma("pool", eaug[:], n_eaug, chc, writes=[t_cst])
                for g in range(4):
                    with k.scope() as st:
                        qTg = k.sb("ns_qT", [128, 4, TTOK], BF16, st)
                        t_qg = [Tok() for _ in range(TT)]
                        featproj(n_win[:, g * 512:(g + 1) * 512], 512, qTg, t_qg)
                        cm_r = Ring(k, "ns_cm", 2, [128, 2, 128], F32, stack=st, chan=True)
                        fv_r = Ring(k, "ns_fv", 2, [128, 3, 32], F32, stack=st, chan=True)
                        pc_r = Ring(k, "ns_pc", 2, [128, 4, 128], F32, stack=st)
                        pcg_r = Ring(k, "ns_pcg", 2, [128, 4, 128], BF16, stack=st)
                        pcT_r = Ring(k, "ns_pcT", 2, [128, 4, 128], F32, stack=st)
                        pgT_r = Ring(k, "ns_pgT", 2, [128, 4, 128], BF16, stack=st)
                        im_r = Ring(k, "ns_im", 2, [128, 4, 32], F32, stack=st)
                        selT_r = Ring(k, "ns_selT", 2, [33, 128], BF16, stack=st)
                        p_r = Ring(k, "ns_p", 2, [128, SEQ], BF16, stack=st)
                        pT_r = Ring(k, "ns_pT", 2, [128, NT, 128], BF16, stack=st)
                        sm_r = Ring(k, "ns_sm", 2, [128, 24], F32, stack=st)
                        oT_r = Ring(k, "ns_oT", 2, [128, 4, 128], BF16, stack=st)
                        for sb_, t_sb in zip(selT_r.bufs, selT_r.toks):
                            k.op("dve", lambda e: e.memset(sb_[:], 1.0), writes=[t_sb])
                        for qt in range(NT):
                            gcol = lambda r, br: (g * 4 + r) * 3 + br
                            cm, t_cm, chm = cm_r.next()
                            k.dma("sp", cm[:, 0, :], n_cm01[tsl(qt), :], chm, writes=[t_cm])
                            fv, t_fv, chf = fv_r.next()
                            k.dma("sp", fv[:], n_fv[tsl(qt)], chf, writes=[t_fv])
                            k.op("dve", lambda e: e.tensor_scalar(out=cm[:, 1, :], in0=cm[:, 0, :], scalar1=32768.0, scalar2=-32768.0,
                                                                  op0=ALU.mult, op1=ALU.add), reads=[t_cm], writes=[t_cm])
                            for r in range(4):
                                k.op("pe", lambda e: e.matmul(out=PB[0][:, r * 128:(r + 1) * 128], lhsT=qTg[:, r, tsl(qt)], rhs=kcT[:, g, :],
                                                              start=True, stop=False), reads=[t_qg[qt], t_cmp], writes=[t_PB[0]])
                                k.op("pe", lambda e: e.matmul(out=PB[0][:, r * 128:(r + 1) * 128], lhsT=ident[:], rhs=cm[:, 1, :],
                                                              start=False, stop=True), reads=[t_cm, t_const], writes=[t_PB[0]])
                            sm, t_sm = sm_r.next()
                            k.op("dve", lambda e: e.reduce_max(out=sm[:, 0:4], in_=PB[0][:].rearrange("p (r n) -> p r n", r=4), axis=AX.X),
                                 reads=[t_PB[0]], writes=[t_sm])
                            k.op("dve", lambda e: e.tensor_scalar(out=sm[:, 4:8], in0=sm[:, 0:4], scalar1=-SC, scalar2=None, op0=ALU.mult),
                                 reads=[t_sm], writes=[t_sm])
                            pc, t_pc = pc_r.next()
                            for r in range(4):
                                k.op("act", lambda e: e.activation(out=pc[:, r, :], in_=PB[0][:, r * 128:(r + 1) * 128], func=AF.Exp, scale=SC,
                                                                   bias=sm[:, 4 + r:5 + r], accum_out=sm[:, 8 + r:9 + r]),
                                     reads=[t_PB[0], t_sm], writes=[t_pc, t_sm])
                            k.op("dve", lambda e: e.reciprocal(out=sm[:, 12:16], in_=sm[:, 8:12]), reads=[t_sm], writes=[t_sm])
                            k.op("dve", lambda e: e.tensor_tensor(out=pc[:], in0=pc[:], in1=sm[:, 12:16].unsqueeze(2).to_broadcast([128, 4, 128]),
                                                                  op=ALU.mult), reads=[t_sm, t_pc], writes=[t_pc])
                            k.op("dve", lambda e: e.tensor_tensor(out=pc[:], in0=pc[:], in1=cm[:, 0, :].unsqueeze(1).to_broadcast([128, 4, 128]),
                                                                  op=ALU.mult), reads=[t_cm, t_pc], writes=[t_pc])
                            for r in range(4):
                                k.op("pe", lambda e: e.transpose(out=PB[1][:, r * 128:(r + 1) * 128], in_=pc[:, r, :], identity=ident[:]),
                                     reads=[t_pc, t_const], writes=[t_PB[1]])
                            pcT, t_pcT = pcT_r.next()
                            k.op("act", lambda e: e.copy(out=pcT[:], in_=PB[1][:].rearrange("p (r t) -> p r t", r=4)), reads=[t_PB[1]], writes=[t_pcT])
                            for r in range(4):
                                k.op("pe", lambda e: e.matmul(out=PB[2][:, 0:32], lhsT=pcT[:, r, :], rhs=ov[:], start=(r == 0), stop=(r == 3)),
                                     reads=[t_pcT, t_cst], writes=[t_PB[2]])
                            im, t_im = im_r.next()
                            imp, v16, tmpm, selm = im[:, 0, :], im[:, 1, :], im[:, 2, :], im[:, 3, :]
                            k.op("dve", lambda e: e.tensor_tensor(out=imp, in0=PB[2][:, 0:32], in1=fv[:, 0, :], op=ALU.add), reads=[t_PB[2], t_fv], writes=[t_im])
                            k.op("dve", lambda e: e.tensor_tensor(out=imp, in0=imp, in1=fv[:, 1, :], op=ALU.mult), reads=[t_fv, t_im], writes=[t_im])
                            k.op("dve", lambda e: e.tensor_tensor(out=imp, in0=imp, in1=fv[:, 2, :], op=ALU.add), reads=[t_fv, t_im], writes=[t_im])
                            k.op("dve", lambda e: e.max(out=v16[:, 0:8], in_=imp), reads=[t_im], writes=[t_im])
                            k.op("dve", lambda e: e.match_replace(out=tmpm, in_to_replace=v16[:, 0:8], in_values=imp, imm_value=-3e38),
                                 reads=[t_im], writes=[t_im])
                            k.op("dve", lambda e: e.max(out=v16[:, 8:16], in_=tmpm), reads=[t_im], writes=[t_im])
                            k.op("dve", lambda e: e.tensor_scalar(out=selm, in0=imp, scalar1=v16[:, 15:16], scalar2=None, op0=ALU.is_ge),
                                 reads=[t_im], writes=[t_im])
                            k.op("dve", lambda e: e.scalar_tensor_tensor(out=selm, in0=imp, scalar=-5e29, in1=selm, op0=ALU.is_gt, op1=ALU.mult),
                                 reads=[t_im], writes=[t_im])
                            k.op("pe", lambda e: e.transpose(out=PB[3][0:32, 0:128], in_=selm, identity=ident[:]), reads=[t_im, t_const], writes=[t_PB[3]])
                            selT, t_selT = selT_r.next()
                            k.op("act", lambda e: e.copy(out=selT[0:32, :], in_=PB[3][0:32, 0:128]), reads=[t_PB[3]], writes=[t_selT])
                            pcg, t_pcg = pcg_r.next()
                            for r in range(4):
                                k.op("dve", lambda e: e.tensor_scalar(out=pcg[:, r, :], in0=pc[:, r, :], scalar1=gates[:, qt, gcol(r, 0):gcol(r, 0) + 1],
                                                                      scalar2=None, op0=ALU.mult), reads=[t_pc, t_gt], writes=[t_pcg])
                            pgT, t_pgT = pgT_r.next()
                            transpose_bf(lambda r: pcg[:, r, :], 4, lambda r0, n: pgT[:, r0:r0 + n, :], t_pcg, t_pgT, PRing([1]))
                            for r in range(4):
                                h = g * 4 + r
                                oslc = PB[7][:, r * 128:(r + 1) * 128]
                                k.op("pe", lambda e: e.matmul(out=oslc, lhsT=vc[:, g, :], rhs=pgT[:, r, :], start=True, stop=False),
                                     reads=[t_cmp, t_pgT], writes=[t_PB[7]])
                                for br in (1, 2):
                                    kT_, v_ = (kTs, vs) if br == 1 else (kTw, vw)
                                    j0 = 0 if br == 1 else max(0, qt - 4)
                                    k0 = j0 * 128
                                    nkeys = (qt + 1) * 128 - k0
                                    nb = (nkeys + 511) // 512
                                    for bi in range(nb):
                                        c0 = bi * 512
                                        w_ = min(512, nkeys - c0)
                                        last = (bi == nb - 1)
                                        k.op("pe", lambda e: e.matmul(out=PB[2 + bi][:, 0:w_], lhsT=qTg[:, r, tsl(qt)], rhs=kT_[:, g, k0 + c0:k0 + c0 + w_],
                                                                      start=True, stop=False), reads=[t_qg[qt], t_kv], writes=[t_PB[2 + bi]])
                                        if br == 1:
                                            k.op("pe", lambda e: e.matmul(out=PB[2 + bi][:, 0:w_], lhsT=selT[:, :], rhs=eaug[:, c0:c0 + w_],
                                                                          start=False, stop=False), reads=[t_selT, t_cst], writes=[t_PB[2 + bi]])
                                        elif bi == 0 and qt >= 4:
                                            k.op("pe", lambda e: e.matmul(out=PB[2 + bi][:, 0:128], lhsT=ident[:], rhs=cst[:, 1, :],
                                                                          start=False, stop=False), reads=[t_cst, t_const], writes=[t_PB[2 + bi]])
                                        if last:
                                            k.op("pe", lambda e: e.matmul(out=PB[2 + bi][:, w_ - 128:w_], lhsT=ident[:], rhs=cst[:, 0, :], start=False, stop=True),
                                                 reads=[t_cst, t_const], writes=[t_PB[2 + bi]])
                                    sm2, t_sm2 = sm_r.next()
                                    for bi in range(nb):
                                        w_ = min(512, nkeys - bi * 512)
                                        k.op("dve", lambda e: e.reduce_max(out=sm2[:, bi:bi + 1], in_=PB[2 + bi][:, 0:w_], axis=AX.X),
                                             reads=[t_PB[2 + bi]], writes=[t_sm2])
                                    k.op("dve", lambda e: e.reduce_max(out=sm2[:, 4:5], in_=sm2[:, 0:nb], axis=AX.X), reads=[t_sm2], writes=[t_sm2])
                                    k.op("dve", lambda e: e.tensor_scalar(out=sm2[:, 5:6], in0=sm2[:, 4:5], scalar1=-SC, scalar2=None, op0=ALU.mult),
                                         reads=[t_sm2], writes=[t_sm2])
                                    p, t_p = p_r.next()
                                    for bi in range(nb):
                                        c0 = bi * 512
                                        w_ = min(512, nkeys - c0)
                                        k.op("act", lambda e: e.activation(out=p[:, c0:c0 + w_], in_=PB[2 + bi][:, 0:w_], func=AF.Exp, scale=SC,
                                                                           bias=sm2[:, 5:6], accum_out=sm2[:, 8 + bi:9 + bi]),
                                             reads=[t_PB[2 + bi], t_sm2], writes=[t_p, t_sm2])
                                    k.op("dve", lambda e: e.reduce_sum(out=sm2[:, 6:7], in_=sm2[:, 8:8 + nb], axis=AX.X), reads=[t_sm2], writes=[t_sm2])
                                    k.op("dve", lambda e: e.reciprocal(out=sm2[:, 7:8], in_=sm2[:, 6:7]), reads=[t_sm2], writes=[t_sm2])
                                    k.op("dve", lambda e: e.tensor_scalar(out=p[:, 0:nkeys], in0=p[:, 0:nkeys], scalar1=sm2[:, 7:8],
                                                                          scalar2=gates[:, qt, gcol(r, br):gcol(r, br) + 1], op0=ALU.mult, op1=ALU.mult),
                                         reads=[t_sm2, t_p, t_gt], writes=[t_p])
                                    nt_ = qt + 1 - j0
                                    pT, t_pT = pT_r.next()
                                    transpose_bf(lambda jb: p[:, jb * 128:(jb + 1) * 128], nt_, lambda jj0, n: pT[:, jj0:jj0 + n, :], t_p, t_pT, PRing([5, 6]))
                                    for jb in range(nt_):
                                        k.op("pe", lambda e: e.matmul(out=oslc, lhsT=v_[:, j0 + jb, g * 128:(g + 1) * 128], rhs=pT[:, jb, :],
                                                                      start=False, stop=(br == 2 and jb == nt_ - 1)),
                                             reads=[t_kv, t_pT], writes=[t_PB[7]])
                            oT, t_oT = oT_r.next()
                            k.op("act", lambda e: e.copy(out=oT[:], in_=PB[7][:].rearrange("p (c t) -> p c t", c=4)), reads=[t_PB[7]], writes=[t_oT])
                            k.dma("sp", ynT_d[g * 512:(g + 1) * 512, tsl(qt)].rearrange("(q p) t -> p q t", p=128), oT[:], ch_ynTd[qt],
                                  reads=[t_oT], writes=[t_ynTd[qt]])
            def get_aT(tt, st, state):
                if "ring" not in state:
                    state["ring"] = Ring(k, "n_aT", 2, [128, DC, 128], BF16, stack=st, chan=True)
                a, t_a, cha = state["ring"].next()
                k.dma("sp", a[:], ynT_d[0:D, tsl(tt)].rearrange("(c p) t -> p c t", p=128), cha, reads=[t_ynTd[tt]], writes=[t_a])
                return a, t_a
            if dbg:
                for tt in range(NT):
                    k.dma("sp", o_dbg2[:, tsl(tt)], ynT_d[0:D, tsl(tt)], ch_dbg, reads=[t_ynTd[tt]])
            proj_ln(get_aT, DC, n_wout, li, 0)

        def mamba_layer(j, li):
            w_in = m_win[j].rearrange("(c p) f -> p c f", p=128)
            with k.scope() as st:
                cw = k.sb("m_cw", [128, 48, 4], F32, st)
                cb = k.sb("m_cb", [128, 48], F32, st)
                dtb = k.sb("m_dtb_s", [128, 64], F32, st)
                abc = k.sb("m_abc", [128, 64], F32, st)
                dsk = k.sb("m_dsk_s", [128, 64], F32, st)
                ng = k.sb("m_ng_s", [128, 32], F32, st)
                t_par = Tok()
                ch_misc = k.dma_chan("ch_par")
                for dst, srcap in ((cw, m_convw[j]), (cb, m_convb[j]), (dtb, m_dtb[j]), (abc, m_alog[j]),
                                   (dsk, m_dsk[j]), (ng, m_ng[j])):
                    k.dma("sp", dst[:], srcap, ch_misc, writes=[t_par])
                k.op("act", lambda e: e.activation(out=abc[:], in_=abc[:], func=AF.Exp), reads=[t_par], writes=[t_par])
                k.op("dve", lambda e: e.tensor_scalar(out=abc[:], in0=abc[:], scalar1=-1.0, scalar2=None, op0=ALU.mult),
                     reads=[t_par], writes=[t_par])
                dt_all = k.sb("m_dt", [128, TT, 64], F32, st)
                la_all = k.sb("m_la", [128, TT, 64], F32, st)
                ac_all = k.sb("m_ac", [128, TT, 64], F32, st)
                ea_all = k.sb("m_ea", [128, TT, 64], F32, st)
                t_dt = [Tok() for _ in range(TT)]
                with k.scope() as st2:
                    wdt = k.sb("m_wdt", [128, DC, 64], BF16, st2)
                    t_wdt = Tok()
                    ch_misc = k.dma_chan("ch_wdt")
                    k.dma("pool", wdt[:], w_in[:, :, 10240:10304], ch_misc, writes=[t_wdt])
                    tmp_r = Ring(k, "m_dtt", 2, [128, 3, 64], F32, stack=st2)
                    pm = PRing([0, 1])
                    pa = PRing([2, 3])
                    for c in range(TT):
                        ps, t_ps = pm.next()
                        for kc in range(DC):
                            k.op("pe", lambda e: e.matmul(out=ps[:, 0:64], lhsT=hT[:, kc, tsl(c)], rhs=wdt[:, kc, :],
                                                          start=(kc == 0), stop=(kc == DC - 1)),
                                 reads=[t_hT[c], t_wdt], writes=[t_ps])
                        tm, t_tm = tmp_r.next()
                        x0, ax, ee = tm[:, 0, :], tm[:, 1, :], tm[:, 2, :]
                        k.op("dve", lambda e: e.tensor_tensor(out=x0, in0=ps[:, 0:64], in1=dtb[:], op=ALU.add),
                             reads=[t_ps, t_par], writes=[t_tm])
                        k.op("dve", lambda e: e.scalar_tensor_tensor(out=ax, in0=x0, scalar=-1.0, in1=x0, op0=ALU.mult,
                                                                     op1=ALU.max), reads=[t_tm], writes=[t_tm])
                        k.op("act", lambda e: e.activation(out=ee, in_=ax, func=AF.Exp, scale=-1.0), reads=[t_tm], writes=[t_tm])
                        k.op("act", lambda e: e.activation(out=ee, in_=ee, func=AF.Ln, bias=1.0), reads=[t_tm], writes=[t_tm])
                        k.op("dve", lambda e: e.scalar_tensor_tensor(out=dt_all[:, c, :], in0=x0, scalar=0.0, in1=ee,
                                                                     op0=ALU.max, op1=ALU.add), reads=[t_tm], writes=[t_dt[c]])
                        k.op("dve", lambda e: e.tensor_tensor(out=la_all[:, c, :], in0=dt_all[:, c, :], in1=abc[:], op=ALU.mult),
                             reads=[t_dt[c], t_par], writes=[t_dt[c]])
                        pa_, t_pa = pa.next()
                        k.op("pe", lambda e: e.matmul(out=pa_[:, 0:64], lhsT=trit[:], rhs=la_all[:, c, :], start=True, stop=True),
                             reads=[t_const, t_dt[c]], writes=[t_pa])
                        k.op("dve", lambda e: e.tensor_copy(out=ac_all[:, c, :], in_=pa_[:, 0:64]), reads=[t_pa], writes=[t_dt[c]])
                        k.op("act", lambda e: e.activation(out=ea_all[:, c, :], in_=ac_all[:, c, :], func=AF.Exp),
                             reads=[t_dt[c]], writes=[t_dt[c]])
                for g in range(int(os.environ.get('M_GROUPS', '8'))):
                    with k.scope() as sg:
                        xtok = k.sb("m_xtok", [128, NT, 512], F32, sg)
                        t_xtok = [Tok() for _ in range(NT)]
                        btok = k.sb("m_btok", [128, NT, 128], BF16, sg)
                        BT = k.sb("m_BT", [128, SEQ], BF16, sg)
                        CT = k.sb("m_CT", [128, SEQ], BF16, sg)
                        t_BC = Tok()
                        us = k.sb("m_us", [1, 768], F32, sg)
                        t_us = Tok()
                        wz = k.sb("m_wz", [128, DC, 512], BF16, sg)
                        t_wz = Tok()
                        k.dma("pool", wz[:], w_in[:, :, g * 512:(g + 1) * 512], k.dma_chan("ch_wz"), writes=[t_wz])
                        with k.scope() as s3:
                            wx = k.sb("m_wx", [128, DC, 768], BF16, s3)
                            t_w = Tok()
                            ch_w = k.dma_chan("ch_w")
                            k.dma("pool", wx[:, :, 0:512], w_in[:, :, 4096 + g * 512:4096 + (g + 1) * 512], ch_w, writes=[t_w])
                            k.dma("pool", wx[:, :, 512:640], w_in[:, :, 8192 + g * 128:8192 + (g + 1) * 128], ch_w, writes=[t_w])
                            k.dma("pool", wx[:, :, 640:768], w_in[:, :, 9216 + g * 128:9216 + (g + 1) * 128], ch_w, writes=[t_w])
                            up_r = Ring(k, "m_up", 1, [128, SEQ + 3], F32, stack=s3)
                            xc_r = Ring(k, "m_xc", 1, [128, SEQ], F32, stack=s3)
                            cvo = Ring(k, "m_cvo", 2, [128, 3], F32, stack=s3, chan=True)
                            pm = PRing([0, 1, 2, 3])
                            pt = PRing([4, 5, 6, 7])
                            for part, (c0, c1) in enumerate(((0, 512), (512, 768))):
                                for kc in range(DC):
                                    k.op("pe", lambda e: e.matmul(out=PB[4 + part][0:1, 0:c1 - c0], lhsT=hT[:, kc, SEQ:SEQ + 1],
                                                                  rhs=wx[:, kc, c0:c1], start=(kc == 0), stop=(kc == DC - 1)),
                                         reads=[t_hT[NT], t_w], writes=[t_PB[4 + part]])
                                k.op("act", lambda e: e.copy(out=us[0:1, c0:c1], in_=PB[4 + part][0:1, 0:c1 - c0]),
                                     reads=[t_PB[4 + part]], writes=[t_us])
                            for fc in range(6):
                                ch_idx = (g * 4 + fc) if fc < 4 else (32 + g if fc == 4 else 40 + g)
                                up, t_up = up_r.next()
                                k.op("dve", lambda e: e.memset(up[:, 0:3], 0.0), writes=[t_up])
                                for tb in range(4):
                                    ps, t_ps = pm.next()
                                    for kc in range(DC):
                                        k.op("pe", lambda e: e.matmul(out=ps[:], lhsT=wx[:, kc, fc * 128:(fc + 1) * 128],
                                                                      rhs=hT[:, kc, tb * 512:(tb + 1) * 512],
                                                                      start=(kc == 0), stop=(kc == DC - 1)),
                                             reads=[t_w] + [t_hT[tb * 4 + q] for q in range(4)], writes=[t_ps])
                                    k.op("act", lambda e: e.copy(out=up[:, 3 + tb * 512:3 + (tb + 1) * 512], in_=ps[:]),
                                         reads=[t_ps], writes=[t_up])
                                co, t_co, chc = cvo.next()
                                k.op("act", lambda e: e.copy(out=co[:], in_=up[:, SEQ:SEQ + 3]), reads=[t_up], writes=[t_co])
                                k.dma("sp", o_conv_p[j, :, ch_idx * 128:(ch_idx + 1) * 128].rearrange("t p -> p t"), co[:], chc,
                                      reads=[t_co], allow_slow_non_contiguous=True)
                                xc, t_xc = xc_r.next()
                                k.op("dve", lambda e: e.tensor_scalar(out=xc[:], in0=up[:, 0:SEQ], scalar1=cw[:, ch_idx, 0:1],
                                                                      scalar2=cb[:, ch_idx:ch_idx + 1], op0=ALU.mult, op1=ALU.add),
                                     reads=[t_up, t_par], writes=[t_xc])
                                for kk in range(1, 4):
                                    k.op("dve", lambda e: e.scalar_tensor_tensor(out=xc[:], in0=up[:, kk:kk + SEQ],
                                                                                 scalar=cw[:, ch_idx, kk:kk + 1], in1=xc[:],
                                                                                 op0=ALU.mult, op1=ALU.add),
                                         reads=[t_up, t_par, t_xc], writes=[t_xc])
                                if fc < 5:
                                    k.op("act", lambda e: e.activation(out=xc[:], in_=xc[:], func=AF.Silu), reads=[t_xc], writes=[t_xc])
                                if fc == 4:
                                    k.op("dve", lambda e: e.tensor_copy(out=BT[:], in_=xc[:]), reads=[t_xc], writes=[t_BC])
                                if fc == 5:
                                    k.op("act", lambda e: e.activation(out=CT[:], in_=xc[:], func=AF.Silu), reads=[t_xc], writes=[t_BC])
                                if fc < 5:
                                    for c4 in range(4):
                                        pp, t_pp = pt.next()
                                        for q in range(4):
                                            c = c4 * 4 + q
                                            k.op("pe", lambda e: e.transpose(out=pp[:, q * 128:(q + 1) * 128], in_=xc[:, tsl(c)],
                                                                             identity=ident[:]),
                                                 reads=[t_xc, t_const], writes=[t_pp])
                                        if fc < 4:
                                            k.op("act", lambda e: e.copy(
                                                out=xtok[:, c4 * 4:(c4 + 1) * 4, fc * 128:(fc + 1) * 128],
                                                in_=pp[:].rearrange("p (c f) -> p c f", c=4)),
                                                 reads=[t_pp], writes=[t_xtok[c4 * 4 + q] for q in range(4)])
                                        else:
                                            k.op("act", lambda e: e.copy(out=btok[:, c4 * 4:(c4 + 1) * 4, :],
                                                                         in_=pp[:].rearrange("p (c f) -> p c f", c=4)),
                                                 reads=[t_pp], writes=[t_BC])
                        with k.scope() as s3:
                            S = k.sb("m_S", [128, 512], F32, s3)
                            Sb = k.sb("m_Sb", [128, 512], BF16, s3)
                            t_S = Tok()
                            k.op("dve", lambda e: e.memset(S[:], 0.0), writes=[t_S])
                            k.op("dve", lambda e: e.memset(Sb[:], 0.0), writes=[t_S])
                            R_r = Ring(k, "m_R", 2, [128, 8, 128], F32, stack=s3)
                            sg_r = Ring(k, "m_seg", 2, [128, 8, 128], F32, stack=s3)
                            cbm_r = Ring(k, "m_cbm", 2, [128, 128], F32, stack=s3)
                            MT_r = Ring(k, "m_MT", 2, [128, 8, 128], BF16, stack=s3)
                            xdt_r = Ring(k, "m_xdt", 2, [128, 512], BF16, stack=s3)
                            xw_r = Ring(k, "m_xw", 2, [128, 512], BF16, stack=s3)
                            zs_r = Ring(k, "m_zs", 2, [128, 512], F32, stack=s3)
                            y_r = Ring(k, "m_y", 2, [128, 512], F32, stack=s3)
                            y2_r = Ring(k, "m_y2", 2, [128, 512], F32, stack=s3)
                            yn_r = Ring(k, "m_yn", 2, [128, 512], F32, stack=s3)
                            ynT_r = Ring(k, "m_ynT", 2, [128, 4, 128], BF16, stack=s3)
                            sm_r = Ring(k, "m_sm", 2, [128, 16], F32, stack=s3)
                            for c in range(int(os.environ.get('M_CHUNKS', '16'))):
                                gs = slice(g * 8, (g + 1) * 8)
                                pz, t_pz = PB[0], t_PB[0]
                                for kc in range(DC):
                                    k.op("pe", lambda e: e.matmul(out=pz[:], lhsT=hT[:, kc, tsl(c)], rhs=wz[:, kc, :],
                                                                  start=(kc == 0), stop=(kc == DC - 1)),
                                         reads=[t_hT[c], t_wz], writes=[t_pz])
                                zs, t_zs = zs_r.next()
                                k.op("act", lambda e: e.activation(out=zs[:], in_=pz[:], func=AF.Silu), reads=[t_pz], writes=[t_zs])
                                Rr, t_R = R_r.next()
                                k.op("pool", lambda e: e.tensor_tensor(out=Rr[:], in0=trit[:].unsqueeze(1).to_broadcast([128, 8, 128]),
                                                                       in1=la_all[:, c, gs].unsqueeze(2).to_broadcast([128, 8, 128]),
                                                                       op=ALU.mult),
                                     reads=[t_const, t_dt[c]], writes=[t_R])
                                Rf = Rr[:].rearrange("p r t -> p (r t)")
                                for hh in range(2):
                                    pa, t_pa = PB[1 + hh], t_PB[1 + hh]
                                    k.op("pe", lambda e: e.matmul(out=pa[:], lhsT=ones[:], rhs=Rf[:, hh * 512:(hh + 1) * 512],
                                                                  start=True, stop=True), reads=[t_const, t_R], writes=[t_pa])
                                seg, t_seg = sg_r.next()
                                sm, t_sm = sm_r.next()
                                for r in range(8):
                                    pa, t_pa = PB[1 + r // 4], t_PB[1 + r // 4]
                                    rr = r % 4
                                    k.op("dve", lambda e: e.tensor_scalar(out=seg[:, r, :], in0=pa[:, rr * 128:(rr + 1) * 128],
                                                                          scalar1=ac_all[:, c, g * 8 + r:g * 8 + r + 1], scalar2=0.0,
                                                                          op0=ALU.subtract, op1=ALU.min),
                                         reads=[t_pa, t_dt[c]], writes=[t_seg])
                                for hh in range(2):
                                    pa, t_pa = PB[1 + hh], t_PB[1 + hh]
                                    k.op("act", lambda e: e.activation(
                                        out=sm[:, hh * 4:(hh + 1) * 4],
                                        in_=pa[:].rearrange("p (r t) -> p r t", r=4)[:, :, 127], func=AF.Exp),
                                         reads=[t_pa], writes=[t_sm])
                                k.op("act", lambda e: e.activation(out=seg[:], in_=seg[:], func=AF.Exp), reads=[t_seg], writes=[t_seg])
                                pc, t_pc = PB[3], t_PB[3]
                                k.op("pe", lambda e: e.matmul(out=pc[:, 0:128], lhsT=BT[:, tsl(c)], rhs=CT[:, tsl(c)], start=True, stop=True),
                                     reads=[t_BC], writes=[t_pc])
                                cbm, t_cbm = cbm_r.next()
                                k.op("dve", lambda e: e.tensor_tensor(out=cbm[:], in0=pc[:, 0:128], in1=trit[:], op=ALU.mult),
                                     reads=[t_pc, t_const], writes=[t_cbm])
                                MT, t_MT = MT_r.next()
                                k.op("dve", lambda e: e.tensor_tensor(out=MT[:], in0=seg[:],
                                                                      in1=cbm[:].unsqueeze(1).to_broadcast([128, 8, 128]), op=ALU.mult),
                                     reads=[t_seg, t_cbm], writes=[t_MT])
                                xdt, t_xdt = xdt_r.next()
                                k.op("pool", lambda e: e.tensor_tensor(
                                    out=xdt[:].rearrange("p (r q) -> p r q", r=8),
                                    in0=xtok[:, c, :].rearrange("p (r q) -> p r q", r=8),
                                    in1=dt_all[:, c, gs].unsqueeze(2).to_broadcast([128, 8, 64]), op=ALU.mult),
                                     reads=[t_xtok[c], t_dt[c]], writes=[t_xdt])
                                py, t_py = PB[4], t_PB[4]
                                for r in range(8):
                                    k.op("pe", lambda e: e.matmul(out=py[:, r * 64:(r + 1) * 64], lhsT=MT[:, r, :],
                                                                  rhs=xdt[:, r * 64:(r + 1) * 64], start=True, stop=True),
                                         reads=[t_MT, t_xdt], writes=[t_py])
                                po, t_po = PB[5], t_PB[5]
                                k.op("pe", lambda e: e.matmul(out=po[:], lhsT=CT[:, tsl(c)], rhs=Sb[:], start=True, stop=True),
                                     reads=[t_BC, t_S], writes=[t_po])
                                y, t_y = y_r.next()
                                k.op("dve", lambda e: e.tensor_tensor(
                                    out=y[:].rearrange("p (r q) -> p r q", r=8), in0=po[:].rearrange("p (r q) -> p r q", r=8),
                                    in1=ea_all[:, c, gs].unsqueeze(2).to_broadcast([128, 8, 64]), op=ALU.mult),
                                     reads=[t_po, t_dt[c]], writes=[t_y])
                                k.op("dve", lambda e: e.tensor_tensor(out=y[:], in0=y[:], in1=py[:], op=ALU.add),
                                     reads=[t_y, t_py], writes=[t_y])
                                y2, t_y2 = y2_r.next()
                                k.op("pool", lambda e: e.tensor_tensor(
                                    out=y2[:].rearrange("p (r q) -> p r q", r=8), in0=xtok[:, c, :].rearrange("p (r q) -> p r q", r=8),
                                    in1=dsk[:, gs].unsqueeze(2).to_broadcast([128, 8, 64]), op=ALU.mult),
                                     reads=[t_xtok[c], t_par], writes=[t_y2])
                                k.op("dve", lambda e: e.tensor_tensor(out=y[:], in0=y[:], in1=y2[:], op=ALU.add),
                                     reads=[t_y, t_y2], writes=[t_y])
                                k.op("dve", lambda e: e.tensor_tensor(out=y[:], in0=y[:], in1=zs[:], op=ALU.mult),
                                     reads=[t_y, t_zs], writes=[t_y])
                                k.op("act", lambda e: e.activation(out=y2[:], in_=y[:], func=AF.Square, accum_out=sm[:, 8:9]),
                                     reads=[t_y], writes=[t_y2, t_sm])
                                k.op("act", lambda e: e.activation(out=sm[:, 9:10], in_=sm[:, 8:9], func=AF.Sqrt, scale=1.0 / 512,
                                                                   bias=float(LN_EPS)), reads=[t_sm], writes=[t_sm])
                                k.op("dve", lambda e: e.reciprocal(out=sm[:, 10:11], in_=sm[:, 9:10]), reads=[t_sm], writes=[t_sm])
                                yn, t_yn = yn_r.next()
                                k.op("dve", lambda e: e.tensor_scalar(out=yn[:], in0=y[:], scalar1=sm[:, 10:11], scalar2=None,
                                                                      op0=ALU.mult), reads=[t_y, t_sm], writes=[t_yn])
                                pT, t_pT = PB[6], t_PB[6]
                                for q in range(4):
                                    k.op("pe", lambda e: e.transpose(out=pT[:, q * 128:(q + 1) * 128], in_=yn[:, q * 128:(q + 1) * 128],
                                                                     identity=ident[:]), reads=[t_yn, t_const], writes=[t_pT])
                                ynT, t_ynT = ynT_r.next()
                                for q in range(4):
                                    k.op("act", lambda e: e.activation(out=ynT[:, q, :], in_=pT[:, q * 128:(q + 1) * 128], func=AF.Copy,
                                                                       scale=ng[:, g * 4 + q:g * 4 + q + 1]),
                                         reads=[t_pT, t_par], writes=[t_ynT])
                                k.dma("sp", ynT_d[g * 512:(g + 1) * 512, tsl(c)].rearrange("(q p) t -> p q t", p=128), ynT[:], ch_ynTd[c],
                                      reads=[t_ynT], writes=[t_ynTd[c]])
                                xw, t_xw = xw_r.next()
                                k.op("pool", lambda e: e.tensor_tensor(
                                    out=xw[:].rearrange("p (r q) -> p r q", r=8), in0=xdt[:].rearrange("p (r q) -> p r q", r=8),
                                    in1=seg[:, :, 127:128].to_broadcast([128, 8, 64]), op=ALU.mult),
                                     reads=[t_xdt, t_seg], writes=[t_xw])
                                pS, t_pS = PB[7], t_PB[7]
                                k.op("pe", lambda e: e.matmul(out=pS[:], lhsT=btok[:, c, :], rhs=xw[:], start=True, stop=True),
                                     reads=[t_BC, t_xw], writes=[t_pS])
                                k.op("dve", lambda e: e.tensor_tensor(
                                    out=S[:].rearrange("p (r q) -> p r q", r=8), in0=S[:].rearrange("p (r q) -> p r q", r=8),
                                    in1=sm[:, 0:8].unsqueeze(2).to_broadcast([128, 8, 64]), op=ALU.mult),
                                     reads=[t_sm, t_S], writes=[t_S])
                                k.op("dve", lambda e: e.tensor_tensor(out=S[:], in0=S[:], in1=pS[:], op=ALU.add),
                                     reads=[t_pS, t_S], writes=[t_S])
                                k.op("act", lambda e: e.copy(out=Sb[:], in_=S[:]), reads=[t_S], writes=[t_S])

                            pT, t_pT = PB[6], t_PB[6]
                            for q in range(4):
                                k.op("pe", lambda e: e.transpose(out=pT[:, q * 128:(q + 1) * 128], in_=S[:, q * 128:(q + 1) * 128],
                                                                 identity=ident[:]), reads=[t_S, t_const], writes=[t_pT])
                            so = k.sb("m_so", [128, 4, 128], F32, s3)
                            t_so = Tok()
                            k.op("act", lambda e: e.copy(out=so[:], in_=pT[:].rearrange("p (q n) -> p q n", q=4)), reads=[t_pT], writes=[t_so])
                            k.dma("sp", o_ssm_p[j, g * 512:(g + 1) * 512, :].rearrange("(q p) n -> p q n", p=128), so[:], k.dma_chan("ch_so"),
                                  reads=[t_so])
                        with k.scope() as s3:
                            chs = k.dma_chan("ch_smp")
                            csr = k.sb("ms_cs", [1, 3, 768], F32, s3)
                            cwr = k.sb("ms_cw", [1, 4, 768], F32, s3)
                            cbr = k.sb("ms_cb", [1, 768], F32, s3)
                            t_sp = Tok()
                            segs = ((0, 512, g * 512), (512, 640, 4096 + g * 128), (640, 768, 5120 + g * 128))
                            for (a0, a1, c0) in segs:
                                k.dma("sp", csr[0:1, :, a0:a1], m_cs[j:j + 1, :, c0:c0 + (a1 - a0)], chs, writes=[t_sp])
                                k.dma("sp", cwr[0:1, :, a0:a1], m_cwr[j:j + 1, :, c0:c0 + (a1 - a0)], chs, writes=[t_sp])
                                k.dma("sp", cbr[0:1, a0:a1], m_cbr[j:j + 1, c0:c0 + (a1 - a0)], chs, writes=[t_sp])
                            for (a0, a1, c0) in segs:
                                k.dma("sp", o_conv_s[j:j + 1, 0:2, c0:c0 + (a1 - a0)], csr[0:1, 1:3, a0:a1], chs, reads=[t_sp])
                                k.dma("sp", o_conv_s[j:j + 1, 2, c0:c0 + (a1 - a0)], us[0:1, a0:a1], chs, reads=[t_us])
                            xr = k.sb("ms_xr", [1, 768], F32, s3)
                            t_xr = Tok()
                            k.op("dve", lambda e: e.tensor_tensor(out=xr[:], in0=us[:], in1=cwr[0:1, 3, :], op=ALU.mult),
                                 reads=[t_us, t_sp], writes=[t_xr])
                            for kk in range(3):
                                tmpr = k.sb(f"ms_tr{kk}", [1, 768], F32, s3)
                                t_tr = Tok()
                                k.op("dve", lambda e: e.tensor_tensor(out=tmpr[:], in0=csr[0:1, kk, :], in1=cwr[0:1, kk, :], op=ALU.mult),
                                     reads=[t_sp], writes=[t_tr])
                                k.op("dve", lambda e: e.tensor_tensor(out=xr[:], in0=xr[:], in1=tmpr[:], op=ALU.add),
                                     reads=[t_tr, t_xr], writes=[t_xr])
                            k.op("dve", lambda e: e.tensor_tensor(out=xr[:], in0=xr[:], in1=cbr[:], op=ALU.add), reads=[t_xr, t_sp], writes=[t_xr])
                            k.op("act", lambda e: e.activation(out=xr[:], in_=xr[:], func=AF.Silu), reads=[t_xr], writes=[t_xr])
                            rep = k.sb("ms_rep", [1, 4, 512], F32, s3)
                            t_rep = Tok()
                            for qi, srcrow in enumerate((dt_all[0:1, NT, gs8(g)], la_all[0:1, NT, gs8(g)], dsk[0:1, gs8(g)])):
                                k.op("dve", lambda e: e.tensor_copy(out=rep[0:1, qi, :].rearrange("o (r q) -> o r q", r=8),
                                                                    in_=srcrow.unsqueeze(2).to_broadcast([1, 8, 64])),
                                     reads=[t_dt[NT], t_par], writes=[t_rep])
                            pz, t_pz = PB[0], t_PB[0]
                            for kc in range(DC):
                                k.op("pe", lambda e: e.matmul(out=pz[0:1, :], lhsT=hT[:, kc, SEQ:SEQ + 1], rhs=wz[:, kc, :],
                                                              start=(kc == 0), stop=(kc == DC - 1)), reads=[t_hT[NT], t_wz], writes=[t_pz])
                            k.op("act", lambda e: e.activation(out=rep[0:1, 3, :], in_=pz[0:1, :], func=AF.Silu), reads=[t_pz], writes=[t_rep])
                            pc_, t_pc_ = PB[1], t_PB[1]
                            for cc in range(4):
                                k.op("pe", lambda e: e.matmul(out=pc_[:, cc:cc + 1], lhsT=xr[0:1, cc * 128:(cc + 1) * 128], rhs=ones[0:1, 0:1],
                                                              start=True, stop=True), reads=[t_xr, t_const], writes=[t_pc_])
                                for qi in range(4):
                                    k.op("pe", lambda e: e.matmul(out=pc_[:, 4 + qi * 4 + cc:5 + qi * 4 + cc],
                                                                  lhsT=rep[0:1, qi, cc * 128:(cc + 1) * 128], rhs=ones[0:1, 0:1],
                                                                  start=True, stop=True), reads=[t_rep, t_const], writes=[t_pc_])
                            pbc, t_pbc = PB[2], t_PB[2]
                            k.op("pe", lambda e: e.matmul(out=pbc[:, 0:256], lhsT=ones[0:1, :], rhs=xr[0:1, 512:768], start=True, stop=True),
                                 reads=[t_xr, t_const], writes=[t_pbc])
                            cols = k.sb("ms_cols", [128, 32], F32, s3)
                            t_cols = Tok()
                            k.op("act", lambda e: e.copy(out=cols[:, 0:20], in_=pc_[:, 0:20]), reads=[t_pc_], writes=[t_cols])
                            bcb = k.sb("ms_bcb", [128, 256], F32, s3)
                            t_bcb = Tok()
                            k.op("act", lambda e: e.copy(out=bcb[:], in_=pbc[:, 0:256]), reads=[t_pbc], writes=[t_bcb])
                            k.op("act", lambda e: e.activation(out=cols[:, 20:24], in_=cols[:, 8:12], func=AF.Exp), reads=[t_cols], writes=[t_cols])
                            k.op("dve", lambda e: e.tensor_tensor(out=cols[:, 24:28], in0=cols[:, 0:4], in1=cols[:, 4:8], op=ALU.mult),
                                 reads=[t_cols], writes=[t_cols])
                            Ss = k.sb("ms_S", [128, 4, 128], F32, s3)
                            t_Ss = Tok()
                            k.dma("sp", Ss[:], m_ss[j, g * 512:(g + 1) * 512, :].rearrange("(c p) n -> p c n", p=128), chs, writes=[t_Ss])
                            tmpS = k.sb("ms_tS", [128, 128], F32, s3)
                            t_tS = Tok()
                            for cc in range(4):
                                k.op("dve", lambda e: e.tensor_scalar(out=tmpS[:], in0=bcb[:, 0:128], scalar1=cols[:, 24 + cc:25 + cc], scalar2=None,
                                                                      op0=ALU.mult), reads=[t_bcb, t_cols], writes=[t_tS])
                                k.op("dve", lambda e: e.scalar_tensor_tensor(out=Ss[:, cc, :], in0=Ss[:, cc, :], scalar=cols[:, 20 + cc:21 + cc],
                                                                             in1=tmpS[:], op0=ALU.mult, op1=ALU.add),
                                     reads=[t_Ss, t_tS, t_cols], writes=[t_Ss])
                                k.op("dve", lambda e: e.tensor_tensor(out=tmpS[:], in0=Ss[:, cc, :], in1=bcb[:, 128:256], op=ALU.mult),
                                     reads=[t_Ss, t_bcb], writes=[t_tS])
                                k.op("dve", lambda e: e.reduce_sum(out=cols[:, 28 + cc:29 + cc], in_=tmpS[:], axis=AX.X), reads=[t_tS], writes=[t_cols])
                            k.dma("sp", o_ssm_s[j, g * 512:(g + 1) * 512, :].rearrange("(c p) n -> p c n", p=128), Ss[:], chs, reads=[t_Ss])
                            ys = k.sb("ms_y", [128, 8], F32, s3)
                            t_ys = Tok()
                            k.op("dve", lambda e: e.tensor_tensor(out=ys[:, 0:4], in0=cols[:, 12:16], in1=cols[:, 0:4], op=ALU.mult), reads=[t_cols], writes=[t_ys])
                            k.op("dve", lambda e: e.tensor_tensor(out=ys[:, 0:4], in0=ys[:, 0:4], in1=cols[:, 28:32], op=ALU.add), reads=[t_cols, t_ys], writes=[t_ys])
                            k.op("dve", lambda e: e.tensor_tensor(out=ys[:, 0:4], in0=ys[:, 0:4], in1=cols[:, 16:20], op=ALU.mult), reads=[t_cols, t_ys], writes=[t_ys])
                            k.op("dve", lambda e: e.tensor_tensor(out=ys[:, 4:8], in0=ys[:, 0:4], in1=ys[:, 0:4], op=ALU.mult), reads=[t_ys], writes=[t_ys])
                            k.op("dve", lambda e: e.reduce_sum(out=cols[:, 0:1], in_=ys[:, 4:8], axis=AX.X), reads=[t_ys], writes=[t_cols])
                            pss, t_pss = PB[3], t_PB[3]
                            k.op("pe", lambda e: e.matmul(out=pss[:, 0:1], lhsT=ones[:], rhs=cols[:, 0:1], start=True, stop=True),
                                 reads=[t_cols, t_const], writes=[t_pss])
                            k.op("act", lambda e: e.activation(out=cols[:, 1:2], in_=pss[:, 0:1], func=AF.Sqrt, scale=1.0 / 512, bias=float(LN_EPS)),
                                 reads=[t_pss], writes=[t_cols])
                            k.op("dve", lambda e: e.reciprocal(out=cols[:, 2:3], in_=cols[:, 1:2]), reads=[t_cols], writes=[t_cols])
                            k.op("dve", lambda e: e.tensor_scalar(out=ys[:, 0:4], in0=ys[:, 0:4], scalar1=cols[:, 2:3], scalar2=None, op0=ALU.mult),
                                 reads=[t_cols, t_ys], writes=[t_ys])
                            ysb = k.sb("ms_yb", [128, 4], BF16, s3)
                            t_ysb = Tok()
                            k.op("dve", lambda e: e.tensor_tensor(out=ysb[:], in0=ys[:, 0:4], in1=ng[:, g * 4:(g + 1) * 4], op=ALU.mult),
                                 reads=[t_ys, t_par], writes=[t_ysb])
                            k.dma("sp", ynT_d[g * 512:(g + 1) * 512, SEQ:SEQ + 1].rearrange("(c p) o -> p (c o)", p=128), ysb[:], ch_ynTd[NT],
                                  reads=[t_ysb], writes=[t_ynTd[NT]], allow_slow_non_contiguous=True)
            def get_aT(tt, st, state):
                if "ring" not in state:
                    state["ring"] = Ring(k, "m_aT", 2, [128, 32, 128], BF16, stack=st, chan=True)
                a, t_a, cha = state["ring"].next()
                k.dma("sp", a[:], ynT_d[:, tsl(tt)].rearrange("(c p) t -> p c t", p=128), cha, reads=[t_ynTd[tt]], writes=[t_a])
                return a, t_a
            if os.environ.get('M_PROJ', '1') == '1':
                proj_ln(get_aT, 32, m_wout[j], li, 0)

        NL = int(os.environ.get("K_LAYERS", "4"))
        for li in range(NL):
            kind, j = li % 3, li // 3
            with k.scope():
                if kind == 0:
                    mamba_layer(j, li)
                elif kind == 1:
                    fox_layer(li)
                else:
                    nsa_layer(li)
            with k.scope():
                memattn_layer(li)
            with k.scope():
                peer_layer(li)
        if NL == 2:
            with k.scope():
                nsa_layer(2, attn=False)
            with k.scope():
                memattn_layer(2, only_kv=True)
                memattn_layer(3, only_kv=True)

        for tt in range(TT):
            k.dma("sp", o_y[tsl(tt), :], res[tsl(tt), :], ch_dbg, reads=[t_res[tt]])
        if dbg:
            for tt in range(TT):
                k.dma("sp", o_dbg[tsl(tt), :], res[tsl(tt), :], ch_dbg, reads=[t_res[tt]])
        k.finish()
        print("instructions:", k.nins)
    return nc, in_names


STAGES = tuple(os.environ.get("K_STAGES", "init,mamba0,mem,peer,fox,nsa").split(","))
DBG = os.environ.get("K_DBG", "0") == "1"
_PROG = None
_LAST = {}


def _bc(v, n=128):
    return np.ascontiguousarray(np.broadcast_to(np.asarray(v, np.float32)[None, :], (n, v.shape[-1])))


def kernel(**inputs):
    global _PROG
    f32 = np.float32
    x_prompt = np.asarray(inputs["x_prompt"])
    B = x_prompt.shape[0]
    if _PROG is None:
        _PROG = build_program(STAGES, DBG)
    nc, in_names = _PROG
    shared = {"ident": np.eye(128, dtype=f32), "trit": np.triu(np.ones((128, 128), f32))}
    if "ln_g" in in_names:
        lg = np.asarray(inputs["ln_g"], f32)
        lb = np.asarray(inputs["ln_b"], f32)
        shared["ln_g"] = np.ascontiguousarray(np.broadcast_to(lg[:, :, None, :], (DEPTH, 3, 128, D)))
        shared["ln_b"] = np.ascontiguousarray(np.broadcast_to(lb[:, :, None, :], (DEPTH, 3, 128, D)))
    if "mem_wkv" in in_names:
        shared["mem_wkv"] = np.ascontiguousarray(inputs["mem_wkv"], dtype=f32)
    if "mem_wq" in in_names:
        shared["mem_wq"] = np.ascontiguousarray(inputs["mem_wq"], dtype=f32)
        shared["mem_wo"] = np.ascontiguousarray(inputs["mem_wo"], dtype=f32)
    if "peer_wq" in in_names:
        shared["peer_wq"] = np.ascontiguousarray(inputs["peer_wq"], dtype=f32)
        sk = np.asarray(inputs["peer_subkeys"], f32)
        shared["p_skT"] = np.ascontiguousarray(sk.reshape(DEPTH, 16, 128, 128).transpose(0, 3, 1, 2))
        npl = int(os.environ.get("K_LAYERS", "4"))
        shared["p_uT"] = np.ascontiguousarray(np.asarray(inputs["peer_u"][:npl], f32).transpose(0, 2, 1))
        shared["peer_v"] = np.ascontiguousarray(inputs["peer_v"][:npl], dtype=f32)
    if "iota" in in_names and "f_win" not in in_names:
        shared["iota"] = np.arange(128, dtype=f32)[:, None]
        m16 = np.zeros((16, 4), f32)
        m16[np.arange(16), np.arange(16) // 4] = 1.0
        shared["mask16"] = m16
    if "f_win" in in_names:
        shared["f_win"] = np.ascontiguousarray(inputs["fox_w_in"][0], dtype=f32)
        shared["f_wout"] = np.ascontiguousarray(inputs["fox_w_out"][0], dtype=f32)
        shared["f_bf"] = _bc(np.asarray(inputs["fox_b_f"][0], f32))
        s16 = np.zeros((16, 16, 128), f32)
        for hh in range(16):
            s16[hh, hh, :] = 1.0
        shared["sel16"] = s16.reshape(16, 2048)
        ckv = np.asarray(inputs["cache_fox_kv"][0], f32).reshape(1280 * 128, 2, 512)
        shared["f_kpool"] = np.ascontiguousarray(ckv[:, 0])
        shared["f_vpool"] = np.ascontiguousarray(ckv[:, 1])
        shared["f_lpool"] = np.ascontiguousarray(np.asarray(inputs["cache_fox_logf"][0], f32).reshape(1280 * 128, 16))
        shared["iota"] = np.arange(128, dtype=f32)[:, None]
        shared["tris"] = np.tril(np.ones((128, 128), f32), -1)
        m16 = np.zeros((16, 4), f32)
        m16[np.arange(16), np.arange(16) // 4] = 1.0
        shared["mask16"] = m16
        shared["mneg"] = np.where(np.arange(128)[None, :] <= np.arange(128)[:, None], 0.0, -1e30).astype(f32)
    if "n_win" in in_names:
        shared["n_win"] = np.ascontiguousarray(inputs["nsa_w_in"][0], dtype=f32)
        shared["n_wout"] = np.ascontiguousarray(inputs["nsa_w_out"][0], dtype=f32)
        cn = np.asarray(inputs["cache_nsa_kv"][0], f32).reshape(1280 * 128, 4, 512)
        for q in range(4):
            shared[f"n_pool{q}"] = np.ascontiguousarray(cn[:, q])
        wpos_ = np.asarray(inputs["nsa_cmp_wpos"][0], f32)
        wps = np.zeros((2, 4, 128, 17, 128), f32)
        for slot in range(16):
            for jj in range(8):
                col = 8 * slot + jj
                nrow = min(32, 128 - 16 * jj)
                wps[:, :, 16 * jj:16 * jj + nrow, slot, col] = wpos_[:, :nrow, :].transpose(0, 2, 1)
            if slot > 0:
                wps[:, :, 0:16, slot, 8 * slot - 1] = wpos_[:, 16:32, :].transpose(0, 2, 1)
        wps[:, :, 0:16, 16, 127] = wpos_[:, 16:32, :].transpose(0, 2, 1)
        shared["n_wps"] = wps
        nall = np.arange(9 * 128)
        mall = np.arange(257)
        ovs = ((16 * nall[:, None] < 64 * mall[None, :] + 64) & (16 * nall[:, None] + 32 > 64 * mall[None, :]) & (nall[:, None] < 1027)).astype(f32)
        shared["n_ovs"] = np.ascontiguousarray(ovs.reshape(9, 128, 257))
        frow = np.zeros((257,), f32)
        frow[[0, 255, 256]] = 1e6
        shared["n_fs"] = np.ascontiguousarray(np.broadcast_to(frow[None, :], (16, 257)))
        shared["n_gs"] = (np.arange(16)[:, None] // 4 == np.arange(16)[None, :] // 4).astype(f32)
        wpos = np.asarray(inputs["nsa_cmp_wpos"][0], f32)
        wp = np.zeros((2, 4, SEQ, 128), f32)
        for n in range(127):
            wp[:, :, 16 * n:16 * n + 32, n] = wpos.transpose(0, 2, 1)
        shared["n_wp"] = wp
        shared["n_proj"] = np.ascontiguousarray(inputs["nsa_cmp_proj"][0], dtype=f32)
        shared["n_bg"] = _bc(np.asarray(inputs["nsa_b_gate"][0], f32))
        tpos = np.arange(SEQ)
        nn = np.arange(128)
        shared["n_cm01"] = (((16 * nn[None, :] + 31) <= tpos[:, None]) & (nn[None, :] < 127)).astype(f32)
        mm = np.arange(32)
        ovl = ((16 * nn[:, None] < 64 * mm[None, :] + 64) & (16 * nn[:, None] + 32 > 64 * mm[None, :]) & (nn[:, None] < 127)).astype(f32)
        shared["n_ov"] = ovl
        qblk = tpos // 64
        forced = (mm[None, :] == 0) | (mm[None, :] == qblk[:, None]) | (mm[None, :] == qblk[:, None] - 1)
        valid = mm[None, :] <= qblk[:, None]
        fvv = np.zeros((SEQ, 3, 32), f32)
        fvv[:, 0] = forced * 1e6
        fvv[:, 1] = valid
        fvv[:, 2] = np.where(valid, 0.0, -1e30)
        shared["n_fv"] = fvv
        ea = np.zeros((33, SEQ), f32)
        ea[tpos // 64, tpos] = 32768.0
        ea[32, :] = -32768.0
        shared["n_eaug"] = ea
        loc = np.arange(128)
        shared["n_mfar"] = np.where(loc[None, :] > loc[:, None], 0.0, -1e30).astype(f32)
        shared["mneg2"] = np.where(loc[None, :] <= loc[:, None], 0.0, -1e30).astype(f32)
    if "m_win" in in_names:
        shared["m_win"] = np.ascontiguousarray(inputs["mamba_w_in"], dtype=f32)
        shared["m_wout"] = np.ascontiguousarray(inputs["mamba_w_out"], dtype=f32)
        cw = np.asarray(inputs["mamba_conv_w"], f32)
        shared["m_convw"] = np.ascontiguousarray(cw.reshape(2, 4, 48, 128).transpose(0, 3, 2, 1))
        shared["m_convb"] = np.ascontiguousarray(np.asarray(inputs["mamba_conv_b"], f32).reshape(2, 48, 128).transpose(0, 2, 1))
        shared["m_cwr"] = np.ascontiguousarray(cw)
        shared["m_cbr"] = np.ascontiguousarray(inputs["mamba_conv_b"], dtype=f32)
        shared["m_dtb"] = np.stack([_bc(inputs["mamba_dt_bias"][j]) for j in range(2)])
        shared["m_alog"] = np.stack([_bc(inputs["mamba_a_log"][j]) for j in range(2)])
        shared["m_dsk"] = np.stack([_bc(inputs["mamba_d"][j]) for j in range(2)])
        shared["m_ng"] = np.ascontiguousarray(np.asarray(inputs["mamba_norm_g"], f32).reshape(2, 32, 128).transpose(0, 2, 1))
    in_maps = []
    for c in range(8):
        b = c % B
        m = {"xp": np.ascontiguousarray(x_prompt[b]), "xs": np.ascontiguousarray(inputs["x_sample"][c]),
             "memp": np.ascontiguousarray(inputs["mem_prompt"][b])}
        if "cmk" in in_names:
            m["cmk"] = np.ascontiguousarray(np.asarray(inputs["cache_mem_kv"])[:, c].reshape(DEPTH, MEM, 2 * D))
        if "pt" in in_names:
            m["pt"] = np.ascontiguousarray(np.asarray(inputs["page_table"])[c:c + 1].astype(np.int32))
        if "n_winbuf" in in_names:
            m["n_winbuf"] = np.ascontiguousarray(np.asarray(inputs["state_nsa_win"], f32)[0, c].reshape(512, 1024))
        if "m_cs" in in_names:
            m["m_cs"] = np.ascontiguousarray(np.asarray(inputs["state_conv"], f32)[:, c])
            m["m_ss"] = np.ascontiguousarray(np.asarray(inputs["state_ssm"], f32)[:, c].reshape(2, M_DI, 128))
        m.update(shared)
        in_maps.append({n: m[n] for n in in_names})
    res = run_bass_kernel_spmd(nc, in_maps, core_ids=list(range(8)))
    R = res.results
    _LAST["R"] = R
    mem_kv_p = np.stack([R[b]["o_memkv"] for b in range(B)], axis=1).reshape(DEPTH, B, MEM, 2, 4, 512)
    ssm_p = np.stack([R[b]["o_ssm_p"] for b in range(B)], axis=1).reshape(2, B, 64, 64, 128)
    conv_p = np.stack([R[b]["o_conv_p"] for b in range(B)], axis=1).reshape(2, B, 3, 6144)

    ssm_s = np.stack([R[c]["o_ssm_s"] for c in range(8)], axis=1).reshape(2, 8, 64, 64, 128)
    conv_s = np.stack([R[c]["o_conv_s"] for c in range(8)], axis=1).reshape(2, 8, 3, 6144)

    fox_kv_p = fox_lf_p = fox_kv_s = fox_lf_s = None
    if "o_fox_kv" in R[0]:
        fox_kv_p = np.stack([R[b]["o_fox_kv"][:SEQ] for b in range(B)]).reshape(1, B, SEQ, 2, 4, 128)
        fox_lf_p = np.stack([R[b]["o_fox_lf"][:SEQ] for b in range(B)]).reshape(1, B, SEQ, 16)
        fox_kv_s = np.stack([R[c]["o_fox_kv"][SEQ:SEQ + 1] for c in range(8)]).reshape(1, 8, 1, 2, 4, 128)
        fox_lf_s = np.stack([R[c]["o_fox_lf"][SEQ:SEQ + 1] for c in range(8)]).reshape(1, 8, 1, 16)

    nsa_kv_p = nsa_win_p = nsa_kv_s = nsa_win_s = None
    if "o_nsa_kv" in R[0]:
        nsa_kv_p = np.stack([R[b]["o_nsa_kv"][:SEQ] for b in range(B)]).reshape(1, B, SEQ, 4, 4, 128)
        nsa_win_p = np.stack([R[b]["o_nsa_win_p"] for b in range(B)]).reshape(1, B, 512, 2, 4, 128)
        nsa_kv_s = np.stack([R[c]["o_nsa_kv"][SEQ:SEQ + 1] for c in range(8)]).reshape(1, 8, 1, 4, 4, 128)
        nsa_win_s = np.stack([R[c]["o_nsa_win_s"] for c in range(8)]).reshape(1, 8, 512, 2, 4, 128)

    def z(*s):
        return np.zeros(s, f32)

    y_p = np.stack([R[b]["o_y"][:SEQ] for b in range(B)])
    y_s = np.stack([R[c]["o_y"][SEQ:SEQ + 1] for c in range(8)])
    return (y_p, y_s, mem_kv_p, ssm_p, conv_p,
            fox_kv_p if fox_kv_p is not None else z(1, 4, 2048, 2, 4, 128), fox_lf_p if fox_lf_p is not None else z(1, 4, 2048, 16),
            nsa_kv_p if nsa_kv_p is not None else z(1, 4, 2048, 4, 4, 128), nsa_win_p if nsa_win_p is not None else z(1, 4, 512, 2, 4, 128),
            ssm_s, conv_s, fox_kv_s if fox_kv_s is not None else z(1, 8, 1, 2, 4, 128),
            fox_lf_s if fox_lf_s is not None else z(1, 8, 1, 16), nsa_kv_s if nsa_kv_s is not None else z(1, 8, 1, 4, 4, 128),
            nsa_win_s if nsa_win_s is not None else z(1, 8, 512, 2, 4, 128))
```

```python
import numpy as np
from contextlib import ExitStack, contextmanager
import concourse.bass as bass
import concourse.mybir as mybir
from concourse.bass_utils import run_bass_kernel_spmd

F32 = mybir.dt.float32
BF16 = mybir.dt.bfloat16
I32 = mybir.dt.int32
ALU = mybir.AluOpType
AF = mybir.ActivationFunctionType
AX = mybir.AxisListType

D = 2048
DC = 16
SEQ = 2048
NT = 16
TT = 17
TTOK = TT * 128
DEPTH = 4
MEM = 256
DN_ALPHA = (2 * DEPTH) ** 0.25
LN_EPS = 1e-5


class Tok:
    __slots__ = ("w", "r", "excl")

    def __init__(self, excl=False):
        self.w = None
        self.r = {}
        self.excl = excl


class Chan:
    def __init__(self, sem, step):
        self.sem = sem
        self.step = step
        self.val = 0


class MK:
    def __init__(self, nc, stack):
        self.nc = nc
        self.stack = stack
        self.engs = {"pe": nc.tensor, "dve": nc.vector, "act": nc.scalar, "pool": nc.gpsimd, "sp": nc.sync}
        self.chan = {n: Chan(self._sem("c_" + n), 1) for n in ("pe", "dve", "act", "pool")}
        self.dchans = []
        self.free_chans = []
        self.scopes = []
        self.seen = {n: {} for n in self.engs}
        self.nins = 0
        self.uid = 0

    def _sem(self, name):
        return self.stack.enter_context(self.nc.semaphore(name))

    def dma_chan(self, name):
        if self.free_chans:
            c = self.free_chans.pop()
        else:
            c = Chan(self._sem(f"dch{len(self.dchans)}"), 16)
            self.dchans.append(c)
        if self.scopes:
            self.scopes[-1].append(c)
        return c

    def barrier(self):
        chans = self.dchans + list(self.chan.values())
        for en in self.engs:
            self._wait(en, [(c, c.val) for c in chans if c.val], allow_pe_self=True)

    @contextmanager
    def scope(self):
        self.scopes.append([])
        with ExitStack() as st:
            yield st
            self.barrier()
        self.free_chans.extend(self.scopes.pop())

    def sb(self, name, shape, dt, stack=None):
        self.uid += 1
        return (stack or self.stack).enter_context(self.nc.sbuf_tensor(f"{name}_u{self.uid}", shape, dt))

    def ps(self, name, shape, dt, stack=None):
        self.uid += 1
        return (stack or self.stack).enter_context(self.nc.psum_tensor(f"{name}_u{self.uid}", shape, dt))

    def _wait(self, ename, deps, allow_pe_self=False):
        need = {}
        pech = self.chan["pe"]
        for ch, v in deps:
            if ename == "pe" and ch is pech and not allow_pe_self:
                continue
            if v > need.get(id(ch), (ch, 0))[1]:
                need[id(ch)] = (ch, v)
        seen = self.seen[ename]
        for cid, (ch, v) in need.items():
            if seen.get(cid, 0) >= v:
                continue
            self.engs[ename].wait_ge(ch.sem, v)
            seen[cid] = v

    @staticmethod
    def _deps(reads, writes):
        deps = []
        for t in reads:
            if t.w:
                deps.append(t.w)
        for t in writes:
            if t.w:
                deps.append(t.w)
            deps.extend(t.r.values())
        return deps

    def op(self, ename, fn, reads=(), writes=()):
        ex = [t for t in reads if t.excl]
        if ex:
            reads = [t for t in reads if not t.excl]
            writes = list(writes) + ex
        self._wait(ename, self._deps(reads, writes))
        ins = fn(self.engs[ename])
        ch = self.chan[ename]
        ch.val += 1
        ins.then_inc(ch.sem, 1)
        for t in reads:
            t.r[id(ch)] = (ch, ch.val)
        for t in writes:
            t.w = (ch, ch.val)
            t.r = {}
        self.nins += 1
        return ins

    def dma(self, qname, out, in_, ch, reads=(), writes=(), **kw):
        self._wait(qname, self._deps(reads, writes))
        ins = self.engs[qname].dma_start(out=out, in_=in_, **kw)
        ch.val += 16
        ins.then_inc(ch.sem, 16)
        for t in reads:
            t.r[id(ch)] = (ch, ch.val)
        for t in writes:
            t.w = (ch, ch.val)
            t.r = {}
        self.nins += 1
        return ins

    def idma(self, out, in_, idx_ap, ch, reads=(), writes=()):
        self._wait("pool", self._deps(reads, writes))
        ins = self.nc.gpsimd.indirect_dma_start(out=out, out_offset=None, in_=in_,
                                                in_offset=bass.IndirectOffsetOnAxis(ap=idx_ap, axis=0))
        ch.val += 16
        ins.then_inc(ch.sem, 16)
        for t in reads:
            t.r[id(ch)] = (ch, ch.val)
        for t in writes:
            t.w = (ch, ch.val)
            t.r = {}
        self.nins += 1
        return ins

    def finish(self):
        for ch in self.dchans + list(self.chan.values()):
            if ch.val:
                self.engs["sp"].wait_ge(ch.sem, ch.val)


class Ring:
    def __init__(self, k, name, n, shape, dt, space="sb", stack=None, chan=False):
        alloc = k.sb if space == "sb" else k.ps
        self.bufs = [alloc(f"{name}{i}", shape, dt, stack) for i in range(n)]
        self.toks = [Tok() for _ in range(n)]
        self.chans = [k.dma_chan(f"ch_{name}{i}") for i in range(n)] if chan else None
        self.i = 0
        self.n = n

    def next(self):
        j = self.i % self.n
        self.i += 1
        if self.chans:
            return self.bufs[j], self.toks[j], self.chans[j]
        return self.bufs[j], self.toks[j]


import os
M_DI = 4096
M_IN = 10304
M_CH = 6144


def build_program(stages=("init", "memkv", "mamba0", "mem", "peer"), dbg=True):
    nc = bass.Bass("TRN2", target_bir_lowering=False)
    stages = set(stages)
    in_names = []

    def din(name, shape, dt=F32):
        in_names.append(name)
        return nc.dram_tensor(name, list(shape), dt, kind="ExternalInput").ap()

    def dout(name, shape, dt=F32):
        return nc.dram_tensor(name, list(shape), dt, kind="ExternalOutput").ap()

    def dscr(name, shape, dt=F32):
        return nc.dram_tensor(name, list(shape), dt, kind="Internal").ap()

    xp = din("xp", [SEQ, D])
    xs = din("xs", [1, D])
    memp = din("memp", [MEM, D])
    ident_d = din("ident", [128, 128])
    trit_d = din("trit", [128, 128])
    ln_gd = din("ln_g", [DEPTH, 3, 128, D])
    ln_bd = din("ln_b", [DEPTH, 3, 128, D])
    o_memkv = dout("o_memkv", [DEPTH, MEM, 2 * D])
    o_ssm_p = dout("o_ssm_p", [2, M_DI, 128])
    o_conv_p = dout("o_conv_p", [2, 3, M_CH])
    o_ssm_s = dout("o_ssm_s", [2, M_DI, 128])
    o_conv_s = dout("o_conv_s", [2, 3, M_CH])
    if "mem" in stages:
        mem_wkv = din("mem_wkv", [DEPTH, D, 2 * D])
        mem_wq = din("mem_wq", [DEPTH, D, D])
        mem_wo = din("mem_wo", [DEPTH, D, D])
        cmk = din("cmk", [DEPTH, MEM, 2 * D])
    if "fox" in stages:
        f_win = din("f_win", [D, 3088])
        f_wout = din("f_wout", [D, D])
        f_bf = din("f_bf", [128, 16])
        sel16_d = din("sel16", [16, 16 * 128])
        mneg_d = din("mneg", [128, 128])
        f_kpool = din("f_kpool", [1280 * 128, 512])
        f_vpool = din("f_vpool", [1280 * 128, 512])
        f_lpool = din("f_lpool", [1280 * 128, 16])
        pt_d = din("pt", [1, 128], I32)
        iota_d = din("iota", [128, 1])
        tris_d = din("tris", [128, 128])
        mask16_d = din("mask16", [16, 4])
        o_fox_kv = dout("o_fox_kv", [TTOK, 1024])
        o_fox_lf = dout("o_fox_lf", [TTOK, 16])
    if "nsa" in stages:
        n_win = din("n_win", [D, 5168])
        n_winbuf = din("n_winbuf", [512, 1024])
        n_wout = din("n_wout", [D, D])
        n_wp = din("n_wp", [2, 4, SEQ, 128])
        n_proj = din("n_proj", [2, 4, 128, 128])
        n_bg = din("n_bg", [128, 48])
        n_cm01 = din("n_cm01", [SEQ, 128])
        n_ov = din("n_ov", [128, 32])
        n_fv = din("n_fv", [SEQ, 3, 32])
        n_eaug = din("n_eaug", [33, SEQ])
        n_mfar = din("n_mfar", [128, 128])
        mneg_d2 = din("mneg2", [128, 128])
        n_pools = [din(f"n_pool{q}", [1280 * 128, 512]) for q in range(4)]
        n_wps = din("n_wps", [2, 4, 128, 17, 128])
        n_ovs = din("n_ovs", [9, 128, 257])
        n_fs = din("n_fs", [16, 257])
        n_gs = din("n_gs", [16, 16])
        o_nsa_kv = dout("o_nsa_kv", [TTOK, 2048])
        o_nsa_win_p = dout("o_nsa_win_p", [512, 1024])
        o_nsa_win_s = dout("o_nsa_win_s", [512, 1024])
    if "peer" in stages:
        peer_wq = din("peer_wq", [DEPTH, D, D])
        p_skT = din("p_skT", [DEPTH, 128, 16, 128])
        NPL = int(os.environ.get("K_LAYERS", "4"))
        p_uT = din("p_uT", [NPL, D, 16384])
        peer_v = din("peer_v", [NPL, 16384, D])
        pe_E = dscr("pe_E", [TTOK, 2064])
        pe_acc = dscr("pe_acc", [TTOK, D])
    if "mamba0" in stages:
        m_win = din("m_win", [2, D, M_IN])
        m_wout = din("m_wout", [2, M_DI, D])
        m_convw = din("m_convw", [2, 128, 48, 4])
        m_convb = din("m_convb", [2, 128, 48])
        m_dtb = din("m_dtb", [2, 128, 64])
        m_alog = din("m_alog", [2, 128, 64])
        m_dsk = din("m_dsk", [2, 128, 64])
        m_ng = din("m_ng", [2, 128, 32])
        m_cs = din("m_cs", [2, 3, M_CH])
        m_ss = din("m_ss", [2, M_DI, 128])
        m_cwr = din("m_cwr", [2, 4, M_CH])
        m_cbr = din("m_cbr", [2, M_CH])
    res = dscr("res", [TTOK, D])
    vbuf = dscr("vbuf", [TTOK, D])
    ynT_d = dscr("ynT_d", [M_DI, TTOK], BF16)
    o_y = dout("o_y", [TTOK, D])
    if dbg:
        o_dbg = dout("o_dbg", [TTOK, D])
        o_dbg2 = dout("o_dbg2", [D, TTOK], BF16)

    with ExitStack() as stack:
        k = MK(nc, stack)
        ident = k.sb("ident_f", [128, 128], F32)
        identb = k.sb("ident_b", [128, 128], BF16)
        trit = k.sb("trit", [128, 128], F32)
        ones = k.sb("ones_f", [128, 128], F32)
        zeros = k.sb("zeros_f", [128, 512], F32)
        zerob = k.sb("zeros_b", [128, 512], BF16)
        t_const = Tok()
        ch_const = k.dma_chan("ch_const")
        k.dma("sp", ident[:], ident_d, ch_const, writes=[t_const])
        k.dma("sp", trit[:], trit_d, ch_const, writes=[t_const])
        k.op("act", lambda e: e.copy(out=identb[:], in_=ident[:]), reads=[t_const], writes=[t_const])
        k.op("dve", lambda e: e.memset(ones[:], 1.0), writes=[t_const])
        k.op("dve", lambda e: e.memset(zeros[:], 0.0), writes=[t_const])
        k.op("dve", lambda e: e.memset(zerob[:], 0.0), writes=[t_const])

        hT = k.sb("hT", [128, DC, TTOK], BF16)
        t_hT = [Tok() for _ in range(TT)]
        t_res = [Tok() for _ in range(TT)]
        t_vbuf = [Tok() for _ in range(TT)]
        ch_res = [k.dma_chan("ch_res") for _ in range(TT)]
        ch_vbuf = [k.dma_chan("ch_vbuf") for _ in range(TT)]
        t_ynTd = [Tok() for _ in range(TT)]
        ch_ynTd = [k.dma_chan("ch_ynTd") for _ in range(TT)]
        ch_dbg = k.dma_chan("ch_dbg")

        PB = [k.ps(f"pb{i}", [128, 512], F32) for i in range(8)]
        t_PB = [Tok(excl=True) for _ in range(8)]

        class PRing:
            def __init__(self, idx):
                self.idx = list(idx)
                self.i = 0

            def next(self):
                j = self.idx[self.i % len(self.idx)]
                self.i += 1
                return PB[j], t_PB[j]

        def tsl(tt):
            return slice(tt * 128, (tt + 1) * 128)

        def gs8(g):
            return slice(g * 8, (g + 1) * 8)

        def transpose_rows(src, t_src, dst3, t_dst, xb_ring, pr):
            xb, t_xb = xb_ring.next()
            k.op("act", lambda e: e.copy(out=xb[:], in_=src), reads=[t_src], writes=[t_xb])
            for q in range(4):
                pt, t_pt = pr.next()
                ptb = pt[:].bitcast(BF16)
                for c4 in range(4):
                    c = q * 4 + c4
                    k.op("pe", lambda e: e.transpose(out=ptb[:, c4 * 128:(c4 + 1) * 128],
                                                     in_=xb[:, c * 128:(c + 1) * 128], identity=identb[:]),
                         reads=[t_xb, t_const], writes=[t_pt])
                k.op("dve", lambda e: e.tensor_copy(out=dst3[:, q * 4:(q + 1) * 4, :],
                                                    in_=ptb[:, 0:512].rearrange("p (c t) -> p c t", c=4)),
                     reads=[t_pt], writes=[t_dst])

        with k.scope() as st:
            xin = Ring(k, "xin", 2, [128, D], F32, stack=st, chan=True)
            xb_ring = Ring(k, "xb", 2, [128, D], BF16, stack=st)
            pr = PRing([0, 1, 2, 3])
            for tt in range(NT):
                k.dma("sp", res[tsl(tt), :], xp[tsl(tt), :], ch_res[tt], writes=[t_res[tt]])
            for q in range(4):
                k.dma("sp", res[SEQ + 1:TTOK, q * 512:(q + 1) * 512], zeros[0:127, :], ch_res[NT], reads=[t_const], writes=[t_res[NT]])
            k.dma("sp", res[SEQ:SEQ + 1, :], xs, ch_res[NT], writes=[t_res[NT]])
            for q in range(8):
                k.dma("sp", ynT_d[q * 512:(q + 1) * 512, SEQ:TTOK].rearrange("(c p) t -> p c t", p=128),
                      zerob[:].rearrange("p (c t) -> p c t", c=4), ch_ynTd[NT], reads=[t_const], writes=[t_ynTd[NT]])
            for tt in range(TT):
                xt, t_xt, ch = xin.next()
                src = xp[tsl(tt), :] if tt < NT else res[tsl(tt), :]
                k.dma("sp", xt[:], src, ch, reads=([] if tt < NT else [t_res[tt]]), writes=[t_xt])
                transpose_rows(xt[:], t_xt, hT[:, :, tsl(tt)], t_hT[tt], xb_ring, pr)

        def proj_ln(get_aT, KC, W, li, lk, wbufs=2):
            proj_v(get_aT, KC, W, wbufs)
            ln_pass(li, lk)

        def proj_v(get_aT, KC, W, wbufs=2):
            with k.scope() as st:
                wr = Ring(k, "pw", wbufs, [128, KC, 512], BF16, stack=st, chan=True)
                hb_r = Ring(k, "phb", 2, [128, 512], F32, stack=st, chan=True)
                vb_r = Ring(k, "pvb", 2, [128, 512], F32, stack=st)
                pm = PRing([0, 1, 2, 3])
                src_state = {}
                for fb in range(4):
                    wb, t_wb, chw = wr.next()
                    k.dma("pool", wb[:], W.rearrange("(c p) f -> p c f", p=128)[:, :, fb * 512:(fb + 1) * 512],
                          chw, writes=[t_wb])
                    for tt in range(TT):
                        aT, t_aT = get_aT(tt, st, src_state)
                        ps, t_ps = pm.next()
                        for c in range(KC):
                            k.op("pe", lambda e: e.matmul(out=ps[:], lhsT=aT[:, c, :], rhs=wb[:, c, :],
                                                          start=(c == 0), stop=(c == KC - 1)),
                                 reads=[t_aT, t_wb], writes=[t_ps])
                        hb, t_hb, chh = hb_r.next()
                        k.dma("sp", hb[:], res[tsl(tt), fb * 512:(fb + 1) * 512], chh, reads=[t_res[tt]], writes=[t_hb])
                        vb, t_vb = vb_r.next()
                        k.op("dve", lambda e: e.scalar_tensor_tensor(out=vb[:], in0=hb[:], scalar=float(DN_ALPHA), in1=ps[:],
                                                                     op0=ALU.mult, op1=ALU.add),
                             reads=[t_hb, t_ps], writes=[t_vb])
                        k.dma("sp", vbuf[tsl(tt), fb * 512:(fb + 1) * 512], vb[:], ch_vbuf[tt], reads=[t_vb], writes=[t_vbuf[tt]])
        def ln_pass(li, lk):
            with k.scope() as st:
                gb = k.sb("ln_gsb", [128, D], F32, st)
                bb = k.sb("ln_bsb", [128, D], F32, st)
                t_gb = Tok()
                ch_gb = k.dma_chan("ch_gb")
                k.dma("sp", gb[:], ln_gd[li, lk], ch_gb, writes=[t_gb])
                k.dma("sp", bb[:], ln_bd[li, lk], ch_gb, writes=[t_gb])
                v_r = Ring(k, "lnv", 2, [128, D], F32, stack=st, chan=True)
                sq_r = Ring(k, "lnsq", 1, [128, D], F32, stack=st)
                hn_r = Ring(k, "lnh", 2, [128, D], F32, stack=st)
                st_r = Ring(k, "lnst", 2, [128, 8], F32, stack=st)
                xb_ring = Ring(k, "lnxb", 2, [128, D], BF16, stack=st)
                pr = PRing([4, 5, 6, 7])
                for tt in range(TT):
                    v, t_v, chv = v_r.next()
                    k.dma("sp", v[:], vbuf[tsl(tt), :], chv, reads=[t_vbuf[tt]], writes=[t_v])
                    sq, t_sq = sq_r.next()
                    s8, t_s8 = st_r.next()
                    k.op("dve", lambda e: e.reduce_sum(out=s8[:, 0:1], in_=v[:], axis=AX.X), reads=[t_v], writes=[t_s8])
                    k.op("act", lambda e: e.activation(out=sq[:], in_=v[:], func=AF.Square, accum_out=s8[:, 1:2]),
                         reads=[t_v], writes=[t_sq, t_s8])
                    k.op("dve", lambda e: e.tensor_scalar(out=s8[:, 2:3], in0=s8[:, 0:1], scalar1=1.0 / D, scalar2=None,
                                                          op0=ALU.mult), reads=[t_s8], writes=[t_s8])
                    k.op("dve", lambda e: e.tensor_tensor(out=s8[:, 3:4], in0=s8[:, 2:3], in1=s8[:, 2:3], op=ALU.mult),
                         reads=[t_s8], writes=[t_s8])
                    k.op("dve", lambda e: e.scalar_tensor_tensor(out=s8[:, 3:4], in0=s8[:, 1:2], scalar=1.0 / D,
                                                                 in1=s8[:, 3:4], op0=ALU.mult, op1=ALU.subtract),
                         reads=[t_s8], writes=[t_s8])
                    k.op("act", lambda e: e.activation(out=s8[:, 4:5], in_=s8[:, 3:4], func=AF.Sqrt, bias=float(LN_EPS)),
                         reads=[t_s8], writes=[t_s8])
                    k.op("dve", lambda e: e.reciprocal(out=s8[:, 5:6], in_=s8[:, 4:5]), reads=[t_s8], writes=[t_s8])
                    k.op("dve", lambda e: e.tensor_scalar(out=s8[:, 6:7], in0=s8[:, 2:3], scalar1=s8[:, 5:6], scalar2=-1.0,
                                                          op0=ALU.mult, op1=ALU.mult), reads=[t_s8], writes=[t_s8])
                    hn, t_hn = hn_r.next()
                    k.op("act", lambda e: e.activation(out=hn[:], in_=v[:], func=AF.Identity, scale=s8[:, 5:6],
                                                       bias=s8[:, 6:7]), reads=[t_v, t_s8], writes=[t_hn])
                    k.op("dve", lambda e: e.tensor_tensor(out=hn[:], in0=hn[:], in1=gb[:], op=ALU.mult),
                         reads=[t_hn, t_gb], writes=[t_hn])
                    k.op("dve", lambda e: e.tensor_tensor(out=hn[:], in0=hn[:], in1=bb[:], op=ALU.add),
                         reads=[t_hn, t_gb], writes=[t_hn])
                    k.dma("sp", res[tsl(tt), :], hn[:], ch_res[tt], reads=[t_hn], writes=[t_res[tt]])
                    transpose_rows(hn[:], t_hn, hT[:, :, tsl(tt)], t_hT[tt], xb_ring, pr)


        def featproj(W, ncols, dst, t_dst):
            with k.scope() as st:
                wr = Ring(k, "fpw", 2, [128, DC, 512], BF16, stack=st, chan=True)
                pm = PRing([0, 1, 2, 3])
                n = 0
                for fb in range((ncols + 511) // 512):
                    cols = min(512, ncols - fb * 512)
                    wb, t_wb, chw = wr.next()
                    k.dma("pool", wb[:, :, 0:cols], W.rearrange("(c p) f -> p c f", p=128)[:, :, fb * 512:fb * 512 + cols],
                          chw, writes=[t_wb])
                    for q in range(cols // 128):
                        fc = fb * 4 + q
                        for tb in range(5):
                            t0 = tb * 512
                            tw = 512 if tb < 4 else 128
                            tiles = list(range(tb * 4, tb * 4 + 4)) if tb < 4 else [NT]
                            ps, t_ps = pm.next()
                            for kc in range(DC):
                                k.op("pe", lambda e: e.matmul(out=ps[:, 0:tw], lhsT=wb[:, kc, q * 128:(q + 1) * 128],
                                                              rhs=hT[:, kc, t0:t0 + tw], start=(kc == 0), stop=(kc == DC - 1)),
                                     reads=[t_wb] + [t_hT[t] for t in tiles], writes=[t_ps])
                            eng = "act" if n % 2 == 0 else "dve"
                            n += 1
                            if eng == "act":
                                k.op("act", lambda e: e.copy(out=dst[:, fc, t0:t0 + tw], in_=ps[:, 0:tw]),
                                     reads=[t_ps], writes=[t_dst[t] for t in tiles])
                            else:
                                k.op("dve", lambda e: e.tensor_copy(out=dst[:, fc, t0:t0 + tw], in_=ps[:, 0:tw]),
                                     reads=[t_ps], writes=[t_dst[t] for t in tiles])

        def transpose_bf(src_fn, nblk, dst_fn, t_src, t_dst, pr):
            for i0 in range(0, nblk, 4):
                nn = min(4, nblk - i0)
                pt, t_pt = pr.next()
                ptb = pt[:].bitcast(BF16)
                for q in range(nn):
                    k.op("pe", lambda e: e.transpose(out=ptb[:, q * 128:(q + 1) * 128], in_=src_fn(i0 + q), identity=identb[:]),
                         reads=[t_src, t_const], writes=[t_pt])
                k.op("act", lambda e: e.copy(out=dst_fn(i0, nn), in_=ptb[:, 0:nn * 128].rearrange("p (c t) -> p c t", c=nn)),
                     reads=[t_pt], writes=[t_dst])

        def memattn_layer(i, only_kv=False):
            MSC = float(512 ** -0.5)
            with k.scope() as sl:
                KT = k.sb("ma_KT", [128, DC, MEM], BF16, sl)
                Vv = k.sb("ma_V", [128, 2, D], BF16, sl)
                KTs = k.sb("ma_KTs", [128, DC, MEM], BF16, sl)
                Vs = k.sb("ma_Vs", [128, 2, D], BF16, sl)
                t_kv, t_kvs = Tok(), Tok()
                with k.scope() as st:
                    memT = k.sb("memT", [128, DC, MEM], BF16, st)
                    t_memT = Tok()
                    Kb = k.sb("ma_Kb", [128, 2, D], BF16, st)
                    t_Kb = Tok()
                    xin = Ring(k, "xinm", 2, [128, D], F32, stack=st, chan=True)
                    xb_ring = Ring(k, "xbm", 2, [128, D], BF16, stack=st)
                    pr = PRing([0, 1, 2, 3])
                    for mt in range(2):
                        xt, t_xt, ch = xin.next()
                        k.dma("sp", xt[:], memp[tsl(mt), :], ch, writes=[t_xt])
                        transpose_rows(xt[:], t_xt, memT[:, :, tsl(mt)], t_memT, xb_ring, pr)
                    wr = Ring(k, "wkv", 2, [128, DC, 512], BF16, stack=st, chan=True)
                    pm = PRing([4, 5])
                    ost = Ring(k, "ost", 2, [128, 512], F32, stack=st, chan=True)
                    for fb in range(8):
                        wb, t_wb, chw = wr.next()
                        k.dma("pool", wb[:], mem_wkv[i].rearrange("(c p) f -> p c f", p=128)[:, :, fb * 512:(fb + 1) * 512],
                              chw, writes=[t_wb])
                        for mt in range(2):
                            ps, t_ps = pm.next()
                            for c in range(DC):
                                k.op("pe", lambda e: e.matmul(out=ps[:], lhsT=memT[:, c, tsl(mt)],
                                                              rhs=wb[:, c, :], start=(c == 0), stop=(c == DC - 1)),
                                     reads=[t_memT, t_wb], writes=[t_ps])
                            ob, t_ob, cho = ost.next()
                            k.op("act", lambda e: e.copy(out=ob[:], in_=ps[:]), reads=[t_ps], writes=[t_ob])
                            k.dma("sp", o_memkv[i, tsl(mt), fb * 512:(fb + 1) * 512], ob[:], cho, reads=[t_ob])
                            if fb < 4:
                                k.op("dve", lambda e: e.tensor_copy(out=Kb[:, mt, fb * 512:(fb + 1) * 512], in_=ps[:]),
                                     reads=[t_ps], writes=[t_Kb])
                            else:
                                k.op("dve", lambda e: e.tensor_copy(out=Vv[:, mt, (fb - 4) * 512:(fb - 3) * 512], in_=ps[:]),
                                     reads=[t_ps], writes=[t_kv])
                    for mt in range(2):
                        transpose_bf(lambda c: Kb[:, mt, c * 128:(c + 1) * 128], DC,
                                     lambda c0, n: KT[:, c0:c0 + n, tsl(mt)], t_Kb, t_kv, pr)
                    for mt in range(2):
                        for half in range(2):
                            xt, t_xt, ch = xin.next()
                            k.dma("sp", xt[:], cmk[i, tsl(mt), half * D:(half + 1) * D], ch, writes=[t_xt])
                            if half == 0:
                                xb, t_xb = xb_ring.next()
                                k.op("dve", lambda e: e.tensor_copy(out=xb[:], in_=xt[:]), reads=[t_xt], writes=[t_xb])
                                transpose_bf(lambda c: xb[:, c * 128:(c + 1) * 128], DC,
                                             lambda c0, n: KTs[:, c0:c0 + n, tsl(mt)], t_xb, t_kvs, pr)
                            else:
                                k.op("dve", lambda e: e.tensor_copy(out=Vs[:, mt, :], in_=xt[:]), reads=[t_xt], writes=[t_kvs])
                if only_kv:
                    return
                qT = k.sb("qT_all", [128, DC, TTOK], BF16, sl)
                t_q = [Tok() for _ in range(TT)]
                featproj(mem_wq[i], D, qT, t_q)
                with k.scope() as st:
                    p_r = Ring(k, "ma_p", 2, [128, 4, 256], F32, stack=st)
                    pn_r = Ring(k, "ma_pn", 2, [128, 4, 256], BF16, stack=st)
                    pT_r = Ring(k, "ma_pT", 2, [128, 8, 128], BF16, stack=st)
                    sm_r = Ring(k, "ma_sm", 2, [128, 16], F32, stack=st)
                    for tt in range(TT):
                        kt, vv, t_k = (KT, Vv, t_kv) if tt < NT else (KTs, Vs, t_kvs)
                        for h in range(4):
                            bk, t_bk = PB[h // 2], t_PB[h // 2]
                            off = (h % 2) * 256
                            for dc in range(4):
                                k.op("pe", lambda e: e.matmul(out=bk[:, off:off + 256], lhsT=qT[:, h * 4 + dc, tsl(tt)],
                                                              rhs=kt[:, h * 4 + dc, :], start=(dc == 0), stop=(dc == 3)),
                                     reads=[t_q[tt], t_k], writes=[t_bk])
                        sm, t_sm = sm_r.next()
                        for hb in range(2):
                            k.op("dve", lambda e: e.reduce_max(out=sm[:, hb * 2:hb * 2 + 2],
                                                               in_=PB[hb][:].rearrange("p (a m) -> p a m", a=2), axis=AX.X),
                                 reads=[t_PB[hb]], writes=[t_sm])
                        k.op("dve", lambda e: e.tensor_scalar(out=sm[:, 4:8], in0=sm[:, 0:4], scalar1=-MSC, scalar2=None, op0=ALU.mult),
                             reads=[t_sm], writes=[t_sm])
                        p, t_p = p_r.next()
                        for h in range(4):
                            off = (h % 2) * 256
                            k.op("act", lambda e: e.activation(out=p[:, h, :], in_=PB[h // 2][:, off:off + 256], func=AF.Exp,
                                                               scale=MSC, bias=sm[:, 4 + h:5 + h], accum_out=sm[:, 8 + h:9 + h]),
                                 reads=[t_PB[h // 2], t_sm], writes=[t_p, t_sm])
                        k.op("dve", lambda e: e.reciprocal(out=sm[:, 12:16], in_=sm[:, 8:12]), reads=[t_sm], writes=[t_sm])
                        pn, t_pn = pn_r.next()
                        k.op("dve", lambda e: e.tensor_tensor(out=pn[:], in0=p[:],
                                                              in1=sm[:, 12:16].unsqueeze(2).to_broadcast([128, 4, 256]), op=ALU.mult),
                             reads=[t_p, t_sm], writes=[t_pn])
                        pT, t_pT = pT_r.next()
                        transpose_bf(lambda j: pn[:, j // 2, (j % 2) * 128:(j % 2 + 1) * 128], 8,
                                     lambda j0, n: pT[:, j0:j0 + n, :], t_pn, t_pT, PRing([2, 3]))
                        for h in range(4):
                            bk, t_bk = PB[4 + h % 2], t_PB[4 + h % 2]
                            for dc in range(4):
                                for mc in range(2):
                                    k.op("pe", lambda e: e.matmul(out=bk[:, dc * 128:(dc + 1) * 128],
                                                                  lhsT=vv[:, mc, h * 512 + dc * 128:h * 512 + (dc + 1) * 128],
                                                                  rhs=pT[:, h * 2 + mc, :], start=(mc == 0), stop=(mc == 1)),
                                         reads=[t_k, t_pT], writes=[t_bk])
                            k.op("act", lambda e: e.copy(out=qT[:, h * 4:(h + 1) * 4, tsl(tt)],
                                                         in_=bk[:].rearrange("p (c t) -> p c t", c=4)),
                                 reads=[t_bk], writes=[t_q[tt]])
                proj_v(lambda tt, st, state: (qT[:, :, tsl(tt)], t_q[tt]), DC, mem_wo[i], wbufs=1)
            ln_pass(i, 1)


        def peer_layer(i):
            NEB = int(os.environ.get("P_NEB", "16"))
            t_E = [Tok() for _ in range(TT)]
            ch_E = [k.dma_chan("ch_E") for _ in range(TT)]
            t_acc = [Tok() for _ in range(TT)]
            ch_acc = [k.dma_chan("ch_acc") for _ in range(TT)]
            with k.scope() as sl:
                qT = k.sb("pq_all", [128, DC, TTOK], BF16, sl)
                t_q = [Tok() for _ in range(TT)]
                skT = k.sb("p_skT_s", [128, 16, 128], BF16, sl)
                t_sk = Tok()
                k.dma("pool", skT[:], p_skT[i], k.dma_chan("ch_sk"), writes=[t_sk])
                featproj(peer_wq[i], D, qT, t_q)
                with k.scope() as st:
                    Et_r = Ring(k, "p_Et", 2, [128, 2064], F32, stack=st)
                    sm_r = Ring(k, "p_sm", 2, [128, 32], F32, stack=st)
                    v16_r = Ring(k, "p_v16", 2, [128, 16, 16], F32, stack=st)
                    tmp_r = Ring(k, "p_tmp", 2, [128, 256], F32, stack=st)
                    cand_r = Ring(k, "p_cand", 2, [128, 8, 256], F32, stack=st)
                    c8_r = Ring(k, "p_c8", 2, [128, 8, 16], F32, stack=st)
                    for tt in range(TT):
                        for hk in range(16):
                            bk, t_bk = PB[hk // 4], t_PB[hk // 4]
                            k.op("pe", lambda e: e.matmul(out=bk[:, (hk % 4) * 128:(hk % 4 + 1) * 128], lhsT=qT[:, hk, tsl(tt)],
                                                          rhs=skT[:, hk, :], start=True, stop=True),
                                 reads=[t_q[tt], t_sk], writes=[t_bk])
                        sm, t_sm = sm_r.next()
                        for bq in range(4):
                            k.op("dve", lambda e: e.reduce_max(out=sm[:, bq * 4:bq * 4 + 4],
                                                               in_=PB[bq][:].rearrange("p (a n) -> p a n", a=4), axis=AX.X),
                                 reads=[t_PB[bq]], writes=[t_sm])
                        k.op("dve", lambda e: e.tensor_scalar(out=sm[:, 16:32], in0=sm[:, 0:16], scalar1=-1.0, scalar2=None, op0=ALU.mult),
                             reads=[t_sm], writes=[t_sm])
                        Et, t_Et = Et_r.next()
                        for hk in range(16):
                            k.op("act", lambda e: e.activation(out=Et[:, hk * 128:(hk + 1) * 128],
                                                               in_=PB[hk // 4][:, (hk % 4) * 128:(hk % 4 + 1) * 128], func=AF.Exp,
                                                               bias=sm[:, 16 + hk:17 + hk]),
                                 reads=[t_PB[hk // 4], t_sm], writes=[t_Et])
                        v16, t_v16 = v16_r.next()
                        for hk in range(16):
                            tmp, t_tmp = tmp_r.next()
                            Eh = Et[:, hk * 128:(hk + 1) * 128]
                            k.op("dve", lambda e: e.max(out=v16[:, hk, 0:8], in_=Eh), reads=[t_Et], writes=[t_v16])
                            k.op("dve", lambda e: e.match_replace(out=tmp[:, 0:128], in_to_replace=v16[:, hk, 0:8], in_values=Eh,
                                                                  imm_value=-1.0), reads=[t_Et, t_v16], writes=[t_tmp])
                            k.op("dve", lambda e: e.max(out=v16[:, hk, 8:16], in_=tmp[:, 0:128]), reads=[t_tmp], writes=[t_v16])
                        cand, t_cand = cand_r.next()
                        c8, t_c8 = c8_r.next()
                        for h in range(8):
                            k.op("dve", lambda e: e.tensor_tensor(
                                out=cand[:, h, :].rearrange("p (a b) -> p a b", a=16),
                                in0=v16[:, 2 * h, :].unsqueeze(2).to_broadcast([128, 16, 16]),
                                in1=v16[:, 2 * h + 1, :].unsqueeze(1).to_broadcast([128, 16, 16]), op=ALU.mult),
                                 reads=[t_v16], writes=[t_cand])
                            tmp, t_tmp = tmp_r.next()
                            k.op("dve", lambda e: e.max(out=c8[:, h, 0:8], in_=cand[:, h, :]), reads=[t_cand], writes=[t_c8])
                            k.op("dve", lambda e: e.match_replace(out=tmp[:], in_to_replace=c8[:, h, 0:8], in_values=cand[:, h, :],
                                                                  imm_value=-1.0), reads=[t_cand, t_c8], writes=[t_tmp])
                            k.op("dve", lambda e: e.max(out=c8[:, h, 8:16], in_=tmp[:]), reads=[t_tmp], writes=[t_c8])
                            k.op("dve", lambda e: e.tensor_copy(out=Et[:, 2048 + h:2049 + h], in_=c8[:, h, 15:16]),
                                 reads=[t_c8], writes=[t_Et])
                            tmp2, t_tmp2 = tmp_r.next()
                            k.op("dve", lambda e: e.scalar_tensor_tensor(out=tmp2[:], in0=cand[:, h, :], scalar=c8[:, h, 15:16],
                                                                         in1=cand[:, h, :], op0=ALU.is_ge, op1=ALU.mult),
                                 reads=[t_cand, t_c8], writes=[t_tmp2])
                            k.op("dve", lambda e: e.reduce_sum(out=Et[:, 2056 + h:2057 + h], in_=tmp2[:], axis=AX.X),
                                 reads=[t_tmp2], writes=[t_Et])
                        k.op("dve", lambda e: e.reciprocal(out=Et[:, 2056:2064], in_=Et[:, 2056:2064]), reads=[t_Et], writes=[t_Et])
                        k.dma("sp", pe_E[tsl(tt), :], Et[:], ch_E[tt], reads=[t_Et], writes=[t_E[tt]])
            with k.scope() as sl:
                UT = k.sb("p_UT", [128, DC, 1024], BF16, sl)
                Vb = k.sb("p_Vb", [128, 8, D], BF16, sl)
                t_UT, t_Vb = Tok(), Tok()
                ch_UT, ch_Vb = k.dma_chan("ch_UT"), k.dma_chan("ch_Vb")
                Et_r = Ring(k, "p_Et2", 2, [128, 2064], F32, stack=sl, chan=True)
                acc_r = Ring(k, "p_acc", 2, [128, D], F32, stack=sl, chan=True)
                gA_r = Ring(k, "p_gA", 1, [128, 1024], F32, stack=sl)
                G_r = Ring(k, "p_G", 1, [128, 1024], F32, stack=sl)
                P_r = Ring(k, "p_P", 3, [128, 1024], F32, stack=sl)
                w_r = Ring(k, "p_w", 2, [128, 1024], BF16, stack=sl)
                wT_r = Ring(k, "p_wT", 2, [128, 8, 128], BF16, stack=sl)
                uT_src = p_uT[i].rearrange("(c p) e -> p c e", p=128)
                def emit_loads(eb_, tt_):
                    Et_, t_Et_, chE_ = Et_r.next()
                    k.dma("sp", Et_[:], pe_E[tsl(tt_), :], chE_, reads=[t_E[tt_]], writes=[t_Et_])
                    acc_, t_ac_, cha_ = acc_r.next()
                    if eb_ > 0:
                        k.dma("sp", acc_[:], pe_acc[tsl(tt_), :], cha_, reads=[t_acc[tt_]], writes=[t_ac_])
                    return Et_, t_Et_, acc_, t_ac_

                seq_ = [(eb_, tt_) for eb_ in range(NEB) for tt_ in range(TT)]
                pending = emit_loads(*seq_[0])
                for eb in range(NEB):
                    i0 = eb * 8
                    for hh in range(2):
                        k.dma("pool", UT[:, :, hh * 512:(hh + 1) * 512], uT_src[:, :, eb * 1024 + hh * 512:eb * 1024 + (hh + 1) * 512],
                              ch_UT, writes=[t_UT])
                    for hh in range(2):
                        k.dma("pool", Vb[:, hh * 4:(hh + 1) * 4, :],
                              peer_v[i, eb * 1024 + hh * 512:eb * 1024 + (hh + 1) * 512, :].rearrange("(c p) d -> p c d", p=128),
                              ch_Vb, writes=[t_Vb])
                    for tt in range(TT):
                        Et, t_Et, acc, t_ac = pending
                        nxt = eb * TT + tt + 1
                        if nxt < len(seq_):
                            pending = emit_loads(*seq_[nxt])
                        gA, t_gA = gA_r.next()
                        for eh in range(2):
                            for kc in range(DC):
                                k.op("pe", lambda e: e.matmul(out=PB[eh][:], lhsT=hT[:, kc, tsl(tt)], rhs=UT[:, kc, eh * 512:(eh + 1) * 512],
                                                              start=(kc == 0), stop=(kc == DC - 1)),
                                     reads=[t_hT[tt], t_UT], writes=[t_PB[eh]])
                            k.op("act", lambda e: e.activation(out=gA[:, eh * 512:(eh + 1) * 512], in_=PB[eh][:], func=AF.Gelu),
                                 reads=[t_PB[eh]], writes=[t_gA])
                        G, t_G = G_r.next()
                        for h in range(8):
                            P, t_P = P_r.next()
                            k.op("pool", lambda e: e.tensor_tensor(
                                out=P[:].rearrange("p (a b) -> p a b", a=8),
                                in0=Et[:, 2 * h * 128 + i0:2 * h * 128 + i0 + 8].unsqueeze(2).to_broadcast([128, 8, 128]),
                                in1=Et[:, (2 * h + 1) * 128:(2 * h + 2) * 128].unsqueeze(1).to_broadcast([128, 8, 128]), op=ALU.mult),
                                 reads=[t_Et], writes=[t_P])
                            M, t_M = P, t_P
                            k.op("dve", lambda e: e.scalar_tensor_tensor(out=P[:], in0=P[:], scalar=Et[:, 2048 + h:2049 + h], in1=P[:],
                                                                         op0=ALU.is_ge, op1=ALU.mult), reads=[t_Et], writes=[t_P])
                            if h == 0:
                                k.op("dve", lambda e: e.tensor_scalar(out=G[:], in0=M[:], scalar1=Et[:, 2056:2057], scalar2=None, op0=ALU.mult),
                                     reads=[t_M, t_Et], writes=[t_G])
                            else:
                                k.op("dve", lambda e: e.scalar_tensor_tensor(out=G[:], in0=M[:], scalar=Et[:, 2056 + h:2057 + h], in1=G[:],
                                                                             op0=ALU.mult, op1=ALU.add), reads=[t_M, t_Et, t_G], writes=[t_G])
                        w, t_w = w_r.next()
                        k.op("dve", lambda e: e.tensor_tensor(out=w[:], in0=G[:], in1=gA[:], op=ALU.mult), reads=[t_G, t_gA], writes=[t_w])
                        wT, t_wT = wT_r.next()
                        transpose_bf(lambda c: w[:, c * 128:(c + 1) * 128], 8, lambda c0, n: wT[:, c0:c0 + n, :], t_w, t_wT, PRing([2, 3]))
                        for db in range(4):
                            for ec in range(8):
                                k.op("pe", lambda e: e.matmul(out=PB[4 + db][:], lhsT=wT[:, ec, :], rhs=Vb[:, ec, db * 512:(db + 1) * 512],
                                                              start=(ec == 0), stop=(ec == 7)),
                                     reads=[t_wT, t_Vb], writes=[t_PB[4 + db]])
                            if eb > 0:
                                k.op("dve", lambda e: e.tensor_tensor(out=acc[:, db * 512:(db + 1) * 512], in0=acc[:, db * 512:(db + 1) * 512],
                                                                      in1=PB[4 + db][:], op=ALU.add), reads=[t_PB[4 + db], t_ac], writes=[t_ac])
                            else:
                                k.op("act", lambda e: e.copy(out=acc[:, db * 512:(db + 1) * 512], in_=PB[4 + db][:]),
                                     reads=[t_PB[4 + db]], writes=[t_ac])
                        k.dma("sp", pe_acc[tsl(tt), :], acc[:], ch_acc[tt], reads=[t_ac], writes=[t_acc[tt]])
            with k.scope() as st:
                r_r = Ring(k, "p_r", 2, [128, D], F32, stack=st, chan=True)
                a_r = Ring(k, "p_a", 2, [128, D], F32, stack=st, chan=True)
                for tt in range(TT):
                    r, t_r, chr_ = r_r.next()
                    a, t_a, cha = a_r.next()
                    k.dma("sp", r[:], res[tsl(tt), :], chr_, reads=[t_res[tt]], writes=[t_r])
                    k.dma("sp", a[:], pe_acc[tsl(tt), :], cha, reads=[t_acc[tt]], writes=[t_a])
                    k.op("dve", lambda e: e.scalar_tensor_tensor(out=a[:], in0=r[:], scalar=float(DN_ALPHA), in1=a[:],
                                                                 op0=ALU.mult, op1=ALU.add), reads=[t_r, t_a], writes=[t_a])
                    k.dma("sp", vbuf[tsl(tt), :], a[:], ch_vbuf[tt], reads=[t_a], writes=[t_vbuf[tt]])
            ln_pass(i, 2)


        def fox_layer(li):
            SC = float(128 ** -0.5)
            w_in = f_win.rearrange("(c p) f -> p c f", p=128)
            with k.scope() as sl:
                kT = k.sb("fx_kT", [128, 4, TTOK], BF16, sl)
                vtok = k.sb("fx_v", [128, TT, 512], BF16, sl)
                negcT = k.sb("fx_ncT", [16, SEQ], F32, sl)
                sel = k.sb("fx_sel", [16, 16 * 128], F32, sl)
                mneg = k.sb("fx_mneg", [128, 128], F32, sl)
                t_kv, t_nc, t_cst = Tok(), Tok(), Tok()
                knew = k.sb("fx_knew", [1, 1024], F32, sl)
                lfnew = k.sb("fx_lfnew", [1, 16], F32, sl)
                t_new = Tok()
                chc = k.dma_chan("ch_fxc")
                k.dma("sp", sel[:], sel16_d, chc, writes=[t_cst])
                k.dma("sp", mneg[:], mneg_d, chc, writes=[t_cst])
                with k.scope() as st:
                    wkv = k.sb("fx_wkv", [128, DC, 1024], BF16, st)
                    wf = k.sb("fx_wf", [128, DC, 16], BF16, st)
                    bf_s = k.sb("fx_bf", [128, 16], F32, st)
                    lf_all = k.sb("fx_lf", [128, NT, 16], F32, st)
                    t_w, t_lf = Tok(), Tok()
                    chw = k.dma_chan("ch_fxw")
                    for hh in range(2):
                        k.dma("pool", wkv[:, :, hh * 512:(hh + 1) * 512], w_in[:, :, 2048 + hh * 512:2048 + (hh + 1) * 512], chw, writes=[t_w])
                    k.dma("pool", wf[:], w_in[:, :, 3072:3088], chw, writes=[t_w])
                    k.dma("sp", bf_s[:], f_bf, chw, writes=[t_w])
                    kvo_r = Ring(k, "fx_kvo", 2, [128, 1024], F32, stack=st, chan=True)
                    kb_r = Ring(k, "fx_kb", 2, [128, 512], BF16, stack=st)
                    lt_r = Ring(k, "fx_lt", 2, [128, 4, 16], F32, stack=st, chan=True)
                    nc_r = Ring(k, "fx_nc", 2, [128, 16], F32, stack=st)
                    for tt in range(TT):
                        kvo, t_kvo, chk = kvo_r.next()
                        for hh in range(2):
                            for kc in range(DC):
                                k.op("pe", lambda e: e.matmul(out=PB[hh][:], lhsT=hT[:, kc, tsl(tt)], rhs=wkv[:, kc, hh * 512:(hh + 1) * 512],
                                                              start=(kc == 0), stop=(kc == DC - 1)), reads=[t_hT[tt], t_w], writes=[t_PB[hh]])
                            k.op("act", lambda e: e.copy(out=kvo[:, hh * 512:(hh + 1) * 512], in_=PB[hh][:]), reads=[t_PB[hh]], writes=[t_kvo])
                        k.dma("sp", o_fox_kv[tsl(tt), :], kvo[:], chk, reads=[t_kvo])
                        kb, t_kb = kb_r.next()
                        k.op("dve", lambda e: e.tensor_copy(out=kb[:], in_=kvo[:, 0:512]), reads=[t_kvo], writes=[t_kb])
                        k.op("dve", lambda e: e.tensor_copy(out=vtok[:, tt, :], in_=kvo[:, 512:1024]), reads=[t_kvo], writes=[t_kv])
                        transpose_bf(lambda c: kb[:, c * 128:(c + 1) * 128], 4, lambda c0, n: kT[:, c0:c0 + n, tsl(tt)], t_kb, t_kv, PRing([2, 3]))
                        for kc in range(DC):
                            k.op("pe", lambda e: e.matmul(out=PB[4][:, 0:16], lhsT=hT[:, kc, tsl(tt)], rhs=wf[:, kc, :],
                                                          start=(kc == 0), stop=(kc == DC - 1)), reads=[t_hT[tt], t_w], writes=[t_PB[4]])
                        lt, t_lt, chl = lt_r.next()
                        x0, ax, ee, lf = lt[:, 0, :], lt[:, 1, :], lt[:, 2, :], lt[:, 3, :]
                        k.op("dve", lambda e: e.tensor_tensor(out=x0, in0=PB[4][:, 0:16], in1=bf_s[:], op=ALU.add), reads=[t_PB[4], t_w], writes=[t_lt])
                        k.op("dve", lambda e: e.scalar_tensor_tensor(out=ax, in0=x0, scalar=-1.0, in1=x0, op0=ALU.mult, op1=ALU.max),
                             reads=[t_lt], writes=[t_lt])
                        k.op("act", lambda e: e.activation(out=ee, in_=ax, func=AF.Exp, scale=-1.0), reads=[t_lt], writes=[t_lt])
                        k.op("act", lambda e: e.activation(out=ee, in_=ee, func=AF.Ln, bias=1.0), reads=[t_lt], writes=[t_lt])
                        k.op("dve", lambda e: e.scalar_tensor_tensor(out=lf, in0=x0, scalar=0.0, in1=ee, op0=ALU.min, op1=ALU.subtract),
                             reads=[t_lt], writes=[t_lt])
                        k.dma("sp", o_fox_lf[tsl(tt), :], lf, chl, reads=[t_lt])
                        if tt == NT:
                            k.op("act", lambda e: e.copy(out=knew[:], in_=kvo[0:1, :]), reads=[t_kvo], writes=[t_new])
                            k.op("act", lambda e: e.copy(out=lfnew[:], in_=lt[0:1, 3, :]), reads=[t_lt], writes=[t_new])
                        if tt < NT:
                            k.op("dve", lambda e: e.tensor_copy(out=lf_all[:, tt, :], in_=lf), reads=[t_lt], writes=[t_lf])
                            for jj in range(tt + 1):
                                k.op("pe", lambda e: e.matmul(out=PB[5][:, 0:16], lhsT=(trit[:] if jj == tt else ones[:]), rhs=lf_all[:, jj, :],
                                                              start=(jj == 0), stop=(jj == tt)), reads=[t_lf, t_const], writes=[t_PB[5]])
                            ncx, t_ncx = nc_r.next()
                            k.op("dve", lambda e: e.tensor_scalar(out=ncx[:], in0=PB[5][:, 0:16], scalar1=-1.0 / SC, scalar2=None, op0=ALU.mult),
                                 reads=[t_PB[5]], writes=[t_ncx])
                            k.op("pe", lambda e: e.transpose(out=PB[6][0:16, 0:128], in_=ncx[:], identity=ident[:]),
                                 reads=[t_ncx, t_const], writes=[t_PB[6]])
                            k.op("act", lambda e: e.copy(out=negcT[:, tsl(tt)], in_=PB[6][0:16, 0:128]), reads=[t_PB[6]], writes=[t_nc])
                if os.environ.get("FOX_SAMPLE", "1") == "1":
                  with k.scope() as st:
                    NP_ = 128
                    pti = k.sb("fs_pti", [128, NP_], I32, st)
                    ptf = k.sb("fs_ptf", [128, NP_], F32, st)
                    idx = k.sb("fs_idx", [128, NP_], I32, st)
                    io = k.sb("fs_io", [128, 1], F32, st)
                    tris = k.sb("fs_tris", [128, 128], F32, st)
                    m16 = k.sb("fs_m16", [16, 4], F32, st)
                    t_ix = Tok()
                    chx = k.dma_chan("ch_fsx")
                    k.dma("sp", pti[:], pt_d.to_broadcast([128, NP_]), chx, writes=[t_ix])
                    k.dma("sp", io[:], iota_d, chx, writes=[t_ix])
                    k.dma("sp", tris[:], tris_d, chx, writes=[t_ix])
                    k.dma("sp", m16[:], mask16_d, chx, writes=[t_ix])
                    k.op("dve", lambda e: e.tensor_copy(out=ptf[:], in_=pti[:]), reads=[t_ix], writes=[t_ix])
                    k.op("dve", lambda e: e.tensor_scalar(out=ptf[:], in0=ptf[:], scalar1=128.0, scalar2=io[:, 0:1], op0=ALU.mult, op1=ALU.add),
                         reads=[t_ix], writes=[t_ix])
                    k.op("dve", lambda e: e.tensor_copy(out=idx[:], in_=ptf[:]), reads=[t_ix], writes=[t_ix])
                    qbc = k.sb("fs_qbc", [128, D], F32, st)
                    t_qr, t_qb = Tok(), Tok()
                    with k.scope() as sq:
                        qrow = k.sb("fs_qrow", [1, D], F32, sq)
                        wq_r = Ring(k, "fs_wq", 1, [128, DC, 512], BF16, stack=sq, chan=True)
                        for blk in range(4):
                            wq, t_wq, chq = wq_r.next()
                            k.dma("pool", wq[:], w_in[:, :, blk * 512:(blk + 1) * 512], chq, writes=[t_wq])
                            for kc in range(DC):
                                k.op("pe", lambda e: e.matmul(out=PB[0][0:1, :], lhsT=hT[:, kc, SEQ:SEQ + 1], rhs=wq[:, kc, :],
                                                              start=(kc == 0), stop=(kc == DC - 1)), reads=[t_hT[NT], t_wq], writes=[t_PB[0]])
                            k.op("act", lambda e: e.copy(out=qrow[0:1, blk * 512:(blk + 1) * 512], in_=PB[0][0:1, :]), reads=[t_PB[0]], writes=[t_qr])
                            k.op("pe", lambda e: e.matmul(out=PB[1][:], lhsT=ones[0:1, :], rhs=qrow[0:1, blk * 512:(blk + 1) * 512], start=True, stop=True),
                                 reads=[t_qr, t_const], writes=[t_PB[1]])
                            k.op("act", lambda e: e.copy(out=qbc[:, blk * 512:(blk + 1) * 512], in_=PB[1][:]), reads=[t_PB[1]], writes=[t_qb])
                    L = k.sb("fs_L", [128, NP_, 16], F32, st)
                    xa = k.sb("fs_xa", [128, NP_, 16], F32, st)
                    xb = k.sb("fs_xb", [128, NP_, 16], F32, st)
                    t_L, t_xa, t_xb = Tok(), Tok(), Tok()
                    chl2 = k.dma_chan("ch_fsl")
                    for pg in range(NP_):
                        k.idma(L[:, pg, :], f_lpool, idx[:, pg:pg + 1], chl2, reads=[t_ix], writes=[t_L])
                    src, t_src, dst, t_dst = L, t_L, xa, t_xa
                    dd = 1
                    while dd < NP_:
                        k.op("dve", lambda e: e.tensor_tensor(out=dst[:, 0:NP_ - dd, :], in0=src[:, 0:NP_ - dd, :], in1=src[:, dd:NP_, :], op=ALU.add),
                             reads=[t_src], writes=[t_dst])
                        k.op("dve", lambda e: e.tensor_copy(out=dst[:, NP_ - dd:NP_, :], in_=src[:, NP_ - dd:NP_, :]), reads=[t_src], writes=[t_dst])
                        if src is L:
                            src, t_src, dst, t_dst = xa, t_xa, xb, t_xb
                        else:
                            src, t_src, dst, t_dst = dst, t_dst, src, t_src
                        dd *= 2
                    k.op("dve", lambda e: e.tensor_tensor(out=dst[:], in0=src[:], in1=L[:], op=ALU.subtract), reads=[t_src, t_L], writes=[t_dst])
                    lfrep = k.sb("fs_lfrep", [1, NP_, 16], F32, st)
                    t_lr = Tok()
                    k.op("dve", lambda e: e.tensor_copy(out=lfrep[:], in_=lfnew[0:1, :].unsqueeze(1).to_broadcast([1, NP_, 16])),
                         reads=[t_new], writes=[t_lr])
                    Sx = k.sb("fs_S", [128, NP_ + 1, 16], F32, st)
                    t_S = Tok()
                    Lf = L[:].rearrange("p g h -> p (g h)")
                    Xf = dst[:].rearrange("p g h -> p (g h)")
                    Rf = lfrep[:].rearrange("p g h -> p (g h)")
                    Sf = Sx[:].rearrange("p g h -> p (g h)")
                    for blk in range(4):
                        cs = slice(blk * 512, (blk + 1) * 512)
                        k.op("pe", lambda e: e.matmul(out=PB[2][:], lhsT=tris[:], rhs=Lf[:, cs], start=True, stop=False), reads=[t_L, t_ix], writes=[t_PB[2]])
                        k.op("pe", lambda e: e.matmul(out=PB[2][:], lhsT=ones[:], rhs=Xf[:, cs], start=False, stop=False), reads=[t_dst, t_const], writes=[t_PB[2]])
                        k.op("pe", lambda e: e.matmul(out=PB[2][:], lhsT=ones[0:1, :], rhs=Rf[0:1, cs], start=False, stop=True), reads=[t_lr, t_const], writes=[t_PB[2]])
                        k.op("act", lambda e: e.copy(out=Sf[:, cs], in_=PB[2][:]), reads=[t_PB[2]], writes=[t_S])
                    kp_r = Ring(k, "fs_kp", 3, [128, 512], F32, stack=st, chan=True)
                    pr_r = Ring(k, "fs_pr", 2, [128, 4, 128], F32, stack=st)
                    s4_r = Ring(k, "fs_s4", 2, [128, 4], F32, stack=st)
                    for pg in range(NP_):
                        kp, t_kp, chk = kp_r.next()
                        k.idma(kp[:], f_kpool, idx[:, pg:pg + 1], chk, reads=[t_ix], writes=[t_kp])
                        for g in range(4):
                            pr, t_pr = pr_r.next()
                            k.op("dve", lambda e: e.tensor_tensor(out=pr[:], in0=kp[:, g * 128:(g + 1) * 128].unsqueeze(1).to_broadcast([128, 4, 128]),
                                                                  in1=qbc[:, g * 512:(g + 1) * 512].rearrange("p (r d) -> p r d", r=4), op=ALU.mult),
                                 reads=[t_kp, t_qb], writes=[t_pr])
                            s4, t_s4 = s4_r.next()
                            k.op("dve", lambda e: e.reduce_sum(out=s4[:], in_=pr[:], axis=AX.X), reads=[t_pr], writes=[t_s4])
                            k.op("dve", lambda e: e.scalar_tensor_tensor(out=Sx[:, pg, g * 4:(g + 1) * 4], in0=s4[:], scalar=SC,
                                                                         in1=Sx[:, pg, g * 4:(g + 1) * 4], op0=ALU.mult, op1=ALU.add),
                                 reads=[t_s4, t_S], writes=[t_S])
                    k.op("dve", lambda e: e.memset(Sx[:, NP_, :], -1e30), reads=[t_S], writes=[t_S])
                    for g in range(4):
                        pr, t_pr = pr_r.next()
                        k.op("dve", lambda e: e.tensor_tensor(out=pr[0:1], in0=knew[0:1, g * 128:(g + 1) * 128].unsqueeze(1).to_broadcast([1, 4, 128]),
                                                              in1=qbc[0:1, g * 512:(g + 1) * 512].rearrange("p (r d) -> p r d", r=4), op=ALU.mult),
                             reads=[t_new, t_qb], writes=[t_pr])
                        s4, t_s4 = s4_r.next()
                        k.op("dve", lambda e: e.reduce_sum(out=s4[0:1, :], in_=pr[0:1], axis=AX.X), reads=[t_pr], writes=[t_s4])
                        k.op("dve", lambda e: e.tensor_scalar(out=Sx[0:1, NP_, g * 4:(g + 1) * 4], in0=s4[0:1, :], scalar1=SC, scalar2=None, op0=ALU.mult),
                             reads=[t_s4, t_S], writes=[t_S])
                    sm = k.sb("fs_sm", [128, 64], F32, st)
                    sm16 = k.sb("fs_sm16", [16, 160], F32, st)
                    t_sm = Tok()
                    k.op("dve", lambda e: e.reduce_max(out=sm[:, 0:16], in_=Sx[:].rearrange("p g h -> p h g"), axis=AX.X), reads=[t_S], writes=[t_sm])
                    k.op("pe", lambda e: e.transpose(out=PB[3][0:16, 0:128], in_=sm[:, 0:16], identity=ident[:]), reads=[t_sm, t_const], writes=[t_PB[3]])
                    k.op("dve", lambda e: e.reduce_max(out=sm16[:, 0:1], in_=PB[3][0:16, 0:128], axis=AX.X), reads=[t_PB[3]], writes=[t_sm])
                    k.op("dve", lambda e: e.tensor_scalar(out=sm16[:, 16:32], in0=ident[0:16, 0:16], scalar1=sm16[:, 0:1], scalar2=None, op0=ALU.mult),
                         reads=[t_sm, t_const], writes=[t_sm])
                    k.op("pe", lambda e: e.matmul(out=PB[3][:, 128:144], lhsT=ones[0:16, :], rhs=sm16[:, 16:32], start=True, stop=True),
                         reads=[t_sm, t_const], writes=[t_PB[3]])
                    k.op("act", lambda e: e.copy(out=sm[:, 16:32], in_=PB[3][:, 128:144]), reads=[t_PB[3]], writes=[t_sm])
                    k.op("dve", lambda e: e.tensor_tensor(out=Sx[:], in0=Sx[:], in1=sm[:, 16:32].unsqueeze(1).to_broadcast([128, NP_ + 1, 16]), op=ALU.subtract),
                         reads=[t_sm, t_S], writes=[t_S])
                    k.op("act", lambda e: e.activation(out=Sx[:], in_=Sx[:], func=AF.Exp), reads=[t_S], writes=[t_S])
                    k.op("dve", lambda e: e.reduce_sum(out=sm[:, 32:48], in_=Sx[:].rearrange("p g h -> p h g"), axis=AX.X), reads=[t_S], writes=[t_sm])
                    k.op("pe", lambda e: e.matmul(out=PB[3][:, 256:272], lhsT=ones[:], rhs=sm[:, 32:48], start=True, stop=True),
                         reads=[t_sm, t_const], writes=[t_PB[3]])
                    k.op("dve", lambda e: e.reciprocal(out=sm[:, 48:64], in_=PB[3][:, 256:272]), reads=[t_PB[3]], writes=[t_sm])
                    k.op("dve", lambda e: e.tensor_tensor(out=Sx[:], in0=Sx[:], in1=sm[:, 48:64].unsqueeze(1).to_broadcast([128, NP_ + 1, 16]), op=ALU.mult),
                         reads=[t_sm, t_S], writes=[t_S])
                    for pg in range(NP_):
                        vp, t_vp, chv = kp_r.next()
                        k.idma(vp[:], f_vpool, idx[:, pg:pg + 1], chv, reads=[t_ix], writes=[t_vp])
                        k.op("pe", lambda e: e.matmul(out=PB[4][0:16, :], lhsT=Sx[:, pg, :], rhs=vp[:], start=(pg == 0), stop=False),
                             reads=[t_S, t_vp], writes=[t_PB[4]])
                    vn, t_vn, chv = kp_r.next()
                    k.op("dve", lambda e: e.memset(vn[:], 0.0), writes=[t_vn])
                    k.op("act", lambda e: e.copy(out=vn[0:1, :], in_=knew[0:1, 512:1024]), reads=[t_new, t_vn], writes=[t_vn])
                    k.op("pe", lambda e: e.matmul(out=PB[4][0:16, :], lhsT=Sx[:, NP_, :], rhs=vn[:], start=False, stop=True),
                         reads=[t_S, t_vn], writes=[t_PB[4]])
                    k.op("dve", lambda e: e.tensor_scalar(out=sm16[:, 32:160], in0=PB[4][0:16, 0:128], scalar1=m16[:, 0:1], scalar2=None, op0=ALU.mult),
                         reads=[t_PB[4], t_ix], writes=[t_sm])
                    for g in range(1, 4):
                        k.op("dve", lambda e: e.scalar_tensor_tensor(out=sm16[:, 32:160], in0=PB[4][0:16, g * 128:(g + 1) * 128], scalar=m16[:, g:g + 1],
                                                                     in1=sm16[:, 32:160], op0=ALU.mult, op1=ALU.add), reads=[t_PB[4], t_ix, t_sm], writes=[t_sm])
                    osb = k.sb("fs_osb", [16, 128], BF16, st)
                    t_osb = Tok()
                    k.op("act", lambda e: e.copy(out=osb[:], in_=sm16[:, 32:160]), reads=[t_sm], writes=[t_osb])
                    k.dma("sp", ynT_d[0:D, SEQ:SEQ + 1].rearrange("(h d) o -> h (d o)", h=16), osb[:], ch_ynTd[NT], reads=[t_osb], writes=[t_ynTd[NT]],
                          allow_slow_non_contiguous=True)
                for g in range(4):
                    with k.scope() as st:
                        qTg = k.sb("fx_qT", [128, 4, TTOK], BF16, st)
                        t_qg = [Tok() for _ in range(TT)]
                        featproj(f_win[:, g * 512:(g + 1) * 512], 512, qTg, t_qg)
                        p_r = Ring(k, "fx_p", 2, [128, SEQ], BF16, stack=st)
                        pT_r = Ring(k, "fx_pT", 2, [128, NT, 128], BF16, stack=st)
                        sm_r = Ring(k, "fx_sm", 2, [128, 16], F32, stack=st)
                        oT_r = Ring(k, "fx_oT", 2, [128, 4, 128], BF16, stack=st)
                        for qt in range(NT):
                            nkeys = (qt + 1) * 128
                            nb = (nkeys + 511) // 512
                            for r in range(4):
                                h = g * 4 + r
                                for bi in range(nb):
                                    c0 = bi * 512
                                    w_ = min(512, nkeys - c0)
                                    last = (bi == nb - 1)
                                    k.op("pe", lambda e: e.matmul(out=PB[bi][:, 0:w_], lhsT=qTg[:, r, tsl(qt)], rhs=kT[:, g, c0:c0 + w_],
                                                                  start=True, stop=False), reads=[t_qg[qt], t_kv], writes=[t_PB[bi]])
                                    k.op("pe", lambda e: e.matmul(out=PB[bi][:, 0:w_], lhsT=sel[:, h * 128:(h + 1) * 128], rhs=negcT[:, c0:c0 + w_],
                                                                  start=False, stop=(not last)), reads=[t_cst, t_nc], writes=[t_PB[bi]])
                                    if last:
                                        k.op("pe", lambda e: e.matmul(out=PB[bi][:, w_ - 128:w_], lhsT=ident[:], rhs=mneg[:], start=False, stop=True),
                                             reads=[t_cst, t_const], writes=[t_PB[bi]])
                                sm, t_sm = sm_r.next()
                                for bi in range(nb):
                                    w_ = min(512, nkeys - bi * 512)
                                    k.op("dve", lambda e: e.reduce_max(out=sm[:, bi:bi + 1], in_=PB[bi][:, 0:w_], axis=AX.X), reads=[t_PB[bi]], writes=[t_sm])
                                k.op("dve", lambda e: e.reduce_max(out=sm[:, 4:5], in_=sm[:, 0:nb], axis=AX.X), reads=[t_sm], writes=[t_sm])
                                k.op("dve", lambda e: e.tensor_scalar(out=sm[:, 5:6], in0=sm[:, 4:5], scalar1=-SC, scalar2=None, op0=ALU.mult),
                                     reads=[t_sm], writes=[t_sm])
                                p, t_p = p_r.next()
                                for bi in range(nb):
                                    c0 = bi * 512
                                    w_ = min(512, nkeys - c0)
                                    k.op("act", lambda e: e.activation(out=p[:, c0:c0 + w_], in_=PB[bi][:, 0:w_], func=AF.Exp, scale=SC,
                                                                       bias=sm[:, 5:6], accum_out=sm[:, 8 + bi:9 + bi]),
                                         reads=[t_PB[bi], t_sm], writes=[t_p, t_sm])
                                k.op("dve", lambda e: e.reduce_sum(out=sm[:, 6:7], in_=sm[:, 8:8 + nb], axis=AX.X), reads=[t_sm], writes=[t_sm])
                                k.op("dve", lambda e: e.reciprocal(out=sm[:, 7:8], in_=sm[:, 6:7]), reads=[t_sm], writes=[t_sm])
                                k.op("dve", lambda e: e.tensor_scalar(out=p[:, 0:nkeys], in0=p[:, 0:nkeys], scalar1=sm[:, 7:8], scalar2=None, op0=ALU.mult),
                                     reads=[t_sm, t_p], writes=[t_p])
                                pT, t_pT = pT_r.next()
                                transpose_bf(lambda jb: p[:, jb * 128:(jb + 1) * 128], qt + 1, lambda j0, n: pT[:, j0:j0 + n, :], t_p, t_pT, PRing([4, 5]))
                                for jb in range(qt + 1):
                                    k.op("pe", lambda e: e.matmul(out=PB[6][:, r * 128:(r + 1) * 128], lhsT=vtok[:, jb, g * 128:(g + 1) * 128],
                                                                  rhs=pT[:, jb, :], start=(jb == 0), stop=(jb == qt)),
                                         reads=[t_kv, t_pT], writes=[t_PB[6]])
                            oT, t_oT = oT_r.next()
                            k.op("act", lambda e: e.copy(out=oT[:], in_=PB[6][:].rearrange("p (c t) -> p c t", c=4)), reads=[t_PB[6]], writes=[t_oT])
                            k.dma("sp", ynT_d[g * 512:(g + 1) * 512, tsl(qt)].rearrange("(q p) t -> p q t", p=128), oT[:], ch_ynTd[qt],
                                  reads=[t_oT], writes=[t_ynTd[qt]])
            def get_aT(tt, st, state):
                if "ring" not in state:
                    state["ring"] = Ring(k, "f_aT", 2, [128, DC, 128], BF16, stack=st, chan=True)
                a, t_a, cha = state["ring"].next()
                k.dma("sp", a[:], ynT_d[0:D, tsl(tt)].rearrange("(c p) t -> p c t", p=128), cha, reads=[t_ynTd[tt]], writes=[t_a])
                return a, t_a
            proj_ln(get_aT, DC, f_wout, li, 0)


        def nsa_sample(li):
            SC = float(128 ** -0.5)
            NPG = 128
            w_in = n_win.rearrange("(c p) f -> p c f", p=128)
            with k.scope() as st:
                pti = k.sb("nz_pti", [128, NPG], I32, st)
                ptf = k.sb("nz_ptf", [128, NPG], F32, st)
                idx = k.sb("nz_idx", [128, NPG], I32, st)
                io = k.sb("nz_io", [128, 1], F32, st)
                m16 = k.sb("nz_m16", [16, 4], F32, st)
                gs = k.sb("nz_gs", [16, 16], F32, st)
                fs = k.sb("nz_fs", [16, 257], F32, st)
                bgr = k.sb("nz_bgr", [1, 48], F32, st)
                pj = k.sb("nz_pj", [128, 2, 4, 128], F32, st)
                t_ix = Tok()
                chx = k.dma_chan("ch_nzx")
                k.dma("sp", pti[:], pt_d.to_broadcast([128, NPG]), chx, writes=[t_ix])
                for dst_, src_ in ((io, iota_d), (m16, mask16_d), (gs, n_gs), (fs, n_fs), (bgr, n_bg[0:1, :])):
                    k.dma("sp", dst_[:], src_, chx, writes=[t_ix])
                k.dma("sp", pj[:], n_proj.rearrange("a g d e -> d a g e"), chx, writes=[t_ix])
                k.op("dve", lambda e: e.tensor_copy(out=ptf[:], in_=pti[:]), reads=[t_ix], writes=[t_ix])
                k.op("dve", lambda e: e.tensor_scalar(out=ptf[:], in0=ptf[:], scalar1=128.0, scalar2=io[:, 0:1], op0=ALU.mult, op1=ALU.add),
                     reads=[t_ix], writes=[t_ix])
                k.op("dve", lambda e: e.tensor_copy(out=idx[:], in_=ptf[:]), reads=[t_ix], writes=[t_ix])
                g3 = k.sb("nz_g3", [1, 3, 16], F32, st)
                gcol = k.sb("nz_gcol", [16, 4], F32, st)
                t_g = Tok()
                qpad = k.sb("nz_qpad", [128, 4, 16], F32, st)
                t_qp = Tok()
                newp = k.sb("nz_newp", [128, 6, 512], F32, st)
                t_np = Tok()
                seln = k.sb("nz_seln", [16, 258], F32, st)
                t_imp = Tok()
                with k.scope() as sp_:
                    prow = k.sb("nz_prow", [1, 5168], F32, sp_)
                    t_pr = Tok()
                    with k.scope() as sq:
                        wq_r = Ring(k, "nz_w", 2, [128, DC, 512], BF16, stack=sq, chan=True)
                        for blk in range(11):
                            c0 = blk * 512
                            w_ = min(512, 5168 - c0)
                            wq, t_wq, chq = wq_r.next()
                            k.dma("pool", wq[:, :, 0:w_], w_in[:, :, c0:c0 + w_], chq, writes=[t_wq])
                            for kc in range(DC):
                                k.op("pe", lambda e: e.matmul(out=PB[0][0:1, 0:w_], lhsT=hT[:, kc, SEQ:SEQ + 1], rhs=wq[:, kc, 0:w_],
                                                              start=(kc == 0), stop=(kc == DC - 1)), reads=[t_hT[NT], t_wq], writes=[t_PB[0]])
                            k.op("act", lambda e: e.copy(out=prow[0:1, c0:c0 + w_], in_=PB[0][0:1, 0:w_]), reads=[t_PB[0]], writes=[t_pr])
                    k.op("dve", lambda e: e.tensor_tensor(out=prow[0:1, 5120:5168], in0=prow[0:1, 5120:5168], in1=bgr[:], op=ALU.add),
                         reads=[t_pr, t_ix], writes=[t_pr])
                    k.op("act", lambda e: e.activation(out=prow[0:1, 5120:5168], in_=prow[0:1, 5120:5168], func=AF.Sigmoid), reads=[t_pr], writes=[t_pr])
                    k.op("dve", lambda e: e.tensor_copy(out=g3[:], in_=prow[0:1, 5120:5168].rearrange("o (h b) -> o b h", b=3)), reads=[t_pr], writes=[t_g])
                    for br in range(3):
                        k.op("pe", lambda e: e.matmul(out=PB[1][0:16, br:br + 1], lhsT=g3[0:1, br, :], rhs=ones[0:1, 0:1], start=True, stop=True),
                             reads=[t_g, t_const], writes=[t_PB[1]])
                    k.op("act", lambda e: e.copy(out=gcol[:, 0:3], in_=PB[1][0:16, 0:3]), reads=[t_PB[1]], writes=[t_g])
                    for h in range(16):
                        k.op("pe", lambda e: e.matmul(out=PB[2][:, h:h + 1], lhsT=prow[0:1, h * 128:(h + 1) * 128], rhs=ones[0:1, 0:1], start=True, stop=True),
                             reads=[t_pr, t_const], writes=[t_PB[2]])
                    k.op("dve", lambda e: e.memset(qpad[:], 0.0), writes=[t_qp])
                    for g in range(4):
                        k.op("act", lambda e: e.copy(out=qpad[:, g, g * 4:(g + 1) * 4], in_=PB[2][:, g * 4:(g + 1) * 4]), reads=[t_PB[2]], writes=[t_qp])
                    k.op("dve", lambda e: e.memset(newp[:], 0.0), writes=[t_np])
                    k.op("act", lambda e: e.copy(out=newp[0:1, :, :], in_=prow[0:1, 2048:5120].rearrange("o (b c) -> o b c", b=6)), reads=[t_pr, t_np], writes=[t_np])
                pg_r = Ring(k, "nz_pg", 4, [128, 512], F32, stack=st, chan=True)
                sm = k.sb("nz_sm", [16, 64], F32, st)
                t_sm = Tok()
                acc_o = k.sb("nz_acco", [16, 128], F32, st)
                t_ao = Tok()

                def get_page(branch, pg):
                    if pg == NPG:
                        return newp[:, branch, :], t_np
                    pt_, t_pt, chp = pg_r.next()
                    k.idma(pt_[:], n_pools[branch], idx[:, pg:pg + 1], chp, reads=[t_ix], writes=[t_pt])
                    return pt_[:], t_pt

                def softmax16(S2, n, t_S, gate_col=None):
                    k.op("dve", lambda e: e.reduce_max(out=sm[:, 0:1], in_=S2, axis=AX.X), reads=[t_S], writes=[t_sm])
                    k.op("dve", lambda e: e.tensor_scalar(out=sm[:, 1:2], in0=sm[:, 0:1], scalar1=-1.0, scalar2=None, op0=ALU.mult), reads=[t_sm], writes=[t_sm])
                    k.op("act", lambda e: e.activation(out=S2, in_=S2, func=AF.Exp, bias=sm[:, 1:2], accum_out=sm[:, 2:3]), reads=[t_S, t_sm], writes=[t_S, t_sm])
                    k.op("dve", lambda e: e.reciprocal(out=sm[:, 3:4], in_=sm[:, 2:3]), reads=[t_sm], writes=[t_sm])
                    if gate_col is not None:
                        k.op("dve", lambda e: e.tensor_tensor(out=sm[:, 3:4], in0=sm[:, 3:4], in1=gate_col, op=ALU.mult), reads=[t_sm, t_g], writes=[t_sm])
                    k.op("dve", lambda e: e.tensor_scalar(out=S2, in0=S2, scalar1=sm[:, 3:4], scalar2=None, op0=ALU.mult), reads=[t_S, t_sm], writes=[t_S])

                def select_add(pbank, first):
                    for g in range(4):
                        if first and g == 0:
                            k.op("dve", lambda e: e.tensor_scalar(out=acc_o[:], in0=pbank[0:16, 0:128], scalar1=m16[:, 0:1], scalar2=None, op0=ALU.mult),
                                 reads=[t_PB[PB.index(pbank)], t_ix], writes=[t_ao])
                        else:
                            k.op("dve", lambda e: e.scalar_tensor_tensor(out=acc_o[:], in0=pbank[0:16, g * 128:(g + 1) * 128], scalar=m16[:, g:g + 1],
                                                                         in1=acc_o[:], op0=ALU.mult, op1=ALU.add),
                                 reads=[t_PB[PB.index(pbank)], t_ix, t_ao], writes=[t_ao])

                with k.scope() as scm:
                    kcT = k.sb("nz_kcT", [128, 4, 1152], F32, scm)
                    vca = k.sb("nz_vc", [128, 9, 512], F32, scm)
                    t_kc = Tok()
                    with k.scope() as sc_:
                        wps = k.sb("nz_wps", [128, 4, 17, 128], F32, sc_)
                        pl_r = Ring(k, "nz_pl", 2, [128, 128], F32, stack=sc_)
                        for kv in range(2):
                            t_wps = Tok()
                            chw = k.dma_chan("ch_nzw")
                            for g in range(4):
                                k.dma("sp", wps[:, g], n_wps[kv, g], chw, writes=[t_wps])
                            for c in range(9):
                                pages = [(pg, pg - 16 * c) for pg in range(16 * c, min(16 * c + 16, NPG + 1))]
                                if 16 * c + 16 <= NPG:
                                    pages.append((16 * c + 16, 16))
                                for pi, (pg, slot) in enumerate(pages):
                                    xp, t_xp = get_page(kv, pg)
                                    for g in range(4):
                                        k.op("pe", lambda e: e.matmul(out=PB[4 + g][:, 0:128], lhsT=xp[:, g * 128:(g + 1) * 128], rhs=wps[:, g, slot, :],
                                                                      start=(pi == 0), stop=(pi == len(pages) - 1)),
                                             reads=[t_xp, t_wps], writes=[t_PB[4 + g]])
                                for g in range(4):
                                    pl, t_pl = pl_r.next()
                                    k.op("act", lambda e: e.copy(out=pl[:], in_=PB[4 + g][:, 0:128]), reads=[t_PB[4 + g]], writes=[t_pl])
                                    if kv == 0:
                                        k.op("pe", lambda e: e.matmul(out=PB[3][:, 0:128], lhsT=pj[:, 0, g, :], rhs=pl[:], start=True, stop=True),
                                             reads=[t_ix, t_pl], writes=[t_PB[3]])
                                        k.op("act", lambda e: e.copy(out=kcT[:, g, c * 128:(c + 1) * 128], in_=PB[3][:, 0:128]), reads=[t_PB[3]], writes=[t_kc])
                                    else:
                                        k.op("pe", lambda e: e.matmul(out=PB[3][:, 0:128], lhsT=pl[:], rhs=pj[:, 1, g, :], start=True, stop=True),
                                             reads=[t_ix, t_pl], writes=[t_PB[3]])
                                        k.op("act", lambda e: e.copy(out=vca[:, c, g * 128:(g + 1) * 128], in_=PB[3][:, 0:128]), reads=[t_PB[3]], writes=[t_kc])
                    Sc = k.sb("nz_Sc", [16, 1152], F32, scm)
                    pcT = k.sb("nz_pcT", [128, 9, 16], F32, scm)
                    imp = k.sb("nz_imp", [16, 4, 264], F32, scm)
                    t_Sc, t_pcT = Tok(), Tok()
                    for nb in range(3):
                        for g in range(4):
                            k.op("pe", lambda e: e.matmul(out=PB[0][0:16, 0:384], lhsT=qpad[:, g, :], rhs=kcT[:, g, nb * 384:(nb + 1) * 384],
                                                          start=(g == 0), stop=(g == 3)), reads=[t_qp, t_kc], writes=[t_PB[0]])
                        k.op("act", lambda e: e.activation(out=Sc[:, nb * 384:(nb + 1) * 384], in_=PB[0][0:16, 0:384], func=AF.Copy, scale=SC),
                             reads=[t_PB[0]], writes=[t_Sc])
                    k.op("dve", lambda e: e.memset(Sc[:, 1023:1152], -1e30), reads=[t_Sc], writes=[t_Sc])
                    softmax16(Sc[:], 1152, t_Sc)
                    for c in range(9):
                        k.op("pe", lambda e: e.transpose(out=PB[1][:, c * 16:(c + 1) * 16], in_=Sc[:, c * 128:(c + 1) * 128], identity=ident[0:16, 0:16]),
                             reads=[t_Sc, t_const], writes=[t_PB[1]])
                    k.op("act", lambda e: e.copy(out=pcT[:], in_=PB[1][:, 0:144].rearrange("p (c h) -> p c h", c=9)), reads=[t_PB[1]], writes=[t_pcT])
                    for c in range(9):
                        k.op("pe", lambda e: e.matmul(out=PB[6][0:16, :], lhsT=pcT[:, c, :], rhs=vca[:, c, :], start=(c == 0), stop=(c == 8)),
                             reads=[t_pcT, t_kc], writes=[t_PB[6]])
                    select_add(PB[6], True)
                    k.op("dve", lambda e: e.tensor_scalar(out=acc_o[:], in0=acc_o[:], scalar1=gcol[:, 0:1], scalar2=None, op0=ALU.mult), reads=[t_g, t_ao], writes=[t_ao])
                    with k.scope() as so:
                        ov_r = Ring(k, "nz_ov", 2, [128, 257], F32, stack=so, chan=True)
                        for c in range(9):
                            ovt, t_ov, cho = ov_r.next()
                            k.dma("sp", ovt[:], n_ovs[c], cho, writes=[t_ov])
                            k.op("pe", lambda e: e.matmul(out=PB[7][0:16, 0:257], lhsT=pcT[:, c, :], rhs=ovt[:], start=(c == 0), stop=(c == 8)),
                                 reads=[t_pcT, t_ov], writes=[t_PB[7]])
                    k.op("act", lambda e: e.copy(out=imp[:, 0, 0:257], in_=PB[7][0:16, 0:257]), reads=[t_PB[7]], writes=[t_imp])
                    k.op("pe", lambda e: e.matmul(out=PB[7][0:16, 0:257], lhsT=gs[:], rhs=imp[:, 0, 0:257], start=True, stop=True), reads=[t_imp, t_ix], writes=[t_PB[7]])
                    k.op("dve", lambda e: e.tensor_tensor(out=imp[:, 1, 0:257], in0=PB[7][0:16, 0:257], in1=fs[:], op=ALU.add), reads=[t_PB[7], t_ix], writes=[t_imp])
                    k.op("dve", lambda e: e.max(out=imp[:, 3, 0:8], in_=imp[:, 1, 0:257]), reads=[t_imp], writes=[t_imp])
                    k.op("dve", lambda e: e.match_replace(out=imp[:, 2, 0:257], in_to_replace=imp[:, 3, 0:8], in_values=imp[:, 1, 0:257], imm_value=-3e38),
                         reads=[t_imp], writes=[t_imp])
                    k.op("dve", lambda e: e.max(out=imp[:, 3, 8:16], in_=imp[:, 2, 0:257]), reads=[t_imp], writes=[t_imp])
                    k.op("dve", lambda e: e.memset(seln[:], -1e30), writes=[t_imp])
                    k.op("dve", lambda e: e.tensor_scalar(out=seln[:, 0:257], in0=imp[:, 1, 0:257], scalar1=imp[:, 3, 15:16], scalar2=1e30, op0=ALU.is_ge, op1=ALU.mult),
                         reads=[t_imp], writes=[t_imp])
                    k.op("dve", lambda e: e.tensor_scalar(out=seln[:, 0:257], in0=seln[:, 0:257], scalar1=-1e30, scalar2=None, op0=ALU.add), reads=[t_imp], writes=[t_imp])

                def attend(npages, kget, vget, mask_fn, gate_col, pbank):
                    with k.scope() as sa:
                        S = k.sb("nz_S", [16, npages, 128], F32, sa)
                        t_S = Tok()
                        kT_r = Ring(k, "nz_kTp", 2, [128, 512], F32, stack=sa)
                        pT_r = Ring(k, "nz_pTp", 2, [128, 16], F32, stack=sa)
                        for pg in range(npages):
                            kp, t_kp = kget(pg)
                            tb = PB[pg % 2]
                            for g in range(4):
                                k.op("pe", lambda e: e.transpose(out=tb[:, g * 128:(g + 1) * 128], in_=kp[:, g * 128:(g + 1) * 128], identity=ident[:]),
                                     reads=[t_kp, t_const], writes=[t_PB[pg % 2]])
                            kTp, t_kTp = kT_r.next()
                            k.op("act", lambda e: e.copy(out=kTp[:], in_=tb[:]), reads=[t_PB[pg % 2]], writes=[t_kTp])
                            sbk = PB[2 + pg % 2]
                            for g in range(4):
                                k.op("pe", lambda e: e.matmul(out=sbk[0:16, 0:128], lhsT=qpad[:, g, :], rhs=kTp[:, g * 128:(g + 1) * 128],
                                                              start=(g == 0), stop=(g == 3)), reads=[t_qp, t_kTp], writes=[t_PB[2 + pg % 2]])
                            k.op("dve", lambda e: e.tensor_scalar(out=S[:, pg, :], in0=sbk[0:16, 0:128], scalar1=SC, scalar2=None, op0=ALU.mult),
                                 reads=[t_PB[2 + pg % 2]], writes=[t_S])
                        mask_fn(S, t_S)
                        softmax16(S[:].rearrange("p g r -> p (g r)"), npages * 128, t_S, gate_col)
                        for pg in range(npages):
                            vp, t_vp = vget(pg)
                            k.op("pe", lambda e: e.transpose(out=PB[pg % 2][:, 0:16], in_=S[:, pg, :], identity=ident[0:16, 0:16]),
                                 reads=[t_S, t_const], writes=[t_PB[pg % 2]])
                            pT, t_pT = pT_r.next()
                            k.op("act", lambda e: e.copy(out=pT[:], in_=PB[pg % 2][:, 0:16]), reads=[t_PB[pg % 2]], writes=[t_pT])
                            k.op("pe", lambda e: e.matmul(out=pbank[0:16, :], lhsT=pT[:], rhs=vp, start=(pg == 0), stop=(pg == npages - 1)),
                                 reads=[t_pT, t_vp], writes=[t_PB[PB.index(pbank)]])
                        select_add(pbank, False)

                def mask_slc(S, t_S):
                    Sv = S[:].rearrange("p g r -> p (g r)")
                    k.op("dve", lambda e: e.tensor_tensor(out=Sv.rearrange("p (m s) -> p m s", s=64), in0=Sv.rearrange("p (m s) -> p m s", s=64),
                                                          in1=seln[:, 0:258].unsqueeze(2).to_broadcast([16, 258, 64]), op=ALU.add),
                         reads=[t_imp, t_S], writes=[t_S])
                    k.op("dve", lambda e: e.memset(S[:, NPG, 1:128], -1e30), reads=[t_S], writes=[t_S])

                attend(NPG + 1, lambda pg: get_page(2, pg), lambda pg: get_page(3, pg), mask_slc, gcol[:, 1:2], PB[4])

                wb_r = Ring(k, "nz_wb", 4, [128, 512], F32, stack=st, chan=True)

                def wget(half, new_branch):
                    def f(pg):
                        if pg == 4:
                            return newp[:, new_branch, :], t_np
                        t_, t_t, ch_ = wb_r.next()
                        k.dma("sp", t_[:], n_winbuf[pg * 128:(pg + 1) * 128, half * 512:(half + 1) * 512], ch_, writes=[t_t])
                        return t_[:], t_t
                    return f

                def mask_win(S, t_S):
                    k.op("dve", lambda e: e.memset(S[:, 0, 0:1], -1e30), reads=[t_S], writes=[t_S])
                    k.op("dve", lambda e: e.memset(S[:, 4, 1:128], -1e30), reads=[t_S], writes=[t_S])

                attend(5, wget(0, 4), wget(1, 5), mask_win, gcol[:, 2:3], PB[5])
                osb = k.sb("nz_osb", [16, 128], BF16, st)
                t_osb = Tok()
                k.op("act", lambda e: e.copy(out=osb[:], in_=acc_o[:]), reads=[t_ao], writes=[t_osb])
                k.dma("sp", ynT_d[0:D, SEQ:SEQ + 1].rearrange("(h d) o -> h (d o)", h=16), osb[:], ch_ynTd[NT], reads=[t_osb], writes=[t_ynTd[NT]],
                      allow_slow_non_contiguous=True)

        def nsa_layer(li, attn=True):
            SC = float(128 ** -0.5)
            w_in = n_win.rearrange("(c p) f -> p c f", p=128)
            if attn and os.environ.get("NSA_SAMPLE", "1") == "1":
                nsa_sample(li)
            with k.scope() as sl:
                kTs = k.sb("ns_kTs", [128, 4, SEQ], BF16, sl)
                vs = k.sb("ns_vs", [128, NT, 512], BF16, sl)
                kTw = k.sb("ns_kTw", [128, 4, SEQ], BF16, sl)
                vw = k.sb("ns_vw", [128, NT, 512], BF16, sl)
                kcT = k.sb("ns_kcT", [128, 4, 128], BF16, sl)
                vc = k.sb("ns_vc", [128, 4, 128], BF16, sl)
                gates = k.sb("ns_gates", [128, TT, 48], F32, sl)
                t_kv, t_cmp, t_gt = Tok(), Tok(), Tok()
                with k.scope() as st:
                    xc1 = k.sb("ns_xc", [128, NT, 512], BF16, st)
                    xc_tok = [xc1, xc1]
                    t_xc = Tok()
                    wp_r = Ring(k, "ns_wp", 2, [128, NT, 128], BF16, stack=st, chan=True)
                    pj_r = Ring(k, "ns_pj", 2, [128, 128], BF16, stack=st, chan=True)
                    pl_r = Ring(k, "ns_pl", 2, [128, 128], BF16, stack=st)

                    def pool_cmp(kv):
                        for g in range(4):
                            wp, t_wp, chp = wp_r.next()
                            k.dma("pool", wp[:], n_wp[kv, g].rearrange("(c p) n -> p c n", p=128), chp, writes=[t_wp])
                            pj, t_pj, chj = pj_r.next()
                            k.dma("pool", pj[:], n_proj[kv, g], chj, writes=[t_pj])
                            for c in range(NT):
                                k.op("pe", lambda e: e.matmul(out=PB[6][:, 0:128], lhsT=xc_tok[kv][:, c, g * 128:(g + 1) * 128], rhs=wp[:, c, :],
                                                              start=(c == 0), stop=(c == NT - 1)), reads=[t_xc, t_wp], writes=[t_PB[6]])
                            pl, t_pl = pl_r.next()
                            k.op("act", lambda e: e.copy(out=pl[:], in_=PB[6][:, 0:128]), reads=[t_PB[6]], writes=[t_pl])
                            if kv == 0:
                                k.op("pe", lambda e: e.matmul(out=PB[7][:, 0:128], lhsT=pj[:], rhs=pl[:], start=True, stop=True),
                                     reads=[t_pj, t_pl], writes=[t_PB[7]])
                                k.op("act", lambda e: e.copy(out=kcT[:, g, :], in_=PB[7][:, 0:128]), reads=[t_PB[7]], writes=[t_cmp])
                            else:
                                k.op("pe", lambda e: e.matmul(out=PB[7][:, 0:128], lhsT=pl[:], rhs=pj[:], start=True, stop=True),
                                     reads=[t_pj, t_pl], writes=[t_PB[7]])
                                k.op("act", lambda e: e.copy(out=vc[:, g, :], in_=PB[7][:, 0:128]), reads=[t_PB[7]], writes=[t_cmp])

                    wr = Ring(k, "ns_w", 1, [128, DC, 512], BF16, stack=st, chan=True)
                    ko_r = Ring(k, "ns_ko", 3, [128, 512], F32, stack=st, chan=True)
                    kb_r = Ring(k, "ns_kb", 2, [128, 512], BF16, stack=st)
                    pm = PRing([0, 1, 2, 3])
                    for fb in range(6):
                        wb, t_wb, chw = wr.next()
                        k.dma("pool", wb[:], w_in[:, :, 2048 + fb * 512:2048 + (fb + 1) * 512], chw, writes=[t_wb])
                        for tt in range(TT):
                            ps, t_ps = pm.next()
                            for kc in range(DC):
                                k.op("pe", lambda e: e.matmul(out=ps[:], lhsT=hT[:, kc, tsl(tt)], rhs=wb[:, kc, :],
                                                              start=(kc == 0), stop=(kc == DC - 1)), reads=[t_hT[tt], t_wb], writes=[t_ps])
                            ko, t_ko, cho = ko_r.next()
                            k.op("act", lambda e: e.copy(out=ko[:], in_=ps[:]), reads=[t_ps], writes=[t_ko])
                            if fb < 4:
                                k.dma("sp", o_nsa_kv[tsl(tt), fb * 512:(fb + 1) * 512], ko[:], cho, reads=[t_ko])
                            else:
                                cs = slice((fb - 4) * 512, (fb - 3) * 512)
                                if 12 <= tt < NT:
                                    k.dma("sp", o_nsa_win_p[(tt - 12) * 128:(tt - 11) * 128, cs], ko[:], cho, reads=[t_ko])
                                elif tt == NT:
                                    k.dma("sp", o_nsa_win_s[511:512, cs], ko[0:1, :], cho, reads=[t_ko])
                            if not attn or tt >= NT:
                                continue
                            if fb < 2:
                                k.op("dve", lambda e: e.tensor_copy(out=xc_tok[fb][:, tt, :], in_=ko[:]), reads=[t_ko], writes=[t_xc])
                            elif fb in (3, 5):
                                dstv = vs if fb == 3 else vw
                                k.op("dve", lambda e: e.tensor_copy(out=dstv[:, tt, :], in_=ko[:]), reads=[t_ko], writes=[t_kv])
                            else:
                                dstk = kTs if fb == 2 else kTw
                                kb, t_kb = kb_r.next()
                                k.op("dve", lambda e: e.tensor_copy(out=kb[:], in_=ko[:]), reads=[t_ko], writes=[t_kb])
                                transpose_bf(lambda c: kb[:, c * 128:(c + 1) * 128], 4, lambda c0, n: dstk[:, c0:c0 + n, tsl(tt)],
                                             t_kb, t_kv, PRing([4, 5]))
                        if attn and fb < 2:
                            pool_cmp(fb)
                    k.dma("sp", o_nsa_win_s[0:511, :], n_winbuf[1:512, :], k.dma_chan("ch_nwin"))
                    if attn:
                        wg = k.sb("ns_wg", [128, DC, 48], BF16, st)
                        bg = k.sb("ns_bg", [128, 48], F32, st)
                        t_wg = Tok()
                        chg = k.dma_chan("ch_nsg")
                        k.dma("pool", wg[:], w_in[:, :, 5120:5168], chg, writes=[t_wg])
                        k.dma("sp", bg[:], n_bg, chg, writes=[t_wg])
                        for tt in range(NT):
                            for kc in range(DC):
                                k.op("pe", lambda e: e.matmul(out=PB[0][:, 0:48], lhsT=hT[:, kc, tsl(tt)], rhs=wg[:, kc, :],
                                                              start=(kc == 0), stop=(kc == DC - 1)), reads=[t_hT[tt], t_wg], writes=[t_PB[0]])
                            k.op("dve", lambda e: e.tensor_tensor(out=gates[:, tt, :], in0=PB[0][:, 0:48], in1=bg[:], op=ALU.add),
                                 reads=[t_PB[0], t_wg], writes=[t_gt])
                            k.op("act", lambda e: e.activation(out=gates[:, tt, :], in_=gates[:, tt, :], func=AF.Sigmoid), reads=[t_gt], writes=[t_gt])
                if not attn:
                    return
                cst = k.sb("ns_cst", [128, 4, 128], F32, sl)
                ov = k.sb("ns_ov", [128, 32], F32, sl)
                eaug = k.sb("ns_eaug", [33, SEQ], BF16, sl)
                t_cst = Tok()
                chc = k.dma_chan("ch_nsc")
                k.dma("sp", cst[:, 0, :], mneg_d2, chc, writes=[t_cst])
                k.dma("sp", cst[:, 1, :], n_mfar, chc, writes=[t_cst])
                k.dma("sp", ov[:], n_ov, chc, writes=[t_cst])
                k.dma("pool", eaug[:], n_eaug, chc, writes=[t_cst])
                for g in range(4):
                    with k.scope() as st:
                        qTg = k.sb("ns_qT", [128, 4, TTOK], BF16, st)
                        t_qg = [Tok() for _ in range(TT)]
                        featproj(n_win[:, g * 512:(g + 1) * 512], 512, qTg, t_qg)
                        cm_r = Ring(k, "ns_cm", 2, [128, 2, 128], F32, stack=st, chan=True)
                        fv_r = Ring(k, "ns_fv", 2, [128, 3, 32], F32, stack=st, chan=True)
                        pc_r = Ring(k, "ns_pc", 2, [128, 4, 128], F32, stack=st)
                        pcg_r = Ring(k, "ns_pcg", 2, [128, 4, 128], BF16, stack=st)
                        pcT_r = Ring(k, "ns_pcT", 2, [128, 4, 128], F32, stack=st)
                        pgT_r = Ring(k, "ns_pgT", 2, [128, 4, 128], BF16, stack=st)
                        im_r = Ring(k, "ns_im", 2, [128, 4, 32], F32, stack=st)
                        selT_r = Ring(k, "ns_selT", 2, [33, 128], BF16, stack=st)
                        p_r = Ring(k, "ns_p", 2, [128, SEQ], BF16, stack=st)
                        pT_r = Ring(k, "ns_pT", 2, [128, NT, 128], BF16, stack=st)
                        sm_r = Ring(k, "ns_sm", 2, [128, 24], F32, stack=st)
                        oT_r = Ring(k, "ns_oT", 2, [128, 4, 128], BF16, stack=st)
                        for sb_, t_sb in zip(selT_r.bufs, selT_r.toks):
                            k.op("dve", lambda e: e.memset(sb_[:], 1.0), writes=[t_sb])
                        for qt in range(NT):
                            gcol = lambda r, br: (g * 4 + r) * 3 + br
                            cm, t_cm, chm = cm_r.next()
                            k.dma("sp", cm[:, 0, :], n_cm01[tsl(qt), :], chm, writes=[t_cm])
                            fv, t_fv, chf = fv_r.next()
                            k.dma("sp", fv[:], n_fv[tsl(qt)], chf, writes=[t_fv])
                            k.op("dve", lambda e: e.tensor_scalar(out=cm[:, 1, :], in0=cm[:, 0, :], scalar1=32768.0, scalar2=-32768.0,
                                                                  op0=ALU.mult, op1=ALU.add), reads=[t_cm], writes=[t_cm])
                            for r in range(4):
                                k.op("pe", lambda e: e.matmul(out=PB[0][:, r * 128:(r + 1) * 128], lhsT=qTg[:, r, tsl(qt)], rhs=kcT[:, g, :],
                                                              start=True, stop=False), reads=[t_qg[qt], t_cmp], writes=[t_PB[0]])
                                k.op("pe", lambda e: e.matmul(out=PB[0][:, r * 128:(r + 1) * 128], lhsT=ident[:], rhs=cm[:, 1, :],
                                                              start=False, stop=True), reads=[t_cm, t_const], writes=[t_PB[0]])
                            sm, t_sm = sm_r.next()
                            k.op("dve", lambda e: e.reduce_max(out=sm[:, 0:4], in_=PB[0][:].rearrange("p (r n) -> p r n", r=4), axis=AX.X),
                                 reads=[t_PB[0]], writes=[t_sm])
                            k.op("dve", lambda e: e.tensor_scalar(out=sm[:, 4:8], in0=sm[:, 0:4], scalar1=-SC, scalar2=None, op0=ALU.mult),
                                 reads=[t_sm], writes=[t_sm])
                            pc, t_pc = pc_r.next()
                            for r in range(4):
                                k.op("act", lambda e: e.activation(out=pc[:, r, :], in_=PB[0][:, r * 128:(r + 1) * 128], func=AF.Exp, scale=SC,
                                                                   bias=sm[:, 4 + r:5 + r], accum_out=sm[:, 8 + r:9 + r]),
                                     reads=[t_PB[0], t_sm], writes=[t_pc, t_sm])
                            k.op("dve", lambda e: e.reciprocal(out=sm[:, 12:16], in_=sm[:, 8:12]), reads=[t_sm], writes=[t_sm])
                            k.op("dve", lambda e: e.tensor_tensor(out=pc[:], in0=pc[:], in1=sm[:, 12:16].unsqueeze(2).to_broadcast([128, 4, 128]),
                                                                  op=ALU.mult), reads=[t_sm, t_pc], writes=[t_pc])
                            k.op("dve", lambda e: e.tensor_tensor(out=pc[:], in0=pc[:], in1=cm[:, 0, :].unsqueeze(1).to_broadcast([128, 4, 128]),
                                                                  op=ALU.mult), reads=[t_cm, t_pc], writes=[t_pc])
                            for r in range(4):
                                k.op("pe", lambda e: e.transpose(out=PB[1][:, r * 128:(r + 1) * 128], in_=pc[:, r, :], identity=ident[:]),
                                     reads=[t_pc, t_const], writes=[t_PB[1]])
                            pcT, t_pcT = pcT_r.next()
                            k.op("act", lambda e: e.copy(out=pcT[:], in_=PB[1][:].rearrange("p (r t) -> p r t", r=4)), reads=[t_PB[1]], writes=[t_pcT])
                            for r in range(4):
                                k.op("pe", lambda e: e.matmul(out=PB[2][:, 0:32], lhsT=pcT[:, r, :], rhs=ov[:], start=(r == 0), stop=(r == 3)),
                                     reads=[t_pcT, t_cst], writes=[t_PB[2]])
                            im, t_im = im_r.next()
                            imp, v16, tmpm, selm = im[:, 0, :], im[:, 1, :], im[:, 2, :], im[:, 3, :]
                            k.op("dve", lambda e: e.tensor_tensor(out=imp, in0=PB[2][:, 0:32], in1=fv[:, 0, :], op=ALU.add), reads=[t_PB[2], t_fv], writes=[t_im])
                            k.op("dve", lambda e: e.tensor_tensor(out=imp, in0=imp, in1=fv[:, 1, :], op=ALU.mult), reads=[t_fv, t_im], writes=[t_im])
                            k.op("dve", lambda e: e.tensor_tensor(out=imp, in0=imp, in1=fv[:, 2, :], op=ALU.add), reads=[t_fv, t_im], writes=[t_im])
                            k.op("dve", lambda e: e.max(out=v16[:, 0:8], in_=imp), reads=[t_im], writes=[t_im])
                            k.op("dve", lambda e: e.match_replace(out=tmpm, in_to_replace=v16[:, 0:8], in_values=imp, imm_value=-3e38),
                                 reads=[t_im], writes=[t_im])
                            k.op("dve", lambda e: e.max(out=v16[:, 8:16], in_=tmpm), reads=[t_im], writes=[t_im])
                            k.op("dve", lambda e: e.tensor_scalar(out=selm, in0=imp, scalar1=v16[:, 15:16], scalar2=None, op0=ALU.is_ge),
                                 reads=[t_im], writes=[t_im])
                            k.op("dve", lambda e: e.scalar_tensor_tensor(out=selm, in0=imp, scalar=-5e29, in1=selm, op0=ALU.is_gt, op1=ALU.mult),
                                 reads=[t_im], writes=[t_im])
                            k.op("pe", lambda e: e.transpose(out=PB[3][0:32, 0:128], in_=selm, identity=ident[:]), reads=[t_im, t_const], writes=[t_PB[3]])
                            selT, t_selT = selT_r.next()
                            k.op("act", lambda e: e.copy(out=selT[0:32, :], in_=PB[3][0:32, 0:128]), reads=[t_PB[3]], writes=[t_selT])
                            pcg, t_pcg = pcg_r.next()
                            for r in range(4):
                                k.op("dve", lambda e: e.tensor_scalar(out=pcg[:, r, :], in0=pc[:, r, :], scalar1=gates[:, qt, gcol(r, 0):gcol(r, 0) + 1],
                                                                      scalar2=None, op0=ALU.mult), reads=[t_pc, t_gt], writes=[t_pcg])
                            pgT, t_pgT = pgT_r.next()
                            transpose_bf(lambda r: pcg[:, r, :], 4, lambda r0, n: pgT[:, r0:r0 + n, :], t_pcg, t_pgT, PRing([1]))
                            for r in range(4):
                                h = g * 4 + r
                                oslc = PB[7][:, r * 128:(r + 1) * 128]
                                k.op("pe", lambda e: e.matmul(out=oslc, lhsT=vc[:, g, :], rhs=pgT[:, r, :], start=True, stop=False),
                                     reads=[t_cmp, t_pgT], writes=[t_PB[7]])
                                for br in (1, 2):
                                    kT_, v_ = (kTs, vs) if br == 1 else (kTw, vw)
                                    j0 = 0 if br == 1 else max(0, qt - 4)
                                    k0 = j0 * 128
                                    nkeys = (qt + 1) * 128 - k0
                                    nb = (nkeys + 511) // 512
                                    for bi in range(nb):
                                        c0 = bi * 512
                                        w_ = min(512, nkeys - c0)
                                        last = (bi == nb - 1)
                                        k.op("pe", lambda e: e.matmul(out=PB[2 + bi][:, 0:w_], lhsT=qTg[:, r, tsl(qt)], rhs=kT_[:, g, k0 + c0:k0 + c0 + w_],
                                                                      start=True, stop=False), reads=[t_qg[qt], t_kv], writes=[t_PB[2 + bi]])
                                        if br == 1:
                                            k.op("pe", lambda e: e.matmul(out=PB[2 + bi][:, 0:w_], lhsT=selT[:, :], rhs=eaug[:, c0:c0 + w_],
                                                                          start=False, stop=False), reads=[t_selT, t_cst], writes=[t_PB[2 + bi]])
                                        elif bi == 0 and qt >= 4:
                                            k.op("pe", lambda e: e.matmul(out=PB[2 + bi][:, 0:128], lhsT=ident[:], rhs=cst[:, 1, :],
                                                                          start=False, stop=False), reads=[t_cst, t_const], writes=[t_PB[2 + bi]])
                                        if last:
                                            k.op("pe", lambda e: e.matmul(out=PB[2 + bi][:, w_ - 128:w_], lhsT=ident[:], rhs=cst[:, 0, :], start=False, stop=True),
                                                 reads=[t_cst, t_const], writes=[t_PB[2 + bi]])
                                    sm2, t_sm2 = sm_r.next()
                                    for bi in range(nb):
                                        w_ = min(512, nkeys - bi * 512)
                                        k.op("dve", lambda e: e.reduce_max(out=sm2[:, bi:bi + 1], in_=PB[2 + bi][:, 0:w_], axis=AX.X),
                                             reads=[t_PB[2 + bi]], writes=[t_sm2])
                                    k.op("dve", lambda e: e.reduce_max(out=sm2[:, 4:5], in_=sm2[:, 0:nb], axis=AX.X), reads=[t_sm2], writes=[t_sm2])
                                    k.op("dve", lambda e: e.tensor_scalar(out=sm2[:, 5:6], in0=sm2[:, 4:5], scalar1=-SC, scalar2=None, op0=ALU.mult),
                                         reads=[t_sm2], writes=[t_sm2])
                                    p, t_p = p_r.next()
                                    for bi in range(nb):
                                        c0 = bi * 512
                                        w_ = min(512, nkeys - c0)
                                        k.op("act", lambda e: e.activation(out=p[:, c0:c0 + w_], in_=PB[2 + bi][:, 0:w_], func=AF.Exp, scale=SC,
                                                                           bias=sm2[:, 5:6], accum_out=sm2[:, 8 + bi:9 + bi]),
                                             reads=[t_PB[2 + bi], t_sm2], writes=[t_p, t_sm2])
                                    k.op("dve", lambda e: e.reduce_sum(out=sm2[:, 6:7], in_=sm2[:, 8:8 + nb], axis=AX.X), reads=[t_sm2], writes=[t_sm2])
                                    k.op("dve", lambda e: e.reciprocal(out=sm2[:, 7:8], in_=sm2[:, 6:7]), reads=[t_sm2], writes=[t_sm2])
                                    k.op("dve", lambda e: e.tensor_scalar(out=p[:, 0:nkeys], in0=p[:, 0:nkeys], scalar1=sm2[:, 7:8],
                                                                          scalar2=gates[:, qt, gcol(r, br):gcol(r, br) + 1], op0=ALU.mult, op1=ALU.mult),
                                         reads=[t_sm2, t_p, t_gt], writes=[t_p])
                                    nt_ = qt + 1 - j0
                                    pT, t_pT = pT_r.next()
                                    transpose_bf(lambda jb: p[:, jb * 128:(jb + 1) * 128], nt_, lambda jj0, n: pT[:, jj0:jj0 + n, :], t_p, t_pT, PRing([5, 6]))
                                    for jb in range(nt_):
                                        k.op("pe", lambda e: e.matmul(out=oslc, lhsT=v_[:, j0 + jb, g * 128:(g + 1) * 128], rhs=pT[:, jb, :],
                                                                      start=False, stop=(br == 2 and jb == nt_ - 1)),
                                             reads=[t_kv, t_pT], writes=[t_PB[7]])
                            oT, t_oT = oT_r.next()
                            k.op("act", lambda e: e.copy(out=oT[:], in_=PB[7][:].rearrange("p (c t) -> p c t", c=4)), reads=[t_PB[7]], writes=[t_oT])
                            k.dma("sp", ynT_d[g * 512:(g + 1) * 512, tsl(qt)].rearrange("(q p) t -> p q t", p=128), oT[:], ch_ynTd[qt],
                                  reads=[t_oT], writes=[t_ynTd[qt]])
            def get_aT(tt, st, state):
                if "ring" not in state:
                    state["ring"] = Ring(k, "n_aT", 2, [128, DC, 128], BF16, stack=st, chan=True)
                a, t_a, cha = state["ring"].next()
                k.dma("sp", a[:], ynT_d[0:D, tsl(tt)].rearrange("(c p) t -> p c t", p=128), cha, reads=[t_ynTd[tt]], writes=[t_a])
                return a, t_a
            if dbg:
                for tt in range(NT):
                    k.dma("sp", o_dbg2[:, tsl(tt)], ynT_d[0:D, tsl(tt)], ch_dbg, reads=[t_ynTd[tt]])
            proj_ln(get_aT, DC, n_wout, li, 0)

        def mamba_layer(j, li):
            w_in = m_win[j].rearrange("(c p) f -> p c f", p=128)
            with k.scope() as st:
                cw = k.sb("m_cw", [128, 48, 4], F32, st)
                cb = k.sb("m_cb", [128, 48], F32, st)
                dtb = k.sb("m_dtb_s", [128, 64], F32, st)
                abc = k.sb("m_abc", [128, 64], F32, st)
                dsk = k.sb("m_dsk_s", [128, 64], F32, st)
                ng = k.sb("m_ng_s", [128, 32], F32, st)
                t_par = Tok()
                ch_misc = k.dma_chan("ch_par")
                for dst, srcap in ((cw, m_convw[j]), (cb, m_convb[j]), (dtb, m_dtb[j]), (abc, m_alog[j]),
                                   (dsk, m_dsk[j]), (ng, m_ng[j])):
                    k.dma("sp", dst[:], srcap, ch_misc, writes=[t_par])
                k.op("act", lambda e: e.activation(out=abc[:], in_=abc[:], func=AF.Exp), reads=[t_par], writes=[t_par])
                k.op("dve", lambda e: e.tensor_scalar(out=abc[:], in0=abc[:], scalar1=-1.0, scalar2=None, op0=ALU.mult),
                     reads=[t_par], writes=[t_par])
                dt_all = k.sb("m_dt", [128, TT, 64], F32, st)
                la_all = k.sb("m_la", [128, TT, 64], F32, st)
                ac_all = k.sb("m_ac", [128, TT, 64], F32, st)
                ea_all = k.sb("m_ea", [128, TT, 64], F32, st)
                t_dt = [Tok() for _ in range(TT)]
                with k.scope() as st2:
                    wdt = k.sb("m_wdt", [128, DC, 64], BF16, st2)
                    t_wdt = Tok()
                    ch_misc = k.dma_chan("ch_wdt")
                    k.dma("pool", wdt[:], w_in[:, :, 10240:10304], ch_misc, writes=[t_wdt])
                    tmp_r = Ring(k, "m_dtt", 2, [128, 3, 64], F32, stack=st2)
                    pm = PRing([0, 1])
                    pa = PRing([2, 3])
                    for c in range(TT):
                        ps, t_ps = pm.next()
                        for kc in range(DC):
                            k.op("pe", lambda e: e.matmul(out=ps[:, 0:64], lhsT=hT[:, kc, tsl(c)], rhs=wdt[:, kc, :],
                                                          start=(kc == 0), stop=(kc == DC - 1)),
                                 reads=[t_hT[c], t_wdt], writes=[t_ps])
                        tm, t_tm = tmp_r.next()
                        x0, ax, ee = tm[:, 0, :], tm[:, 1, :], tm[:, 2, :]
                        k.op("dve", lambda e: e.tensor_tensor(out=x0, in0=ps[:, 0:64], in1=dtb[:], op=ALU.add),
                             reads=[t_ps, t_par], writes=[t_tm])
                        k.op("dve", lambda e: e.scalar_tensor_tensor(out=ax, in0=x0, scalar=-1.0, in1=x0, op0=ALU.mult,
                                                                     op1=ALU.max), reads=[t_tm], writes=[t_tm])
                        k.op("act", lambda e: e.activation(out=ee, in_=ax, func=AF.Exp, scale=-1.0), reads=[t_tm], writes=[t_tm])
                        k.op("act", lambda e: e.activation(out=ee, in_=ee, func=AF.Ln, bias=1.0), reads=[t_tm], writes=[t_tm])
                        k.op("dve", lambda e: e.scalar_tensor_tensor(out=dt_all[:, c, :], in0=x0, scalar=0.0, in1=ee,
                                                                     op0=ALU.max, op1=ALU.add), reads=[t_tm], writes=[t_dt[c]])
                        k.op("dve", lambda e: e.tensor_tensor(out=la_all[:, c, :], in0=dt_all[:, c, :], in1=abc[:], op=ALU.mult),
                             reads=[t_dt[c], t_par], writes=[t_dt[c]])
                        pa_, t_pa = pa.next()
                        k.op("pe", lambda e: e.matmul(out=pa_[:, 0:64], lhsT=trit[:], rhs=la_all[:, c, :], start=True, stop=True),
                             reads=[t_const, t_dt[c]], writes=[t_pa])
                        k.op("dve", lambda e: e.tensor_copy(out=ac_all[:, c, :], in_=pa_[:, 0:64]), reads=[t_pa], writes=[t_dt[c]])
                        k.op("act", lambda e: e.activation(out=ea_all[:, c, :], in_=ac_all[:, c, :], func=AF.Exp),
                             reads=[t_dt[c]], writes=[t_dt[c]])
                for g in range(int(os.environ.get('M_GROUPS', '8'))):
                    with k.scope() as sg:
                        xtok = k.sb("m_xtok", [128, NT, 512], F32, sg)
                        t_xtok = [Tok() for _ in range(NT)]
                        btok = k.sb("m_btok", [128, NT, 128], BF16, sg)
                        BT = k.sb("m_BT", [128, SEQ], BF16, sg)
                        CT = k.sb("m_CT", [128, SEQ], BF16, sg)
                        t_BC = Tok()
                        us = k.sb("m_us", [1, 768], F32, sg)
                        t_us = Tok()
                        wz = k.sb("m_wz", [128, DC, 512], BF16, sg)
                        t_wz = Tok()
                        k.dma("pool", wz[:], w_in[:, :, g * 512:(g + 1) * 512], k.dma_chan("ch_wz"), writes=[t_wz])
                        with k.scope() as s3:
                            wx = k.sb("m_wx", [128, DC, 768], BF16, s3)
                            t_w = Tok()
                            ch_w = k.dma_chan("ch_w")
                            k.dma("pool", wx[:, :, 0:512], w_in[:, :, 4096 + g * 512:4096 + (g + 1) * 512], ch_w, writes=[t_w])
                            k.dma("pool", wx[:, :, 512:640], w_in[:, :, 8192 + g * 128:8192 + (g + 1) * 128], ch_w, writes=[t_w])
                            k.dma("pool", wx[:, :, 640:768], w_in[:, :, 9216 + g * 128:9216 + (g + 1) * 128], ch_w, writes=[t_w])
                            up_r = Ring(k, "m_up", 1, [128, SEQ + 3], F32, stack=s3)
                            xc_r = Ring(k, "m_xc", 1, [128, SEQ], F32, stack=s3)
                            cvo = Ring(k, "m_cvo", 2, [128, 3], F32, stack=s3, chan=True)
                            pm = PRing([0, 1, 2, 3])
                            pt = PRing([4, 5, 6, 7])
                            for part, (c0, c1) in enumerate(((0, 512), (512, 768))):
                                for kc in range(DC):
                                    k.op("pe", lambda e: e.matmul(out=PB[4 + part][0:1, 0:c1 - c0], lhsT=hT[:, kc, SEQ:SEQ + 1],
                                                                  rhs=wx[:, kc, c0:c1], start=(kc == 0), stop=(kc == DC - 1)),
                                         reads=[t_hT[NT], t_w], writes=[t_PB[4 + part]])
                                k.op("act", lambda e: e.copy(out=us[0:1, c0:c1], in_=PB[4 + part][0:1, 0:c1 - c0]),
                                     reads=[t_PB[4 + part]], writes=[t_us])
                            for fc in range(6):
                                ch_idx = (g * 4 + fc) if fc < 4 else (32 + g if fc == 4 else 40 + g)
                                up, t_up = up_r.next()
                                k.op("dve", lambda e: e.memset(up[:, 0:3], 0.0), writes=[t_up])
                                for tb in range(4):
                                    ps, t_ps = pm.next()
                                    for kc in range(DC):
                                        k.op("pe", lambda e: e.matmul(out=ps[:], lhsT=wx[:, kc, fc * 128:(fc + 1) * 128],
                                                                      rhs=hT[:, kc, tb * 512:(tb + 1) * 512],
                                                                      start=(kc == 0), stop=(kc == DC - 1)),
                                             reads=[t_w] + [t_hT[tb * 4 + q] for q in range(4)], writes=[t_ps])
                                    k.op("act", lambda e: e.copy(out=up[:, 3 + tb * 512:3 + (tb + 1) * 512], in_=ps[:]),
                                         reads=[t_ps], writes=[t_up])
                                co, t_co, chc = cvo.next()
                                k.op("act", lambda e: e.copy(out=co[:], in_=up[:, SEQ:SEQ + 3]), reads=[t_up], writes=[t_co])
                                k.dma("sp", o_conv_p[j, :, ch_idx * 128:(ch_idx + 1) * 128].rearrange("t p -> p t"), co[:], chc,
                                      reads=[t_co], allow_slow_non_contiguous=True)
                                xc, t_xc = xc_r.next()
                                k.op("dve", lambda e: e.tensor_scalar(out=xc[:], in0=up[:, 0:SEQ], scalar1=cw[:, ch_idx, 0:1],
                                                                      scalar2=cb[:, ch_idx:ch_idx + 1], op0=ALU.mult, op1=ALU.add),
                                     reads=[t_up, t_par], writes=[t_xc])
                                for kk in range(1, 4):
                                    k.op("dve", lambda e: e.scalar_tensor_tensor(out=xc[:], in0=up[:, kk:kk + SEQ],
                                                                                 scalar=cw[:, ch_idx, kk:kk + 1], in1=xc[:],
                                                                                 op0=ALU.mult, op1=ALU.add),
                                         reads=[t_up, t_par, t_xc], writes=[t_xc])
                                if fc < 5:
                                    k.op("act", lambda e: e.activation(out=xc[:], in_=xc[:], func=AF.Silu), reads=[t_xc], writes=[t_xc])
                                if fc == 4:
                                    k.op("dve", lambda e: e.tensor_copy(out=BT[:], in_=xc[:]), reads=[t_xc], writes=[t_BC])
                                if fc == 5:
                                    k.op("act", lambda e: e.activation(out=CT[:], in_=xc[:], func=AF.Silu), reads=[t_xc], writes=[t_BC])
                                if fc < 5:
                                    for c4 in range(4):
                                        pp, t_pp = pt.next()
                                        for q in range(4):
                                            c = c4 * 4 + q
                                            k.op("pe", lambda e: e.transpose(out=pp[:, q * 128:(q + 1) * 128], in_=xc[:, tsl(c)],
                                                                             identity=ident[:]),
                                                 reads=[t_xc, t_const], writes=[t_pp])
                                        if fc < 4:
                                            k.op("act", lambda e: e.copy(
                                                out=xtok[:, c4 * 4:(c4 + 1) * 4, fc * 128:(fc + 1) * 128],
                                                in_=pp[:].rearrange("p (c f) -> p c f", c=4)),
                                                 reads=[t_pp], writes=[t_xtok[c4 * 4 + q] for q in range(4)])
                                        else:
                                            k.op("act", lambda e: e.copy(out=btok[:, c4 * 4:(c4 + 1) * 4, :],
                                                                         in_=pp[:].rearrange("p (c f) -> p c f", c=4)),
                                                 reads=[t_pp], writes=[t_BC])
                        with k.scope() as s3:
                            S = k.sb("m_S", [128, 512], F32, s3)
                            Sb = k.sb("m_Sb", [128, 512], BF16, s3)
                            t_S = Tok()
                            k.op("dve", lambda e: e.memset(S[:], 0.0), writes=[t_S])
                            k.op("dve", lambda e: e.memset(Sb[:], 0.0), writes=[t_S])
                            R_r = Ring(k, "m_R", 2, [128, 8, 128], F32, stack=s3)
                            sg_r = Ring(k, "m_seg", 2, [128, 8, 128], F32, stack=s3)
                            cbm_r = Ring(k, "m_cbm", 2, [128, 128], F32, stack=s3)
                            MT_r = Ring(k, "m_MT", 2, [128, 8, 128], BF16, stack=s3)
                            xdt_r = Ring(k, "m_xdt", 2, [128, 512], BF16, stack=s3)
                            xw_r = Ring(k, "m_xw", 2, [128, 512], BF16, stack=s3)
                            zs_r = Ring(k, "m_zs", 2, [128, 512], F32, stack=s3)
                            y_r = Ring(k, "m_y", 2, [128, 512], F32, stack=s3)
                            y2_r = Ring(k, "m_y2", 2, [128, 512], F32, stack=s3)
                            yn_r = Ring(k, "m_yn", 2, [128, 512], F32, stack=s3)
                            ynT_r = Ring(k, "m_ynT", 2, [128, 4, 128], BF16, stack=s3)
                            sm_r = Ring(k, "m_sm", 2, [128, 16], F32, stack=s3)
                            for c in range(int(os.environ.get('M_CHUNKS', '16'))):
                                gs = slice(g * 8, (g + 1) * 8)
                                pz, t_pz = PB[0], t_PB[0]
                                for kc in range(DC):
                                    k.op("pe", lambda e: e.matmul(out=pz[:], lhsT=hT[:, kc, tsl(c)], rhs=wz[:, kc, :],
                                                                  start=(kc == 0), stop=(kc == DC - 1)),
                                         reads=[t_hT[c], t_wz], writes=[t_pz])
                                zs, t_zs = zs_r.next()
                                k.op("act", lambda e: e.activation(out=zs[:], in_=pz[:], func=AF.Silu), reads=[t_pz], writes=[t_zs])
                                Rr, t_R = R_r.next()
                                k.op("pool", lambda e: e.tensor_tensor(out=Rr[:], in0=trit[:].unsqueeze(1).to_broadcast([128, 8, 128]),
                                                                       in1=la_all[:, c, gs].unsqueeze(2).to_broadcast([128, 8, 128]),
                                                                       op=ALU.mult),
                                     reads=[t_const, t_dt[c]], writes=[t_R])
                                Rf = Rr[:].rearrange("p r t -> p (r t)")
                                for hh in range(2):
                                    pa, t_pa = PB[1 + hh], t_PB[1 + hh]
                                    k.op("pe", lambda e: e.matmul(out=pa[:], lhsT=ones[:], rhs=Rf[:, hh * 512:(hh + 1) * 512],
                                                                  start=True, stop=True), reads=[t_const, t_R], writes=[t_pa])
                                seg, t_seg = sg_r.next()
                                sm, t_sm = sm_r.next()
                                for r in range(8):
                                    pa, t_pa = PB[1 + r // 4], t_PB[1 + r // 4]
                                    rr = r % 4
                                    k.op("dve", lambda e: e.tensor_scalar(out=seg[:, r, :], in0=pa[:, rr * 128:(rr + 1) * 128],
                                                                          scalar1=ac_all[:, c, g * 8 + r:g * 8 + r + 1], scalar2=0.0,
                                                                          op0=ALU.subtract, op1=ALU.min),
                                         reads=[t_pa, t_dt[c]], writes=[t_seg])
                                for hh in range(2):
                                    pa, t_pa = PB[1 + hh], t_PB[1 + hh]
                                    k.op("act", lambda e: e.activation(
                                        out=sm[:, hh * 4:(hh + 1) * 4],
                                        in_=pa[:].rearrange("p (r t) -> p r t", r=4)[:, :, 127], func=AF.Exp),
                                         reads=[t_pa], writes=[t_sm])
                                k.op("act", lambda e: e.activation(out=seg[:], in_=seg[:], func=AF.Exp), reads=[t_seg], writes=[t_seg])
                                pc, t_pc = PB[3], t_PB[3]
                                k.op("pe", lambda e: e.matmul(out=pc[:, 0:128], lhsT=BT[:, tsl(c)], rhs=CT[:, tsl(c)], start=True, stop=True),
                                     reads=[t_BC], writes=[t_pc])
                                cbm, t_cbm = cbm_r.next()
                                k.op("dve", lambda e: e.tensor_tensor(out=cbm[:], in0=pc[:, 0:128], in1=trit[:], op=ALU.mult),
                                     reads=[t_pc, t_const], writes=[t_cbm])
                                MT, t_MT = MT_r.next()
                                k.op("dve", lambda e: e.tensor_tensor(out=MT[:], in0=seg[:],
                                                                      in1=cbm[:].unsqueeze(1).to_broadcast([128, 8, 128]), op=ALU.mult),
                                     reads=[t_seg, t_cbm], writes=[t_MT])
                                xdt, t_xdt = xdt_r.next()
                                k.op("pool", lambda e: e.tensor_tensor(
                                    out=xdt[:].rearrange("p (r q) -> p r q", r=8),
                                    in0=xtok[:, c, :].rearrange("p (r q) -> p r q", r=8),
                                    in1=dt_all[:, c, gs].unsqueeze(2).to_broadcast([128, 8, 64]), op=ALU.mult),
                                     reads=[t_xtok[c], t_dt[c]], writes=[t_xdt])
                                py, t_py = PB[4], t_PB[4]
                                for r in range(8):
                                    k.op("pe", lambda e: e.matmul(out=py[:, r * 64:(r + 1) * 64], lhsT=MT[:, r, :],
                                                                  rhs=xdt[:, r * 64:(r + 1) * 64], start=True, stop=True),
                                         reads=[t_MT, t_xdt], writes=[t_py])
                                po, t_po = PB[5], t_PB[5]
                                k.op("pe", lambda e: e.matmul(out=po[:], lhsT=CT[:, tsl(c)], rhs=Sb[:], start=True, stop=True),
                                     reads=[t_BC, t_S], writes=[t_po])
                                y, t_y = y_r.next()
                                k.op("dve", lambda e: e.tensor_tensor(
                                    out=y[:].rearrange("p (r q) -> p r q", r=8), in0=po[:].rearrange("p (r q) -> p r q", r=8),
                                    in1=ea_all[:, c, gs].unsqueeze(2).to_broadcast([128, 8, 64]), op=ALU.mult),
                                     reads=[t_po, t_dt[c]], writes=[t_y])
                                k.op("dve", lambda e: e.tensor_tensor(out=y[:], in0=y[:], in1=py[:], op=ALU.add),
                                     reads=[t_y, t_py], writes=[t_y])
                                y2, t_y2 = y2_r.next()
                                k.op("pool", lambda e: e.tensor_tensor(
                                    out=y2[:].rearrange("p (r q) -> p r q", r=8), in0=xtok[:, c, :].rearrange("p (r q) -> p r q", r=8),
                                    in1=dsk[:, gs].unsqueeze(2).to_broadcast([128, 8, 64]), op=ALU.mult),
                                     reads=[t_xtok[c], t_par], writes=[t_y2])
                                k.op("dve", lambda e: e.tensor_tensor(out=y[:], in0=y[:], in1=y2[:], op=ALU.add),
                                     reads=[t_y, t_y2], writes=[t_y])
                                k.op("dve", lambda e: e.tensor_tensor(out=y[:], in0=y[:], in1=zs[:], op=ALU.mult),
                                     reads=[t_y, t_zs], writes=[t_y])
                                k.op("act", lambda e: e.activation(out=y2[:], in_=y[:], func=AF.Square, accum_out=sm[:, 8:9]),
                                     reads=[t_y], writes=[t_y2, t_sm])
                                k.op("act", lambda e: e.activation(out=sm[:, 9:10], in_=sm[:, 8:9], func=AF.Sqrt, scale=1.0 / 512,
                                                                   bias=float(LN_EPS)), reads=[t_sm], writes=[t_sm])
                                k.op("dve", lambda e: e.reciprocal(out=sm[:, 10:11], in_=sm[:, 9:10]), reads=[t_sm], writes=[t_sm])
                                yn, t_yn = yn_r.next()
                                k.op("dve", lambda e: e.tensor_scalar(out=yn[:], in0=y[:], scalar1=sm[:, 10:11], scalar2=None,
                                                                      op0=ALU.mult), reads=[t_y, t_sm], writes=[t_yn])
                                pT, t_pT = PB[6], t_PB[6]
                                for q in range(4):
                                    k.op("pe", lambda e: e.transpose(out=pT[:, q * 128:(q + 1) * 128], in_=yn[:, q * 128:(q + 1) * 128],
                                                                     identity=ident[:]), reads=[t_yn, t_const], writes=[t_pT])
                                ynT, t_ynT = ynT_r.next()
                                for q in range(4):
                                    k.op("act", lambda e: e.activation(out=ynT[:, q, :], in_=pT[:, q * 128:(q + 1) * 128], func=AF.Copy,
                                                                       scale=ng[:, g * 4 + q:g * 4 + q + 1]),
                                         reads=[t_pT, t_par], writes=[t_ynT])
                                k.dma("sp", ynT_d[g * 512:(g + 1) * 512, tsl(c)].rearrange("(q p) t -> p q t", p=128), ynT[:], ch_ynTd[c],
                                      reads=[t_ynT], writes=[t_ynTd[c]])
                                xw, t_xw = xw_r.next()
                                k.op("pool", lambda e: e.tensor_tensor(
                                    out=xw[:].rearrange("p (r q) -> p r q", r=8), in0=xdt[:].rearrange("p (r q) -> p r q", r=8),
                                    in1=seg[:, :, 127:128].to_broadcast([128, 8, 64]), op=ALU.mult),
                                     reads=[t_xdt, t_seg], writes=[t_xw])
                                pS, t_pS = PB[7], t_PB[7]
                                k.op("pe", lambda e: e.matmul(out=pS[:], lhsT=btok[:, c, :], rhs=xw[:], start=True, stop=True),
                                     reads=[t_BC, t_xw], writes=[t_pS])
                                k.op("dve", lambda e: e.tensor_tensor(
                                    out=S[:].rearrange("p (r q) -> p r q", r=8), in0=S[:].rearrange("p (r q) -> p r q", r=8),
                                    in1=sm[:, 0:8].unsqueeze(2).to_broadcast([128, 8, 64]), op=ALU.mult),
                                     reads=[t_sm, t_S], writes=[t_S])
                                k.op("dve", lambda e: e.tensor_tensor(out=S[:], in0=S[:], in1=pS[:], op=ALU.add),
                                     reads=[t_pS, t_S], writes=[t_S])
                                k.op("act", lambda e: e.copy(out=Sb[:], in_=S[:]), reads=[t_S], writes=[t_S])

                            pT, t_pT = PB[6], t_PB[6]
                            for q in range(4):
                                k.op("pe", lambda e: e.transpose(out=pT[:, q * 128:(q + 1) * 128], in_=S[:, q * 128:(q + 1) * 128],
                                                                 identity=ident[:]), reads=[t_S, t_const], writes=[t_pT])
                            so = k.sb("m_so", [128, 4, 128], F32, s3)
                            t_so = Tok()
                            k.op("act", lambda e: e.copy(out=so[:], in_=pT[:].rearrange("p (q n) -> p q n", q=4)), reads=[t_pT], writes=[t_so])
                            k.dma("sp", o_ssm_p[j, g * 512:(g + 1) * 512, :].rearrange("(q p) n -> p q n", p=128), so[:], k.dma_chan("ch_so"),
                                  reads=[t_so])
                        with k.scope() as s3:
                            chs = k.dma_chan("ch_smp")
                            csr = k.sb("ms_cs", [1, 3, 768], F32, s3)
                            cwr = k.sb("ms_cw", [1, 4, 768], F32, s3)
                            cbr = k.sb("ms_cb", [1, 768], F32, s3)
                            t_sp = Tok()
                            segs = ((0, 512, g * 512), (512, 640, 4096 + g * 128), (640, 768, 5120 + g * 128))
                            for (a0, a1, c0) in segs:
                                k.dma("sp", csr[0:1, :, a0:a1], m_cs[j:j + 1, :, c0:c0 + (a1 - a0)], chs, writes=[t_sp])
                                k.dma("sp", cwr[0:1, :, a0:a1], m_cwr[j:j + 1, :, c0:c0 + (a1 - a0)], chs, writes=[t_sp])
                                k.dma("sp", cbr[0:1, a0:a1], m_cbr[j:j + 1, c0:c0 + (a1 - a0)], chs, writes=[t_sp])
                            for (a0, a1, c0) in segs:
                                k.dma("sp", o_conv_s[j:j + 1, 0:2, c0:c0 + (a1 - a0)], csr[0:1, 1:3, a0:a1], chs, reads=[t_sp])
                                k.dma("sp", o_conv_s[j:j + 1, 2, c0:c0 + (a1 - a0)], us[0:1, a0:a1], chs, reads=[t_us])
                            xr = k.sb("ms_xr", [1, 768], F32, s3)
                            t_xr = Tok()
                            k.op("dve", lambda e: e.tensor_tensor(out=xr[:], in0=us[:], in1=cwr[0:1, 3, :], op=ALU.mult),
                                 reads=[t_us, t_sp], writes=[t_xr])
                            for kk in range(3):
                                tmpr = k.sb(f"ms_tr{kk}", [1, 768], F32, s3)
                                t_tr = Tok()
                                k.op("dve", lambda e: e.tensor_tensor(out=tmpr[:], in0=csr[0:1, kk, :], in1=cwr[0:1, kk, :], op=ALU.mult),
                                     reads=[t_sp], writes=[t_tr])
                                k.op("dve", lambda e: e.tensor_tensor(out=xr[:], in0=xr[:], in1=tmpr[:], op=ALU.add),
                                     reads=[t_tr, t_xr], writes=[t_xr])
                            k.op("dve", lambda e: e.tensor_tensor(out=xr[:], in0=xr[:], in1=cbr[:], op=ALU.add), reads=[t_xr, t_sp], writes=[t_xr])
                            k.op("act", lambda e: e.activation(out=xr[:], in_=xr[:], func=AF.Silu), reads=[t_xr], writes=[t_xr])
                            rep = k.sb("ms_rep", [1, 4, 512], F32, s3)
                            t_rep = Tok()
                            for qi, srcrow in enumerate((dt_all[0:1, NT, gs8(g)], la_all[0:1, NT, gs8(g)], dsk[0:1, gs8(g)])):
                                k.op("dve", lambda e: e.tensor_copy(out=rep[0:1, qi, :].rearrange("o (r q) -> o r q", r=8),
                                                                    in_=srcrow.unsqueeze(2).to_broadcast([1, 8, 64])),
                                     reads=[t_dt[NT], t_par], writes=[t_rep])
                            pz, t_pz = PB[0], t_PB[0]
                            for kc in range(DC):
                                k.op("pe", lambda e: e.matmul(out=pz[0:1, :], lhsT=hT[:, kc, SEQ:SEQ + 1], rhs=wz[:, kc, :],
                                                              start=(kc == 0), stop=(kc == DC - 1)), reads=[t_hT[NT], t_wz], writes=[t_pz])
                            k.op("act", lambda e: e.activation(out=rep[0:1, 3, :], in_=pz[0:1, :], func=AF.Silu), reads=[t_pz], writes=[t_rep])
                            pc_, t_pc_ = PB[1], t_PB[1]
                            for cc in range(4):
                                k.op("pe", lambda e: e.matmul(out=pc_[:, cc:cc + 1], lhsT=xr[0:1, cc * 128:(cc + 1) * 128], rhs=ones[0:1, 0:1],
                                                              start=True, stop=True), reads=[t_xr, t_const], writes=[t_pc_])
                                for qi in range(4):
                                    k.op("pe", lambda e: e.matmul(out=pc_[:, 4 + qi * 4 + cc:5 + qi * 4 + cc],
                                                                  lhsT=rep[0:1, qi, cc * 128:(cc + 1) * 128], rhs=ones[0:1, 0:1],
                                                                  start=True, stop=True), reads=[t_rep, t_const], writes=[t_pc_])
                            pbc, t_pbc = PB[2], t_PB[2]
                            k.op("pe", lambda e: e.matmul(out=pbc[:, 0:256], lhsT=ones[0:1, :], rhs=xr[0:1, 512:768], start=True, stop=True),
                                 reads=[t_xr, t_const], writes=[t_pbc])
                            cols = k.sb("ms_cols", [128, 32], F32, s3)
                            t_cols = Tok()
                            k.op("act", lambda e: e.copy(out=cols[:, 0:20], in_=pc_[:, 0:20]), reads=[t_pc_], writes=[t_cols])
                            bcb = k.sb("ms_bcb", [128, 256], F32, s3)
                            t_bcb = Tok()
                            k.op("act", lambda e: e.copy(out=bcb[:], in_=pbc[:, 0:256]), reads=[t_pbc], writes=[t_bcb])
                            k.op("act", lambda e: e.activation(out=cols[:, 20:24], in_=cols[:, 8:12], func=AF.Exp), reads=[t_cols], writes=[t_cols])
                            k.op("dve", lambda e: e.tensor_tensor(out=cols[:, 24:28], in0=cols[:, 0:4], in1=cols[:, 4:8], op=ALU.mult),
                                 reads=[t_cols], writes=[t_cols])
                            Ss = k.sb("ms_S", [128, 4, 128], F32, s3)
                            t_Ss = Tok()
                            k.dma("sp", Ss[:], m_ss[j, g * 512:(g + 1) * 512, :].rearrange("(c p) n -> p c n", p=128), chs, writes=[t_Ss])
                            tmpS = k.sb("ms_tS", [128, 128], F32, s3)
                            t_tS = Tok()
                            for cc in range(4):
                                k.op("dve", lambda e: e.tensor_scalar(out=tmpS[:], in0=bcb[:, 0:128], scalar1=cols[:, 24 + cc:25 + cc], scalar2=None,
                                                                      op0=ALU.mult), reads=[t_bcb, t_cols], writes=[t_tS])
                                k.op("dve", lambda e: e.scalar_tensor_tensor(out=Ss[:, cc, :], in0=Ss[:, cc, :], scalar=cols[:, 20 + cc:21 + cc],
                                                                             in1=tmpS[:], op0=ALU.mult, op1=ALU.add),
                                     reads=[t_Ss, t_tS, t_cols], writes=[t_Ss])
                                k.op("dve", lambda e: e.tensor_tensor(out=tmpS[:], in0=Ss[:, cc, :], in1=bcb[:, 128:256], op=ALU.mult),
                                     reads=[t_Ss, t_bcb], writes=[t_tS])
                                k.op("dve", lambda e: e.reduce_sum(out=cols[:, 28 + cc:29 + cc], in_=tmpS[:], axis=AX.X), reads=[t_tS], writes=[t_cols])
                            k.dma("sp", o_ssm_s[j, g * 512:(g + 1) * 512, :].rearrange("(c p) n -> p c n", p=128), Ss[:], chs, reads=[t_Ss])
                            ys = k.sb("ms_y", [128, 8], F32, s3)
                            t_ys = Tok()
                            k.op("dve", lambda e: e.tensor_tensor(out=ys[:, 0:4], in0=cols[:, 12:16], in1=cols[:, 0:4], op=ALU.mult), reads=[t_cols], writes=[t_ys])
                            k.op("dve", lambda e: e.tensor_tensor(out=ys[:, 0:4], in0=ys[:, 0:4], in1=cols[:, 28:32], op=ALU.add), reads=[t_cols, t_ys], writes=[t_ys])
                            k.op("dve", lambda e: e.tensor_tensor(out=ys[:, 0:4], in0=ys[:, 0:4], in1=cols[:, 16:20], op=ALU.mult), reads=[t_cols, t_ys], writes=[t_ys])
                            k.op("dve", lambda e: e.tensor_tensor(out=ys[:, 4:8], in0=ys[:, 0:4], in1=ys[:, 0:4], op=ALU.mult), reads=[t_ys], writes=[t_ys])
                            k.op("dve", lambda e: e.reduce_sum(out=cols[:, 0:1], in_=ys[:, 4:8], axis=AX.X), reads=[t_ys], writes=[t_cols])
                            pss, t_pss = PB[3], t_PB[3]
                            k.op("pe", lambda e: e.matmul(out=pss[:, 0:1], lhsT=ones[:], rhs=cols[:, 0:1], start=True, stop=True),
                                 reads=[t_cols, t_const], writes=[t_pss])
                            k.op("act", lambda e: e.activation(out=cols[:, 1:2], in_=pss[:, 0:1], func=AF.Sqrt, scale=1.0 / 512, bias=float(LN_EPS)),
                                 reads=[t_pss], writes=[t_cols])
                            k.op("dve", lambda e: e.reciprocal(out=cols[:, 2:3], in_=cols[:, 1:2]), reads=[t_cols], writes=[t_cols])
                            k.op("dve", lambda e: e.tensor_scalar(out=ys[:, 0:4], in0=ys[:, 0:4], scalar1=cols[:, 2:3], scalar2=None, op0=ALU.mult),
                                 reads=[t_cols, t_ys], writes=[t_ys])
                            ysb = k.sb("ms_yb", [128, 4], BF16, s3)
                            t_ysb = Tok()
                            k.op("dve", lambda e: e.tensor_tensor(out=ysb[:], in0=ys[:, 0:4], in1=ng[:, g * 4:(g + 1) * 4], op=ALU.mult),
                                 reads=[t_ys, t_par], writes=[t_ysb])
                            k.dma("sp", ynT_d[g * 512:(g + 1) * 512, SEQ:SEQ + 1].rearrange("(c p) o -> p (c o)", p=128), ysb[:], ch_ynTd[NT],
                                  reads=[t_ysb], writes=[t_ynTd[NT]], allow_slow_non_contiguous=True)
            def get_aT(tt, st, state):
                if "ring" not in state:
                    state["ring"] = Ring(k, "m_aT", 2, [128, 32, 128], BF16, stack=st, chan=True)
                a, t_a, cha = state["ring"].next()
                k.dma("sp", a[:], ynT_d[:, tsl(tt)].rearrange("(c p) t -> p c t", p=128), cha, reads=[t_ynTd[tt]], writes=[t_a])
                return a, t_a
            if os.environ.get('M_PROJ', '1') == '1':
                proj_ln(get_aT, 32, m_wout[j], li, 0)

        NL = int(os.environ.get("K_LAYERS", "4"))
        for li in range(NL):
            kind, j = li % 3, li // 3
            with k.scope():
                if kind == 0:
                    mamba_layer(j, li)
                elif kind == 1:
                    fox_layer(li)
                else:
                    nsa_layer(li)
            with k.scope():
                memattn_layer(li)
            with k.scope():
                peer_layer(li)
        if NL == 2:
            with k.scope():
                nsa_layer(2, attn=False)
            with k.scope():
                memattn_layer(2, only_kv=True)
                memattn_layer(3, only_kv=True)

        for tt in range(TT):
            k.dma("sp", o_y[tsl(tt), :], res[tsl(tt), :], ch_dbg, reads=[t_res[tt]])
        if dbg:
            for tt in range(TT):
                k.dma("sp", o_dbg[tsl(tt), :], res[tsl(tt), :], ch_dbg, reads=[t_res[tt]])
        k.finish()
        print("instructions:", k.nins)
    return nc, in_names


STAGES = tuple(os.environ.get("K_STAGES", "init,mamba0,mem,peer,fox,nsa").split(","))
DBG = os.environ.get("K_DBG", "0") == "1"
_PROG = None
_LAST = {}


def _bc(v, n=128):
    return np.ascontiguousarray(np.broadcast_to(np.asarray(v, np.float32)[None, :], (n, v.shape[-1])))


def kernel(**inputs):
    global _PROG
    f32 = np.float32
    x_prompt = np.asarray(inputs["x_prompt"])
    B = x_prompt.shape[0]
    if _PROG is None:
        _PROG = build_program(STAGES, DBG)
    nc, in_names = _PROG
    shared = {"ident": np.eye(128, dtype=f32), "trit": np.triu(np.ones((128, 128), f32))}
    if "ln_g" in in_names:
        lg = np.asarray(inputs["ln_g"], f32)
        lb = np.asarray(inputs["ln_b"], f32)
        shared["ln_g"] = np.ascontiguousarray(np.broadcast_to(lg[:, :, None, :], (DEPTH, 3, 128, D)))
        shared["ln_b"] = np.ascontiguousarray(np.broadcast_to(lb[:, :, None, :], (DEPTH, 3, 128, D)))
    if "mem_wkv" in in_names:
        shared["mem_wkv"] = np.ascontiguousarray(inputs["mem_wkv"], dtype=f32)
    if "mem_wq" in in_names:
        shared["mem_wq"] = np.ascontiguousarray(inputs["mem_wq"], dtype=f32)
        shared["mem_wo"] = np.ascontiguousarray(inputs["mem_wo"], dtype=f32)
    if "peer_wq" in in_names:
        shared["peer_wq"] = np.ascontiguousarray(inputs["peer_wq"], dtype=f32)
        sk = np.asarray(inputs["peer_subkeys"], f32)
        shared["p_skT"] = np.ascontiguousarray(sk.reshape(DEPTH, 16, 128, 128).transpose(0, 3, 1, 2))
        npl = int(os.environ.get("K_LAYERS", "4"))
        shared["p_uT"] = np.ascontiguousarray(np.asarray(inputs["peer_u"][:npl], f32).transpose(0, 2, 1))
        shared["peer_v"] = np.ascontiguousarray(inputs["peer_v"][:npl], dtype=f32)
    if "iota" in in_names and "f_win" not in in_names:
        shared["iota"] = np.arange(128, dtype=f32)[:, None]
        m16 = np.zeros((16, 4), f32)
        m16[np.arange(16), np.arange(16) // 4] = 1.0
        shared["mask16"] = m16
    if "f_win" in in_names:
        shared["f_win"] = np.ascontiguousarray(inputs["fox_w_in"][0], dtype=f32)
        shared["f_wout"] = np.ascontiguousarray(inputs["fox_w_out"][0], dtype=f32)
        shared["f_bf"] = _bc(np.asarray(inputs["fox_b_f"][0], f32))
        s16 = np.zeros((16, 16, 128), f32)
        for hh in range(16):
            s16[hh, hh, :] = 1.0
        shared["sel16"] = s16.reshape(16, 2048)
        ckv = np.asarray(inputs["cache_fox_kv"][0], f32).reshape(1280 * 128, 2, 512)
        shared["f_kpool"] = np.ascontiguousarray(ckv[:, 0])
        shared["f_vpool"] = np.ascontiguousarray(ckv[:, 1])
        shared["f_lpool"] = np.ascontiguousarray(np.asarray(inputs["cache_fox_logf"][0], f32).reshape(1280 * 128, 16))
        shared["iota"] = np.arange(128, dtype=f32)[:, None]
        shared["tris"] = np.tril(np.ones((128, 128), f32), -1)
        m16 = np.zeros((16, 4), f32)
        m16[np.arange(16), np.arange(16) // 4] = 1.0
        shared["mask16"] = m16
        shared["mneg"] = np.where(np.arange(128)[None, :] <= np.arange(128)[:, None], 0.0, -1e30).astype(f32)
    if "n_win" in in_names:
        shared["n_win"] = np.ascontiguousarray(inputs["nsa_w_in"][0], dtype=f32)
        shared["n_wout"] = np.ascontiguousarray(inputs["nsa_w_out"][0], dtype=f32)
        cn = np.asarray(inputs["cache_nsa_kv"][0], f32).reshape(1280 * 128, 4, 512)
        for q in range(4):
            shared[f"n_pool{q}"] = np.ascontiguousarray(cn[:, q])
        wpos_ = np.asarray(inputs["nsa_cmp_wpos"][0], f32)
        wps = np.zeros((2, 4, 128, 17, 128), f32)
        for slot in range(16):
            for jj in range(8):
                col = 8 * slot + jj
                nrow = min(32, 128 - 16 * jj)
                wps[:, :, 16 * jj:16 * jj + nrow, slot, col] = wpos_[:, :nrow, :].transpose(0, 2, 1)
            if slot > 0:
                wps[:, :, 0:16, slot, 8 * slot - 1] = wpos_[:, 16:32, :].transpose(0, 2, 1)
        wps[:, :, 0:16, 16, 127] = wpos_[:, 16:32, :].transpose(0, 2, 1)
        shared["n_wps"] = wps
        nall = np.arange(9 * 128)
        mall = np.arange(257)
        ovs = ((16 * nall[:, None] < 64 * mall[None, :] + 64) & (16 * nall[:, None] + 32 > 64 * mall[None, :]) & (nall[:, None] < 1027)).astype(f32)
        shared["n_ovs"] = np.ascontiguousarray(ovs.reshape(9, 128, 257))
        frow = np.zeros((257,), f32)
        frow[[0, 255, 256]] = 1e6
        shared["n_fs"] = np.ascontiguousarray(np.broadcast_to(frow[None, :], (16, 257)))
        shared["n_gs"] = (np.arange(16)[:, None] // 4 == np.arange(16)[None, :] // 4).astype(f32)
        wpos = np.asarray(inputs["nsa_cmp_wpos"][0], f32)
        wp = np.zeros((2, 4, SEQ, 128), f32)
        for n in range(127):
            wp[:, :, 16 * n:16 * n + 32, n] = wpos.transpose(0, 2, 1)
        shared["n_wp"] = wp
        shared["n_proj"] = np.ascontiguousarray(inputs["nsa_cmp_proj"][0], dtype=f32)
        shared["n_bg"] = _bc(np.asarray(inputs["nsa_b_gate"][0], f32))
        tpos = np.arange(SEQ)
        nn = np.arange(128)
        shared["n_cm01"] = (((16 * nn[None, :] + 31) <= tpos[:, None]) & (nn[None, :] < 127)).astype(f32)
        mm = np.arange(32)
        ovl = ((16 * nn[:, None] < 64 * mm[None, :] + 64) & (16 * nn[:, None] + 32 > 64 * mm[None, :]) & (nn[:, None] < 127)).astype(f32)
        shared["n_ov"] = ovl
        qblk = tpos // 64
        forced = (mm[None, :] == 0) | (mm[None, :] == qblk[:, None]) | (mm[None, :] == qblk[:, None] - 1)
        valid = mm[None, :] <= qblk[:, None]
        fvv = np.zeros((SEQ, 3, 32), f32)
        fvv[:, 0] = forced * 1e6
        fvv[:, 1] = valid
        fvv[:, 2] = np.where(valid, 0.0, -1e30)
        shared["n_fv"] = fvv
        ea = np.zeros((33, SEQ), f32)
        ea[tpos // 64, tpos] = 32768.0
        ea[32, :] = -32768.0
        shared["n_eaug"] = ea
        loc = np.arange(128)
        shared["n_mfar"] = np.where(loc[None, :] > loc[:, None], 0.0, -1e30).astype(f32)
        shared["mneg2"] = np.where(loc[None, :] <= loc[:, None], 0.0, -1e30).astype(f32)
    if "m_win" in in_names:
        shared["m_win"] = np.ascontiguousarray(inputs["mamba_w_in"], dtype=f32)
        shared["m_wout"] = np.ascontiguousarray(inputs["mamba_w_out"], dtype=f32)
        cw = np.asarray(inputs["mamba_conv_w"], f32)
        shared["m_convw"] = np.ascontiguousarray(cw.reshape(2, 4, 48, 128).transpose(0, 3, 2, 1))
        shared["m_convb"] = np.ascontiguousarray(np.asarray(inputs["mamba_conv_b"], f32).reshape(2, 48, 128).transpose(0, 2, 1))
        shared["m_cwr"] = np.ascontiguousarray(cw)
        shared["m_cbr"] = np.ascontiguousarray(inputs["mamba_conv_b"], dtype=f32)
        shared["m_dtb"] = np.stack([_bc(inputs["mamba_dt_bias"][j]) for j in range(2)])
        shared["m_alog"] = np.stack([_bc(inputs["mamba_a_log"][j]) for j in range(2)])
        shared["m_dsk"] = np.stack([_bc(inputs["mamba_d"][j]) for j in range(2)])
        shared["m_ng"] = np.ascontiguousarray(np.asarray(inputs["mamba_norm_g"], f32).reshape(2, 32, 128).transpose(0, 2, 1))
    in_maps = []
    for c in range(8):
        b = c % B
        m = {"xp": np.ascontiguousarray(x_prompt[b]), "xs": np.ascontiguousarray(inputs["x_sample"][c]),
             "memp": np.ascontiguousarray(inputs["mem_prompt"][b])}
        if "cmk" in in_names:
            m["cmk"] = np.ascontiguousarray(np.asarray(inputs["cache_mem_kv"])[:, c].reshape(DEPTH, MEM, 2 * D))
        if "pt" in in_names:
            m["pt"] = np.ascontiguousarray(np.asarray(inputs["page_table"])[c:c + 1].astype(np.int32))
        if "n_winbuf" in in_names:
            m["n_winbuf"] = np.ascontiguousarray(np.asarray(inputs["state_nsa_win"], f32)[0, c].reshape(512, 1024))
        if "m_cs" in in_names:
            m["m_cs"] = np.ascontiguousarray(np.asarray(inputs["state_conv"], f32)[:, c])
            m["m_ss"] = np.ascontiguousarray(np.asarray(inputs["state_ssm"], f32)[:, c].reshape(2, M_DI, 128))
        m.update(shared)
        in_maps.append({n: m[n] for n in in_names})
    res = run_bass_kernel_spmd(nc, in_maps, core_ids=list(range(8)))
    R = res.results
    _LAST["R"] = R
    mem_kv_p = np.stack([R[b]["o_memkv"] for b in range(B)], axis=1).reshape(DEPTH, B, MEM, 2, 4, 512)
    ssm_p = np.stack([R[b]["o_ssm_p"] for b in range(B)], axis=1).reshape(2, B, 64, 64, 128)
    conv_p = np.stack([R[b]["o_conv_p"] for b in range(B)], axis=1).reshape(2, B, 3, 6144)

    ssm_s = np.stack([R[c]["o_ssm_s"] for c in range(8)], axis=1).reshape(2, 8, 64, 64, 128)
    conv_s = np.stack([R[c]["o_conv_s"] for c in range(8)], axis=1).reshape(2, 8, 3, 6144)

    fox_kv_p = fox_lf_p = fox_kv_s = fox_lf_s = None
    if "o_fox_kv" in R[0]:
        fox_kv_p = np.stack([R[b]["o_fox_kv"][:SEQ] for b in range(B)]).reshape(1, B, SEQ, 2, 4, 128)
        fox_lf_p = np.stack([R[b]["o_fox_lf"][:SEQ] for b in range(B)]).reshape(1, B, SEQ, 16)
        fox_kv_s = np.stack([R[c]["o_fox_kv"][SEQ:SEQ + 1] for c in range(8)]).reshape(1, 8, 1, 2, 4, 128)
        fox_lf_s = np.stack([R[c]["o_fox_lf"][SEQ:SEQ + 1] for c in range(8)]).reshape(1, 8, 1, 16)

    nsa_kv_p = nsa_win_p = nsa_kv_s = nsa_win_s = None
    if "o_nsa_kv" in R[0]:
        nsa_kv_p = np.stack([R[b]["o_nsa_kv"][:SEQ] for b in range(B)]).reshape(1, B, SEQ, 4, 4, 128)
        nsa_win_p = np.stack([R[b]["o_nsa_win_p"] for b in range(B)]).reshape(1, B, 512, 2, 4, 128)
        nsa_kv_s = np.stack([R[c]["o_nsa_kv"][SEQ:SEQ + 1] for c in range(8)]).reshape(1, 8, 1, 4, 4, 128)
        nsa_win_s = np.stack([R[c]["o_nsa_win_s"] for c in range(8)]).reshape(1, 8, 512, 2, 4, 128)

    def z(*s):
        return np.zeros(s, f32)

    y_p = np.stack([R[b]["o_y"][:SEQ] for b in range(B)])
    y_s = np.stack([R[c]["o_y"][SEQ:SEQ + 1] for c in range(8)])
    return (y_p, y_s, mem_kv_p, ssm_p, conv_p,
            fox_kv_p if fox_kv_p is not None else z(1, 4, 2048, 2, 4, 128), fox_lf_p if fox_lf_p is not None else z(1, 4, 2048, 16),
            nsa_kv_p if nsa_kv_p is not None else z(1, 4, 2048, 4, 4, 128), nsa_win_p if nsa_win_p is not None else z(1, 4, 512, 2, 4, 128),
            ssm_s, conv_s, fox_kv_s if fox_kv_s is not None else z(1, 8, 1, 2, 4, 128),
            fox_lf_s if fox_lf_s is not None else z(1, 8, 1, 16), nsa_kv_s if nsa_kv_s is not None else z(1, 8, 1, 4, 4, 128),
            nsa_win_s if nsa_win_s is not None else z(1, 8, 512, 2, 4, 128))
```
